# Optimizing a Trainium2 kernel written in Bass

```python
import math
import jax, jax.numpy as jnp
from jax import lax
import numpy as np

D_MODEL = 2048
BATCH = 4
SEQ = 4096
DEPTH = 4

N_A_LAYERS = DEPTH // 2
N_B_LAYERS = DEPTH - N_A_LAYERS
MIX_WIDTH = D_MODEL
N_MEM = 256
MEM_HEADS = 4
MEM_HEAD_DIM = 128
MEM_WIDTH = MEM_HEADS * MEM_HEAD_DIM
TOK_WIDTH = MIX_WIDTH - MEM_WIDTH
S5_GROUP = 16
S5_GROUPS = TOK_WIDTH // S5_GROUP
S5_STATE = 64
S5_DT_MIN = 1e-3
S5_DT_MAX = 1e-1
MLA_NOPE = 128
MLA_ROPE = 64
MLA_V = 128
MLA_HEADS = TOK_WIDTH // MLA_V
MLA_Q_RANK = 512
MLA_KV_RANK = 512
ROPE_THETA = 10000.0
D_FF = 5632
Q_BLOCK = 128
EPS = 1e-6

kernel_name = "yoco_s5_mla_macaron_memory_trunk"


def rmsnorm(x, g):
    xf = x.astype(jnp.float32)
    y = xf * lax.rsqrt(jnp.mean(xf * xf, axis=-1, keepdims=True) + EPS)
    return (y * g.astype(jnp.float32)).astype(x.dtype)


def swiglu(h, w_gate, w_up, w_down):
    return (jax.nn.silu(h @ w_gate) * (h @ w_up)) @ w_down


def rope_tables(positions):
    inv_freq = ROPE_THETA ** (-jnp.arange(0, MLA_ROPE, 2, dtype=jnp.float32) / MLA_ROPE)
    ang = positions.astype(jnp.float32)[..., None] * inv_freq
    return jnp.cos(ang), jnp.sin(ang)


def apply_rope(t, cos, sin):
    half = t.shape[-1] // 2
    tf = t.astype(jnp.float32)
    t1, t2 = tf[..., :half], tf[..., half:]
    return jnp.concatenate([t1 * cos - t2 * sin, t1 * sin + t2 * cos], axis=-1).astype(t.dtype)


def s5_mix(u, lam_re, lam_im, b_re, b_im, c_re, c_im, d, log_dt, w_glu, b_glu):
    bsz, seq, _ = u.shape
    f32 = jnp.float32
    uf = u.astype(f32).reshape(bsz, seq, S5_GROUPS, S5_GROUP)
    lam = lax.complex(lam_re.astype(f32), lam_im.astype(f32))
    dt = jnp.exp(log_dt.astype(f32))[:, None]
    lam_bar = jnp.exp(lam * dt)
    b = lax.complex(b_re.astype(f32), b_im.astype(f32))
    b_bar = ((lam_bar - 1.0) / lam)[..., None] * b
    bu = jnp.einsum('gpc,bsgc->bsgp', b_bar, uf.astype(jnp.complex64))
    a = jnp.broadcast_to(lam_bar, bu.shape)

    def combine(left, right):
        a_l, b_l = left
        a_r, b_r = right
        return a_r * a_l, a_r * b_l + b_r

    _, states = lax.associative_scan(combine, (a, bu), axis=1)
    c = lax.complex(c_re.astype(f32), c_im.astype(f32))
    y = jnp.real(jnp.einsum('gcp,bsgp->bsgc', c, states)) + d.astype(f32).reshape(S5_GROUPS, S5_GROUP) * uf
    y = jax.nn.gelu(y.reshape(bsz, seq, TOK_WIDTH))
    y = y * jax.nn.sigmoid(y @ w_glu.astype(f32) + b_glu.astype(f32))
    return y.astype(u.dtype)


def shared_latent_kv(x, kv_in_norm, w_dkv, kv_norm, w_uk, w_uv, w_kr, cos, sin):
    bsz, seq, _ = x.shape
    h = rmsnorm(x, kv_in_norm)
    c_kv = rmsnorm(h @ w_dkv, kv_norm)
    k_nope = (c_kv @ w_uk).reshape(bsz, seq, MLA_HEADS, MLA_NOPE)
    v = (c_kv @ w_uv).reshape(bsz, seq, MLA_HEADS, MLA_V)
    k_rope = apply_rope(h @ w_kr, cos, sin)
    return k_nope, k_rope, v


def mla_attend(q_nope, q_rope, k_nope, k_rope, v):
    bsz, seq = q_nope.shape[0], q_nope.shape[1]
    scale = (MLA_NOPE + MLA_ROPE) ** -0.5
    outs = []
    for start in range(0, seq, Q_BLOCK):
        end = start + Q_BLOCK
        s = (jnp.einsum('bqhd,bkhd->bhqk', q_nope[:, start:end], k_nope[:, :end])
             + jnp.einsum('bqhr,bkr->bhqk', q_rope[:, start:end], k_rope[:, :end]))
        s = s.astype(jnp.float32) * scale
        qi = jnp.arange(start, end)[:, None]
        ki = jnp.arange(end)[None, :]
        s = jnp.where(ki <= qi, s, -jnp.inf)
        p = jax.nn.softmax(s, axis=-1).astype(v.dtype)
        outs.append(jnp.einsum('bhqk,bkhd->bqhd', p, v[:, :end]))
    o = jnp.concatenate(outs, axis=1)
    return o.reshape(bsz, seq, MLA_HEADS * MLA_V)


def mem_attend(q, mem_k, mem_v):
    bsz, seq, _ = q.shape
    qh = q.reshape(bsz, seq, MEM_HEADS, MEM_HEAD_DIM)
    s = jnp.einsum('bqhd,bkhd->bhqk', qh, mem_k).astype(jnp.float32) * (MEM_HEAD_DIM ** -0.5)
    p = jax.nn.softmax(s, axis=-1).astype(mem_v.dtype)
    return jnp.einsum('bhqk,bkhd->bqhd', p, mem_v).reshape(bsz, seq, MEM_WIDTH)


def setup_inputs(seed: int = 0) -> dict:
    key = jax.random.key(seed)
    ks = iter(jax.random.split(key, 40))
    f32 = jnp.float32

    def nrm(shape, fan_in):
        return jax.random.normal(next(ks), shape, f32) * (fan_in ** -0.5)

    def gain(shape):
        return 1.0 + 0.02 * jax.random.normal(next(ks), shape, f32)

    x = jax.random.normal(next(ks), (BATCH, SEQ, D_MODEL), f32)
    mem = jax.random.normal(next(ks), (BATCH, N_MEM, D_MODEL), f32)
    offset = jax.random.randint(next(ks), (BATCH, 1), 0, 1024, dtype=jnp.int32)
    positions = offset + jnp.arange(SEQ, dtype=jnp.int32)[None, :]

    n_idx = jnp.arange(S5_STATE, dtype=f32)[None, None, :]
    lam_re = -0.5 + 0.01 * jax.random.normal(next(ks), (N_A_LAYERS, S5_GROUPS, S5_STATE), f32)
    lam_im = math.pi * n_idx + 0.01 * jax.random.normal(next(ks), (N_A_LAYERS, S5_GROUPS, S5_STATE), f32)
    log_dt = jax.random.uniform(next(ks), (N_A_LAYERS, S5_GROUPS), f32,
                                math.log(S5_DT_MIN), math.log(S5_DT_MAX))

    return {
        'x': x,
        'mem': mem,
        'positions': positions,
        'norms': gain((DEPTH, 6, D_MODEL)),
        'ffn_w_gate': nrm((DEPTH, 2, D_MODEL, D_FF), D_MODEL),
        'ffn_w_up': nrm((DEPTH, 2, D_MODEL, D_FF), D_MODEL),
        'ffn_w_down': nrm((DEPTH, 2, D_FF, D_MODEL), D_FF),
        'w_out': nrm((DEPTH, MIX_WIDTH, D_MODEL), MIX_WIDTH),
        'mem_norm': gain((DEPTH, D_MODEL)),
        'mem_w_kv': nrm((DEPTH, D_MODEL, 2 * MEM_WIDTH), D_MODEL),
        'a_w_in': nrm((N_A_LAYERS, D_MODEL, TOK_WIDTH + MEM_WIDTH), D_MODEL),
        's5_lambda_re': lam_re,
        's5_lambda_im': lam_im,
        's5_b_re': nrm((N_A_LAYERS, S5_GROUPS, S5_STATE, S5_GROUP), 2 * S5_GROUP),
        's5_b_im': nrm((N_A_LAYERS, S5_GROUPS, S5_STATE, S5_GROUP), 2 * S5_GROUP),
        's5_c_re': nrm((N_A_LAYERS, S5_GROUPS, S5_GROUP, S5_STATE), 2 * S5_STATE),
        's5_c_im': nrm((N_A_LAYERS, S5_GROUPS, S5_GROUP, S5_STATE), 2 * S5_STATE),
        's5_d': jax.random.normal(next(ks), (N_A_LAYERS, TOK_WIDTH), f32),
        's5_log_dt': log_dt,
        's5_w_glu': nrm((N_A_LAYERS, TOK_WIDTH, TOK_WIDTH), TOK_WIDTH),
        's5_b_glu': 0.01 * jax.random.normal(next(ks), (N_A_LAYERS, TOK_WIDTH), f32),
        'b_w_in': nrm((N_B_LAYERS, D_MODEL, MLA_Q_RANK + MEM_WIDTH), D_MODEL),
        'mla_q_norm': gain((N_B_LAYERS, MLA_Q_RANK)),
        'mla_w_uq': nrm((N_B_LAYERS, MLA_Q_RANK, MLA_HEADS * (MLA_NOPE + MLA_ROPE)), MLA_Q_RANK),
        'kv_in_norm': gain((D_MODEL,)),
        'w_dkv': nrm((D_MODEL, MLA_KV_RANK), D_MODEL),
        'kv_norm': gain((MLA_KV_RANK,)),
        'w_uk': nrm((MLA_KV_RANK, MLA_HEADS * MLA_NOPE), MLA_KV_RANK),
        'w_uv': nrm((MLA_KV_RANK, MLA_HEADS * MLA_V), MLA_KV_RANK),
        'w_kr': nrm((D_MODEL, MLA_ROPE), D_MODEL),
    }


def reference(x, mem, positions, norms, ffn_w_gate, ffn_w_up, ffn_w_down, w_out,
              mem_norm, mem_w_kv, a_w_in, s5_lambda_re, s5_lambda_im, s5_b_re, s5_b_im,
              s5_c_re, s5_c_im, s5_d, s5_log_dt, s5_w_glu, s5_b_glu, b_w_in, mla_q_norm,
              mla_w_uq, kv_in_norm, w_dkv, kv_norm, w_uk, w_uv, w_kr):
    bsz, seq, _ = x.shape
    n_mem = mem.shape[1]
    cos, sin = rope_tables(positions)
    cos_h, sin_h = cos[:, :, None, :], sin[:, :, None, :]
    k_nope = k_rope = v_shared = None

    for l in range(DEPTH):
        if l == N_A_LAYERS:
            k_nope, k_rope, v_shared = shared_latent_kv(x, kv_in_norm, w_dkv, kv_norm,
                                                        w_uk, w_uv, w_kr, cos, sin)
        g = norms[l]
        h = rmsnorm(x, g[0])
        x = x + 0.5 * rmsnorm(swiglu(h, ffn_w_gate[l, 0], ffn_w_up[l, 0], ffn_w_down[l, 0]), g[1])

        mkv = rmsnorm(mem, mem_norm[l]) @ mem_w_kv[l]
        mem_k = mkv[..., :MEM_WIDTH].reshape(bsz, n_mem, MEM_HEADS, MEM_HEAD_DIM)
        mem_v = mkv[..., MEM_WIDTH:].reshape(bsz, n_mem, MEM_HEADS, MEM_HEAD_DIM)

        h = rmsnorm(x, g[2])
        if l < N_A_LAYERS:
            z = h @ a_w_in[l]
            tok = s5_mix(z[..., :TOK_WIDTH], s5_lambda_re[l], s5_lambda_im[l], s5_b_re[l],
                         s5_b_im[l], s5_c_re[l], s5_c_im[l], s5_d[l], s5_log_dt[l],
                         s5_w_glu[l], s5_b_glu[l])
            q_mem = z[..., TOK_WIDTH:]
        else:
            j = l - N_A_LAYERS
            z = h @ b_w_in[j]
            c_q = rmsnorm(z[..., :MLA_Q_RANK], mla_q_norm[j])
            q = (c_q @ mla_w_uq[j]).reshape(bsz, seq, MLA_HEADS, MLA_NOPE + MLA_ROPE)
            q_nope = q[..., :MLA_NOPE]
            q_rope = apply_rope(q[..., MLA_NOPE:], cos_h, sin_h)
            tok = mla_attend(q_nope, q_rope, k_nope, k_rope, v_shared)
            q_mem = z[..., MLA_Q_RANK:]
        mem_o = mem_attend(q_mem, mem_k, mem_v)
        o = jnp.concatenate([tok, mem_o], axis=-1) @ w_out[l]
        x = x + rmsnorm(o, g[3])

        h = rmsnorm(x, g[4])
        x = x + 0.5 * rmsnorm(swiglu(h, ffn_w_gate[l, 1], ffn_w_up[l, 1], ffn_w_down[l, 1]), g[5])
    return x
```

```python
import numpy as np
import concourse.bass as bass
import concourse.mybir as mybir
from concourse.bass_utils import run_bass_kernel_spmd

F32 = mybir.dt.float32
BF16 = mybir.dt.bfloat16
I32 = mybir.dt.int32
AF = mybir.ActivationFunctionType
ALU = mybir.AluOpType
AX = mybir.AxisListType

ENGS = ("pe", "act", "dve", "pool", "sp")
NDMASEM = 12


class Buf:
    __slots__ = ("name", "w", "r")

    def __init__(self, name):
        self.name = name
        self.w = None
        self.r = []


class Op:
    __slots__ = ("eng", "fn", "deps", "sig", "dma", "semi", "semv", "idx", "prev_semv")

    def __init__(self, eng, fn, dma):
        self.eng = eng
        self.fn = fn
        self.dma = dma
        self.deps = []
        self.sig = False
        self.semi = None
        self.semv = None
        self.prev_semv = None


class Prog:
    def __init__(self, nc):
        self.nc = nc
        self.ops = {e: [] for e in ENGS}
        self.all_ops = []
        self.sb_off = 16384 + 2048
        self.sb_cap = 16384 + 212000
        self.sb_hi = 0
        self.ntens = 0
        self.dma_rr = 0
        self.dma_sem_total = [0] * NDMASEM
        self.dma_sem_last = [None] * NDMASEM
        self.bufs = []

    def sb(self, shape, dtype, name=None):
        nbytes = int(np.prod(shape[1:])) * mybir.dt.size(dtype)
        nbytes = (nbytes + 63) // 64 * 64
        off = self.sb_off
        assert off + nbytes <= self.sb_cap, f"SBUF overflow {off}+{nbytes} ({name})"
        self.sb_off += nbytes
        self.sb_hi = max(self.sb_hi, self.sb_off)
        self.ntens += 1
        return self.nc.alloc_sbuf_tensor_at(f"t{self.ntens}_{name or ''}", list(shape), dtype, offset=off)

    def mark(self):
        return self.sb_off

    def release(self, mark):
        self.sb_off = mark

    def buf(self, name="b"):
        b = Buf(name)
        self.bufs.append(b)
        return b

    def bufs_n(self, n, name="b"):
        return [self.buf(name) for _ in range(n)]

    def op(self, eng, fn, reads=(), writes=(), dma=0):
        o = Op(eng, fn, dma)
        deps = set()
        for b in reads:
            if b.w is not None:
                deps.add(b.w)
        for b in writes:
            if b.w is not None:
                deps.add(b.w)
            for r in b.r:
                deps.add(r)
        for b in reads:
            b.r.append(o)
        for b in writes:
            b.w = o
            b.r = []
        deps.discard(o)
        for d in sorted(deps, key=lambda x: x.idx):
            if d.eng == "pe" and eng == "pe" and not d.dma and not dma:
                continue
            o.deps.append(d)
            d.sig = True
        if dma:
            o.sig = True
            k = self.dma_rr % NDMASEM
            self.dma_rr += 1
            o.semi = ("d", k)
            o.prev_semv = self.dma_sem_total[k]
            self.dma_sem_total[k] += 16 * dma
            o.semv = self.dma_sem_total[k]
            self.dma_sem_last[k] = o
        o.idx = len(self.all_ops)
        self.ops[eng].append(o)
        self.all_ops.append(o)
        return o

    def chain(self, eng, fns, reads=(), writes=()):
        c = self.buf("chain")
        o = None
        for f in fns:
            o = self.op(eng, f, reads=list(reads) + [c], writes=list(writes) + [c])
        return o

    def barrier(self):
        last = []
        for e in ENGS:
            if self.ops[e]:
                last.append(self.ops[e][-1])
        last += [o for o in self.dma_sem_last if o is not None]
        for d in last:
            d.sig = True
        for e in ENGS:
            o = Op(e, lambda h: None, 0)
            o.deps = list(last)
            o.idx = len(self.all_ops)
            self.ops[e].append(o)
            self.all_ops.append(o)
        for bb in self.bufs:
            bb.w = None
            bb.r = []

    def emit(self):
        nc = self.nc
        self.sems = {e: nc.alloc_semaphore(f"s_{e}") for e in ENGS}
        self.dsems = [nc.alloc_semaphore(f"s_dma{i}") for i in range(NDMASEM)]
        cnt = {e: 0 for e in ENGS}
        for o in self.all_ops:
            if not o.dma and o.sig:
                cnt[o.eng] += 1
                o.semi = ("e", o.eng)
                o.semv = cnt[o.eng]
        handles = {"pe": "tensor", "act": "scalar", "dve": "vector", "pool": "gpsimd", "sp": "sync"}

        def emit_engine(e, h):
            known = {}
            for o in self.ops[e]:
                waits = {}
                for d in o.deps:
                    waits[d.semi] = max(waits.get(d.semi, 0), d.semv)
                if o.dma and o.prev_semv:
                    waits[o.semi] = max(waits.get(o.semi, 0), o.prev_semv)
                for key, v in waits.items():
                    if known.get(key, 0) >= v:
                        continue
                    known[key] = v
                    sem = self.dsems[key[1]] if key[0] == "d" else self.sems[key[1]]
                    h.wait_ge(sem, v)
                r = o.fn(h)
                if o.dma:
                    sem = self.dsems[o.semi[1]]
                    assert len(r) == o.dma, (len(r), o.dma)
                    for ins in r:
                        ins.then_inc(sem, 16)
                elif o.sig:
                    if r is None:
                        r = h.nop()
                    r.then_inc(self.sems[e], 1)

        with nc.Block() as block:
            for e in ENGS:
                getattr(block, handles[e])(lambda h, e=e: emit_engine(e, h))


NT = 512
EPS = 1e-6


class KB:
    def __init__(self, nc):
        self.nc = nc
        self.P = Prog(nc)
        P = self.P
        self.ps = [nc.alloc_psum_tensor(f"psb{i}", [128, NT], F32) for i in range(8)]
        self.b_ps = P.bufs_n(8, "ps")
        self.ones = P.sb([128, 128], BF16, "ones")
        self.b_ones = P.buf("ones")
        P.op("dve", lambda h: h.memset(self.ones[:], 1.0), writes=[self.b_ones])
        self.init_consts()

    def mm(self, psi, pairs, reads, n=NT, m=128):
        ps = self.ps[psi]

        def f(h):
            L = len(pairs)
            for i, (a, b) in enumerate(pairs):
                r = h.matmul(ps[0:m, 0:n], lhsT=a, rhs=b, start=(i == 0), stop=(i == L - 1))
            return r
        return self.P.op("pe", f, reads=reads, writes=[self.b_ps[psi]])

    def dma(self, out, in_, reads, writes, eng="sp"):
        return self.P.op(eng, lambda h: [h.dma_start(out=out, in_=in_)], reads=reads, writes=writes, dma=1)

    def rstd_from_sq(self, sq, nk, n, psi, rstd, b_sq, b_rstd, dim):
        P = self.P
        ps = self.ps[psi]

        def mm(h):
            for k in range(nk):
                r = h.matmul(ps[:, 0:n], lhsT=self.ones[:], rhs=sq[:, k, 0:n], start=(k == 0), stop=(k == nk - 1))
            return r
        P.op("pe", mm, reads=[b_sq, self.b_ones], writes=[self.b_ps[psi]])
        P.op("act", lambda h: h.activation(out=rstd[:, 0:n], in_=ps[:, 0:n], func=AF.Sqrt, scale=1.0 / dim, bias=self.eps_ap()),
             reads=[self.b_ps[psi]], writes=[b_rstd])
        P.op("dve", lambda h: h.reciprocal(out=rstd[:, 0:n], in_=rstd[:, 0:n]), reads=[b_rstd], writes=[b_rstd])

    def eps_ap(self):
        return self.epsT[:, 0:1]

    def init_consts(self):
        P = self.P
        self.epsT = P.sb([128, 1], F32, "eps")
        self.b_eps = P.buf("eps")
        P.op("dve", lambda h: h.memset(self.epsT[:], EPS), writes=[self.b_eps])
        self.negpi = P.sb([128, 1], F32, "negpi")
        P.op("dve", lambda h: h.memset(self.negpi[:], -float(np.pi)), writes=[self.b_eps])


def ffn_stage(K, x_d, gin, gout, b_g, Wg, Wu, Wd, D, DFF, T, Tp, SUB=128, GW=256, res_scale=0.5):
    P = K.P
    KT = D // 128
    GC = GW // 128
    NTn = Tp // NT
    NG = DFF // GW
    mark = P.mark()
    hT = P.sb([128, KT, Tp], BF16, "hT")
    b_hT = P.bufs_n(Tp // SUB, "hT")
    acc = P.sb([128, KT, Tp], F32, "acc")
    b_acc = [P.bufs_n(NTn, "acc") for _ in range(KT)]
    xs = [P.sb([128, KT, SUB], F32, "xs") for _ in range(2)]
    b_xs = P.bufs_n(2, "xs")
    sq = P.sb([128, KT, SUB], BF16, "sq")
    b_sq = P.buf("sq")
    rstd = P.sb([128, SUB], F32, "rstd")
    b_rstd = P.buf("rstd")
    wg = [P.sb([128, KT, GW], BF16, "wg") for _ in range(2)]
    wu = [P.sb([128, KT, GW], BF16, "wu") for _ in range(2)]
    wd = [P.sb([128, GC, D], BF16, "wd") for _ in range(2)]
    b_wg = P.bufs_n(2, "wg")
    b_wu = P.bufs_n(2, "wu")
    b_wd = P.bufs_n(2, "wd")
    hid = [P.sb([128, GC, Tp], BF16, "hid") for _ in range(2)]
    b_hid = [[P.bufs_n(NTn, "hid") for _ in range(GC)] for _ in range(2)]
    sg = [P.sb([128, NT], F32, "sg") for _ in range(2)]
    b_sg = P.bufs_n(2, "sg")
    b_x = P.buf("xdram")
    gsc = P.sb([128, KT], F32, "gsc")
    b_gsc = P.buf("gsc")
    P.op("dve", lambda h: h.tensor_scalar(out=gsc[:], in0=gout, scalar1=float(res_scale), scalar2=None, op0=ALU.mult),
         reads=[b_g], writes=[b_gsc])
    Wg_v = Wg.rearrange("(kt p) c -> p kt c", p=128)
    Wu_v = Wu.rearrange("(kt p) c -> p kt c", p=128)
    Wd_v = Wd.rearrange("(c p) d -> p c d", p=128)
    x_v = x_d.rearrange("(kt p) t -> p kt t", p=128)
    PS_G, PS_U, PS_D, PS_M = (0, 1), (2, 3), (4, 5), 6
    cnt = {"gu": 0, "d": 0, "xs": 0}

    for p in range(T // Tp):
        t0 = p * Tp
        for s in range(Tp // SUB):
            xi = cnt["xs"] % 2
            cnt["xs"] += 1
            K.dma(xs[xi][:], x_v[:, :, t0 + s * SUB: t0 + (s + 1) * SUB], [b_x], [b_xs[xi]])
            P.op("act", lambda h, xi=xi: h.activation(out=sq[:], in_=xs[xi][:], func=AF.Square),
                 reads=[b_xs[xi]], writes=[b_sq])
            K.rstd_from_sq(sq, KT, SUB, PS_M, rstd, b_sq, b_rstd, D)

            def nrm(h, xi=xi, s=s):
                for kt in range(KT):
                    r = h.scalar_tensor_tensor(out=hT[:, kt, s * SUB:(s + 1) * SUB], in0=xs[xi][:, kt, :],
                                               scalar=gin[:, kt:kt + 1], in1=rstd[:], op0=ALU.mult, op1=ALU.mult)
                return r
            P.op("dve", nrm, reads=[b_xs[xi], b_rstd, b_g], writes=[b_hT[s]])

        def load_w(j):
            sl = j % 2
            K.dma(wg[sl][:], Wg_v[:, :, j * GW:(j + 1) * GW], [], [b_wg[sl]], eng="pool")
            K.dma(wu[sl][:], Wu_v[:, :, j * GW:(j + 1) * GW], [], [b_wu[sl]], eng="pool")
            K.dma(wd[sl][:], Wd_v[:, j * GC:(j + 1) * GC, :], [], [b_wd[sl]], eng="pool")

        def gateup(j):
            sl = j % 2
            for c in range(GC):
                for nt in range(NTn):
                    gi = cnt["gu"] % 2
                    cnt["gu"] += 1
                    pg, pu = PS_G[gi], PS_U[gi]
                    hbufs = b_hT[nt * (NT // SUB):(nt + 1) * (NT // SUB)]

                    K.mm(pg, [(wg[sl][:, kt, c * 128:(c + 1) * 128], hT[:, kt, nt * NT:(nt + 1) * NT]) for kt in range(KT)],
                         [b_wg[sl]] + hbufs)
                    K.mm(pu, [(wu[sl][:, kt, c * 128:(c + 1) * 128], hT[:, kt, nt * NT:(nt + 1) * NT]) for kt in range(KT)],
                         [b_wu[sl]] + hbufs)
                    P.op("act", lambda h, gi=gi, pg=pg: h.activation(out=sg[gi][:], in_=K.ps[pg][:], func=AF.Silu),
                         reads=[K.b_ps[pg]], writes=[b_sg[gi]])
                    P.op("dve", lambda h, gi=gi, pu=pu, sl=sl, c=c, nt=nt: h.tensor_tensor(
                        out=hid[sl][:, c, nt * NT:(nt + 1) * NT], in0=sg[gi][:], in1=K.ps[pu][:], op=ALU.mult),
                        reads=[b_sg[gi], K.b_ps[pu]], writes=[b_hid[sl][c][nt]])

        def down(j):
            sl = j % 2
            for m in range(KT):
                for nt in range(NTn):
                    di = cnt["d"] % 2
                    cnt["d"] += 1
                    pd = PS_D[di]

                    K.mm(pd, [(wd[sl][:, c, m * 128:(m + 1) * 128], hid[sl][:, c, nt * NT:(nt + 1) * NT]) for c in range(GC)],
                         [b_wd[sl]] + [b_hid[sl][c][nt] for c in range(GC)])
                    if j == 0:
                        P.op("dve", lambda h, m=m, nt=nt, pd=pd: h.tensor_copy(out=acc[:, m, nt * NT:(nt + 1) * NT], in_=K.ps[pd][:]),
                             reads=[K.b_ps[pd]], writes=[b_acc[m][nt]])
                    else:
                        P.op("dve", lambda h, m=m, nt=nt, pd=pd: h.tensor_tensor(
                            out=acc[:, m, nt * NT:(nt + 1) * NT], in0=acc[:, m, nt * NT:(nt + 1) * NT], in1=K.ps[pd][:], op=ALU.add),
                            reads=[K.b_ps[pd], b_acc[m][nt]], writes=[b_acc[m][nt]])

        load_w(0)
        gateup(0)
        for j in range(NG):
            if j + 1 < NG:
                load_w(j + 1)
                gateup(j + 1)
            down(j)

        for s in range(Tp // SUB):
            nt = (s * SUB) // NT
            accb = [b_acc[m][nt] for m in range(KT)]
            xi = cnt["xs"] % 2
            cnt["xs"] += 1
            K.dma(xs[xi][:], x_v[:, :, t0 + s * SUB: t0 + (s + 1) * SUB], [b_x], [b_xs[xi]])
            P.op("act", lambda h, s=s: h.activation(out=sq[:], in_=acc[:, :, s * SUB:(s + 1) * SUB], func=AF.Square),
                 reads=accb, writes=[b_sq])
            K.rstd_from_sq(sq, KT, SUB, PS_M, rstd, b_sq, b_rstd, D)

            def fin1(h, s=s):
                for kt in range(KT):
                    r = h.scalar_tensor_tensor(out=acc[:, kt, s * SUB:(s + 1) * SUB], in0=acc[:, kt, s * SUB:(s + 1) * SUB],
                                               scalar=gsc[:, kt:kt + 1], in1=rstd[:], op0=ALU.mult, op1=ALU.mult)
                return r
            P.op("dve", fin1, reads=accb + [b_rstd, b_gsc], writes=accb)
            P.op("pool", lambda h, xi=xi, s=s: h.tensor_tensor(
                out=xs[xi][:], in0=acc[:, :, s * SUB:(s + 1) * SUB], in1=xs[xi][:], op=ALU.add),
                reads=accb + [b_xs[xi]], writes=[b_xs[xi]])
            K.dma(x_v[:, :, t0 + s * SUB: t0 + (s + 1) * SUB], xs[xi][:], [b_xs[xi]], [b_x])
    P.barrier()
    P.release(mark)


def ffn_multi(K, x_d, specs, b_g, D, DFF, T, Tp, SUB=128, GW=256, res_scale=0.5):
    P = K.P
    KT = D // 128
    GC = GW // 128
    NTn = Tp // NT
    NG = DFF // GW
    NS = Tp // SUB
    mark = P.mark()
    hT = P.sb([128, KT, Tp], BF16, "hT")
    b_hT = P.bufs_n(NS, "hT")
    acc = P.sb([128, KT, Tp], F32, "acc")
    b_acc = [P.bufs_n(NTn, "acc") for _ in range(KT)]
    fixed = (KT * Tp * 6 + 2 * KT * SUB * 2 + 2 * SUB * 4 + 2 * (2 * KT * GW * 2 + GC * D * 2) + 2 * GC * Tp * 2 + 2 * NT * 4
             + len(specs) * KT * 4 + 2048)
    NXS = 4 if P.sb_cap - P.sb_off - fixed >= 4 * KT * SUB * 4 else 2
    xs = [P.sb([128, KT, SUB], F32, "xs") for _ in range(NXS)]
    b_xs = P.bufs_n(NXS, "xs")
    sq = [P.sb([128, KT, SUB], BF16, "sq") for _ in range(2)]
    b_sq = P.bufs_n(2, "sq")
    rstd = [P.sb([128, SUB], F32, "rstd") for _ in range(2)]
    b_rstd = P.bufs_n(2, "rstd")
    wg = [P.sb([128, KT, GW], BF16, "wg") for _ in range(2)]
    wu = [P.sb([128, KT, GW], BF16, "wu") for _ in range(2)]
    wd = [P.sb([128, GC, D], BF16, "wd") for _ in range(2)]
    b_wg, b_wu, b_wd = P.bufs_n(2, "wg"), P.bufs_n(2, "wu"), P.bufs_n(2, "wd")
    hid = [P.sb([128, GC, Tp], BF16, "hid") for _ in range(2)]
    b_hid = [[P.bufs_n(NTn, "hid") for _ in range(GC)] for _ in range(2)]
    sg = [P.sb([128, NT], F32, "sg") for _ in range(2)]
    b_sg = P.bufs_n(2, "sg")
    b_x = P.buf("xdram")
    gscs = []
    b_gsc = P.buf("gsc")
    for (gin, gout, _, _, _) in specs:
        g_ = P.sb([128, KT], F32, "gsc")
        P.op("dve", lambda h, g_=g_, gout=gout: h.tensor_scalar(out=g_[:], in0=gout, scalar1=float(res_scale), scalar2=None, op0=ALU.mult),
             reads=[b_g], writes=[b_gsc])
        gscs.append(g_)
    x_v = x_d.rearrange("(kt p) t -> p kt t", p=128)
    views = [(Wg.rearrange("(kt p) c -> p kt c", p=128), Wu.rearrange("(kt p) c -> p kt c", p=128), Wd.rearrange("(c p) d -> p c d", p=128))
             for (_, _, Wg, Wu, Wd) in specs]
    PS_G, PS_U, PS_D, PS_M = (0, 1), (2, 3), (4, 5), (6, 7)
    cnt = {"gu": 0, "d": 0, "xs": 0, "sq": 0}
    jobs = [(f, p) for f in range(len(specs)) for p in range(T // Tp)]

    def rstd_calc(src_ap, reads):
        qi = cnt["sq"] % 2
        cnt["sq"] += 1
        P.op("act", lambda h: h.activation(out=sq[qi][:], in_=src_ap, func=AF.Square), reads=reads, writes=[b_sq[qi]])
        K.rstd_from_sq(sq[qi], KT, SUB, PS_M[qi], rstd[qi], b_sq[qi], b_rstd[qi], D)
        return qi

    def norm(k):
        f, p = jobs[k]
        gin = specs[f][0]
        t0 = p * Tp
        for s_ in range(NS):
            xi = cnt["xs"] % 2
            cnt["xs"] += 1
            K.dma(xs[xi][:], x_v[:, :, t0 + s_ * SUB: t0 + (s_ + 1) * SUB], [b_x], [b_xs[xi]])
            qi = rstd_calc(xs[xi][:], [b_xs[xi]])

            def nrm(h, xi=xi, s_=s_, qi=qi):
                for kt in range(KT):
                    r = h.scalar_tensor_tensor(out=hT[:, kt, s_ * SUB:(s_ + 1) * SUB], in0=xs[xi][:, kt, :],
                                               scalar=gin[:, kt:kt + 1], in1=rstd[qi][:], op0=ALU.mult, op1=ALU.mult)
                return r
            P.op("dve", nrm, reads=[b_xs[xi], b_rstd[qi], b_g], writes=[b_hT[s_]])

    def load_w(k, j):
        f, _ = jobs[k]
        Wg_v, Wu_v, Wd_v = views[f]
        sl = j % 2
        K.dma(wg[sl][:], Wg_v[:, :, j * GW:(j + 1) * GW], [], [b_wg[sl]], eng="pool")
        K.dma(wu[sl][:], Wu_v[:, :, j * GW:(j + 1) * GW], [], [b_wu[sl]], eng="pool")
        K.dma(wd[sl][:], Wd_v[:, j * GC:(j + 1) * GC, :], [], [b_wd[sl]], eng="pool")

    def gateup(j):
        sl = j % 2
        for c in range(GC):
            for nt in range(NTn):
                gi = cnt["gu"] % 2
                cnt["gu"] += 1
                pg, pu = PS_G[gi], PS_U[gi]
                hbufs = b_hT[nt * (NT // SUB):(nt + 1) * (NT // SUB)]
                K.mm(pg, [(wg[sl][:, kt, c * 128:(c + 1) * 128], hT[:, kt, nt * NT:(nt + 1) * NT]) for kt in range(KT)], [b_wg[sl]] + hbufs)
                K.mm(pu, [(wu[sl][:, kt, c * 128:(c + 1) * 128], hT[:, kt, nt * NT:(nt + 1) * NT]) for kt in range(KT)], [b_wu[sl]] + hbufs)
                P.op("act", lambda h, gi=gi, pg=pg: h.activation(out=sg[gi][:], in_=K.ps[pg][:], func=AF.Silu),
                     reads=[K.b_ps[pg]], writes=[b_sg[gi]])
                P.op("dve", lambda h, gi=gi, pu=pu, sl=sl, c=c, nt=nt: h.tensor_tensor(
                    out=hid[sl][:, c, nt * NT:(nt + 1) * NT], in0=sg[gi][:], in1=K.ps[pu][:], op=ALU.mult),
                    reads=[b_sg[gi], K.b_ps[pu]], writes=[b_hid[sl][c][nt]])

    def down(j):
        sl = j % 2
        for m in range(KT):
            for nt in range(NTn):
                pd = PS_D[cnt["d"] % 2]
                cnt["d"] += 1
                K.mm(pd, [(wd[sl][:, c, m * 128:(m + 1) * 128], hid[sl][:, c, nt * NT:(nt + 1) * NT]) for c in range(GC)],
                     [b_wd[sl]] + [b_hid[sl][c][nt] for c in range(GC)])
                if j == 0:
                    P.op("dve", lambda h, m=m, nt=nt, pd=pd: h.tensor_copy(out=acc[:, m, nt * NT:(nt + 1) * NT], in_=K.ps[pd][:]),
                         reads=[K.b_ps[pd]], writes=[b_acc[m][nt]])
                else:
                    P.op("dve", lambda h, m=m, nt=nt, pd=pd: h.tensor_tensor(
                        out=acc[:, m, nt * NT:(nt + 1) * NT], in0=acc[:, m, nt * NT:(nt + 1) * NT], in1=K.ps[pd][:], op=ALU.add),
                        reads=[K.b_ps[pd], b_acc[m][nt]], writes=[b_acc[m][nt]])

    def finalize_job(k):
        f, p = jobs[k]
        gsc = gscs[f]
        t0 = p * Tp
        def xload(s_):
            xi_ = (cnt["xs"] + s_) % 2
            K.dma(xs[xi_][:], x_v[:, :, t0 + s_ * SUB: t0 + (s_ + 1) * SUB], [b_x], [b_xs[xi_]])
        xload(0)
        base = cnt["xs"]
        for s_ in range(NS):
            nt = (s_ * SUB) // NT
            accb = [b_acc[m][nt] for m in range(KT)]
            xi = (base + s_) % 2
            if s_ + 1 < NS:
                cnt["xs"] = base
                xload(s_ + 1)
            cnt["xs"] = base + s_ + 1
            qi = rstd_calc(acc[:, :, s_ * SUB:(s_ + 1) * SUB], accb)

            if NXS == 4:
                ti = 2 + cnt["xs"] % 2
                tmp, b_tmp = xs[ti], b_xs[ti]

                def fin1(h, s_=s_, qi=qi, tmp=tmp):
                    for kt in range(KT):
                        r = h.scalar_tensor_tensor(out=tmp[:, kt, :], in0=acc[:, kt, s_ * SUB:(s_ + 1) * SUB],
                                                   scalar=gsc[:, kt:kt + 1], in1=rstd[qi][:], op0=ALU.mult, op1=ALU.mult)
                    return r
                P.op("dve", fin1, reads=accb + [b_rstd[qi], b_gsc], writes=[b_tmp])
                P.op("pool", lambda h, xi=xi, tmp=tmp: h.tensor_tensor(out=xs[xi][:], in0=tmp[:], in1=xs[xi][:], op=ALU.add),
                     reads=[b_tmp, b_xs[xi]], writes=[b_xs[xi]])
            else:
                def fin1(h, s_=s_, qi=qi):
                    for kt in range(KT):
                        r = h.scalar_tensor_tensor(out=acc[:, kt, s_ * SUB:(s_ + 1) * SUB], in0=acc[:, kt, s_ * SUB:(s_ + 1) * SUB],
                                                   scalar=gsc[:, kt:kt + 1], in1=rstd[qi][:], op0=ALU.mult, op1=ALU.mult)
                    return r
                P.op("dve", fin1, reads=accb + [b_rstd[qi], b_gsc], writes=accb)
                P.op("pool", lambda h, xi=xi, s_=s_: h.tensor_tensor(
                    out=xs[xi][:], in0=acc[:, :, s_ * SUB:(s_ + 1) * SUB], in1=xs[xi][:], op=ALU.add),
                    reads=accb + [b_xs[xi]], writes=[b_xs[xi]])
            K.dma(x_v[:, :, t0 + s_ * SUB: t0 + (s_ + 1) * SUB], xs[xi][:], [b_xs[xi]], [b_x])

    assert NG % 2 == 0
    norm(0)
    load_w(0, 0)
    gateup(0)
    for k in range(len(jobs)):
        for j in range(NG):
            if j + 1 < NG:
                load_w(k, j + 1)
                gateup(j + 1)
                down(j)
            else:
                if k + 1 < len(jobs):
                    norm(k + 1)
                    load_w(k + 1, 0)
                    gateup(0)
                down(j)
        finalize_job(k)
    P.barrier()
    P.release(mark)


S5_L = 128


def s5_host_layout(lam_re, lam_im, log_dt, b_re, b_im, c_re, c_im, d):
    G, Pn, C = b_re.shape
    NP = G // 2

    def st(a):
        return np.ascontiguousarray(a.reshape(NP, 2 * Pn).T)
    ldt = np.repeat(log_dt[:, None], Pn, 1)
    lam_s = np.stack([st(lam_re), st(lam_im), st(ldt)], 1)
    row = np.stack([lam_re.reshape(-1), lam_im.reshape(-1), ldt.reshape(-1)], 0)
    lam_r = np.ascontiguousarray(np.broadcast_to(row[None], (128, 3, NP * 128)))
    bT = np.zeros((2, 128, NP, 128), np.float32)
    cP = np.zeros((2, 128, NP, 128), np.float32)
    for g in range(G):
        q, hh = g // 2, g % 2
        off = (g % 8) * 16
        bT[0, off:off + 16, q, hh * 64:(hh + 1) * 64] = b_re[g].T
        bT[1, off:off + 16, q, hh * 64:(hh + 1) * 64] = b_im[g].T
        cP[0, hh * 64:(hh + 1) * 64, q, off:off + 16] = c_re[g].T
        cP[1, hh * 64:(hh + 1) * 64, q, off:off + 16] = c_im[g].T
    d_s = np.ascontiguousarray(d.reshape(-1, 128).T)
    return dict(lam_s=lam_s, lam_r=lam_r, bT=bT, cP=cP, d_s=d_s)


def s5_setup(K, prm, NP):
    P = K.P
    L = S5_L
    PI = float(np.pi)
    S = {}
    lam_s = P.sb([128, 3, NP], F32, "lam_s")
    b_l = P.buf("lam_s")
    K.dma(lam_s[:], prm["lam_s"], [], [b_l])
    dt = P.sb([128, NP], F32, "dt")
    th = P.sb([128, NP], F32, "th")
    r = P.sb([128, NP], F32, "r")
    b_t = P.buf("s5tab")
    P.op("act", lambda h: h.activation(out=dt[:], in_=lam_s[:, 2, :], func=AF.Exp), reads=[b_l], writes=[b_t])
    P.op("dve", lambda h: h.tensor_tensor(out=th[:], in0=lam_s[:, 1, :], in1=dt[:], op=ALU.mult), reads=[b_l, b_t], writes=[b_t])
    P.op("dve", lambda h: h.tensor_tensor(out=r[:], in0=lam_s[:, 0, :], in1=dt[:], op=ALU.mult), reads=[b_l, b_t], writes=[b_t])
    P.op("act", lambda h: h.activation(out=r[:], in_=r[:], func=AF.Exp), reads=[b_t], writes=[b_t])
    jrow_i = P.sb([128, L], I32, "jrow_i")
    jrow = P.sb([128, L], F32, "jrow")
    P.op("pool", lambda h: h.iota(jrow_i[:], pattern=[[1, L]], base=0, channel_multiplier=0), writes=[b_t], reads=[b_t])
    P.op("dve", lambda h: h.tensor_copy(out=jrow[:], in_=jrow_i[:]), reads=[b_t], writes=[b_t])
    cosT = P.sb([128, NP, L], F32, "cosT")
    sinT = P.sb([128, NP, L], F32, "sinT")
    Rz = P.sb([128, NP, L], F32, "Rz")
    b_tab = P.buf("tabs")

    TWO_PI = 2 * PI
    MAGIC = 12582912.0
    thn = P.sb([128, NP], F32, "thn")
    P.op("dve", lambda h: h.tensor_scalar(out=thn[:], in0=th[:], scalar1=1.0 / TWO_PI, scalar2=None, op0=ALU.mult), reads=[b_t], writes=[b_t])
    Kre = P.sb([128, NP], F32, "Kre")
    Kim = P.sb([128, NP], F32, "Kim")
    mk0 = P.mark()
    tmpT = P.sb([128, NP, L], F32, "tmpT")

    def sin_cycles(tens, shift, b_r, b_w_):
        tv = tmpT_v(tens)
        steps = []
        if shift:
            steps.append(lambda h: h.tensor_scalar(out=tens, in0=tens, scalar1=float(shift), scalar2=None, op0=ALU.add))
        steps.append(lambda h: h.tensor_scalar(out=tv, in0=tens, scalar1=MAGIC, scalar2=None, op0=ALU.add))
        steps.append(lambda h: h.tensor_scalar(out=tv, in0=tv, scalar1=-MAGIC, scalar2=None, op0=ALU.add))
        steps.append(lambda h: h.tensor_tensor(out=tens, in0=tens, in1=tv, op=ALU.subtract))
        P.chain("dve", steps, reads=b_r, writes=b_w_)
        P.op("act", lambda h: h.activation(out=tens, in_=tens, func=AF.Sin, scale=TWO_PI), reads=b_w_, writes=b_w_)

    def tmpT_v(tens):
        shp = tens.shape
        if len(shp) == 3:
            return tmpT[:, 0:shp[1], 0:shp[2]]
        return tmpT[:, 0, 0:shp[1]]

    def angs(h):
        for q in range(NP):
            h.tensor_scalar(out=sinT[:, q, :], in0=jrow[:], scalar1=thn[:, q:q + 1], scalar2=None, op0=ALU.mult)
            r_ = h.tensor_scalar(out=cosT[:, q, :], in0=jrow[:], scalar1=thn[:, q:q + 1], scalar2=None, op0=ALU.mult)
        return r_
    P.op("dve", angs, reads=[b_t], writes=[b_tab])
    sin_cycles(sinT[:], 0.0, [b_tab], [b_tab])
    sin_cycles(cosT[:], 0.25, [b_tab], [b_tab])

    def rz(h):
        for q in range(NP):
            h.tensor_scalar(out=Rz[:, q, 1:L], in0=jrow[:, 1:L], scalar1=0.0, scalar2=r[:, q:q + 1], op0=ALU.mult, op1=ALU.add)
        return h.memset(Rz[:, :, 0:1], 0.0)
    P.op("dve", rz, reads=[b_t], writes=[b_tab])
    def kang(h):
        h.tensor_scalar(out=Kim[:], in0=thn[:], scalar1=float(L), scalar2=None, op0=ALU.mult)
        return h.tensor_scalar(out=Kre[:], in0=thn[:], scalar1=float(L), scalar2=None, op0=ALU.mult)
    P.op("dve", kang, reads=[b_t], writes=[b_tab])
    sin_cycles(Kim[:], 0.0, [b_tab], [b_tab])
    sin_cycles(Kre[:], 0.25, [b_tab], [b_tab])

    def kmul(h):
        h.tensor_tensor(out=Kim[:], in0=Kim[:], in1=r[:], op=ALU.mult)
        return h.tensor_tensor(out=Kre[:], in0=Kre[:], in1=r[:], op=ALU.mult)
    P.op("dve", kmul, reads=[b_tab, b_t], writes=[b_tab])

    P.barrier()
    P.release(mk0)
    BT = [P.sb([128, NP, 128], BF16, f"BT{i}") for i in range(2)]
    CT = [P.sb([128, NP, 128], BF16, f"CT{i}") for i in range(2)]
    b_BT = P.buf("BT")
    b_CT = P.buf("CT")
    K.dma(CT[0][:], prm["cP"][0], [], [b_CT], eng="pool")
    K.dma(CT[1][:], prm["cP"][1], [], [b_CT], eng="pool")
    P.op("dve", lambda h: h.tensor_scalar(out=CT[1][:], in0=CT[1][:], scalar1=-1.0, scalar2=None, op0=ALU.mult), reads=[b_CT], writes=[b_CT])
    mk = P.mark()
    QB = 4
    W = QB * 128
    lr = P.sb([128, 3, W], F32, "lr")
    tb = [P.sb([128, W], F32, f"tb{i}") for i in range(6)]
    bt = [P.sb([128, QB, 128], F32, f"bt{i}") for i in range(2)]
    b_lr = P.buf("lr")
    b_bt = P.buf("bt")
    b_w = P.buf("w")
    for blk in range(NP // QB):
        cs = slice(blk * W, (blk + 1) * W)
        K.dma(lr[:], prm["lam_r"][:, :, cs], [], [b_lr])
        K.dma(bt[0][:], prm["bT"][0][:, blk * QB:(blk + 1) * QB, :], [], [b_bt])
        K.dma(bt[1][:], prm["bT"][1][:, blk * QB:(blk + 1) * QB, :], [], [b_bt])
        dtr, thr, rr, ca, sa, den = tb
        P.op("act", lambda h: h.activation(out=dtr[:], in_=lr[:, 2, :], func=AF.Exp), reads=[b_lr], writes=[b_w])

        def c1(h):
            h.tensor_tensor(out=thr[:], in0=lr[:, 1, :], in1=dtr[:], op=ALU.mult)
            h.tensor_tensor(out=rr[:], in0=lr[:, 0, :], in1=dtr[:], op=ALU.mult)
            h.tensor_scalar(out=sa[:], in0=thr[:], scalar1=1.0 / TWO_PI, scalar2=None, op0=ALU.mult)
            h.tensor_scalar(out=ca[:], in0=thr[:], scalar1=1.0 / TWO_PI, scalar2=0.25, op0=ALU.mult, op1=ALU.add)
            for t_ in (sa, ca):
                h.tensor_scalar(out=den[:], in0=t_[:], scalar1=MAGIC, scalar2=None, op0=ALU.add)
                h.tensor_scalar(out=den[:], in0=den[:], scalar1=-MAGIC, scalar2=None, op0=ALU.add)
                r_ = h.tensor_tensor(out=t_[:], in0=t_[:], in1=den[:], op=ALU.subtract)
            return r_
        P.op("dve", c1, reads=[b_lr, b_w], writes=[b_w])
        P.op("act", lambda h: h.activation(out=rr[:], in_=rr[:], func=AF.Exp), reads=[b_w], writes=[b_w])
        P.op("act", lambda h: h.activation(out=sa[:], in_=sa[:], func=AF.Sin, scale=TWO_PI), reads=[b_w], writes=[b_w])
        P.op("act", lambda h: h.activation(out=ca[:], in_=ca[:], func=AF.Sin, scale=TWO_PI), reads=[b_w], writes=[b_w])

        def c2(h):
            h.tensor_tensor(out=ca[:], in0=ca[:], in1=rr[:], op=ALU.mult)
            h.tensor_scalar(out=ca[:], in0=ca[:], scalar1=-1.0, scalar2=None, op0=ALU.add)
            h.tensor_tensor(out=sa[:], in0=sa[:], in1=rr[:], op=ALU.mult)
            h.tensor_tensor(out=den[:], in0=lr[:, 0, :], in1=lr[:, 0, :], op=ALU.mult)
            h.tensor_tensor(out=dtr[:], in0=lr[:, 1, :], in1=lr[:, 1, :], op=ALU.mult)
            h.tensor_tensor(out=den[:], in0=den[:], in1=dtr[:], op=ALU.add)
            h.reciprocal(out=den[:], in_=den[:])
            h.tensor_tensor(out=thr[:], in0=ca[:], in1=lr[:, 0, :], op=ALU.mult)
            h.tensor_tensor(out=dtr[:], in0=sa[:], in1=lr[:, 1, :], op=ALU.mult)
            h.tensor_tensor(out=thr[:], in0=thr[:], in1=dtr[:], op=ALU.add)
            h.tensor_tensor(out=thr[:], in0=thr[:], in1=den[:], op=ALU.mult)
            h.tensor_tensor(out=rr[:], in0=sa[:], in1=lr[:, 0, :], op=ALU.mult)
            h.tensor_tensor(out=dtr[:], in0=ca[:], in1=lr[:, 1, :], op=ALU.mult)
            h.tensor_tensor(out=rr[:], in0=rr[:], in1=dtr[:], op=ALU.subtract)
            return h.tensor_tensor(out=rr[:], in0=rr[:], in1=den[:], op=ALU.mult)
        P.op("dve", c2, reads=[b_lr, b_w], writes=[b_w])

        def c3(h, blk=blk):
            qs = slice(blk * QB, (blk + 1) * QB)
            b0 = bt[0][:].rearrange("p q s -> p (q s)")
            b1 = bt[1][:].rearrange("p q s -> p (q s)")
            o0 = BT[0][:, qs, :].rearrange("p q s -> p (q s)")
            o1 = BT[1][:, qs, :].rearrange("p q s -> p (q s)")
            h.tensor_tensor(out=ca[:], in0=thr[:], in1=b0, op=ALU.mult)
            h.tensor_tensor(out=sa[:], in0=rr[:], in1=b1, op=ALU.mult)
            h.tensor_tensor(out=o0, in0=ca[:], in1=sa[:], op=ALU.subtract)
            h.tensor_tensor(out=ca[:], in0=thr[:], in1=b1, op=ALU.mult)
            h.tensor_tensor(out=sa[:], in0=rr[:], in1=b0, op=ALU.mult)
            return h.tensor_tensor(out=o1, in0=ca[:], in1=sa[:], op=ALU.add)
        P.op("dve", c3, reads=[b_w, b_bt], writes=[b_BT, b_w])
    P.barrier()
    P.release(mk)
    S.update(cosT=cosT, sinT=sinT, Rz=Rz, Kre=Kre, Kim=Kim, BT=BT, CT=CT, b_tab=b_tab, b_BT=b_BT, b_CT=b_CT)
    return S


def s5_scan(K, S, NP, T, u_d, carry_in_d, carry_out_d, yg_d, d_d, full, flag=None, b_flag=None):
    P = K.P
    L = S5_L
    NQ = NP // 4
    NG4 = (NQ + 3) // 4
    NCH = T // L
    mark = P.mark()
    cosT, sinT, Rz, Kre, Kim, BT, CT = (S[k] for k in ("cosT", "sinT", "Rz", "Kre", "Kim", "BT", "CT"))
    b_tab, b_BT, b_CT = S["b_tab"], S["b_BT"], S["b_CT"]
    u_v = u_d.rearrange("(q p) t -> p q t", p=128)
    SC = 4 * L
    uT = [P.sb([128, NQ, SC], BF16, "uT") for _ in range(2)]
    b_u = P.bufs_n(2, "uT")
    rc = P.sb([128, 2, NP], F32, "rc")
    b_rc = P.buf("rc")
    zl = P.sb([128, 2, NP], F32, "zl")
    b_zl = P.bufs_n(NQ, "zl")
    tmpc = [P.sb([128, NP], F32, f"tmpc{i}") for i in range(2)]
    K.dma(rc[:], carry_in_d, [], [b_rc])
    if flag is not None:
        P.op("dve", lambda h: h.tensor_scalar(out=rc[:].rearrange("p a q -> p (a q)"), in0=rc[:].rearrange("p a q -> p (a q)"),
                                              scalar1=flag[:, 0:1], scalar2=None, op0=ALU.mult), reads=[b_flag], writes=[b_rc])
    NB = 2
    v = [[P.sb([128, 4, L], F32, f"v{i}{j}") for j in range(2)] for i in range(NB)]
    z = [[P.sb([128, 4, L], F32, f"z{i}{j}") for j in range(2)] for i in range(NB)]
    b_v = P.bufs_n(NB, "v")
    xo = [[P.sb([128, 4, L], BF16, f"xo{i}{j}") for j in range(2)] for i in range(NB)]
    b_xo = P.bufs_n(NB, "xo")
    fl = lambda t: t[:].rearrange("p q l -> p (q l)")
    col = lambda ap: ap.rearrange("p (q o) -> p q o", o=1)
    if full:
        d_s = P.sb([128, NQ], F32, "d_s")
        b_d = P.buf("d")
        K.dma(d_s[:], d_d, [], [b_d])
        du = [P.sb([128, NQ, L], F32, f"du{i}") for i in range(2)]
        b_du = P.bufs_n(2, "du")
        yt = [P.sb([128, 4, L], F32, f"yt{i}") for i in range(2)]
        y2 = [P.sb([128, 4, L], F32, f"y2{i}") for i in range(2)]
        yo = [P.sb([128, 4, L], BF16, f"yo{i}") for i in range(2)]
        b_yt = P.bufs_n(2, "yt")
        b_yo = P.bufs_n(2, "yo")
        b_ygd = P.buf("ygd")
        yg_v = yg_d.rearrange("(q p) t -> p q t", p=128)
    b_ud = P.buf("ud")
    PS_B = [(0, 1), (2, 3)]
    PS_Y = [4, 5, 6, 7]
    yit = [0]
    items = [(c, qd) for c in range(NCH) for qd in range(NQ)]
    usl = {}

    def emit_mmb(i):
        c, qd = items[i]
        sc, cc = divmod(c, 4)
        us = sc % 2
        if cc == 0 and qd == 0:
            K.dma(uT[us][:], u_v[:, :, sc * SC:(sc + 1) * SC], [b_ud], [b_u[us]])
        pr, pi_ = PS_B[i % 2]
        prs = list(range(qd * 4, qd * 4 + 4))
        rhs = uT[us][:, qd, cc * L:(cc + 1) * L]

        def mmb(h, pr=pr, pi_=pi_, prs=prs, rhs=rhs):
            for k, q in enumerate(prs):
                h.matmul(K.ps[pr][:, k * L:(k + 1) * L], lhsT=BT[0][:, q, :], rhs=rhs, start=True, stop=True)
            for k, q in enumerate(prs):
                r_ = h.matmul(K.ps[pi_][:, k * L:(k + 1) * L], lhsT=BT[1][:, q, :], rhs=rhs, start=True, stop=True)
            return r_
        P.op("pe", mmb, reads=[b_BT, b_u[us]], writes=[K.b_ps[pr], K.b_ps[pi_]])

    def emit_rest(i):
        c, qd = items[i]
        sc, cc = divmod(c, 4)
        us = sc % 2
        dui = c % 2
        if full and qd == 0:
            def mkdu(h, dui=dui, us=us, cc=cc):
                for q_ in range(NQ):
                    r_ = h.tensor_scalar(out=du[dui][:, q_, :], in0=uT[us][:, q_, cc * L:(cc + 1) * L], scalar1=d_s[:, q_:q_ + 1],
                                         scalar2=None, op0=ALU.mult)
                return r_
            P.op("pool", mkdu, reads=[b_u[us], b_d], writes=[b_du[dui]])
        vi = i % NB
        pr, pi_ = PS_B[i % 2]
        prs = list(range(qd * 4, qd * 4 + 4))
        vr, vim = v[vi]
        zr, zi = z[vi]
        qs = slice(qd * 4, qd * 4 + 4)
        cs_ = cosT[:, qs, :].rearrange("p q l -> p (q l)")
        sn_ = sinT[:, qs, :].rearrange("p q l -> p (q l)")
        rz_ = Rz[:, qs, :].rearrange("p q l -> p (q l)")

        def rot_in(h):
            h.tensor_tensor(out=fl(zr), in0=K.ps[pr][:], in1=cs_, op=ALU.mult)
            h.tensor_tensor(out=fl(zi), in0=K.ps[pi_][:], in1=sn_, op=ALU.mult)
            h.tensor_tensor(out=fl(vr), in0=fl(zr), in1=fl(zi), op=ALU.add)
            h.tensor_tensor(out=fl(zr), in0=K.ps[pi_][:], in1=cs_, op=ALU.mult)
            h.tensor_tensor(out=fl(zi), in0=K.ps[pr][:], in1=sn_, op=ALU.mult)
            return h.tensor_tensor(out=fl(vim), in0=fl(zr), in1=fl(zi), op=ALU.subtract)
        P.op("dve", rot_in, reads=[K.b_ps[pr], K.b_ps[pi_], b_tab], writes=[b_v[vi]])

        def sc1(h):
            h.tensor_tensor(out=vr[:, :, 0:1], in0=vr[:, :, 0:1], in1=col(rc[:, 0, qs]), op=ALU.add)
            return h.tensor_tensor(out=vim[:, :, 0:1], in0=vim[:, :, 0:1], in1=col(rc[:, 1, qs]), op=ALU.add)

        def sc2(h):
            h.tensor_tensor_scan(out=fl(zr), data0=rz_, data1=fl(vr), initial=0.0, op0=ALU.mult, op1=ALU.add)
            return h.tensor_tensor_scan(out=fl(zi), data0=rz_, data1=fl(vim), initial=0.0, op0=ALU.mult, op1=ALU.add)

        def sc3(h):
            h.tensor_copy(out=col(zl[:, 0, qs]), in_=zr[:, :, L - 1:L])
            return h.tensor_copy(out=col(zl[:, 1, qs]), in_=zi[:, :, L - 1:L])
        P.chain("dve", [sc1, sc2, sc3], reads=[b_rc, b_tab], writes=[b_v[vi], b_zl[qd]])
        if full:
            xr, xi = xo[vi]

            def rot_out(h):
                h.tensor_tensor(out=fl(vr), in0=fl(zr), in1=cs_, op=ALU.mult)
                h.tensor_tensor(out=fl(vim), in0=fl(zi), in1=sn_, op=ALU.mult)
                h.tensor_tensor(out=fl(xr), in0=fl(vr), in1=fl(vim), op=ALU.subtract)
                h.tensor_tensor(out=fl(vr), in0=fl(zr), in1=sn_, op=ALU.mult)
                h.tensor_tensor(out=fl(vim), in0=fl(zi), in1=cs_, op=ALU.mult)
                return h.tensor_tensor(out=fl(xi), in0=fl(vr), in1=fl(vim), op=ALU.add)
            P.op("pool", rot_out, reads=[b_tab], writes=[b_xo[vi], b_v[vi]])
        if i + 2 < len(items):
            emit_mmb(i + 2)
        if full:
            g4, q4 = divmod(qd, 4)
            py = PS_Y[(c * NG4 + g4) % 4]

            def mmy(h):
                for k, q in enumerate(prs):
                    h.matmul(K.ps[py][:, q4 * L:(q4 + 1) * L], lhsT=CT[0][:, q, :], rhs=xr[:, k, :], start=(k == 0), stop=False)
                for k, q in enumerate(prs):
                    r_ = h.matmul(K.ps[py][:, q4 * L:(q4 + 1) * L], lhsT=CT[1][:, q, :], rhs=xi[:, k, :], start=False, stop=(k == 3))
                return r_
            P.op("pe", mmy, reads=[b_xo[vi], b_CT], writes=[K.b_ps[py]])
            if q4 == 3 or qd == NQ - 1:
                nq4 = q4 + 1
                yi = yit[0] % 2
                yit[0] += 1
                W4 = nq4 * L
                P.op("dve", lambda h: h.tensor_tensor(
                    out=yt[yi][:, 0:nq4, :].rearrange("p q l -> p (q l)"), in0=K.ps[py][:, 0:W4],
                    in1=du[dui][:, g4 * 4:g4 * 4 + nq4, :].rearrange("p q l -> p (q l)"), op=ALU.add),
                    reads=[K.b_ps[py], b_du[dui], b_yo[yi]], writes=[b_yt[yi]])
                P.op("act", lambda h: h.activation(out=y2[yi][:, 0:nq4, :], in_=yt[yi][:, 0:nq4, :], func=AF.Square),
                     reads=[b_yt[yi]], writes=[b_yt[yi]])

                def g2(h):
                    h.tensor_scalar(out=y2[yi][:, 0:nq4, :], in0=y2[yi][:, 0:nq4, :], scalar1=0.044715, scalar2=1.0, op0=ALU.mult, op1=ALU.add)
                    return h.tensor_tensor(out=y2[yi][:, 0:nq4, :], in0=y2[yi][:, 0:nq4, :], in1=yt[yi][:, 0:nq4, :], op=ALU.mult)
                P.op("pool", g2, reads=[b_yt[yi]], writes=[b_yt[yi]])
                P.op("act", lambda h: h.activation(out=y2[yi][:, 0:nq4, :], in_=y2[yi][:, 0:nq4, :], func=AF.Sigmoid, scale=1.5957691216),
                     reads=[b_yt[yi]], writes=[b_yt[yi]])
                P.op("pool", lambda h: h.tensor_tensor(out=yo[yi][:, 0:nq4, :], in0=y2[yi][:, 0:nq4, :], in1=yt[yi][:, 0:nq4, :], op=ALU.mult),
                     reads=[b_yt[yi]], writes=[b_yo[yi]])
                K.dma(yg_v[:, g4 * 4:g4 * 4 + nq4, c * L:(c + 1) * L], yo[yi][:, 0:nq4, :], [b_yo[yi]], [b_ygd])

    emit_mmb(0)
    if len(items) > 1:
        emit_mmb(1)
    for c in range(NCH):
        for qd in range(NQ):
            emit_rest(c * NQ + qd)

        def cu1(h):
            h.tensor_tensor(out=tmpc[0][:], in0=zl[:, 0, :], in1=Kre[:], op=ALU.mult)
            return h.tensor_tensor(out=tmpc[1][:], in0=zl[:, 1, :], in1=Kim[:], op=ALU.mult)

        def cu2(h):
            return h.tensor_tensor(out=rc[:, 0, :], in0=tmpc[0][:], in1=tmpc[1][:], op=ALU.subtract)

        def cu3(h):
            h.tensor_tensor(out=tmpc[0][:], in0=zl[:, 0, :], in1=Kim[:], op=ALU.mult)
            return h.tensor_tensor(out=tmpc[1][:], in0=zl[:, 1, :], in1=Kre[:], op=ALU.mult)

        def cu4(h):
            return h.tensor_tensor(out=rc[:, 1, :], in0=tmpc[0][:], in1=tmpc[1][:], op=ALU.add)
        P.chain("dve", [cu1, cu2, cu3, cu4], reads=b_zl + [b_tab], writes=[b_rc])
    b_co = P.buf("carry_out")
    K.dma(carry_out_d, rc[:], [b_rc], [b_co])
    P.barrier()
    P.release(mark)


def alloc_norm_tmp(K, KT, SUB):
    P = K.P
    return dict(xs=[P.sb([128, KT, SUB], F32, "xs") for _ in range(2)], b_xs=P.bufs_n(2, "xs"),
                sq=P.sb([128, KT, SUB], BF16, "sq"), b_sq=P.buf("sq"),
                rstd=P.sb([128, SUB], F32, "rstd"), b_rstd=P.buf("rstd"), n=0, SUB=SUB, KT=KT)


def norm_in(K, tm, x_v, b_x, t0, Tp, g_ap, b_g, hT, b_hT, D, PS_M):
    P = K.P
    SUB, KT = tm["SUB"], tm["KT"]
    for s in range(Tp // SUB):
        xi = tm["n"] % 2
        tm["n"] += 1
        xs, sq, rstd = tm["xs"][xi], tm["sq"], tm["rstd"]
        K.dma(xs[:], x_v[:, :, t0 + s * SUB: t0 + (s + 1) * SUB], [b_x], [tm["b_xs"][xi]])
        P.op("act", lambda h, xs=xs: h.activation(out=sq[:], in_=xs[:], func=AF.Square), reads=[tm["b_xs"][xi]], writes=[tm["b_sq"]])
        K.rstd_from_sq(sq, KT, SUB, PS_M, rstd, tm["b_sq"], tm["b_rstd"], D)

        def nrm(h, xs=xs, s=s):
            for kt in range(KT):
                r = h.scalar_tensor_tensor(out=hT[:, kt, s * SUB:(s + 1) * SUB], in0=xs[:, kt, :],
                                           scalar=g_ap[:, kt:kt + 1], in1=rstd[:], op0=ALU.mult, op1=ALU.mult)
            return r
        P.op("dve", nrm, reads=[tm["b_xs"][xi], tm["b_rstd"], b_g], writes=[b_hT[s]])


def finalize(K, tm, acc, b_acc, x_v, b_x, t0, Tp, gsc, b_gsc, D, PS_M):
    P = K.P
    SUB, KT = tm["SUB"], tm["KT"]
    if "tmp" not in tm and P.sb_cap - P.sb_off >= KT * SUB * 4 + 1024:
        tm["tmp"] = P.sb([128, KT, SUB], F32, "fintmp")
        tm["b_tmp"] = P.buf("fintmp")
    tmp, b_tmp = tm.get("tmp"), tm.get("b_tmp")
    base = tm["n"]

    def xload(s):
        xi_ = (base + s) % 2
        K.dma(tm["xs"][xi_][:], x_v[:, :, t0 + s * SUB: t0 + (s + 1) * SUB], [b_x], [tm["b_xs"][xi_]])
    xload(0)
    for s in range(Tp // SUB):
        nt = (s * SUB) // NT
        accb = [b_acc[m][nt] for m in range(KT)]
        xi = (base + s) % 2
        tm["n"] = base + s + 1
        xs, sq, rstd = tm["xs"][xi], tm["sq"], tm["rstd"]
        if s + 1 < Tp // SUB:
            xload(s + 1)
        P.op("act", lambda h, s=s: h.activation(out=sq[:], in_=acc[:, :, s * SUB:(s + 1) * SUB], func=AF.Square),
             reads=accb, writes=[tm["b_sq"]])
        K.rstd_from_sq(sq, KT, SUB, PS_M, rstd, tm["b_sq"], tm["b_rstd"], D)
        if tmp is not None:
            def fin1(h, s=s):
                for kt in range(KT):
                    r = h.scalar_tensor_tensor(out=tmp[:, kt, :], in0=acc[:, kt, s * SUB:(s + 1) * SUB],
                                               scalar=gsc[:, kt:kt + 1], in1=rstd[:], op0=ALU.mult, op1=ALU.mult)
                return r
            P.op("dve", fin1, reads=accb + [tm["b_rstd"], b_gsc], writes=[b_tmp])
            P.op("pool", lambda h, xs=xs: h.tensor_tensor(out=xs[:], in0=tmp[:], in1=xs[:], op=ALU.add),
                 reads=[b_tmp, tm["b_xs"][xi]], writes=[tm["b_xs"][xi]])
        else:
            def fin1(h, s=s):
                for kt in range(KT):
                    r = h.scalar_tensor_tensor(out=acc[:, kt, s * SUB:(s + 1) * SUB], in0=acc[:, kt, s * SUB:(s + 1) * SUB],
                                               scalar=gsc[:, kt:kt + 1], in1=rstd[:], op0=ALU.mult, op1=ALU.mult)
                return r
            P.op("dve", fin1, reads=accb + [tm["b_rstd"], b_gsc], writes=accb)
            P.op("pool", lambda h, xs=xs, s=s: h.tensor_tensor(out=xs[:], in0=acc[:, :, s * SUB:(s + 1) * SUB], in1=xs[:], op=ALU.add),
                 reads=accb + [tm["b_xs"][xi]], writes=[tm["b_xs"][xi]])
        K.dma(x_v[:, :, t0 + s * SUB: t0 + (s + 1) * SUB], xs[:], [tm["b_xs"][xi]], [b_x])


class WStream:
    def __init__(self, K, KT, MW, name="w"):
        P = K.P
        self.K, self.KT, self.MW = K, KT, MW
        self.w = [P.sb([128, KT, MW], BF16, name) for _ in range(2)]
        self.b = P.bufs_n(2, name)
        self.n = 0

    def load(self, W_v, c0, ncols=None):
        ncols = ncols or self.MW
        sl = self.n % 2
        self.n += 1
        self.K.dma(self.w[sl][:, :, 0:ncols], W_v[:, :, c0:c0 + ncols], [], [self.b[sl]], eng="pool")
        return self.w[sl], self.b[sl]


class Ring:
    def __init__(self, items):
        self.items = items
        self.n = 0

    def next(self):
        r = self.items[self.n % len(self.items)]
        self.n += 1
        return r


def mem_kv_setup(K, mem_d, gm, b_gm, Wkv, D, NM, MEMW):
    P = K.P
    KT = D // 128
    memK = P.sb([128, MEMW // 128, NM], BF16, "memK")
    memV = P.sb([128, NM // 128, MEMW], BF16, "memV")
    b_mk = P.buf("memK")
    b_mv = P.buf("memV")
    mark = P.mark()
    tm = alloc_norm_tmp(K, KT, NM)
    nm = P.sb([128, KT, NM], BF16, "nmem")
    b_nm = [P.buf("nmem")]
    wkv = P.sb([128, KT, 2 * MEMW], BF16, "wkv")
    b_w = P.buf("wkv")
    K.dma(wkv[:], Wkv.rearrange("(kt p) c -> p kt c", p=128), [], [b_w], eng="pool")
    mem_v = mem_d.rearrange("(kt p) t -> p kt t", p=128)
    norm_in(K, tm, mem_v, P.buf("memd"), 0, NM, gm, b_gm, nm, b_nm, D, 6)
    for h in range(MEMW // 128):
        psi = h % 2
        K.mm(psi, [(wkv[:, kt, h * 128:(h + 1) * 128], nm[:, kt, :]) for kt in range(KT)], [b_w] + b_nm, n=NM)
        P.op("act", lambda h_, h=h, psi=psi: h_.activation(out=memK[:, h, :], in_=K.ps[psi][:, 0:NM], func=AF.Copy),
             reads=[K.b_ps[psi]], writes=[b_mk])
    for kt_ in range(NM // 128):
        psi = 2 + kt_ % 2
        K.mm(psi, [(nm[:, kt, kt_ * 128:(kt_ + 1) * 128], wkv[:, kt, MEMW:2 * MEMW]) for kt in range(KT)], [b_w] + b_nm, n=MEMW)
        P.op("dve", lambda h_, kt_=kt_, psi=psi: h_.tensor_copy(out=memV[:, kt_, :], in_=K.ps[psi][:, 0:MEMW]),
             reads=[K.b_ps[psi]], writes=[b_mv])
    P.barrier()
    P.release(mark)
    return dict(memK=memK, memV=memV, b_mk=b_mk, b_mv=b_mv)


def alloc_mem_attn(K, NM):
    P = K.P
    return dict(pt=[P.sb([128, NM // 128, NT], BF16, "mpt") for _ in range(2)], b_pt=P.bufs_n(2, "mpt"),
                rec=[P.sb([128, NT], F32, "mrec") for _ in range(2)], b_rec=P.bufs_n(2, "mrec"),
                mo=[P.sb([128, NT], BF16, "mo") for _ in range(2)], b_mo=P.bufs_n(2, "mo"), n=0)


def mem_attn(K, MA, MKV, qm, b_qm, memo_d, b_md, t0, Tp, NM, MEMW, ps_s, ps_o, ps_r):
    P = K.P
    H = MEMW // 128
    NKT = NM // 128
    scale = 128.0 ** -0.5
    for h in range(H):
        for nt in range(Tp // NT):
            i = MA["n"] % 2
            MA["n"] += 1
            pt, rec, mo = MA["pt"][i], MA["rec"][i], MA["mo"][i]
            for kt in range(NKT):
                psi = ps_s.next()
                K.mm(psi, [(MKV["memK"][:, h, kt * 128:(kt + 1) * 128], qm[:, h, nt * NT:(nt + 1) * NT])], [MKV["b_mk"]] + b_qm)
                P.op("act", lambda h_, psi=psi, pt=pt, kt=kt: h_.activation(out=pt[:, kt, :], in_=K.ps[psi][:], func=AF.Exp, scale=scale),
                     reads=[K.b_ps[psi]], writes=[MA["b_pt"][i]])
            po, pr = ps_o.next(), ps_r.next()
            K.mm(po, [(MKV["memV"][:, kt, h * 128:(h + 1) * 128], pt[:, kt, :]) for kt in range(NKT)], [MKV["b_mv"], MA["b_pt"][i]])
            K.mm(pr, [(K.ones[:], pt[:, kt, :]) for kt in range(NKT)], [K.b_ones, MA["b_pt"][i]])
            P.op("dve", lambda h_, pr=pr, rec=rec: h_.reciprocal(out=rec[:], in_=K.ps[pr][:]), reads=[K.b_ps[pr]], writes=[MA["b_rec"][i]])
            P.op("dve", lambda h_, po=po, rec=rec, mo=mo: h_.tensor_tensor(out=mo[:], in0=K.ps[po][:], in1=rec[:], op=ALU.mult),
                 reads=[K.b_ps[po], MA["b_rec"][i]], writes=[MA["b_mo"][i]])
            K.dma(memo_d[h * 128:(h + 1) * 128, t0 + nt * NT: t0 + (nt + 1) * NT], mo[:], [MA["b_mo"][i]], [b_md])


def mixer_pre_A(K, x_d, g2, b_g, W_in, u_d, memo_d, MKV, D, TOKW, MEMW, NM, T, Tp, SUB=256):
    P = K.P
    KT = D // 128
    mark = P.mark()
    tm = alloc_norm_tmp(K, KT, SUB)
    hT = P.sb([128, KT, Tp], BF16, "hT")
    b_hT = P.bufs_n(Tp // SUB, "hT")
    qm = P.sb([128, MEMW // 128, Tp], BF16, "qm")
    b_qm = P.bufs_n(MEMW // 128, "qm")
    ws = WStream(K, KT, 256, "win")
    st = [P.sb([128, NT], BF16, "stg") for _ in range(3)]
    b_st = P.bufs_n(3, "stg")
    sti = Ring([0, 1, 2])
    MA = alloc_mem_attn(K, NM)
    x_v = x_d.rearrange("(kt p) t -> p kt t", p=128)
    W_v = W_in.rearrange("(kt p) c -> p kt c", p=128)
    b_x, b_ud, b_md = P.buf("x"), P.buf("ud"), P.buf("md")
    ps_mm = Ring([0, 1, 2])
    ps_s, ps_o, ps_r = Ring([0, 1, 2]), Ring([3, 4]), Ring([5, 7])
    MT = (TOKW + MEMW) // 128
    for p in range(T // Tp):
        t0 = p * Tp
        norm_in(K, tm, x_v, b_x, t0, Tp, g2, b_g, hT, b_hT, D, 6)
        for mg in range(MT // 2):
            w, bw = ws.load(W_v, mg * 256)
            for mi in range(2):
                m = mg * 2 + mi
                for nt in range(Tp // NT):
                    psi = ps_mm.next()
                    hb = b_hT[nt * (NT // SUB):(nt + 1) * (NT // SUB)]
                    K.mm(psi, [(w[:, kt, mi * 128:(mi + 1) * 128], hT[:, kt, nt * NT:(nt + 1) * NT]) for kt in range(KT)], [bw] + hb)
                    if m < TOKW // 128:
                        si = sti.next()
                        P.op("act", lambda h_, psi=psi, si=si: h_.activation(out=st[si][:], in_=K.ps[psi][:], func=AF.Copy),
                             reads=[K.b_ps[psi]], writes=[b_st[si]])
                        K.dma(u_d[m * 128:(m + 1) * 128, t0 + nt * NT: t0 + (nt + 1) * NT], st[si][:], [b_st[si]], [b_ud])
                    else:
                        hh = m - TOKW // 128
                        P.op("dve", lambda h_, psi=psi, hh=hh, nt=nt: h_.tensor_copy(out=qm[:, hh, nt * NT:(nt + 1) * NT], in_=K.ps[psi][:]),
                             reads=[K.b_ps[psi]], writes=[b_qm[hh]])
        mem_attn(K, MA, MKV, qm, b_qm, memo_d, b_md, t0, Tp, NM, MEMW, ps_s, ps_o, ps_r)
    P.barrier()
    P.release(mark)


def mixer_post(K, x_d, g3, b_g, W_out, tok_d, memo_d, D, TOKW, MEMW, T, Tp, W_glu=None, bglu=None, SUB=256):
    P = K.P
    KT = D // 128
    NTK, NMK = TOKW // 128, MEMW // 128
    mark = P.mark()
    tm = alloc_norm_tmp(K, KT, SUB)
    acc = P.sb([128, KT, Tp], F32, "acc")
    b_acc = [P.bufs_n(Tp // NT, "acc") for _ in range(KT)]
    tk = P.sb([128, NTK, Tp], BF16, "tk")
    b_tk = P.bufs_n(NTK, "tk")
    mo = P.sb([128, NMK, Tp], BF16, "mo")
    b_mo = P.buf("mo")
    b_x, b_td, b_md = P.buf("x"), P.buf("td"), P.buf("md")
    x_v = x_d.rearrange("(kt p) t -> p kt t", p=128)
    tok_v = tok_d.rearrange("(kt p) t -> p kt t", p=128)
    memo_v = memo_d.rearrange("(kt p) t -> p kt t", p=128)
    Wo_v = W_out.rearrange("(kt p) c -> p kt c", p=128)
    wso = WStream(K, KT, 256, "wout")
    ps_mm = Ring([0, 1, 2, 3])
    if W_glu is not None:
        yg = P.sb([128, NTK, Tp], BF16, "yg")
        b_yg = P.buf("yg")
        wsg = WStream(K, NTK, 256, "wglu")
        Wg_v = W_glu.rearrange("(kt p) c -> p kt c", p=128)
        gt = [P.sb([128, NT], F32, "gt") for _ in range(2)]
        b_gt = P.bufs_n(2, "gt")
        gti = Ring([0, 1])
    for p in range(T // Tp):
        t0 = p * Tp
        K.dma(mo[:], memo_v[:, :, t0:t0 + Tp], [b_md], [b_mo])
        if W_glu is None:
            K.dma(tk[:], tok_v[:, :, t0:t0 + Tp], [b_td], b_tk)
        else:
            K.dma(yg[:], tok_v[:, :, t0:t0 + Tp], [b_td], [b_yg])
            for mg in range((NTK + 1) // 2):
                nm_ = min(2, NTK - mg * 2)
                w, bw = wsg.load(Wg_v, mg * 256, nm_ * 128)
                for mi in range(nm_):
                    m = mg * 2 + mi
                    for nt in range(Tp // NT):
                        psi = ps_mm.next()
                        gi = gti.next()
                        K.mm(psi, [(w[:, kt, mi * 128:(mi + 1) * 128], yg[:, kt, nt * NT:(nt + 1) * NT]) for kt in range(NTK)], [bw, b_yg])
                        P.op("act", lambda h_, psi=psi, gi=gi, m=m: h_.activation(out=gt[gi][:], in_=K.ps[psi][:], func=AF.Sigmoid, bias=bglu[:, m:m + 1]),
                             reads=[K.b_ps[psi], b_g], writes=[b_gt[gi]])
                        P.op("dve", lambda h_, gi=gi, m=m, nt=nt: h_.tensor_tensor(out=tk[:, m, nt * NT:(nt + 1) * NT], in0=gt[gi][:],
                                                                                 in1=yg[:, m, nt * NT:(nt + 1) * NT], op=ALU.mult),
                             reads=[b_gt[gi], b_yg], writes=[b_tk[m]])
        for mg in range(KT // 2):
            w, bw = wso.load(Wo_v, mg * 256)
            for mi in range(2):
                m = mg * 2 + mi
                for nt in range(Tp // NT):
                    psi = ps_mm.next()
                    pairs = [(w[:, kt, mi * 128:(mi + 1) * 128], tk[:, kt, nt * NT:(nt + 1) * NT]) for kt in range(NTK)]
                    pairs += [(w[:, NTK + kt, mi * 128:(mi + 1) * 128], mo[:, kt, nt * NT:(nt + 1) * NT]) for kt in range(NMK)]
                    K.mm(psi, pairs, [bw, b_mo] + b_tk)
                    P.op("act", lambda h_, psi=psi, m=m, nt=nt: h_.activation(out=acc[:, m, nt * NT:(nt + 1) * NT], in_=K.ps[psi][:], func=AF.Copy),
                         reads=[K.b_ps[psi]], writes=[b_acc[m][nt]])
        finalize(K, tm, acc, b_acc, x_v, b_x, t0, Tp, g3, b_g, D, 6)
    P.barrier()
    P.release(mark)


def rope_tables(K, pos_d, invf_d, sgn_d, T):
    P = K.P
    cs = P.sb([64, T], F32, "rope_cs")
    sn = P.sb([64, T], F32, "rope_sn")
    b_r = P.buf("rope")
    mark = P.mark()
    pi_ = P.sb([64, T], I32, "pos_i")
    tmp = P.sb([64, T], F32, "rtmp")
    cf = P.sb([64, 2], F32, "rcf")
    K.dma(pi_[:], pos_d.rearrange("(o t) -> o t", o=1).broadcast_to([64, T]), [], [b_r])
    K.dma(cf[:, 0:1], invf_d, [], [b_r])
    K.dma(cf[:, 1:2], sgn_d, [], [b_r])
    MAGIC = 12582912.0
    steps = [
        lambda h: h.tensor_copy(out=sn[:], in_=pi_[:]),
        lambda h: h.tensor_scalar(out=sn[:], in0=sn[:], scalar1=cf[:, 0:1], scalar2=None, op0=ALU.mult),
        lambda h: h.tensor_scalar(out=cs[:], in0=sn[:], scalar1=0.25, scalar2=None, op0=ALU.add),
        lambda h: h.tensor_scalar(out=tmp[:], in0=sn[:], scalar1=MAGIC, scalar2=None, op0=ALU.add),
        lambda h: h.tensor_scalar(out=tmp[:], in0=tmp[:], scalar1=-MAGIC, scalar2=None, op0=ALU.add),
        lambda h: h.tensor_tensor(out=sn[:], in0=sn[:], in1=tmp[:], op=ALU.subtract),
        lambda h: h.tensor_scalar(out=tmp[:], in0=cs[:], scalar1=MAGIC, scalar2=None, op0=ALU.add),
        lambda h: h.tensor_scalar(out=tmp[:], in0=tmp[:], scalar1=-MAGIC, scalar2=None, op0=ALU.add),
        lambda h: h.tensor_tensor(out=cs[:], in0=cs[:], in1=tmp[:], op=ALU.subtract),
    ]
    P.chain("dve", steps, reads=[b_r], writes=[b_r])
    P.op("act", lambda h: h.activation(out=sn[:], in_=sn[:], func=AF.Sin, scale=2 * float(np.pi)), reads=[b_r], writes=[b_r])
    P.op("act", lambda h: h.activation(out=cs[:], in_=cs[:], func=AF.Sin, scale=2 * float(np.pi)), reads=[b_r], writes=[b_r])
    P.op("dve", lambda h: h.tensor_scalar(out=sn[:], in0=sn[:], scalar1=cf[:, 1:2], scalar2=None, op0=ALU.mult), reads=[b_r], writes=[b_r])
    P.barrier()
    P.release(mark)
    return dict(cs=cs, sn=sn, b=b_r)


def apply_rope(K, RT, psa, psb, out, t0, n, tmp, b_tmp, b_out):
    P = K.P
    P.op("dve", lambda h: h.tensor_tensor(out=tmp[0][0:64, 0:n], in0=K.ps[psa][0:64, 0:n], in1=RT["cs"][:, t0:t0 + n], op=ALU.mult),
         reads=[K.b_ps[psa], RT["b"]], writes=[b_tmp[0]])
    P.op("dve", lambda h: h.tensor_tensor(out=tmp[1][0:64, 0:n], in0=K.ps[psb][0:64, 0:n], in1=RT["sn"][:, t0:t0 + n], op=ALU.mult),
         reads=[K.b_ps[psb], RT["b"]], writes=[b_tmp[1]])
    P.op("pool", lambda h: h.tensor_tensor(out=out, in0=tmp[0][0:64, 0:n], in1=tmp[1][0:64, 0:n], op=ALU.add),
         reads=[b_tmp[0], b_tmp[1]], writes=[b_out])


def sub_rmsnorm(K, src, b_src, dst, b_dst, g_ap, b_g, nk, Tp, sq, b_sq, rstd, b_rstd, PS_M):
    P = K.P
    for nt in range(Tp // NT):
        sl = slice(nt * NT, (nt + 1) * NT)
        P.op("act", lambda h, sl=sl: h.activation(out=sq[:, :, :], in_=src[:, :, sl], func=AF.Square), reads=b_src, writes=[b_sq])
        K.rstd_from_sq(sq, nk, NT, PS_M, rstd, b_sq, b_rstd, nk * 128)

        def f(h, sl=sl):
            for kt in range(nk):
                r = h.scalar_tensor_tensor(out=dst[:, kt, sl], in0=src[:, kt, sl], scalar=g_ap[:, kt:kt + 1], in1=rstd[:],
                                           op0=ALU.mult, op1=ALU.mult)
            return r
        P.op("dve", f, reads=b_src + [b_rstd, b_g], writes=[b_dst[nt]])


def kv_stage(K, x_d, gkv_in, gkv, b_g, W_dkv, W_kr, W_uk, W_uv, RT, kn_d, kr_d, v_d, D, R, H, T, Tp, SUB=256):
    P = K.P
    KT = D // 128
    RK = R // 128
    mark = P.mark()
    tm = alloc_norm_tmp(K, KT, SUB)
    hT = P.sb([128, KT, Tp], BF16, "hT")
    b_hT = P.bufs_n(Tp // SUB, "hT")
    ck = P.sb([128, RK, Tp], F32, "ck")
    b_ck = P.bufs_n(1, "ck")
    ckn = P.sb([128, RK, Tp], BF16, "ckn")
    b_ckn = P.bufs_n(Tp // NT, "ckn")
    sq = P.sb([128, RK, NT], BF16, "sq2")
    rstd = P.sb([128, NT], F32, "rstd2")
    b_sq, b_rstd = P.buf("sq2"), P.buf("rstd2")
    wd = P.sb([128, KT, R], BF16, "wdkv")
    wkr = P.sb([128, KT, 128], BF16, "wkr")
    wuk = P.sb([128, RK, H * 128], BF16, "wuk")
    wuv = P.sb([128, RK, H * 128], BF16, "wuv")
    b_w = P.buf("kvw")
    K.dma(wd[:], W_dkv.rearrange("(kt p) c -> p kt c", p=128), [], [b_w], eng="pool")
    wkr_v = W_kr.rearrange("(kt p) c -> p kt c", p=128)
    K.dma(wkr[:, :, 0:64], wkr_v, [], [b_w], eng="pool")
    K.dma(wkr[:, :, 64:96], wkr_v[:, :, 32:64], [], [b_w], eng="pool")
    K.dma(wkr[:, :, 96:128], wkr_v[:, :, 0:32], [], [b_w], eng="pool")
    K.dma(wuk[:], W_uk.rearrange("(kt p) c -> p kt c", p=128), [], [b_w], eng="pool")
    K.dma(wuv[:], W_uv.rearrange("(kt p) c -> p kt c", p=128), [], [b_w], eng="pool")
    st = [P.sb([128, NT], BF16, "stg") for _ in range(3)]
    b_st = P.bufs_n(3, "stg")
    sti = Ring([0, 1, 2])
    rtmp = [P.sb([128, NT], F32, "rtmp") for _ in range(2)]
    b_rtmp = P.bufs_n(2, "rtmp")
    x_v = x_d.rearrange("(kt p) t -> p kt t", p=128)
    b_x, b_kn, b_kr, b_v = P.buf("x"), P.buf("kn"), P.buf("kr"), P.buf("v")
    ps_mm = Ring([0, 1, 2, 3])
    for p in range(T // Tp):
        t0 = p * Tp
        norm_in(K, tm, x_v, b_x, t0, Tp, gkv_in, b_g, hT, b_hT, D, 6)
        for nt in range(Tp // NT):
            hb = b_hT[nt * (NT // SUB):(nt + 1) * (NT // SUB)]
            sl = slice(nt * NT, (nt + 1) * NT)
            for m in range(RK):
                psi = ps_mm.next()
                K.mm(psi, [(wd[:, kt, m * 128:(m + 1) * 128], hT[:, kt, sl]) for kt in range(KT)], [b_w] + hb)
                P.op("act", lambda h_, psi=psi, m=m, sl=sl: h_.activation(out=ck[:, m, sl], in_=K.ps[psi][:], func=AF.Copy),
                     reads=[K.b_ps[psi]], writes=b_ck)
            pa, pb = ps_mm.next(), ps_mm.next()
            K.mm(pa, [(wkr[:, kt, 0:64], hT[:, kt, sl]) for kt in range(KT)], [b_w] + hb, m=64)
            K.mm(pb, [(wkr[:, kt, 64:128], hT[:, kt, sl]) for kt in range(KT)], [b_w] + hb, m=64)
            si = sti.next()
            apply_rope(K, RT, pa, pb, st[si][0:64, :], t0 + nt * NT, NT, rtmp, b_rtmp, b_st[si])
            K.dma(kr_d[:, t0 + nt * NT: t0 + (nt + 1) * NT], st[si][0:64, :], [b_st[si]], [b_kr])
        sub_rmsnorm(K, ck, b_ck, ckn, b_ckn, gkv, b_g, RK, Tp, sq, b_sq, rstd, b_rstd, 6)
        for nt in range(Tp // NT):
            sl = slice(nt * NT, (nt + 1) * NT)
            for hh in range(H):
                psi = ps_mm.next()
                K.mm(psi, [(wuk[:, kt, hh * 128:(hh + 1) * 128], ckn[:, kt, sl]) for kt in range(RK)], [b_w, b_ckn[nt]])
                si = sti.next()
                P.op("act", lambda h_, psi=psi, si=si: h_.activation(out=st[si][:], in_=K.ps[psi][:], func=AF.Copy),
                     reads=[K.b_ps[psi]], writes=[b_st[si]])
                K.dma(kn_d[hh, :, t0 + nt * NT: t0 + (nt + 1) * NT], st[si][:], [b_st[si]], [b_kn])
            for tt in range(NT // 128):
                tsl = slice(nt * NT + tt * 128, nt * NT + (tt + 1) * 128)
                CW = min(NT, H * 128)
                for cc in range(H * 128 // CW):
                    psi = ps_mm.next()
                    K.mm(psi, [(ckn[:, kt, tsl], wuv[:, kt, cc * CW:(cc + 1) * CW]) for kt in range(RK)], [b_w, b_ckn[nt]], n=CW)
                    si = sti.next()
                    P.op("dve", lambda h_, psi=psi, si=si, CW=CW: h_.tensor_copy(out=st[si][:, 0:CW], in_=K.ps[psi][:, 0:CW]),
                         reads=[K.b_ps[psi]], writes=[b_st[si]])
                    K.dma(v_d[t0 + nt * NT + tt * 128: t0 + nt * NT + (tt + 1) * 128, cc * CW:(cc + 1) * CW], st[si][:, 0:CW], [b_st[si]], [b_v])
    P.barrier()
    P.release(mark)


def mixer_pre_B(K, x_d, g2, gq, b_g, W_in, W_uq, RT, qn_d, qr_d, memo_d, MKV, D, R, H, MEMW, NM, T, Tp, SUB=256):
    P = K.P
    KT = D // 128
    RK = R // 128
    mark = P.mark()
    tm = alloc_norm_tmp(K, KT, SUB)
    hT = P.sb([128, KT, Tp], BF16, "hT")
    b_hT = P.bufs_n(Tp // SUB, "hT")
    cq = P.sb([128, RK, Tp], F32, "cq")
    b_cq = P.bufs_n(1, "cq")
    cqn = P.sb([128, RK, Tp], BF16, "cqn")
    b_cqn = P.bufs_n(Tp // NT, "cqn")
    sq = P.sb([128, RK, NT], BF16, "sq2")
    rstd = P.sb([128, NT], F32, "rstd2")
    b_sq, b_rstd = P.buf("sq2"), P.buf("rstd2")
    qm = P.sb([128, MEMW // 128, Tp], BF16, "qm")
    b_qm = P.bufs_n(MEMW // 128, "qm")
    ws = WStream(K, KT, 256, "win")
    HD = 192
    wuq = P.sb([128, RK, H, HD + 64], BF16, "wuq")
    b_wq = P.buf("wuq")
    wq_v = W_uq.rearrange("(kt p) (h e) -> p kt h e", p=128, e=HD)
    for kt in range(RK):
        K.dma(wuq[:, kt, :, 0:HD], wq_v[:, kt, :, :], [], [b_wq], eng="pool")
        K.dma(wuq[:, kt, :, HD:HD + 32], wq_v[:, kt, :, 160:192], [], [b_wq], eng="pool")
        K.dma(wuq[:, kt, :, HD + 32:HD + 64], wq_v[:, kt, :, 128:160], [], [b_wq], eng="pool")
    st = [P.sb([128, NT], BF16, "stg") for _ in range(3)]
    b_st = P.bufs_n(3, "stg")
    sti = Ring([0, 1, 2])
    rtmp = [P.sb([128, NT], F32, "rtmp") for _ in range(2)]
    b_rtmp = P.bufs_n(2, "rtmp")
    MA = alloc_mem_attn(K, NM)
    x_v = x_d.rearrange("(kt p) t -> p kt t", p=128)
    W_v = W_in.rearrange("(kt p) c -> p kt c", p=128)
    b_x, b_qn, b_qr, b_md = P.buf("x"), P.buf("qn"), P.buf("qr"), P.buf("md")
    ps_mm = Ring([0, 1, 2])
    ps_s, ps_o, ps_r = Ring([0, 1, 2]), Ring([3, 4]), Ring([5, 7])
    MT = (R + MEMW) // 128
    for p in range(T // Tp):
        t0 = p * Tp
        norm_in(K, tm, x_v, b_x, t0, Tp, g2, b_g, hT, b_hT, D, 6)
        for mg in range(MT // 2):
            w, bw = ws.load(W_v, mg * 256)
            for mi in range(2):
                m = mg * 2 + mi
                for nt in range(Tp // NT):
                    psi = ps_mm.next()
                    hb = b_hT[nt * (NT // SUB):(nt + 1) * (NT // SUB)]
                    sl = slice(nt * NT, (nt + 1) * NT)
                    K.mm(psi, [(w[:, kt, mi * 128:(mi + 1) * 128], hT[:, kt, sl]) for kt in range(KT)], [bw] + hb)
                    if m < RK:
                        P.op("act", lambda h_, psi=psi, m=m, sl=sl: h_.activation(out=cq[:, m, sl], in_=K.ps[psi][:], func=AF.Copy),
                             reads=[K.b_ps[psi]], writes=b_cq)
                    else:
                        hh = m - RK
                        P.op("dve", lambda h_, psi=psi, hh=hh, sl=sl: h_.tensor_copy(out=qm[:, hh, sl], in_=K.ps[psi][:]),
                             reads=[K.b_ps[psi]], writes=[b_qm[hh]])
        mem_attn(K, MA, MKV, qm, b_qm, memo_d, b_md, t0, Tp, NM, MEMW, ps_s, ps_o, ps_r)
        sub_rmsnorm(K, cq, b_cq, cqn, b_cqn, gq, b_g, RK, Tp, sq, b_sq, rstd, b_rstd, 6)
        for nt in range(Tp // NT):
            sl = slice(nt * NT, (nt + 1) * NT)
            for hh in range(H):
                psi = ps_mm.next()
                K.mm(psi, [(wuq[:, kt, hh, 0:128], cqn[:, kt, sl]) for kt in range(RK)], [b_wq, b_cqn[nt]])
                si = sti.next()
                P.op("act", lambda h_, psi=psi, si=si: h_.activation(out=st[si][:], in_=K.ps[psi][:], func=AF.Copy),
                     reads=[K.b_ps[psi]], writes=[b_st[si]])
                K.dma(qn_d[hh, :, t0 + nt * NT: t0 + (nt + 1) * NT], st[si][:], [b_st[si]], [b_qn])
                pa, pb = ps_mm.next(), ps_mm.next()
                K.mm(pa, [(wuq[:, kt, hh, 128:192], cqn[:, kt, sl]) for kt in range(RK)], [b_wq, b_cqn[nt]], m=64)
                K.mm(pb, [(wuq[:, kt, hh, 192:256], cqn[:, kt, sl]) for kt in range(RK)], [b_wq, b_cqn[nt]], m=64)
                si = sti.next()
                apply_rope(K, RT, pa, pb, st[si][0:64, :], t0 + nt * NT, NT, rtmp, b_rtmp, b_st[si])
                K.dma(qr_d[hh, :, t0 + nt * NT: t0 + (nt + 1) * NT], st[si][0:64, :], [b_st[si]], [b_qr])
    P.barrier()
    P.release(mark)


def mla_attn(K, qn_d, qr_d, kn_d, kr_d, v_d, knp_d, krp_d, vp_d, pbias_d, mask_d, tok_d, H, T):
    P = K.P
    mark = P.mark()
    NKT = T // 128
    QB = T // NT
    scale = 192.0 ** -0.5
    kr = P.sb([64, 2 * T], BF16, "kr")
    b_krs = P.buf("kr")
    K.dma(kr[:, 0:T], krp_d, [], [b_krs])
    K.dma(kr[:, T:2 * T], kr_d, [], [b_krs])
    pb = P.sb([128, 1], F32, "pbias")
    b_pb = P.buf("pbias")
    K.dma(pb[:], pbias_d, [], [b_pb])
    msk = P.sb([128, 4, NT], BF16, "mask")
    b_msk = P.buf("mask")
    K.dma(msk[:], mask_d.rearrange("i p q -> p i q"), [], [b_msk], eng="pool")
    kn = [P.sb([128, 2 * T], BF16, "kn") for _ in range(2)]
    vv = [P.sb([128, 2 * NKT, 128], BF16, "vv") for _ in range(2)]
    qn = [P.sb([128, T], BF16, "qn") for _ in range(2)]
    qr = [P.sb([64, T], BF16, "qr") for _ in range(2)]
    b_hd = P.bufs_n(2, "headin")
    pt = [P.sb([128, NT], BF16, "pt") for _ in range(3)]
    b_pt = P.bufs_n(3, "pt")
    pti = Ring([0, 1, 2])
    rec = [P.sb([128, NT], F32, "rec") for _ in range(2)]
    b_rec = P.bufs_n(2, "rec")
    ob = [P.sb([128, NT], BF16, "ob") for _ in range(2)]
    b_ob = P.bufs_n(2, "ob")
    b_td = P.buf("tokd")
    ps_s, ps_o, ps_r = Ring([0, 1, 2]), Ring([3, 4]), Ring([5, 6])
    fin = 0
    def load_head(h):
        s_ = h % 2
        K.dma(kn[s_][:, 0:T], knp_d[h], [], [b_hd[s_]])
        K.dma(kn[s_][:, T:2 * T], kn_d[h], [], [b_hd[s_]])
        K.dma(vv[s_][:, 0:NKT, :], vp_d[:, h * 128:(h + 1) * 128].rearrange("(t p) d -> p t d", p=128), [], [b_hd[s_]])
        K.dma(vv[s_][:, NKT:2 * NKT, :], v_d[:, h * 128:(h + 1) * 128].rearrange("(t p) d -> p t d", p=128), [], [b_hd[s_]])
        K.dma(qn[s_][:], qn_d[h], [], [b_hd[s_]])
        K.dma(qr[s_][:], qr_d[h], [], [b_hd[s_]])
    load_head(0)
    for h in range(H):
        s_ = h % 2
        if h + 1 < H:
            load_head(h + 1)
        for qb in range(QB):
            qsl = slice(qb * NT, (qb + 1) * NT)
            tiles = list(range(NKT)) + [NKT + j for j in range(4 * qb + 4)]
            po, pr = ps_o.next(), ps_r.next()
            n = len(tiles)

            def score(kt):
                psi = ps_s.next()
                ksl = slice(kt * 128, (kt + 1) * 128)
                K.mm(psi, [(kn[s_][:, ksl], qn[s_][:, qsl]), (kr[0:64, ksl], qr[s_][0:64, qsl])], [b_hd[s_], b_krs])
                return psi
            pend = [score(tiles[0])]
            if n > 1:
                pend.append(score(tiles[1]))
            for i, kt in enumerate(tiles):
                psi = pend.pop(0)
                if i + 2 < n:
                    pend.append(score(tiles[i + 2]))
                pi_ = pti.next()
                prev = kt < NKT
                if prev:
                    P.op("act", lambda h_, psi=psi, pi_=pi_: h_.activation(out=pt[pi_][:], in_=K.ps[psi][:], func=AF.Exp, scale=scale, bias=pb[:, 0:1]),
                         reads=[K.b_ps[psi], b_pb], writes=[b_pt[pi_]])
                else:
                    P.op("act", lambda h_, psi=psi, pi_=pi_: h_.activation(out=pt[pi_][:], in_=K.ps[psi][:], func=AF.Exp, scale=scale),
                         reads=[K.b_ps[psi]], writes=[b_pt[pi_]])
                    di = kt - NKT - 4 * qb
                    if di >= 0:
                        P.op("pool", lambda h_, pi_=pi_, di=di: h_.tensor_tensor(out=pt[pi_][:], in0=pt[pi_][:], in1=msk[:, di, :], op=ALU.mult),
                             reads=[b_msk], writes=[b_pt[pi_]])
                ptap = pt[pi_][:]

                def mo(h_, po=po, pr=pr, kt=kt, ptap=ptap, i=i, n=n, s_=s_):
                    h_.matmul(K.ps[po][:], lhsT=vv[s_][:, kt, :], rhs=ptap, start=(i == 0), stop=(i == n - 1))
                    return h_.matmul(K.ps[pr][:], lhsT=K.ones[:], rhs=ptap, start=(i == 0), stop=(i == n - 1))
                P.op("pe", mo, reads=[b_pt[pi_], b_hd[s_], K.b_ones], writes=[K.b_ps[po], K.b_ps[pr]])
            fi = fin % 2
            fin += 1
            P.op("dve", lambda h_, pr=pr, fi=fi: h_.reciprocal(out=rec[fi][:], in_=K.ps[pr][:]), reads=[K.b_ps[pr]], writes=[b_rec[fi]])
            P.op("dve", lambda h_, po=po, fi=fi: h_.tensor_tensor(out=ob[fi][:], in0=K.ps[po][:], in1=rec[fi][:], op=ALU.mult),
                 reads=[K.b_ps[po], b_rec[fi]], writes=[b_ob[fi]])
            K.dma(tok_d[h * 128:(h + 1) * 128, qsl], ob[fi][:], [b_ob[fi]], [b_td])
    P.barrier()
    P.release(mark)


class Cfg:
    def __init__(self, D=2048, DFF=5632, TOKW=1536, MEMW=512, NM=256, G=96, R=512, H=12, SEQ=4096, B=4, L=4, Tp=1024):
        self.D, self.DFF, self.TOKW, self.MEMW, self.NM, self.G, self.R, self.H = D, DFF, TOKW, MEMW, NM, G, R, H
        self.SEQ, self.B, self.L, self.Tp = SEQ, B, L, Tp
        self.T = SEQ // 2
        self.KT = D // 128
        self.NA = L // 2
        self.NP = G // 2
        self.NQ = G // 8
        self.RK = R // 128
        self.NTK = TOKW // 128


def gain_layout(cfg, norms, mem_norm, kv_in_norm, kv_norm, mla_q_norm, s5_b_glu):
    cols, off = [], {}

    def add(name, v):
        v = np.asarray(v, np.float32)
        n = v.shape[-1] // 128
        a = v.reshape(-1, n, 128)
        a = np.transpose(a, (2, 0, 1)).reshape(128, -1)
        off[name] = (sum(c.shape[1] for c in cols), n)
        cols.append(a)
    add("norms", norms)
    add("mem_norm", mem_norm)
    add("kv_in", kv_in_norm)
    add("kv", kv_norm)
    add("q", mla_q_norm)
    add("bglu", s5_b_glu)
    return np.ascontiguousarray(np.concatenate(cols, 1)), off


def build_segment(cfg, seg, W):
    c = cfg
    nc = bass.Bass("TRN2", target_bir_lowering=False)
    D, T, Tp = c.D, c.T, c.Tp

    def din(name, shape, dt=F32):
        return nc.dram_tensor(name, list(shape), dt, kind="ExternalInput").ap()

    def dout(name, shape, dt=F32):
        return nc.dram_tensor(name, list(shape), dt, kind="ExternalOutput").ap()

    def dtmp(name, shape, dt=F32):
        return nc.dram_tensor(name, list(shape), dt, kind="Internal").ap()
    K = KB(nc)
    P = K.P
    gshape = W["gains"].shape
    gains_d = din("gains", gshape)
    goff = W["goff"]
    gains = P.sb(list(gshape), F32, "gains")
    b_g = P.buf("gains")
    K.dma(gains[:], gains_d, [], [b_g])

    def gn(l, i):
        o = goff["norms"][0] + (l * 6 + i) * c.KT
        return gains[:, o:o + c.KT]

    def gsl(name, idx, n):
        o = goff[name][0] + idx * n
        return gains[:, o:o + n]
    x_in = din("x_in", [D, T])
    x = dout("x_out", [D, T])
    b_xc = P.buf("xcopy")
    K.dma(x, x_in, [], [b_xc])
    P.barrier()
    wts = {}

    def wt(name, l=None, j=None):
        key = (name, l, j)
        if key not in wts:
            a = W[name]
            shp = a.shape
            if l is not None:
                shp = shp[1:]
            if j is not None:
                shp = shp[1:]
            nm_ = f"{name}_{l}_{j}".replace("None", "x")
            wts[key] = (nm_, din(nm_, shp))
        return wts[key][1]

    def ffn(l, i):
        ffn_stage(K, x, gn(l, 2 * i if i == 0 else 4), gn(l, 1 if i == 0 else 5), b_g,
                  wt("ffn_w_gate", l, i), wt("ffn_w_up", l, i), wt("ffn_w_down", l, i), D, c.DFF, T, Tp)

    def s5prm(l):
        return dict(lam_s=din(f"s5lam_s{l}", [128, 3, c.NP]), lam_r=din(f"s5lam_r{l}", [128, 3, c.NP * 128]),
                    bT=din(f"s5bT{l}", [2, 128, c.NP, 128]), cP=din(f"s5cP{l}", [2, 128, c.NP, 128]))
    mem_d = din("memT", [D, c.NM])

    def pre_A(l, u_d, memo_d):
        mk = P.mark()
        MKV = mem_kv_setup(K, mem_d, gsl("mem_norm", l, c.KT), b_g, wt("mem_w_kv", l), D, c.NM, c.MEMW)
        mixer_pre_A(K, x, gn(l, 2), b_g, wt("a_w_in", l), u_d, memo_d, MKV, D, c.TOKW, c.MEMW, c.NM, T, Tp)
        P.release(mk)

    def post_A(l, yg_d, memo_d):
        mixer_post(K, x, gn(l, 3), b_g, wt("w_out", l), yg_d, memo_d, D, c.TOKW, c.MEMW, T, Tp,
                   W_glu=wt("s5_w_glu", l), bglu=gsl("bglu", l, c.NTK))

    if seg in (1, 2, 3):
        l_scan1 = {1: 0, 2: 1, 3: None}[seg]
        l_scan2 = {1: None, 2: 0, 3: 1}[seg]
        if l_scan2 is not None:
            l = l_scan2
            u_d = din("u_in", [c.TOKW, T], BF16)
            memo_d = din("memo_in", [c.MEMW, T], BF16)
            carry_in = din("carry_in", [128, 2, c.NP])
            carry_dummy = dtmp("carry_dummy", [128, 2, c.NP])
            yg_d = dtmp("yg", [c.TOKW, T], BF16)
            mk = P.mark()
            S = s5_setup(K, s5prm(l), c.NP)
            s5_scan(K, S, c.NP, T, u_d, carry_in, carry_dummy, yg_d, din(f"s5d{l}", [128, c.NQ]), True)
            P.release(mk)
            post_A(l, yg_d, memo_d)
            ffn(l, 1)
        if l_scan1 is not None:
            l = l_scan1
            ffn(l, 0)
            u_o = dout("u_out", [c.TOKW, T], BF16)
            memo_o = dout("memo_out", [c.MEMW, T], BF16)
            carry_o = dout("carry_out", [128, 2, c.NP])
            zero_c = din("zero_carry", [128, 2, c.NP])
            pre_A(l, u_o, memo_o)
            mk = P.mark()
            S = s5_setup(K, s5prm(l), c.NP)
            s5_scan(K, S, c.NP, T, u_o, zero_c, carry_o, None, None, False)
            P.release(mk)
        if seg == 3:
            RT = rope_tables(K, din("pos", [T], I32), din("invf", [64, 1]), din("sgn", [64, 1]), T)
            kv_stage(K, x, gsl("kv_in", 0, c.KT), gsl("kv", 0, c.RK), b_g, wt("w_dkv"), wt("w_kr"), wt("w_uk"), wt("w_uv"), RT,
                     dout("kn_out", [c.H, 128, T], BF16), dout("kr_out", [64, T], BF16), dout("v_out", [T, c.H * 128], BF16),
                     D, c.R, c.H, T, Tp)
    else:
        RT = rope_tables(K, din("pos", [T], I32), din("invf", [64, 1]), din("sgn", [64, 1]), T)
        kn_d, kr_d, v_d = din("kn", [c.H, 128, T], BF16), din("kr", [64, T], BF16), din("v", [T, c.H * 128], BF16)
        knp_d, krp_d, vp_d = din("knp", [c.H, 128, T], BF16), din("krp", [64, T], BF16), din("vp", [T, c.H * 128], BF16)
        pbias_d = din("pbias", [128, 1])
        mask_d = din("cmask", [4, 128, NT])
        qn_d = dtmp("qn", [c.H, 128, T], BF16)
        qr_d = dtmp("qr", [c.H, 64, T], BF16)
        memo_d = dtmp("memo", [c.MEMW, T], BF16)
        tok_d = dtmp("tok", [c.TOKW, T], BF16)
        for l in range(c.NA, c.L):
            j = l - c.NA
            ffn(l, 0)
            mk = P.mark()
            MKV = mem_kv_setup(K, mem_d, gsl("mem_norm", l, c.KT), b_g, wt("mem_w_kv", l), D, c.NM, c.MEMW)
            mixer_pre_B(K, x, gn(l, 2), gsl("q", j, c.RK), b_g, wt("b_w_in", j), wt("mla_w_uq", j), RT, qn_d, qr_d, memo_d, MKV,
                        D, c.R, c.H, c.MEMW, c.NM, T, Tp)
            P.release(mk)
            mla_attn(K, qn_d, qr_d, kn_d, kr_d, v_d, knp_d, krp_d, vp_d, pbias_d, mask_d, tok_d, c.H, T)
            mixer_post(K, x, gn(l, 3), b_g, wt("w_out", l), tok_d, memo_d, D, c.TOKW, c.MEMW, T, Tp)
            ffn(l, 1)
    P.barrier()
    P.emit()
    return nc, wts


def run_model(cfg, inp, dbg=None):
    c = cfg
    T = c.T
    NCORE = 2 * c.B
    f32 = np.float32
    gains, goff = gain_layout(c, inp["norms"], inp["mem_norm"], inp["kv_in_norm"], inp["kv_norm"], inp["mla_q_norm"], inp["s5_b_glu"])
    W = dict(inp)
    W["gains"], W["goff"] = gains, goff
    s5l = [s5_host_layout(*(np.asarray(inp[k][l], f32) for k in ("s5_lambda_re", "s5_lambda_im", "s5_log_dt", "s5_b_re", "s5_b_im",
                                                                 "s5_c_re", "s5_c_im", "s5_d"))) for l in range(c.NA)]
    xT = [np.ascontiguousarray(np.asarray(inp["x"][cid // 2, (cid % 2) * T:(cid % 2 + 1) * T, :], f32).T) for cid in range(NCORE)]
    memT = [np.ascontiguousarray(np.asarray(inp["mem"][b], f32).T) for b in range(c.B)]
    inv_freq = (10000.0 ** (-np.arange(0, 64, 2, dtype=np.float32) / 64)).astype(f32)
    invf = np.concatenate([inv_freq, inv_freq])[:, None].astype(f32) / f32(2 * np.pi)
    sgn = np.concatenate([-np.ones(32, f32), np.ones(32, f32)])[:, None]
    kk, qq = np.arange(128)[:, None], np.arange(NT)[None, :]
    cmask = np.stack([(qq >= 128 * i + kk).astype(f32) for i in range(4)], 0)
    zero_carry = np.zeros((128, 2, c.NP), f32)

    def launch(seg, per_core):
        nc, wts = build_segment(c, seg, W)
        shared = {"gains": gains}
        for (name, l, j), (nm_, ap) in wts.items():
            a = inp[name]
            if l is not None:
                a = a[l]
            if j is not None:
                a = a[j]
            shared[nm_] = np.ascontiguousarray(np.asarray(a, f32))
        maps = []
        for cid in range(NCORE):
            m = dict(shared)
            m["memT"] = memT[cid // 2]
            m.update(per_core[cid])
            maps.append(m)
        res = run_bass_kernel_spmd(nc, maps, core_ids=list(range(NCORE)))
        if dbg is not None:
            dbg[seg] = res.results
        return res.results

    def s5in(l):
        return {f"s5lam_s{l}": s5l[l]["lam_s"], f"s5lam_r{l}": s5l[l]["lam_r"], f"s5bT{l}": s5l[l]["bT"], f"s5cP{l}": s5l[l]["cP"]}
    pos = [np.ascontiguousarray(np.asarray(inp["positions"][cid // 2, (cid % 2) * T:(cid % 2 + 1) * T], np.int32)) for cid in range(NCORE)]
    r = launch(1, [dict(x_in=xT[cid], zero_carry=zero_carry, **s5in(0)) for cid in range(NCORE)])
    for seg in (2, 3):
        l2 = seg - 2
        pc = []
        for cid in range(NCORE):
            cin = r[cid - 1]["carry_out"] if cid % 2 == 1 else zero_carry
            d = dict(x_in=r[cid]["x_out"], u_in=r[cid]["u_out"], memo_in=r[cid]["memo_out"], carry_in=cin,
                     zero_carry=zero_carry, **s5in(l2))
            d[f"s5d{l2}"] = s5l[l2]["d_s"]
            if seg == 2:
                d.update(s5in(1))
            else:
                d.update(pos=pos[cid], invf=invf, sgn=sgn)
            pc.append(d)
        r = launch(seg, pc)
    pc = []
    for cid in range(NCORE):
        prev = r[cid - 1] if cid % 2 == 1 else r[cid]
        pc.append(dict(x_in=r[cid]["x_out"], kn=r[cid]["kn_out"], kr=r[cid]["kr_out"], v=r[cid]["v_out"],
                       knp=prev["kn_out"], krp=prev["kr_out"], vp=prev["v_out"],
                       pbias=np.full((128, 1), 0.0 if cid % 2 == 1 else -30000.0, f32), cmask=cmask,
                       pos=pos[cid], invf=invf, sgn=sgn))
    r = launch(4, pc)
    out = np.empty((c.B, c.SEQ, c.D), f32)
    for cid in range(NCORE):
        out[cid // 2, (cid % 2) * T:(cid % 2 + 1) * T, :] = r[cid]["x_out"].T
    return out


def build_fused(cfg, W):
    c = cfg
    nc = bass.Bass("TRN2", target_bir_lowering=False)
    D, T, Tp = c.D, c.T, c.Tp

    def din(name, shape, dt=F32):
        return nc.dram_tensor(name, list(shape), dt, kind="ExternalInput").ap()

    def dout(name, shape, dt=F32):
        return nc.dram_tensor(name, list(shape), dt, kind="ExternalOutput").ap()

    def dtmp(name, shape, dt=F32):
        return nc.dram_tensor(name, list(shape), dt, kind="Internal").ap()
    K = KB(nc)
    P = K.P
    gshape = W["gains"].shape
    goff = W["goff"]
    gains = P.sb(list(gshape), F32, "gains")
    b_g = P.buf("gains")
    K.dma(gains[:], din("gains", gshape), [], [b_g])
    cflag = P.sb([128, 1], F32, "cflag")
    K.dma(cflag[:], din("cflag", [128, 1]), [], [b_g])

    def gn(l, i):
        o = goff["norms"][0] + (l * 6 + i) * c.KT
        return gains[:, o:o + c.KT]

    def gsl(name, idx, n):
        o = goff[name][0] + idx * n
        return gains[:, o:o + n]
    x = dout("x_out", [D, T])
    xp = dtmp("xp", [D, T])
    b_xc = P.buf("xcopy")
    K.dma(x, din("x_in", [D, T]), [], [b_xc])
    K.dma(xp, din("xp_in", [D, T]), [], [b_xc])
    P.barrier()
    wts = {}

    def wt(name, l=None, j=None):
        key = (name, l, j)
        if key not in wts:
            shp = W[name].shape
            if l is not None:
                shp = shp[1:]
            if j is not None:
                shp = shp[1:]
            nm_ = f"{name}_{l}_{j}".replace("None", "x")
            wts[key] = (nm_, din(nm_, shp))
        return wts[key][1]

    def fspec(l, i):
        return (gn(l, 0 if i == 0 else 4), gn(l, 1 if i == 0 else 5), wt("ffn_w_gate", l, i), wt("ffn_w_up", l, i), wt("ffn_w_down", l, i))

    def ffn(l, i, xx):
        ffn_multi(K, xx, [fspec(l, i)], b_g, D, c.DFF, T, Tp)

    def ffn2(l, xx):
        ffn_multi(K, xx, [fspec(l, 1), fspec(l + 1, 0)], b_g, D, c.DFF, T, Tp)
    s5p = [dict(lam_s=din(f"s5lam_s{l}", [128, 3, c.NP]), lam_r=din(f"s5lam_r{l}", [128, 3, c.NP * 128]),
                bT=din(f"s5bT{l}", [2, 128, c.NP, 128]), cP=din(f"s5cP{l}", [2, 128, c.NP, 128]),
                d=din(f"s5d{l}", [128, c.NQ])) for l in range(c.NA)]
    mem_d = din("memT", [D, c.NM])
    u_d = dtmp("u", [c.TOKW, T], BF16)
    memo_d = dtmp("memo", [c.MEMW, T], BF16)
    yg_d = dtmp("yg", [c.TOKW, T], BF16)
    zero_c = din("zero_carry", [128, 2, c.NP])
    carryA = [dtmp(f"carryA{l}", [128, 2, c.NP]) for l in range(c.NA)]
    carry_dummy = dtmp("carry_dummy", [128, 2, c.NP])
    kn_d, kr_d, v_d = dtmp("kn", [c.H, 128, T], BF16), dtmp("kr", [64, T], BF16), dtmp("v", [T, c.H * 128], BF16)
    knp_d, krp_d, vp_d = dtmp("knp", [c.H, 128, T], BF16), dtmp("krp", [64, T], BF16), dtmp("vp", [T, c.H * 128], BF16)
    invf_d, sgn_d = din("invf", [64, 1]), din("sgn", [64, 1])

    def a_layer(l, xx, cin, cout, flag, first, last_single):
        if first:
            ffn(l, 0, xx)
        mk = P.mark()
        MKV = mem_kv_setup(K, mem_d, gsl("mem_norm", l, c.KT), b_g, wt("mem_w_kv", l), D, c.NM, c.MEMW)
        mixer_pre_A(K, xx, gn(l, 2), b_g, wt("a_w_in", l), u_d, memo_d, MKV, D, c.TOKW, c.MEMW, c.NM, T, Tp)
        P.release(mk)
        mk = P.mark()
        S = s5_setup(K, s5p[l], c.NP)
        s5_scan(K, S, c.NP, T, u_d, cin, cout, yg_d, s5p[l]["d"], True, flag=flag, b_flag=b_g)
        P.release(mk)
        mixer_post(K, xx, gn(l, 3), b_g, wt("w_out", l), yg_d, memo_d, D, c.TOKW, c.MEMW, T, Tp,
                   W_glu=wt("s5_w_glu", l), bglu=gsl("bglu", l, c.NTK))
        if last_single:
            ffn(l, 1, xx)
        else:
            ffn2(l, xx)

    def kv(xx, pos_name, kn_, kr_, v_):
        mk = P.mark()
        RT = rope_tables(K, din(pos_name, [T], I32), invf_d, sgn_d, T)
        kv_stage(K, xx, gsl("kv_in", 0, c.KT), gsl("kv", 0, c.RK), b_g, wt("w_dkv"), wt("w_kr"), wt("w_uk"), wt("w_uv"), RT,
                 kn_, kr_, v_, D, c.R, c.H, T, Tp)
        return mk, RT
    for l in range(c.NA):
        a_layer(l, xp, zero_c, carryA[l], None, l == 0, l == c.NA - 1)
    mk, _ = kv(xp, "posp", knp_d, krp_d, vp_d)
    P.release(mk)
    for l in range(c.NA):
        a_layer(l, x, carryA[l], carry_dummy, cflag, l == 0, l == c.NA - 1)
    mk, RT = kv(x, "pos", kn_d, kr_d, v_d)
    pbias_d = din("pbias", [128, 1])
    mask_d = din("cmask", [4, 128, NT])
    qn_d = dtmp("qn", [c.H, 128, T], BF16)
    qr_d = dtmp("qr", [c.H, 64, T], BF16)
    tok_d = dtmp("tok", [c.TOKW, T], BF16)
    for l in range(c.NA, c.L):
        j = l - c.NA
        if l == c.NA:
            ffn(l, 0, x)
        mk2 = P.mark()
        MKV = mem_kv_setup(K, mem_d, gsl("mem_norm", l, c.KT), b_g, wt("mem_w_kv", l), D, c.NM, c.MEMW)
        mixer_pre_B(K, x, gn(l, 2), gsl("q", j, c.RK), b_g, wt("b_w_in", j), wt("mla_w_uq", j), RT, qn_d, qr_d, memo_d, MKV,
                    D, c.R, c.H, c.MEMW, c.NM, T, Tp)
        P.release(mk2)
        mla_attn(K, qn_d, qr_d, kn_d, kr_d, v_d, knp_d, krp_d, vp_d, pbias_d, mask_d, tok_d, c.H, T)
        mixer_post(K, x, gn(l, 3), b_g, wt("w_out", l), tok_d, memo_d, D, c.TOKW, c.MEMW, T, Tp)
        if l == c.L - 1:
            ffn(l, 1, x)
        else:
            ffn2(l, x)
    P.barrier()
    P.emit()
    return nc, wts


def run_fused(cfg, inp):
    c = cfg
    T = c.T
    NCORE = 2 * c.B
    f32 = np.float32
    gains, goff = gain_layout(c, inp["norms"], inp["mem_norm"], inp["kv_in_norm"], inp["kv_norm"], inp["mla_q_norm"], inp["s5_b_glu"])
    W = dict(inp)
    W["gains"], W["goff"] = gains, goff
    nc, wts = build_fused(c, W)
    shared = {"gains": gains}
    for (name, l, j), (nm_, ap) in wts.items():
        a = inp[name]
        if l is not None:
            a = a[l]
        if j is not None:
            a = a[j]
        shared[nm_] = np.ascontiguousarray(np.asarray(a, f32))
    for l in range(c.NA):
        lay = s5_host_layout(*(np.asarray(inp[k][l], f32) for k in ("s5_lambda_re", "s5_lambda_im", "s5_log_dt", "s5_b_re", "s5_b_im",
                                                                    "s5_c_re", "s5_c_im", "s5_d")))
        shared.update({f"s5lam_s{l}": lay["lam_s"], f"s5lam_r{l}": lay["lam_r"], f"s5bT{l}": lay["bT"], f"s5cP{l}": lay["cP"],
                       f"s5d{l}": lay["d_s"]})
    inv_freq = (10000.0 ** (-np.arange(0, 64, 2, dtype=np.float32) / 64)).astype(f32)
    shared["invf"] = np.concatenate([inv_freq, inv_freq])[:, None].astype(f32) / f32(2 * np.pi)
    shared["sgn"] = np.concatenate([-np.ones(32, f32), np.ones(32, f32)])[:, None]
    kk, qq = np.arange(128)[:, None], np.arange(NT)[None, :]
    shared["cmask"] = np.stack([(qq >= 128 * i + kk).astype(f32) for i in range(4)], 0)
    shared["zero_carry"] = np.zeros((128, 2, c.NP), f32)
    xa = np.asarray(inp["x"], f32)
    pa = np.asarray(inp["positions"], np.int32)
    maps = []
    for cid in range(NCORE):
        b, hh = divmod(cid, 2)
        m = dict(shared)
        m["memT"] = np.ascontiguousarray(np.asarray(inp["mem"][b], f32).T)
        m["x_in"] = np.ascontiguousarray(xa[b, hh * T:(hh + 1) * T, :].T)
        m["xp_in"] = np.ascontiguousarray(xa[b, 0:T, :].T)
        m["pos"] = np.ascontiguousarray(pa[b, hh * T:(hh + 1) * T])
        m["posp"] = np.ascontiguousarray(pa[b, 0:T])
        m["cflag"] = np.full((128, 1), float(hh), f32)
        m["pbias"] = np.full((128, 1), 0.0 if hh == 1 else -30000.0, f32)
        maps.append(m)
    res = run_bass_kernel_spmd(nc, maps, core_ids=list(range(NCORE)))
    out = np.empty((c.B, c.SEQ, c.D), f32)
    for cid in range(NCORE):
        b, hh = divmod(cid, 2)
        out[b, hh * T:(hh + 1) * T, :] = res.results[cid]["x_out"].T
    return out


def kernel(**inputs):
    return run_fused(Cfg(), inputs)
```

```python
import numpy as np
import concourse.bass as bass
import concourse.mybir as mybir
from concourse.bass_utils import run_bass_kernel_spmd

F32 = mybir.dt.float32
BF16 = mybir.dt.bfloat16
I32 = mybir.dt.int32
AF = mybir.ActivationFunctionType
ALU = mybir.AluOpType
AX = mybir.AxisListType

ENGS = ("pe", "act", "dve", "pool", "sp")
NDMASEM = 12


class Buf:
    __slots__ = ("name", "w", "r")

    def __init__(self, name):
        self.name = name
        self.w = None
        self.r = []


class Op:
    __slots__ = ("eng", "fn", "deps", "sig", "dma", "semi", "semv", "idx", "prev_semv")

    def __init__(self, eng, fn, dma):
        self.eng = eng
        self.fn = fn
        self.dma = dma
        self.deps = []
        self.sig = False
        self.semi = None
        self.semv = None
        self.prev_semv = None


class Prog:
    def __init__(self, nc):
        self.nc = nc
        self.ops = {e: [] for e in ENGS}
        self.all_ops = []
        self.sb_off = 16384 + 2048
        self.sb_cap = 16384 + 212000
        self.sb_hi = 0
        self.ntens = 0
        self.dma_rr = 0
        self.dma_sem_total = [0] * NDMASEM
        self.dma_sem_last = [None] * NDMASEM
        self.bufs = []

    def sb(self, shape, dtype, name=None):
        nbytes = int(np.prod(shape[1:])) * mybir.dt.size(dtype)
        nbytes = (nbytes + 63) // 64 * 64
        off = self.sb_off
        assert off + nbytes <= self.sb_cap, f"SBUF overflow {off}+{nbytes} ({name})"
        self.sb_off += nbytes
        self.sb_hi = max(self.sb_hi, self.sb_off)
        self.ntens += 1
        return self.nc.alloc_sbuf_tensor_at(f"t{self.ntens}_{name or ''}", list(shape), dtype, offset=off)

    def mark(self):
        return self.sb_off

    def release(self, mark):
        self.sb_off = mark

    def buf(self, name="b"):
        b = Buf(name)
        self.bufs.append(b)
        return b

    def bufs_n(self, n, name="b"):
        return [self.buf(name) for _ in range(n)]

    def op(self, eng, fn, reads=(), writes=(), dma=0):
        o = Op(eng, fn, dma)
        deps = set()
        for b in reads:
            if b.w is not None:
                deps.add(b.w)
        for b in writes:
            if b.w is not None:
                deps.add(b.w)
            for r in b.r:
                deps.add(r)
        for b in reads:
            b.r.append(o)
        for b in writes:
            b.w = o
            b.r = []
        deps.discard(o)
        for d in sorted(deps, key=lambda x: x.idx):
            if d.eng == "pe" and eng == "pe" and not d.dma and not dma:
                continue
            o.deps.append(d)
            d.sig = True
        if dma:
            o.sig = True
            k = self.dma_rr % NDMASEM
            self.dma_rr += 1
            o.semi = ("d", k)
            o.prev_semv = self.dma_sem_total[k]
            self.dma_sem_total[k] += 16 * dma
            o.semv = self.dma_sem_total[k]
            self.dma_sem_last[k] = o
        o.idx = len(self.all_ops)
        self.ops[eng].append(o)
        self.all_ops.append(o)
        return o

    def chain(self, eng, fns, reads=(), writes=()):
        c = self.buf("chain")
        o = None
        for f in fns:
            o = self.op(eng, f, reads=list(reads) + [c], writes=list(writes) + [c])
        return o

    def barrier(self):
        last = []
        for e in ENGS:
            if self.ops[e]:
                last.append(self.ops[e][-1])
        last += [o for o in self.dma_sem_last if o is not None]
        for d in last:
            d.sig = True
        for e in ENGS:
            o = Op(e, lambda h: None, 0)
            o.deps = list(last)
            o.idx = len(self.all_ops)
            self.ops[e].append(o)
            self.all_ops.append(o)
        for bb in self.bufs:
            bb.w = None
            bb.r = []

    def emit(self):
        nc = self.nc
        self.sems = {e: nc.alloc_semaphore(f"s_{e}") for e in ENGS}
        self.dsems = [nc.alloc_semaphore(f"s_dma{i}") for i in range(NDMASEM)]
        cnt = {e: 0 for e in ENGS}
        for o in self.all_ops:
            if not o.dma and o.sig:
                cnt[o.eng] += 1
                o.semi = ("e", o.eng)
                o.semv = cnt[o.eng]
        handles = {"pe": "tensor", "act": "scalar", "dve": "vector", "pool": "gpsimd", "sp": "sync"}

        def emit_engine(e, h):
            known = {}
            for o in self.ops[e]:
                waits = {}
                for d in o.deps:
                    waits[d.semi] = max(waits.get(d.semi, 0), d.semv)
                if o.dma and o.prev_semv:
                    waits[o.semi] = max(waits.get(o.semi, 0), o.prev_semv)
                for key, v in waits.items():
                    if known.get(key, 0) >= v:
                        continue
                    known[key] = v
                    sem = self.dsems[key[1]] if key[0] == "d" else self.sems[key[1]]
                    h.wait_ge(sem, v)
                r = o.fn(h)
                if o.dma:
                    sem = self.dsems[o.semi[1]]
                    assert len(r) == o.dma, (len(r), o.dma)
                    for ins in r:
                        ins.then_inc(sem, 16)
                elif o.sig:
                    if r is None:
                        r = h.nop()
                    r.then_inc(self.sems[e], 1)

        with nc.Block() as block:
            for e in ENGS:
                getattr(block, handles[e])(lambda h, e=e: emit_engine(e, h))


NT = 512
EPS = 1e-6


class KB:
    def __init__(self, nc):
        self.nc = nc
        self.P = Prog(nc)
        P = self.P
        self.ps = [nc.alloc_psum_tensor(f"psb{i}", [128, NT], F32) for i in range(8)]
        self.b_ps = P.bufs_n(8, "ps")
        self.ones = P.sb([128, 128], BF16, "ones")
        self.b_ones = P.buf("ones")
        P.op("dve", lambda h: h.memset(self.ones[:], 1.0), writes=[self.b_ones])
        self.init_consts()

    def mm(self, psi, pairs, reads, n=NT, m=128):
        ps = self.ps[psi]

        def f(h):
            L = len(pairs)
            for i, (a, b) in enumerate(pairs):
                r = h.matmul(ps[0:m, 0:n], lhsT=a, rhs=b, start=(i == 0), stop=(i == L - 1))
            return r
        return self.P.op("pe", f, reads=reads, writes=[self.b_ps[psi]])

    def dma(self, out, in_, reads, writes, eng="sp"):
        return self.P.op(eng, lambda h: [h.dma_start(out=out, in_=in_)], reads=reads, writes=writes, dma=1)

    def rstd_from_sq(self, sq, nk, n, psi, rstd, b_sq, b_rstd, dim):
        P = self.P
        ps = self.ps[psi]

        def mm(h):
            for k in range(nk):
                r = h.matmul(ps[:, 0:n], lhsT=self.ones[:], rhs=sq[:, k, 0:n], start=(k == 0), stop=(k == nk - 1))
            return r
        P.op("pe", mm, reads=[b_sq, self.b_ones], writes=[self.b_ps[psi]])
        P.op("act", lambda h: h.activation(out=rstd[:, 0:n], in_=ps[:, 0:n], func=AF.Sqrt, scale=1.0 / dim, bias=self.eps_ap()),
             reads=[self.b_ps[psi]], writes=[b_rstd])
        P.op("dve", lambda h: h.reciprocal(out=rstd[:, 0:n], in_=rstd[:, 0:n]), reads=[b_rstd], writes=[b_rstd])

    def eps_ap(self):
        return self.epsT[:, 0:1]

    def init_consts(self):
        P = self.P
        self.epsT = P.sb([128, 1], F32, "eps")
        self.b_eps = P.buf("eps")
        P.op("dve", lambda h: h.memset(self.epsT[:], EPS), writes=[self.b_eps])
        self.negpi = P.sb([128, 1], F32, "negpi")
        P.op("dve", lambda h: h.memset(self.negpi[:], -float(np.pi)), writes=[self.b_eps])


def ffn_stage(K, x_d, gin, gout, b_g, Wg, Wu, Wd, D, DFF, T, Tp, SUB=128, GW=256, res_scale=0.5):
    P = K.P
    KT = D // 128
    GC = GW // 128
    NTn = Tp // NT
    NG = DFF // GW
    mark = P.mark()
    hT = P.sb([128, KT, Tp], BF16, "hT")
    b_hT = P.bufs_n(Tp // SUB, "hT")
    acc = P.sb([128, KT, Tp], F32, "acc")
    b_acc = [P.bufs_n(NTn, "acc") for _ in range(KT)]
    xs = [P.sb([128, KT, SUB], F32, "xs") for _ in range(2)]
    b_xs = P.bufs_n(2, "xs")
    sq = P.sb([128, KT, SUB], BF16, "sq")
    b_sq = P.buf("sq")
    rstd = P.sb([128, SUB], F32, "rstd")
    b_rstd = P.buf("rstd")
    wg = [P.sb([128, KT, GW], BF16, "wg") for _ in range(2)]
    wu = [P.sb([128, KT, GW], BF16, "wu") for _ in range(2)]
    wd = [P.sb([128, GC, D], BF16, "wd") for _ in range(2)]
    b_wg = P.bufs_n(2, "wg")
    b_wu = P.bufs_n(2, "wu")
    b_wd = P.bufs_n(2, "wd")
    hid = [P.sb([128, GC, Tp], BF16, "hid") for _ in range(2)]
    b_hid = [[P.bufs_n(NTn, "hid") for _ in range(GC)] for _ in range(2)]
    sg = [P.sb([128, NT], F32, "sg") for _ in range(2)]
    b_sg = P.bufs_n(2, "sg")
    b_x = P.buf("xdram")
    gsc = P.sb([128, KT], F32, "gsc")
    b_gsc = P.buf("gsc")
    P.op("dve", lambda h: h.tensor_scalar(out=gsc[:], in0=gout, scalar1=float(res_scale), scalar2=None, op0=ALU.mult),
         reads=[b_g], writes=[b_gsc])
    Wg_v = Wg.rearrange("(kt p) c -> p kt c", p=128)
    Wu_v = Wu.rearrange("(kt p) c -> p kt c", p=128)
    Wd_v = Wd.rearrange("(c p) d -> p c d", p=128)
    x_v = x_d.rearrange("(kt p) t -> p kt t", p=128)
    PS_G, PS_U, PS_D, PS_M = (0, 1), (2, 3), (4, 5), 6
    cnt = {"gu": 0, "d": 0, "xs": 0}

    for p in range(T // Tp):
        t0 = p * Tp
        for s in range(Tp // SUB):
            xi = cnt["xs"] % 2
            cnt["xs"] += 1
            K.dma(xs[xi][:], x_v[:, :, t0 + s * SUB: t0 + (s + 1) * SUB], [b_x], [b_xs[xi]])
            P.op("act", lambda h, xi=xi: h.activation(out=sq[:], in_=xs[xi][:], func=AF.Square),
                 reads=[b_xs[xi]], writes=[b_sq])
            K.rstd_from_sq(sq, KT, SUB, PS_M, rstd, b_sq, b_rstd, D)

            def nrm(h, xi=xi, s=s):
                for kt in range(KT):
                    r = h.scalar_tensor_tensor(out=hT[:, kt, s * SUB:(s + 1) * SUB], in0=xs[xi][:, kt, :],
                                               scalar=gin[:, kt:kt + 1], in1=rstd[:], op0=ALU.mult, op1=ALU.mult)
                return r
            P.op("dve", nrm, reads=[b_xs[xi], b_rstd, b_g], writes=[b_hT[s]])

        def load_w(j):
            sl = j % 2
            K.dma(wg[sl][:], Wg_v[:, :, j * GW:(j + 1) * GW], [], [b_wg[sl]], eng="pool")
            K.dma(wu[sl][:], Wu_v[:, :, j * GW:(j + 1) * GW], [], [b_wu[sl]], eng="pool")
            K.dma(wd[sl][:], Wd_v[:, j * GC:(j + 1) * GC, :], [], [b_wd[sl]], eng="pool")

        def gateup(j):
            sl = j % 2
            for c in range(GC):
                for nt in range(NTn):
                    gi = cnt["gu"] % 2
                    cnt["gu"] += 1
                    pg, pu = PS_G[gi], PS_U[gi]
                    hbufs = b_hT[nt * (NT // SUB):(nt + 1) * (NT // SUB)]

                    K.mm(pg, [(wg[sl][:, kt, c * 128:(c + 1) * 128], hT[:, kt, nt * NT:(nt + 1) * NT]) for kt in range(KT)],
                         [b_wg[sl]] + hbufs)
                    K.mm(pu, [(wu[sl][:, kt, c * 128:(c + 1) * 128], hT[:, kt, nt * NT:(nt + 1) * NT]) for kt in range(KT)],
                         [b_wu[sl]] + hbufs)
                    P.op("act", lambda h, gi=gi, pg=pg: h.activation(out=sg[gi][:], in_=K.ps[pg][:], func=AF.Silu),
                         reads=[K.b_ps[pg]], writes=[b_sg[gi]])
                    P.op("dve", lambda h, gi=gi, pu=pu, sl=sl, c=c, nt=nt: h.tensor_tensor(
                        out=hid[sl][:, c, nt * NT:(nt + 1) * NT], in0=sg[gi][:], in1=K.ps[pu][:], op=ALU.mult),
                        reads=[b_sg[gi], K.b_ps[pu]], writes=[b_hid[sl][c][nt]])

        def down(j):
            sl = j % 2
            for m in range(KT):
                for nt in range(NTn):
                    di = cnt["d"] % 2
                    cnt["d"] += 1
                    pd = PS_D[di]

                    K.mm(pd, [(wd[sl][:, c, m * 128:(m + 1) * 128], hid[sl][:, c, nt * NT:(nt + 1) * NT]) for c in range(GC)],
                         [b_wd[sl]] + [b_hid[sl][c][nt] for c in range(GC)])
                    if j == 0:
                        P.op("dve", lambda h, m=m, nt=nt, pd=pd: h.tensor_copy(out=acc[:, m, nt * NT:(nt + 1) * NT], in_=K.ps[pd][:]),
                             reads=[K.b_ps[pd]], writes=[b_acc[m][nt]])
                    else:
                        P.op("dve", lambda h, m=m, nt=nt, pd=pd: h.tensor_tensor(
                            out=acc[:, m, nt * NT:(nt + 1) * NT], in0=acc[:, m, nt * NT:(nt + 1) * NT], in1=K.ps[pd][:], op=ALU.add),
                            reads=[K.b_ps[pd], b_acc[m][nt]], writes=[b_acc[m][nt]])

        load_w(0)
        gateup(0)
        for j in range(NG):
            if j + 1 < NG:
                load_w(j + 1)
                gateup(j + 1)
            down(j)

        for s in range(Tp // SUB):
            nt = (s * SUB) // NT
            accb = [b_acc[m][nt] for m in range(KT)]
            xi = cnt["xs"] % 2
            cnt["xs"] += 1
            K.dma(xs[xi][:], x_v[:, :, t0 + s * SUB: t0 + (s + 1) * SUB], [b_x], [b_xs[xi]])
            P.op("act", lambda h, s=s: h.activation(out=sq[:], in_=acc[:, :, s * SUB:(s + 1) * SUB], func=AF.Square),
                 reads=accb, writes=[b_sq])
            K.rstd_from_sq(sq, KT, SUB, PS_M, rstd, b_sq, b_rstd, D)

            def fin1(h, s=s):
                for kt in range(KT):
                    r = h.scalar_tensor_tensor(out=acc[:, kt, s * SUB:(s + 1) * SUB], in0=acc[:, kt, s * SUB:(s + 1) * SUB],
                                               scalar=gsc[:, kt:kt + 1], in1=rstd[:], op0=ALU.mult, op1=ALU.mult)
                return r
            P.op("dve", fin1, reads=accb + [b_rstd, b_gsc], writes=accb)
            P.op("pool", lambda h, xi=xi, s=s: h.tensor_tensor(
                out=xs[xi][:], in0=acc[:, :, s * SUB:(s + 1) * SUB], in1=xs[xi][:], op=ALU.add),
                reads=accb + [b_xs[xi]], writes=[b_xs[xi]])
            K.dma(x_v[:, :, t0 + s * SUB: t0 + (s + 1) * SUB], xs[xi][:], [b_xs[xi]], [b_x])
    P.barrier()
    P.release(mark)


def ffn_multi(K, x_d, specs, b_g, D, DFF, T, Tp, SUB=128, GW=256, res_scale=0.5):
    P = K.P
    KT = D // 128
    GC = GW // 128
    NTn = Tp // NT
    NG = DFF // GW
    NS = Tp // SUB
    mark = P.mark()
    hT = P.sb([128, KT, Tp], BF16, "hT")
    b_hT = P.bufs_n(NS, "hT")
    acc = P.sb([128, KT, Tp], F32, "acc")
    b_acc = [P.bufs_n(NTn, "acc") for _ in range(KT)]
    fixed = (KT * Tp * 6 + 2 * KT * SUB * 2 + 2 * SUB * 4 + 2 * (2 * KT * GW * 2 + GC * D * 2) + 2 * GC * Tp * 2 + 2 * NT * 4
             + len(specs) * KT * 4 + 2048)
    NXS = 4 if P.sb_cap - P.sb_off - fixed >= 4 * KT * SUB * 4 else 2
    xs = [P.sb([128, KT, SUB], F32, "xs") for _ in range(NXS)]
    b_xs = P.bufs_n(NXS, "xs")
    sq = [P.sb([128, KT, SUB], BF16, "sq") for _ in range(2)]
    b_sq = P.bufs_n(2, "sq")
    rstd = [P.sb([128, SUB], F32, "rstd") for _ in range(2)]
    b_rstd = P.bufs_n(2, "rstd")
    wg = [P.sb([128, KT, GW], BF16, "wg") for _ in range(2)]
    wu = [P.sb([128, KT, GW], BF16, "wu") for _ in range(2)]
    wd = [P.sb([128, GC, D], BF16, "wd") for _ in range(2)]
    b_wg, b_wu, b_wd = P.bufs_n(2, "wg"), P.bufs_n(2, "wu"), P.bufs_n(2, "wd")
    hid = [P.sb([128, GC, Tp], BF16, "hid") for _ in range(2)]
    b_hid = [[P.bufs_n(NTn, "hid") for _ in range(GC)] for _ in range(2)]
    sg = [P.sb([128, NT], F32, "sg") for _ in range(2)]
    b_sg = P.bufs_n(2, "sg")
    b_x = P.buf("xdram")
    gscs = []
    b_gsc = P.buf("gsc")
    for (gin, gout, _, _, _) in specs:
        g_ = P.sb([128, KT], F32, "gsc")
        P.op("dve", lambda h, g_=g_, gout=gout: h.tensor_scalar(out=g_[:], in0=gout, scalar1=float(res_scale), scalar2=None, op0=ALU.mult),
             reads=[b_g], writes=[b_gsc])
        gscs.append(g_)
    x_v = x_d.rearrange("(kt p) t -> p kt t", p=128)
    views = [(Wg.rearrange("(kt p) c -> p kt c", p=128), Wu.rearrange("(kt p) c -> p kt c", p=128), Wd.rearrange("(c p) d -> p c d", p=128))
             for (_, _, Wg, Wu, Wd) in specs]
    PS_G, PS_U, PS_D, PS_M = (0, 1), (2, 3), (4, 5), (6, 7)
    cnt = {"gu": 0, "d": 0, "xs": 0, "sq": 0}
    jobs = [(f, p) for f in range(len(specs)) for p in range(T // Tp)]

    def rstd_calc(src_ap, reads):
        qi = cnt["sq"] % 2
        cnt["sq"] += 1
        P.op("act", lambda h: h.activation(out=sq[qi][:], in_=src_ap, func=AF.Square), reads=reads, writes=[b_sq[qi]])
        K.rstd_from_sq(sq[qi], KT, SUB, PS_M[qi], rstd[qi], b_sq[qi], b_rstd[qi], D)
        return qi

    def norm(k):
        f, p = jobs[k]
        gin = specs[f][0]
        t0 = p * Tp
        for s_ in range(NS):
            xi = cnt["xs"] % 2
            cnt["xs"] += 1
            K.dma(xs[xi][:], x_v[:, :, t0 + s_ * SUB: t0 + (s_ + 1) * SUB], [b_x], [b_xs[xi]])
            qi = rstd_calc(xs[xi][:], [b_xs[xi]])

            def nrm(h, xi=xi, s_=s_, qi=qi):
                for kt in range(KT):
                    r = h.scalar_tensor_tensor(out=hT[:, kt, s_ * SUB:(s_ + 1) * SUB], in0=xs[xi][:, kt, :],
                                               scalar=gin[:, kt:kt + 1], in1=rstd[qi][:], op0=ALU.mult, op1=ALU.mult)
                return r
            P.op("dve", nrm, reads=[b_xs[xi], b_rstd[qi], b_g], writes=[b_hT[s_]])

    def load_w(k, j):
        f, _ = jobs[k]
        Wg_v, Wu_v, Wd_v = views[f]
        sl = j % 2
        K.dma(wg[sl][:], Wg_v[:, :, j * GW:(j + 1) * GW], [], [b_wg[sl]], eng="pool")
        K.dma(wu[sl][:], Wu_v[:, :, j * GW:(j + 1) * GW], [], [b_wu[sl]], eng="pool")
        K.dma(wd[sl][:], Wd_v[:, j * GC:(j + 1) * GC, :], [], [b_wd[sl]], eng="pool")

    def gateup(j):
        sl = j % 2
        for c in range(GC):
            for nt in range(NTn):
                gi = cnt["gu"] % 2
                cnt["gu"] += 1
                pg, pu = PS_G[gi], PS_U[gi]
                hbufs = b_hT[nt * (NT // SUB):(nt + 1) * (NT // SUB)]
                K.mm(pg, [(wg[sl][:, kt, c * 128:(c + 1) * 128], hT[:, kt, nt * NT:(nt + 1) * NT]) for kt in range(KT)], [b_wg[sl]] + hbufs)
                K.mm(pu, [(wu[sl][:, kt, c * 128:(c + 1) * 128], hT[:, kt, nt * NT:(nt + 1) * NT]) for kt in range(KT)], [b_wu[sl]] + hbufs)
                P.op("act", lambda h, gi=gi, pg=pg: h.activation(out=sg[gi][:], in_=K.ps[pg][:], func=AF.Silu),
                     reads=[K.b_ps[pg]], writes=[b_sg[gi]])
                P.op("dve", lambda h, gi=gi, pu=pu, sl=sl, c=c, nt=nt: h.tensor_tensor(
                    out=hid[sl][:, c, nt * NT:(nt + 1) * NT], in0=sg[gi][:], in1=K.ps[pu][:], op=ALU.mult),
                    reads=[b_sg[gi], K.b_ps[pu]], writes=[b_hid[sl][c][nt]])

    def down(j):
        sl = j % 2
        for m in range(KT):
            for nt in range(NTn):
                pd = PS_D[cnt["d"] % 2]
                cnt["d"] += 1
                K.mm(pd, [(wd[sl][:, c, m * 128:(m + 1) * 128], hid[sl][:, c, nt * NT:(nt + 1) * NT]) for c in range(GC)],
                     [b_wd[sl]] + [b_hid[sl][c][nt] for c in range(GC)])
                if j == 0:
                    P.op("dve", lambda h, m=m, nt=nt, pd=pd: h.tensor_copy(out=acc[:, m, nt * NT:(nt + 1) * NT], in_=K.ps[pd][:]),
                         reads=[K.b_ps[pd]], writes=[b_acc[m][nt]])
                else:
                    P.op("dve", lambda h, m=m, nt=nt, pd=pd: h.tensor_tensor(
                        out=acc[:, m, nt * NT:(nt + 1) * NT], in0=acc[:, m, nt * NT:(nt + 1) * NT], in1=K.ps[pd][:], op=ALU.add),
                        reads=[K.b_ps[pd], b_acc[m][nt]], writes=[b_acc[m][nt]])

    def finalize_job(k):
        f, p = jobs[k]
        gsc = gscs[f]
        t0 = p * Tp
        def xload(s_):
            xi_ = (cnt["xs"] + s_) % 2
            K.dma(xs[xi_][:], x_v[:, :, t0 + s_ * SUB: t0 + (s_ + 1) * SUB], [b_x], [b_xs[xi_]])
        xload(0)
        base = cnt["xs"]
        for s_ in range(NS):
            nt = (s_ * SUB) // NT
            accb = [b_acc[m][nt] for m in range(KT)]
            xi = (base + s_) % 2
            if s_ + 1 < NS:
                cnt["xs"] = base
                xload(s_ + 1)
            cnt["xs"] = base + s_ + 1
            qi = rstd_calc(acc[:, :, s_ * SUB:(s_ + 1) * SUB], accb)

            if NXS == 4:
                ti = 2 + cnt["xs"] % 2
                tmp, b_tmp = xs[ti], b_xs[ti]

                def fin1(h, s_=s_, qi=qi, tmp=tmp):
                    for kt in range(KT):
                        r = h.scalar_tensor_tensor(out=tmp[:, kt, :], in0=acc[:, kt, s_ * SUB:(s_ + 1) * SUB],
                                                   scalar=gsc[:, kt:kt + 1], in1=rstd[qi][:], op0=ALU.mult, op1=ALU.mult)
                    return r
                P.op("dve", fin1, reads=accb + [b_rstd[qi], b_gsc], writes=[b_tmp])
                P.op("pool", lambda h, xi=xi, tmp=tmp: h.tensor_tensor(out=xs[xi][:], in0=tmp[:], in1=xs[xi][:], op=ALU.add),
                     reads=[b_tmp, b_xs[xi]], writes=[b_xs[xi]])
            else:
                def fin1(h, s_=s_, qi=qi):
                    for kt in range(KT):
                        r = h.scalar_tensor_tensor(out=acc[:, kt, s_ * SUB:(s_ + 1) * SUB], in0=acc[:, kt, s_ * SUB:(s_ + 1) * SUB],
                                                   scalar=gsc[:, kt:kt + 1], in1=rstd[qi][:], op0=ALU.mult, op1=ALU.mult)
                    return r
                P.op("dve", fin1, reads=accb + [b_rstd[qi], b_gsc], writes=accb)
                P.op("pool", lambda h, xi=xi, s_=s_: h.tensor_tensor(
                    out=xs[xi][:], in0=acc[:, :, s_ * SUB:(s_ + 1) * SUB], in1=xs[xi][:], op=ALU.add),
                    reads=accb + [b_xs[xi]], writes=[b_xs[xi]])
            K.dma(x_v[:, :, t0 + s_ * SUB: t0 + (s_ + 1) * SUB], xs[xi][:], [b_xs[xi]], [b_x])

    assert NG % 2 == 0
    norm(0)
    load_w(0, 0)
    gateup(0)
    for k in range(len(jobs)):
        for j in range(NG):
            if j + 1 < NG:
                load_w(k, j + 1)
                gateup(j + 1)
                down(j)
            else:
                if k + 1 < len(jobs):
                    norm(k + 1)
                    load_w(k + 1, 0)
                    gateup(0)
                down(j)
        finalize_job(k)
    P.barrier()
    P.release(mark)


S5_L = 128


def s5_host_layout(lam_re, lam_im, log_dt, b_re, b_im, c_re, c_im, d):
    G, Pn, C = b_re.shape
    NP = G // 2

    def st(a):
        return np.ascontiguousarray(a.reshape(NP, 2 * Pn).T)
    ldt = np.repeat(log_dt[:, None], Pn, 1)
    lam_s = np.stack([st(lam_re), st(lam_im), st(ldt)], 1)
    row = np.stack([lam_re.reshape(-1), lam_im.reshape(-1), ldt.reshape(-1)], 0)
    lam_r = np.ascontiguousarray(np.broadcast_to(row[None], (128, 3, NP * 128)))
    bT = np.zeros((2, 128, NP, 128), np.float32)
    cP = np.zeros((2, 128, NP, 128), np.float32)
    for g in range(G):
        q, hh = g // 2, g % 2
        off = (g % 8) * 16
        bT[0, off:off + 16, q, hh * 64:(hh + 1) * 64] = b_re[g].T
        bT[1, off:off + 16, q, hh * 64:(hh + 1) * 64] = b_im[g].T
        cP[0, hh * 64:(hh + 1) * 64, q, off:off + 16] = c_re[g].T
        cP[1, hh * 64:(hh + 1) * 64, q, off:off + 16] = c_im[g].T
    d_s = np.ascontiguousarray(d.reshape(-1, 128).T)
    return dict(lam_s=lam_s, lam_r=lam_r, bT=bT, cP=cP, d_s=d_s)


def s5_setup(K, prm, NP):
    P = K.P
    L = S5_L
    PI = float(np.pi)
    S = {}
    lam_s = P.sb([128, 3, NP], F32, "lam_s")
    b_l = P.buf("lam_s")
    K.dma(lam_s[:], prm["lam_s"], [], [b_l])
    dt = P.sb([128, NP], F32, "dt")
    th = P.sb([128, NP], F32, "th")
    r = P.sb([128, NP], F32, "r")
    b_t = P.buf("s5tab")
    P.op("act", lambda h: h.activation(out=dt[:], in_=lam_s[:, 2, :], func=AF.Exp), reads=[b_l], writes=[b_t])
    P.op("dve", lambda h: h.tensor_tensor(out=th[:], in0=lam_s[:, 1, :], in1=dt[:], op=ALU.mult), reads=[b_l, b_t], writes=[b_t])
    P.op("dve", lambda h: h.tensor_tensor(out=r[:], in0=lam_s[:, 0, :], in1=dt[:], op=ALU.mult), reads=[b_l, b_t], writes=[b_t])
    P.op("act", lambda h: h.activation(out=r[:], in_=r[:], func=AF.Exp), reads=[b_t], writes=[b_t])
    jrow_i = P.sb([128, L], I32, "jrow_i")
    jrow = P.sb([128, L], F32, "jrow")
    P.op("pool", lambda h: h.iota(jrow_i[:], pattern=[[1, L]], base=0, channel_multiplier=0), writes=[b_t], reads=[b_t])
    P.op("dve", lambda h: h.tensor_copy(out=jrow[:], in_=jrow_i[:]), reads=[b_t], writes=[b_t])
    cosT = P.sb([128, NP, L], F32, "cosT")
    sinT = P.sb([128, NP, L], F32, "sinT")
    Rz = P.sb([128, NP, L], F32, "Rz")
    b_tab = P.buf("tabs")

    TWO_PI = 2 * PI
    MAGIC = 12582912.0
    thn = P.sb([128, NP], F32, "thn")
    P.op("dve", lambda h: h.tensor_scalar(out=thn[:], in0=th[:], scalar1=1.0 / TWO_PI, scalar2=None, op0=ALU.mult), reads=[b_t], writes=[b_t])
    Kre = P.sb([128, NP], F32, "Kre")
    Kim = P.sb([128, NP], F32, "Kim")
    mk0 = P.mark()
    tmpT = P.sb([128, NP, L], F32, "tmpT")

    def sin_cycles(tens, shift, b_r, b_w_):
        tv = tmpT_v(tens)
        steps = []
        if shift:
            steps.append(lambda h: h.tensor_scalar(out=tens, in0=tens, scalar1=float(shift), scalar2=None, op0=ALU.add))
        steps.append(lambda h: h.tensor_scalar(out=tv, in0=tens, scalar1=MAGIC, scalar2=None, op0=ALU.add))
        steps.append(lambda h: h.tensor_scalar(out=tv, in0=tv, scalar1=-MAGIC, scalar2=None, op0=ALU.add))
        steps.append(lambda h: h.tensor_tensor(out=tens, in0=tens, in1=tv, op=ALU.subtract))
        P.chain("dve", steps, reads=b_r, writes=b_w_)
        P.op("act", lambda h: h.activation(out=tens, in_=tens, func=AF.Sin, scale=TWO_PI), reads=b_w_, writes=b_w_)

    def tmpT_v(tens):
        shp = tens.shape
        if len(shp) == 3:
            return tmpT[:, 0:shp[1], 0:shp[2]]
        return tmpT[:, 0, 0:shp[1]]

    def angs(h):
        for q in range(NP):
            h.tensor_scalar(out=sinT[:, q, :], in0=jrow[:], scalar1=thn[:, q:q + 1], scalar2=None, op0=ALU.mult)
            r_ = h.tensor_scalar(out=cosT[:, q, :], in0=jrow[:], scalar1=thn[:, q:q + 1], scalar2=None, op0=ALU.mult)
        return r_
    P.op("dve", angs, reads=[b_t], writes=[b_tab])
    sin_cycles(sinT[:], 0.0, [b_tab], [b_tab])
    sin_cycles(cosT[:], 0.25, [b_tab], [b_tab])

    def rz(h):
        for q in range(NP):
            h.tensor_scalar(out=Rz[:, q, 1:L], in0=jrow[:, 1:L], scalar1=0.0, scalar2=r[:, q:q + 1], op0=ALU.mult, op1=ALU.add)
        return h.memset(Rz[:, :, 0:1], 0.0)
    P.op("dve", rz, reads=[b_t], writes=[b_tab])
    def kang(h):
        h.tensor_scalar(out=Kim[:], in0=thn[:], scalar1=float(L), scalar2=None, op0=ALU.mult)
        return h.tensor_scalar(out=Kre[:], in0=thn[:], scalar1=float(L), scalar2=None, op0=ALU.mult)
    P.op("dve", kang, reads=[b_t], writes=[b_tab])
    sin_cycles(Kim[:], 0.0, [b_tab], [b_tab])
    sin_cycles(Kre[:], 0.25, [b_tab], [b_tab])

    def kmul(h):
        h.tensor_tensor(out=Kim[:], in0=Kim[:], in1=r[:], op=ALU.mult)
        return h.tensor_tensor(out=Kre[:], in0=Kre[:], in1=r[:], op=ALU.mult)
    P.op("dve", kmul, reads=[b_tab, b_t], writes=[b_tab])

    P.barrier()
    P.release(mk0)
    BT = [P.sb([128, NP, 128], BF16, f"BT{i}") for i in range(2)]
    CT = [P.sb([128, NP, 128], BF16, f"CT{i}") for i in range(2)]
    b_BT = P.buf("BT")
    b_CT = P.buf("CT")
    K.dma(CT[0][:], prm["cP"][0], [], [b_CT], eng="pool")
    K.dma(CT[1][:], prm["cP"][1], [], [b_CT], eng="pool")
    P.op("dve", lambda h: h.tensor_scalar(out=CT[1][:], in0=CT[1][:], scalar1=-1.0, scalar2=None, op0=ALU.mult), reads=[b_CT], writes=[b_CT])
    mk = P.mark()
    QB = 4
    W = QB * 128
    lr = P.sb([128, 3, W], F32, "lr")
    tb = [P.sb([128, W], F32, f"tb{i}") for i in range(6)]
    bt = [P.sb([128, QB, 128], F32, f"bt{i}") for i in range(2)]
    b_lr = P.buf("lr")
    b_bt = P.buf("bt")
    b_w = P.buf("w")
    for blk in range(NP // QB):
        cs = slice(blk * W, (blk + 1) * W)
        K.dma(lr[:], prm["lam_r"][:, :, cs], [], [b_lr])
        K.dma(bt[0][:], prm["bT"][0][:, blk * QB:(blk + 1) * QB, :], [], [b_bt])
        K.dma(bt[1][:], prm["bT"][1][:, blk * QB:(blk + 1) * QB, :], [], [b_bt])
        dtr, thr, rr, ca, sa, den = tb
        P.op("act", lambda h: h.activation(out=dtr[:], in_=lr[:, 2, :], func=AF.Exp), reads=[b_lr], writes=[b_w])

        def c1(h):
            h.tensor_tensor(out=thr[:], in0=lr[:, 1, :], in1=dtr[:], op=ALU.mult)
            h.tensor_tensor(out=rr[:], in0=lr[:, 0, :], in1=dtr[:], op=ALU.mult)
            h.tensor_scalar(out=sa[:], in0=thr[:], scalar1=1.0 / TWO_PI, scalar2=None, op0=ALU.mult)
            h.tensor_scalar(out=ca[:], in0=thr[:], scalar1=1.0 / TWO_PI, scalar2=0.25, op0=ALU.mult, op1=ALU.add)
            for t_ in (sa, ca):
                h.tensor_scalar(out=den[:], in0=t_[:], scalar1=MAGIC, scalar2=None, op0=ALU.add)
                h.tensor_scalar(out=den[:], in0=den[:], scalar1=-MAGIC, scalar2=None, op0=ALU.add)
                r_ = h.tensor_tensor(out=t_[:], in0=t_[:], in1=den[:], op=ALU.subtract)
            return r_
        P.op("dve", c1, reads=[b_lr, b_w], writes=[b_w])
        P.op("act", lambda h: h.activation(out=rr[:], in_=rr[:], func=AF.Exp), reads=[b_w], writes=[b_w])
        P.op("act", lambda h: h.activation(out=sa[:], in_=sa[:], func=AF.Sin, scale=TWO_PI), reads=[b_w], writes=[b_w])
        P.op("act", lambda h: h.activation(out=ca[:], in_=ca[:], func=AF.Sin, scale=TWO_PI), reads=[b_w], writes=[b_w])

        def c2(h):
            h.tensor_tensor(out=ca[:], in0=ca[:], in1=rr[:], op=ALU.mult)
            h.tensor_scalar(out=ca[:], in0=ca[:], scalar1=-1.0, scalar2=None, op0=ALU.add)
            h.tensor_tensor(out=sa[:], in0=sa[:], in1=rr[:], op=ALU.mult)
            h.tensor_tensor(out=den[:], in0=lr[:, 0, :], in1=lr[:, 0, :], op=ALU.mult)
            h.tensor_tensor(out=dtr[:], in0=lr[:, 1, :], in1=lr[:, 1, :], op=ALU.mult)
            h.tensor_tensor(out=den[:], in0=den[:], in1=dtr[:], op=ALU.add)
            h.reciprocal(out=den[:], in_=den[:])
            h.tensor_tensor(out=thr[:], in0=ca[:], in1=lr[:, 0, :], op=ALU.mult)
            h.tensor_tensor(out=dtr[:], in0=sa[:], in1=lr[:, 1, :], op=ALU.mult)
            h.tensor_tensor(out=thr[:], in0=thr[:], in1=dtr[:], op=ALU.add)
            h.tensor_tensor(out=thr[:], in0=thr[:], in1=den[:], op=ALU.mult)
            h.tensor_tensor(out=rr[:], in0=sa[:], in1=lr[:, 0, :], op=ALU.mult)
            h.tensor_tensor(out=dtr[:], in0=ca[:], in1=lr[:, 1, :], op=ALU.mult)
            h.tensor_tensor(out=rr[:], in0=rr[:], in1=dtr[:], op=ALU.subtract)
            return h.tensor_tensor(out=rr[:], in0=rr[:], in1=den[:], op=ALU.mult)
        P.op("dve", c2, reads=[b_lr, b_w], writes=[b_w])

        def c3(h, blk=blk):
            qs = slice(blk * QB, (blk + 1) * QB)
            b0 = bt[0][:].rearrange("p q s -> p (q s)")
            b1 = bt[1][:].rearrange("p q s -> p (q s)")
            o0 = BT[0][:, qs, :].rearrange("p q s -> p (q s)")
            o1 = BT[1][:, qs, :].rearrange("p q s -> p (q s)")
            h.tensor_tensor(out=ca[:], in0=thr[:], in1=b0, op=ALU.mult)
            h.tensor_tensor(out=sa[:], in0=rr[:], in1=b1, op=ALU.mult)
            h.tensor_tensor(out=o0, in0=ca[:], in1=sa[:], op=ALU.subtract)
            h.tensor_tensor(out=ca[:], in0=thr[:], in1=b1, op=ALU.mult)
            h.tensor_tensor(out=sa[:], in0=rr[:], in1=b0, op=ALU.mult)
            return h.tensor_tensor(out=o1, in0=ca[:], in1=sa[:], op=ALU.add)
        P.op("dve", c3, reads=[b_w, b_bt], writes=[b_BT, b_w])
    P.barrier()
    P.release(mk)
    S.update(cosT=cosT, sinT=sinT, Rz=Rz, Kre=Kre, Kim=Kim, BT=BT, CT=CT, b_tab=b_tab, b_BT=b_BT, b_CT=b_CT)
    return S


def s5_scan(K, S, NP, T, u_d, carry_in_d, carry_out_d, yg_d, d_d, full, flag=None, b_flag=None):
    P = K.P
    L = S5_L
    NQ = NP // 4
    NG4 = (NQ + 3) // 4
    NCH = T // L
    mark = P.mark()
    cosT, sinT, Rz, Kre, Kim, BT, CT = (S[k] for k in ("cosT", "sinT", "Rz", "Kre", "Kim", "BT", "CT"))
    b_tab, b_BT, b_CT = S["b_tab"], S["b_BT"], S["b_CT"]
    u_v = u_d.rearrange("(q p) t -> p q t", p=128)
    SC = 4 * L
    uT = [P.sb([128, NQ, SC], BF16, "uT") for _ in range(2)]
    b_u = P.bufs_n(2, "uT")
    rc = P.sb([128, 2, NP], F32, "rc")
    b_rc = P.buf("rc")
    zl = P.sb([128, 2, NP], F32, "zl")
    b_zl = P.bufs_n(NQ, "zl")
    tmpc = [P.sb([128, NP], F32, f"tmpc{i}") for i in range(2)]
    K.dma(rc[:], carry_in_d, [], [b_rc])
    if flag is not None:
        P.op("dve", lambda h: h.tensor_scalar(out=rc[:].rearrange("p a q -> p (a q)"), in0=rc[:].rearrange("p a q -> p (a q)"),
                                              scalar1=flag[:, 0:1], scalar2=None, op0=ALU.mult), reads=[b_flag], writes=[b_rc])
    NB = 2
    v = [[P.sb([128, 4, L], F32, f"v{i}{j}") for j in range(2)] for i in range(NB)]
    z = [[P.sb([128, 4, L], F32, f"z{i}{j}") for j in range(2)] for i in range(NB)]
    b_v = P.bufs_n(NB, "v")
    xo = [[P.sb([128, 4, L], BF16, f"xo{i}{j}") for j in range(2)] for i in range(NB)]
    b_xo = P.bufs_n(NB, "xo")
    b_xr = P.bufs_n(NB, "xr")
    b_zz = P.bufs_n(NB, "zz")
    fl = lambda t: t[:].rearrange("p q l -> p (q l)")
    col = lambda ap: ap.rearrange("p (q o) -> p q o", o=1)
    if full:
        d_s = P.sb([128, NQ], F32, "d_s")
        b_d = P.buf("d")
        K.dma(d_s[:], d_d, [], [b_d])
        du = [P.sb([128, NQ, L], F32, f"du{i}") for i in range(2)]
        b_du = P.bufs_n(2, "du")
        yt = [P.sb([128, 4, L], F32, f"yt{i}") for i in range(2)]
        y2 = [P.sb([128, 4, L], F32, f"y2{i}") for i in range(2)]
        yo = [P.sb([128, 4, L], BF16, f"yo{i}") for i in range(2)]
        b_yt = P.bufs_n(2, "yt")
        b_yo = P.bufs_n(2, "yo")
        b_ygd = P.buf("ygd")
        yg_v = yg_d.rearrange("(q p) t -> p q t", p=128)
    b_ud = P.buf("ud")
    PS_B = [(0, 1), (2, 3)]
    PS_Y = [4, 5, 6, 7]
    yit = [0]
    items = [(c, qd) for c in range(NCH) for qd in range(NQ)]
    usl = {}

    def emit_mmb(i):
        c, qd = items[i]
        sc, cc = divmod(c, 4)
        us = sc % 2
        if cc == 0 and qd == 0:
            K.dma(uT[us][:], u_v[:, :, sc * SC:(sc + 1) * SC], [b_ud], [b_u[us]])
        pr, pi_ = PS_B[i % 2]
        prs = list(range(qd * 4, qd * 4 + 4))
        rhs = uT[us][:, qd, cc * L:(cc + 1) * L]

        def mmb(h, pr=pr, pi_=pi_, prs=prs, rhs=rhs):
            for k, q in enumerate(prs):
                h.matmul(K.ps[pr][:, k * L:(k + 1) * L], lhsT=BT[0][:, q, :], rhs=rhs, start=True, stop=True)
            for k, q in enumerate(prs):
                r_ = h.matmul(K.ps[pi_][:, k * L:(k + 1) * L], lhsT=BT[1][:, q, :], rhs=rhs, start=True, stop=True)
            return r_
        P.op("pe", mmb, reads=[b_BT, b_u[us]], writes=[K.b_ps[pr], K.b_ps[pi_]])

    def emit_rest(i):
        c, qd = items[i]
        sc, cc = divmod(c, 4)
        us = sc % 2
        dui = c % 2
        if full and qd == 0:
            def mkdu(h, dui=dui, us=us, cc=cc):
                for q_ in range(NQ):
                    r_ = h.tensor_scalar(out=du[dui][:, q_, :], in0=uT[us][:, q_, cc * L:(cc + 1) * L], scalar1=d_s[:, q_:q_ + 1],
                                         scalar2=None, op0=ALU.mult)
                return r_
            P.op("pool", mkdu, reads=[b_u[us], b_d], writes=[b_du[dui]])
        vi = i % NB
        pr, pi_ = PS_B[i % 2]
        prs = list(range(qd * 4, qd * 4 + 4))
        vr, vim = v[vi]
        zr, zi = z[vi]
        qs = slice(qd * 4, qd * 4 + 4)
        cs_ = cosT[:, qs, :].rearrange("p q l -> p (q l)")
        sn_ = sinT[:, qs, :].rearrange("p q l -> p (q l)")
        rz_ = Rz[:, qs, :].rearrange("p q l -> p (q l)")

        def rot_in(h):
            h.tensor_tensor(out=fl(zr), in0=K.ps[pr][:], in1=cs_, op=ALU.mult)
            h.tensor_tensor(out=fl(zi), in0=K.ps[pi_][:], in1=sn_, op=ALU.mult)
            h.tensor_tensor(out=fl(vr), in0=fl(zr), in1=fl(zi), op=ALU.add)
            h.tensor_tensor(out=fl(zr), in0=K.ps[pi_][:], in1=cs_, op=ALU.mult)
            h.tensor_tensor(out=fl(zi), in0=K.ps[pr][:], in1=sn_, op=ALU.mult)
            return h.tensor_tensor(out=fl(vim), in0=fl(zr), in1=fl(zi), op=ALU.subtract)
        P.op("dve", rot_in, reads=[K.b_ps[pr], K.b_ps[pi_], b_tab], writes=[b_v[vi], b_zz[vi]])

        def sc1(h):
            h.tensor_tensor(out=vr[:, :, 0:1], in0=vr[:, :, 0:1], in1=col(rc[:, 0, qs]), op=ALU.add)
            return h.tensor_tensor(out=vim[:, :, 0:1], in0=vim[:, :, 0:1], in1=col(rc[:, 1, qs]), op=ALU.add)

        def sc2(h):
            h.tensor_tensor_scan(out=fl(zr), data0=rz_, data1=fl(vr), initial=0.0, op0=ALU.mult, op1=ALU.add)
            return h.tensor_tensor_scan(out=fl(zi), data0=rz_, data1=fl(vim), initial=0.0, op0=ALU.mult, op1=ALU.add)

        def sc3(h):
            h.tensor_copy(out=col(zl[:, 0, qs]), in_=zr[:, :, L - 1:L])
            return h.tensor_copy(out=col(zl[:, 1, qs]), in_=zi[:, :, L - 1:L])
        P.chain("dve", [sc1, sc2, sc3], reads=[b_rc, b_tab], writes=[b_v[vi], b_zz[vi], b_zl[qd]])
        if full:
            xr, xi = xo[vi]


            def rot_re(h):
                h.tensor_tensor(out=fl(vr), in0=fl(zr), in1=cs_, op=ALU.mult)
                h.tensor_tensor(out=fl(vim), in0=fl(zi), in1=sn_, op=ALU.mult)
                return h.tensor_tensor(out=fl(xr), in0=fl(vr), in1=fl(vim), op=ALU.subtract)
            P.op("dve", rot_re, reads=[b_tab, b_zz[vi]], writes=[b_xr[vi], b_v[vi]])

            def rot_im(h):
                h.tensor_tensor(out=fl(vr), in0=fl(zr), in1=sn_, op=ALU.mult)
                h.tensor_tensor(out=fl(vim), in0=fl(zi), in1=cs_, op=ALU.mult)
                return h.tensor_tensor(out=fl(xi), in0=fl(vr), in1=fl(vim), op=ALU.add)
            P.op("dve", rot_im, reads=[b_tab, b_zz[vi]], writes=[b_xo[vi], b_v[vi]])
        if i + 2 < len(items):
            emit_mmb(i + 2)
        if full:
            g4, q4 = divmod(qd, 4)
            py = PS_Y[(c * NG4 + g4) % 4]

            def mmy(h):
                for k, q in enumerate(prs):
                    h.matmul(K.ps[py][:, q4 * L:(q4 + 1) * L], lhsT=CT[0][:, q, :], rhs=xr[:, k, :], start=(k == 0), stop=False)
                for k, q in enumerate(prs):
                    r_ = h.matmul(K.ps[py][:, q4 * L:(q4 + 1) * L], lhsT=CT[1][:, q, :], rhs=xi[:, k, :], start=False, stop=(k == 3))
                return r_
            P.op("pe", mmy, reads=[b_xo[vi], b_xr[vi], b_CT], writes=[K.b_ps[py]])
            if q4 == 3 or qd == NQ - 1:
                nq4 = q4 + 1
                yi = yit[0] % 2
                yit[0] += 1
                W4 = nq4 * L
                P.op("dve", lambda h: h.tensor_tensor(
                    out=yt[yi][:, 0:nq4, :].rearrange("p q l -> p (q l)"), in0=K.ps[py][:, 0:W4],
                    in1=du[dui][:, g4 * 4:g4 * 4 + nq4, :].rearrange("p q l -> p (q l)"), op=ALU.add),
                    reads=[K.b_ps[py], b_du[dui], b_yo[yi]], writes=[b_yt[yi]])
                P.op("act", lambda h: h.activation(out=y2[yi][:, 0:nq4, :], in_=yt[yi][:, 0:nq4, :], func=AF.Square),
                     reads=[b_yt[yi]], writes=[b_yt[yi]])

                def g2(h):
                    h.tensor_scalar(out=y2[yi][:, 0:nq4, :], in0=y2[yi][:, 0:nq4, :], scalar1=0.044715, scalar2=1.0, op0=ALU.mult, op1=ALU.add)
                    return h.tensor_tensor(out=y2[yi][:, 0:nq4, :], in0=y2[yi][:, 0:nq4, :], in1=yt[yi][:, 0:nq4, :], op=ALU.mult)
                P.op("dve", g2, reads=[b_yt[yi]], writes=[b_yt[yi]])
                P.op("act", lambda h: h.activation(out=y2[yi][:, 0:nq4, :], in_=y2[yi][:, 0:nq4, :], func=AF.Sigmoid, scale=1.5957691216),
                     reads=[b_yt[yi]], writes=[b_yt[yi]])
                P.op("dve", lambda h: h.tensor_tensor(out=yo[yi][:, 0:nq4, :], in0=y2[yi][:, 0:nq4, :], in1=yt[yi][:, 0:nq4, :], op=ALU.mult),
                     reads=[b_yt[yi]], writes=[b_yo[yi]])
                K.dma(yg_v[:, g4 * 4:g4 * 4 + nq4, c * L:(c + 1) * L], yo[yi][:, 0:nq4, :], [b_yo[yi]], [b_ygd])

    emit_mmb(0)
    if len(items) > 1:
        emit_mmb(1)
    for c in range(NCH):
        for qd in range(NQ):
            emit_rest(c * NQ + qd)

        def cu1(h):
            h.tensor_tensor(out=tmpc[0][:], in0=zl[:, 0, :], in1=Kre[:], op=ALU.mult)
            return h.tensor_tensor(out=tmpc[1][:], in0=zl[:, 1, :], in1=Kim[:], op=ALU.mult)

        def cu2(h):
            return h.tensor_tensor(out=rc[:, 0, :], in0=tmpc[0][:], in1=tmpc[1][:], op=ALU.subtract)

        def cu3(h):
            h.tensor_tensor(out=tmpc[0][:], in0=zl[:, 0, :], in1=Kim[:], op=ALU.mult)
            return h.tensor_tensor(out=tmpc[1][:], in0=zl[:, 1, :], in1=Kre[:], op=ALU.mult)

        def cu4(h):
            return h.tensor_tensor(out=rc[:, 1, :], in0=tmpc[0][:], in1=tmpc[1][:], op=ALU.add)
        P.chain("dve", [cu1, cu2, cu3, cu4], reads=b_zl + [b_tab], writes=[b_rc])
    b_co = P.buf("carry_out")
    K.dma(carry_out_d, rc[:], [b_rc], [b_co])
    P.barrier()
    P.release(mark)


def alloc_norm_tmp(K, KT, SUB):
    P = K.P
    return dict(xs=[P.sb([128, KT, SUB], F32, "xs") for _ in range(2)], b_xs=P.bufs_n(2, "xs"),
                sq=P.sb([128, KT, SUB], BF16, "sq"), b_sq=P.buf("sq"),
                rstd=P.sb([128, SUB], F32, "rstd"), b_rstd=P.buf("rstd"), n=0, SUB=SUB, KT=KT)


def norm_in(K, tm, x_v, b_x, t0, Tp, g_ap, b_g, hT, b_hT, D, PS_M):
    P = K.P
    SUB, KT = tm["SUB"], tm["KT"]
    for s in range(Tp // SUB):
        xi = tm["n"] % 2
        tm["n"] += 1
        xs, sq, rstd = tm["xs"][xi], tm["sq"], tm["rstd"]
        K.dma(xs[:], x_v[:, :, t0 + s * SUB: t0 + (s + 1) * SUB], [b_x], [tm["b_xs"][xi]])
        P.op("act", lambda h, xs=xs: h.activation(out=sq[:], in_=xs[:], func=AF.Square), reads=[tm["b_xs"][xi]], writes=[tm["b_sq"]])
        K.rstd_from_sq(sq, KT, SUB, PS_M, rstd, tm["b_sq"], tm["b_rstd"], D)

        def nrm(h, xs=xs, s=s):
            for kt in range(KT):
                r = h.scalar_tensor_tensor(out=hT[:, kt, s * SUB:(s + 1) * SUB], in0=xs[:, kt, :],
                                           scalar=g_ap[:, kt:kt + 1], in1=rstd[:], op0=ALU.mult, op1=ALU.mult)
            return r
        P.op("dve", nrm, reads=[tm["b_xs"][xi], tm["b_rstd"], b_g], writes=[b_hT[s]])


def finalize(K, tm, acc, b_acc, x_v, b_x, t0, Tp, gsc, b_gsc, D, PS_M):
    P = K.P
    SUB, KT = tm["SUB"], tm["KT"]
    if "tmp" not in tm and P.sb_cap - P.sb_off >= KT * SUB * 4 + 1024:
        tm["tmp"] = P.sb([128, KT, SUB], F32, "fintmp")
        tm["b_tmp"] = P.buf("fintmp")
    tmp, b_tmp = tm.get("tmp"), tm.get("b_tmp")
    base = tm["n"]

    def xload(s):
        xi_ = (base + s) % 2
        K.dma(tm["xs"][xi_][:], x_v[:, :, t0 + s * SUB: t0 + (s + 1) * SUB], [b_x], [tm["b_xs"][xi_]])
    xload(0)
    for s in range(Tp // SUB):
        nt = (s * SUB) // NT
        accb = [b_acc[m][nt] for m in range(KT)]
        xi = (base + s) % 2
        tm["n"] = base + s + 1
        xs, sq, rstd = tm["xs"][xi], tm["sq"], tm["rstd"]
        if s + 1 < Tp // SUB:
            xload(s + 1)
        P.op("act", lambda h, s=s: h.activation(out=sq[:], in_=acc[:, :, s * SUB:(s + 1) * SUB], func=AF.Square),
             reads=accb, writes=[tm["b_sq"]])
        K.rstd_from_sq(sq, KT, SUB, PS_M, rstd, tm["b_sq"], tm["b_rstd"], D)
        if tmp is not None:
            def fin1(h, s=s):
                for kt in range(KT):
                    r = h.scalar_tensor_tensor(out=tmp[:, kt, :], in0=acc[:, kt, s * SUB:(s + 1) * SUB],
                                               scalar=gsc[:, kt:kt + 1], in1=rstd[:], op0=ALU.mult, op1=ALU.mult)
                return r
            P.op("dve", fin1, reads=accb + [tm["b_rstd"], b_gsc], writes=[b_tmp])
            P.op("pool", lambda h, xs=xs: h.tensor_tensor(out=xs[:], in0=tmp[:], in1=xs[:], op=ALU.add),
                 reads=[b_tmp, tm["b_xs"][xi]], writes=[tm["b_xs"][xi]])
        else:
            def fin1(h, s=s):
                for kt in range(KT):
                    r = h.scalar_tensor_tensor(out=acc[:, kt, s * SUB:(s + 1) * SUB], in0=acc[:, kt, s * SUB:(s + 1) * SUB],
                                               scalar=gsc[:, kt:kt + 1], in1=rstd[:], op0=ALU.mult, op1=ALU.mult)
                return r
            P.op("dve", fin1, reads=accb + [tm["b_rstd"], b_gsc], writes=accb)
            P.op("pool", lambda h, xs=xs, s=s: h.tensor_tensor(out=xs[:], in0=acc[:, :, s * SUB:(s + 1) * SUB], in1=xs[:], op=ALU.add),
                 reads=accb + [tm["b_xs"][xi]], writes=[tm["b_xs"][xi]])
        K.dma(x_v[:, :, t0 + s * SUB: t0 + (s + 1) * SUB], xs[:], [tm["b_xs"][xi]], [b_x])


class WStream:
    def __init__(self, K, KT, MW, name="w"):
        P = K.P
        self.K, self.KT, self.MW = K, KT, MW
        self.w = [P.sb([128, KT, MW], BF16, name) for _ in range(2)]
        self.b = P.bufs_n(2, name)
        self.n = 0

    def load(self, W_v, c0, ncols=None):
        ncols = ncols or self.MW
        sl = self.n % 2
        self.n += 1
        self.K.dma(self.w[sl][:, :, 0:ncols], W_v[:, :, c0:c0 + ncols], [], [self.b[sl]], eng="pool")
        return self.w[sl], self.b[sl]


class Ring:
    def __init__(self, items):
        self.items = items
        self.n = 0

    def next(self):
        r = self.items[self.n % len(self.items)]
        self.n += 1
        return r


def mem_kv_setup(K, mem_d, gm, b_gm, Wkv, D, NM, MEMW):
    P = K.P
    KT = D // 128
    memK = P.sb([128, MEMW // 128, NM], BF16, "memK")
    memV = P.sb([128, NM // 128, MEMW], BF16, "memV")
    b_mk = P.buf("memK")
    b_mv = P.buf("memV")
    mark = P.mark()
    tm = alloc_norm_tmp(K, KT, NM)
    nm = P.sb([128, KT, NM], BF16, "nmem")
    b_nm = [P.buf("nmem")]
    wkv = P.sb([128, KT, 2 * MEMW], BF16, "wkv")
    b_w = P.buf("wkv")
    K.dma(wkv[:], Wkv.rearrange("(kt p) c -> p kt c", p=128), [], [b_w], eng="pool")
    mem_v = mem_d.rearrange("(kt p) t -> p kt t", p=128)
    norm_in(K, tm, mem_v, P.buf("memd"), 0, NM, gm, b_gm, nm, b_nm, D, 6)
    for h in range(MEMW // 128):
        psi = h % 2
        K.mm(psi, [(wkv[:, kt, h * 128:(h + 1) * 128], nm[:, kt, :]) for kt in range(KT)], [b_w] + b_nm, n=NM)
        P.op("act", lambda h_, h=h, psi=psi: h_.activation(out=memK[:, h, :], in_=K.ps[psi][:, 0:NM], func=AF.Copy),
             reads=[K.b_ps[psi]], writes=[b_mk])
    for kt_ in range(NM // 128):
        psi = 2 + kt_ % 2
        K.mm(psi, [(nm[:, kt, kt_ * 128:(kt_ + 1) * 128], wkv[:, kt, MEMW:2 * MEMW]) for kt in range(KT)], [b_w] + b_nm, n=MEMW)
        P.op("dve", lambda h_, kt_=kt_, psi=psi: h_.tensor_copy(out=memV[:, kt_, :], in_=K.ps[psi][:, 0:MEMW]),
             reads=[K.b_ps[psi]], writes=[b_mv])
    P.barrier()
    P.release(mark)
    return dict(memK=memK, memV=memV, b_mk=b_mk, b_mv=b_mv)


def alloc_mem_attn(K, NM):
    P = K.P
    return dict(pt=[P.sb([128, NM // 128, NT], BF16, "mpt") for _ in range(2)], b_pt=P.bufs_n(2, "mpt"),
                rec=[P.sb([128, NT], F32, "mrec") for _ in range(2)], b_rec=P.bufs_n(2, "mrec"),
                mo=[P.sb([128, NT], BF16, "mo") for _ in range(2)], b_mo=P.bufs_n(2, "mo"), n=0)


def mem_attn(K, MA, MKV, qm, b_qm, memo_d, b_md, t0, Tp, NM, MEMW, ps_s, ps_o, ps_r):
    P = K.P
    H = MEMW // 128
    NKT = NM // 128
    scale = 128.0 ** -0.5
    for h in range(H):
        for nt in range(Tp // NT):
            i = MA["n"] % 2
            MA["n"] += 1
            pt, rec, mo = MA["pt"][i], MA["rec"][i], MA["mo"][i]
            for kt in range(NKT):
                psi = ps_s.next()
                K.mm(psi, [(MKV["memK"][:, h, kt * 128:(kt + 1) * 128], qm[:, h, nt * NT:(nt + 1) * NT])], [MKV["b_mk"]] + b_qm)
                P.op("act", lambda h_, psi=psi, pt=pt, kt=kt: h_.activation(out=pt[:, kt, :], in_=K.ps[psi][:], func=AF.Exp, scale=scale),
                     reads=[K.b_ps[psi]], writes=[MA["b_pt"][i]])
            po, pr = ps_o.next(), ps_r.next()
            K.mm(po, [(MKV["memV"][:, kt, h * 128:(h + 1) * 128], pt[:, kt, :]) for kt in range(NKT)], [MKV["b_mv"], MA["b_pt"][i]])
            K.mm(pr, [(K.ones[:], pt[:, kt, :]) for kt in range(NKT)], [K.b_ones, MA["b_pt"][i]])
            P.op("dve", lambda h_, pr=pr, rec=rec: h_.reciprocal(out=rec[:], in_=K.ps[pr][:]), reads=[K.b_ps[pr]], writes=[MA["b_rec"][i]])
            P.op("dve", lambda h_, po=po, rec=rec, mo=mo: h_.tensor_tensor(out=mo[:], in0=K.ps[po][:], in1=rec[:], op=ALU.mult),
                 reads=[K.b_ps[po], MA["b_rec"][i]], writes=[MA["b_mo"][i]])
            K.dma(memo_d[h * 128:(h + 1) * 128, t0 + nt * NT: t0 + (nt + 1) * NT], mo[:], [MA["b_mo"][i]], [b_md])


def mixer_pre_A(K, x_d, g2, b_g, W_in, u_d, memo_d, MKV, D, TOKW, MEMW, NM, T, Tp, SUB=256):
    P = K.P
    KT = D // 128
    mark = P.mark()
    tm = alloc_norm_tmp(K, KT, SUB)
    hT = P.sb([128, KT, Tp], BF16, "hT")
    b_hT = P.bufs_n(Tp // SUB, "hT")
    qm = P.sb([128, MEMW // 128, Tp], BF16, "qm")
    b_qm = P.bufs_n(MEMW // 128, "qm")
    ws = WStream(K, KT, 256, "win")
    st = [P.sb([128, NT], BF16, "stg") for _ in range(3)]
    b_st = P.bufs_n(3, "stg")
    sti = Ring([0, 1, 2])
    MA = alloc_mem_attn(K, NM)
    x_v = x_d.rearrange("(kt p) t -> p kt t", p=128)
    W_v = W_in.rearrange("(kt p) c -> p kt c", p=128)
    b_x, b_ud, b_md = P.buf("x"), P.buf("ud"), P.buf("md")
    ps_mm = Ring([0, 1, 2])
    ps_s, ps_o, ps_r = Ring([0, 1, 2]), Ring([3, 4]), Ring([5, 7])
    MT = (TOKW + MEMW) // 128
    for p in range(T // Tp):
        t0 = p * Tp
        norm_in(K, tm, x_v, b_x, t0, Tp, g2, b_g, hT, b_hT, D, 6)
        for mg in range(MT // 2):
            w, bw = ws.load(W_v, mg * 256)
            for mi in range(2):
                m = mg * 2 + mi
                for nt in range(Tp // NT):
                    psi = ps_mm.next()
                    hb = b_hT[nt * (NT // SUB):(nt + 1) * (NT // SUB)]
                    K.mm(psi, [(w[:, kt, mi * 128:(mi + 1) * 128], hT[:, kt, nt * NT:(nt + 1) * NT]) for kt in range(KT)], [bw] + hb)
                    if m < TOKW // 128:
                        si = sti.next()
                        P.op("act", lambda h_, psi=psi, si=si: h_.activation(out=st[si][:], in_=K.ps[psi][:], func=AF.Copy),
                             reads=[K.b_ps[psi]], writes=[b_st[si]])
                        K.dma(u_d[m * 128:(m + 1) * 128, t0 + nt * NT: t0 + (nt + 1) * NT], st[si][:], [b_st[si]], [b_ud])
                    else:
                        hh = m - TOKW // 128
                        P.op("dve", lambda h_, psi=psi, hh=hh, nt=nt: h_.tensor_copy(out=qm[:, hh, nt * NT:(nt + 1) * NT], in_=K.ps[psi][:]),
                             reads=[K.b_ps[psi]], writes=[b_qm[hh]])
        mem_attn(K, MA, MKV, qm, b_qm, memo_d, b_md, t0, Tp, NM, MEMW, ps_s, ps_o, ps_r)
    P.barrier()
    P.release(mark)


def mixer_post(K, x_d, g3, b_g, W_out, tok_d, memo_d, D, TOKW, MEMW, T, Tp, W_glu=None, bglu=None, SUB=256):
    P = K.P
    KT = D // 128
    NTK, NMK = TOKW // 128, MEMW // 128
    mark = P.mark()
    tm = alloc_norm_tmp(K, KT, SUB)
    acc = P.sb([128, KT, Tp], F32, "acc")
    b_acc = [P.bufs_n(Tp // NT, "acc") for _ in range(KT)]
    tk = P.sb([128, NTK, Tp], BF16, "tk")
    b_tk = P.bufs_n(NTK, "tk")
    mo = P.sb([128, NMK, Tp], BF16, "mo")
    b_mo = P.buf("mo")
    b_x, b_td, b_md = P.buf("x"), P.buf("td"), P.buf("md")
    x_v = x_d.rearrange("(kt p) t -> p kt t", p=128)
    tok_v = tok_d.rearrange("(kt p) t -> p kt t", p=128)
    memo_v = memo_d.rearrange("(kt p) t -> p kt t", p=128)
    Wo_v = W_out.rearrange("(kt p) c -> p kt c", p=128)
    wso = WStream(K, KT, 256, "wout")
    ps_mm = Ring([0, 1, 2, 3])
    if W_glu is not None:
        yg = P.sb([128, NTK, Tp], BF16, "yg")
        b_yg = P.buf("yg")
        wsg = WStream(K, NTK, 256, "wglu")
        Wg_v = W_glu.rearrange("(kt p) c -> p kt c", p=128)
        gt = [P.sb([128, NT], F32, "gt") for _ in range(2)]
        b_gt = P.bufs_n(2, "gt")
        gti = Ring([0, 1])
    for p in range(T // Tp):
        t0 = p * Tp
        K.dma(mo[:], memo_v[:, :, t0:t0 + Tp], [b_md], [b_mo])
        if W_glu is None:
            K.dma(tk[:], tok_v[:, :, t0:t0 + Tp], [b_td], b_tk)
        else:
            K.dma(yg[:], tok_v[:, :, t0:t0 + Tp], [b_td], [b_yg])
            for mg in range((NTK + 1) // 2):
                nm_ = min(2, NTK - mg * 2)
                w, bw = wsg.load(Wg_v, mg * 256, nm_ * 128)
                for mi in range(nm_):
                    m = mg * 2 + mi
                    for nt in range(Tp // NT):
                        psi = ps_mm.next()
                        gi = gti.next()
                        K.mm(psi, [(w[:, kt, mi * 128:(mi + 1) * 128], yg[:, kt, nt * NT:(nt + 1) * NT]) for kt in range(NTK)], [bw, b_yg])
                        P.op("act", lambda h_, psi=psi, gi=gi, m=m: h_.activation(out=gt[gi][:], in_=K.ps[psi][:], func=AF.Sigmoid, bias=bglu[:, m:m + 1]),
                             reads=[K.b_ps[psi], b_g], writes=[b_gt[gi]])
                        P.op("dve", lambda h_, gi=gi, m=m, nt=nt: h_.tensor_tensor(out=tk[:, m, nt * NT:(nt + 1) * NT], in0=gt[gi][:],
                                                                                 in1=yg[:, m, nt * NT:(nt + 1) * NT], op=ALU.mult),
                             reads=[b_gt[gi], b_yg], writes=[b_tk[m]])
        for mg in range(KT // 2):
            w, bw = wso.load(Wo_v, mg * 256)
            for mi in range(2):
                m = mg * 2 + mi
                for nt in range(Tp // NT):
                    psi = ps_mm.next()
                    pairs = [(w[:, kt, mi * 128:(mi + 1) * 128], tk[:, kt, nt * NT:(nt + 1) * NT]) for kt in range(NTK)]
                    pairs += [(w[:, NTK + kt, mi * 128:(mi + 1) * 128], mo[:, kt, nt * NT:(nt + 1) * NT]) for kt in range(NMK)]
                    K.mm(psi, pairs, [bw, b_mo] + b_tk)
                    P.op("act", lambda h_, psi=psi, m=m, nt=nt: h_.activation(out=acc[:, m, nt * NT:(nt + 1) * NT], in_=K.ps[psi][:], func=AF.Copy),
                         reads=[K.b_ps[psi]], writes=[b_acc[m][nt]])
        finalize(K, tm, acc, b_acc, x_v, b_x, t0, Tp, g3, b_g, D, 6)
    P.barrier()
    P.release(mark)


def rope_tables(K, pos_d, invf_d, sgn_d, T):
    P = K.P
    cs = P.sb([64, T], F32, "rope_cs")
    sn = P.sb([64, T], F32, "rope_sn")
    b_r = P.buf("rope")
    mark = P.mark()
    pi_ = P.sb([64, T], I32, "pos_i")
    tmp = P.sb([64, T], F32, "rtmp")
    cf = P.sb([64, 2], F32, "rcf")
    K.dma(pi_[:], pos_d.rearrange("(o t) -> o t", o=1).broadcast_to([64, T]), [], [b_r])
    K.dma(cf[:, 0:1], invf_d, [], [b_r])
    K.dma(cf[:, 1:2], sgn_d, [], [b_r])
    MAGIC = 12582912.0
    steps = [
        lambda h: h.tensor_copy(out=sn[:], in_=pi_[:]),
        lambda h: h.tensor_scalar(out=sn[:], in0=sn[:], scalar1=cf[:, 0:1], scalar2=None, op0=ALU.mult),
        lambda h: h.tensor_scalar(out=cs[:], in0=sn[:], scalar1=0.25, scalar2=None, op0=ALU.add),
        lambda h: h.tensor_scalar(out=tmp[:], in0=sn[:], scalar1=MAGIC, scalar2=None, op0=ALU.add),
        lambda h: h.tensor_scalar(out=tmp[:], in0=tmp[:], scalar1=-MAGIC, scalar2=None, op0=ALU.add),
        lambda h: h.tensor_tensor(out=sn[:], in0=sn[:], in1=tmp[:], op=ALU.subtract),
        lambda h: h.tensor_scalar(out=tmp[:], in0=cs[:], scalar1=MAGIC, scalar2=None, op0=ALU.add),
        lambda h: h.tensor_scalar(out=tmp[:], in0=tmp[:], scalar1=-MAGIC, scalar2=None, op0=ALU.add),
        lambda h: h.tensor_tensor(out=cs[:], in0=cs[:], in1=tmp[:], op=ALU.subtract),
    ]
    P.chain("dve", steps, reads=[b_r], writes=[b_r])
    P.op("act", lambda h: h.activation(out=sn[:], in_=sn[:], func=AF.Sin, scale=2 * float(np.pi)), reads=[b_r], writes=[b_r])
    P.op("act", lambda h: h.activation(out=cs[:], in_=cs[:], func=AF.Sin, scale=2 * float(np.pi)), reads=[b_r], writes=[b_r])
    P.op("dve", lambda h: h.tensor_scalar(out=sn[:], in0=sn[:], scalar1=cf[:, 1:2], scalar2=None, op0=ALU.mult), reads=[b_r], writes=[b_r])
    P.barrier()
    P.release(mark)
    return dict(cs=cs, sn=sn, b=b_r)


def apply_rope(K, RT, psa, psb, out, t0, n, tmp, b_tmp, b_out):
    P = K.P
    P.op("dve", lambda h: h.tensor_tensor(out=tmp[0][0:64, 0:n], in0=K.ps[psa][0:64, 0:n], in1=RT["cs"][:, t0:t0 + n], op=ALU.mult),
         reads=[K.b_ps[psa], RT["b"]], writes=[b_tmp[0]])
    P.op("dve", lambda h: h.tensor_tensor(out=tmp[1][0:64, 0:n], in0=K.ps[psb][0:64, 0:n], in1=RT["sn"][:, t0:t0 + n], op=ALU.mult),
         reads=[K.b_ps[psb], RT["b"]], writes=[b_tmp[1]])
    P.op("pool", lambda h: h.tensor_tensor(out=out, in0=tmp[0][0:64, 0:n], in1=tmp[1][0:64, 0:n], op=ALU.add),
         reads=[b_tmp[0], b_tmp[1]], writes=[b_out])


def sub_rmsnorm(K, src, b_src, dst, b_dst, g_ap, b_g, nk, Tp, sq, b_sq, rstd, b_rstd, PS_M):
    P = K.P
    for nt in range(Tp // NT):
        sl = slice(nt * NT, (nt + 1) * NT)
        P.op("act", lambda h, sl=sl: h.activation(out=sq[:, :, :], in_=src[:, :, sl], func=AF.Square), reads=b_src, writes=[b_sq])
        K.rstd_from_sq(sq, nk, NT, PS_M, rstd, b_sq, b_rstd, nk * 128)

        def f(h, sl=sl):
            for kt in range(nk):
                r = h.scalar_tensor_tensor(out=dst[:, kt, sl], in0=src[:, kt, sl], scalar=g_ap[:, kt:kt + 1], in1=rstd[:],
                                           op0=ALU.mult, op1=ALU.mult)
            return r
        P.op("dve", f, reads=b_src + [b_rstd, b_g], writes=[b_dst[nt]])


def kv_stage(K, x_d, gkv_in, gkv, b_g, W_dkv, W_kr, W_uk, W_uv, RT, kn_d, kr_d, v_d, D, R, H, T, Tp, SUB=256):
    P = K.P
    KT = D // 128
    RK = R // 128
    mark = P.mark()
    tm = alloc_norm_tmp(K, KT, SUB)
    hT = P.sb([128, KT, Tp], BF16, "hT")
    b_hT = P.bufs_n(Tp // SUB, "hT")
    ck = P.sb([128, RK, Tp], F32, "ck")
    b_ck = P.bufs_n(1, "ck")
    ckn = P.sb([128, RK, Tp], BF16, "ckn")
    b_ckn = P.bufs_n(Tp // NT, "ckn")
    sq = P.sb([128, RK, NT], BF16, "sq2")
    rstd = P.sb([128, NT], F32, "rstd2")
    b_sq, b_rstd = P.buf("sq2"), P.buf("rstd2")
    wd = P.sb([128, KT, R], BF16, "wdkv")
    wkr = P.sb([128, KT, 128], BF16, "wkr")
    wuk = P.sb([128, RK, H * 128], BF16, "wuk")
    wuv = P.sb([128, RK, H * 128], BF16, "wuv")
    b_w = P.buf("kvw")
    K.dma(wd[:], W_dkv.rearrange("(kt p) c -> p kt c", p=128), [], [b_w], eng="pool")
    wkr_v = W_kr.rearrange("(kt p) c -> p kt c", p=128)
    K.dma(wkr[:, :, 0:64], wkr_v, [], [b_w], eng="pool")
    K.dma(wkr[:, :, 64:96], wkr_v[:, :, 32:64], [], [b_w], eng="pool")
    K.dma(wkr[:, :, 96:128], wkr_v[:, :, 0:32], [], [b_w], eng="pool")
    K.dma(wuk[:], W_uk.rearrange("(kt p) c -> p kt c", p=128), [], [b_w], eng="pool")
    K.dma(wuv[:], W_uv.rearrange("(kt p) c -> p kt c", p=128), [], [b_w], eng="pool")
    st = [P.sb([128, NT], BF16, "stg") for _ in range(3)]
    b_st = P.bufs_n(3, "stg")
    sti = Ring([0, 1, 2])
    rtmp = [P.sb([128, NT], F32, "rtmp") for _ in range(2)]
    b_rtmp = P.bufs_n(2, "rtmp")
    x_v = x_d.rearrange("(kt p) t -> p kt t", p=128)
    b_x, b_kn, b_kr, b_v = P.buf("x"), P.buf("kn"), P.buf("kr"), P.buf("v")
    ps_mm = Ring([0, 1, 2, 3])
    for p in range(T // Tp):
        t0 = p * Tp
        norm_in(K, tm, x_v, b_x, t0, Tp, gkv_in, b_g, hT, b_hT, D, 6)
        for nt in range(Tp // NT):
            hb = b_hT[nt * (NT // SUB):(nt + 1) * (NT // SUB)]
            sl = slice(nt * NT, (nt + 1) * NT)
            for m in range(RK):
                psi = ps_mm.next()
                K.mm(psi, [(wd[:, kt, m * 128:(m + 1) * 128], hT[:, kt, sl]) for kt in range(KT)], [b_w] + hb)
                P.op("act", lambda h_, psi=psi, m=m, sl=sl: h_.activation(out=ck[:, m, sl], in_=K.ps[psi][:], func=AF.Copy),
                     reads=[K.b_ps[psi]], writes=b_ck)
            pa, pb = ps_mm.next(), ps_mm.next()
            K.mm(pa, [(wkr[:, kt, 0:64], hT[:, kt, sl]) for kt in range(KT)], [b_w] + hb, m=64)
            K.mm(pb, [(wkr[:, kt, 64:128], hT[:, kt, sl]) for kt in range(KT)], [b_w] + hb, m=64)
            si = sti.next()
            apply_rope(K, RT, pa, pb, st[si][0:64, :], t0 + nt * NT, NT, rtmp, b_rtmp, b_st[si])
            K.dma(kr_d[:, t0 + nt * NT: t0 + (nt + 1) * NT], st[si][0:64, :], [b_st[si]], [b_kr])
        sub_rmsnorm(K, ck, b_ck, ckn, b_ckn, gkv, b_g, RK, Tp, sq, b_sq, rstd, b_rstd, 6)
        for nt in range(Tp // NT):
            sl = slice(nt * NT, (nt + 1) * NT)
            for hh in range(H):
                psi = ps_mm.next()
                K.mm(psi, [(wuk[:, kt, hh * 128:(hh + 1) * 128], ckn[:, kt, sl]) for kt in range(RK)], [b_w, b_ckn[nt]])
                si = sti.next()
                P.op("act", lambda h_, psi=psi, si=si: h_.activation(out=st[si][:], in_=K.ps[psi][:], func=AF.Copy),
                     reads=[K.b_ps[psi]], writes=[b_st[si]])
                K.dma(kn_d[hh, :, t0 + nt * NT: t0 + (nt + 1) * NT], st[si][:], [b_st[si]], [b_kn])
            for tt in range(NT // 128):
                tsl = slice(nt * NT + tt * 128, nt * NT + (tt + 1) * 128)
                CW = min(NT, H * 128)
                for cc in range(H * 128 // CW):
                    psi = ps_mm.next()
                    K.mm(psi, [(ckn[:, kt, tsl], wuv[:, kt, cc * CW:(cc + 1) * CW]) for kt in range(RK)], [b_w, b_ckn[nt]], n=CW)
                    si = sti.next()
                    P.op("dve", lambda h_, psi=psi, si=si, CW=CW: h_.tensor_copy(out=st[si][:, 0:CW], in_=K.ps[psi][:, 0:CW]),
                         reads=[K.b_ps[psi]], writes=[b_st[si]])
                    K.dma(v_d[t0 + nt * NT + tt * 128: t0 + nt * NT + (tt + 1) * 128, cc * CW:(cc + 1) * CW], st[si][:, 0:CW], [b_st[si]], [b_v])
    P.barrier()
    P.release(mark)


def mixer_pre_B(K, x_d, g2, gq, b_g, W_in, W_uq, RT, qn_d, qr_d, memo_d, MKV, D, R, H, MEMW, NM, T, Tp, SUB=256):
    P = K.P
    KT = D // 128
    RK = R // 128
    mark = P.mark()
    tm = alloc_norm_tmp(K, KT, SUB)
    hT = P.sb([128, KT, Tp], BF16, "hT")
    b_hT = P.bufs_n(Tp // SUB, "hT")
    cq = P.sb([128, RK, Tp], F32, "cq")
    b_cq = P.bufs_n(1, "cq")
    cqn = P.sb([128, RK, Tp], BF16, "cqn")
    b_cqn = P.bufs_n(Tp // NT, "cqn")
    sq = P.sb([128, RK, NT], BF16, "sq2")
    rstd = P.sb([128, NT], F32, "rstd2")
    b_sq, b_rstd = P.buf("sq2"), P.buf("rstd2")
    qm = P.sb([128, MEMW // 128, Tp], BF16, "qm")
    b_qm = P.bufs_n(MEMW // 128, "qm")
    ws = WStream(K, KT, 256, "win")
    HD = 192
    wuq = P.sb([128, RK, H, HD + 64], BF16, "wuq")
    b_wq = P.buf("wuq")
    wq_v = W_uq.rearrange("(kt p) (h e) -> p kt h e", p=128, e=HD)
    for kt in range(RK):
        K.dma(wuq[:, kt, :, 0:HD], wq_v[:, kt, :, :], [], [b_wq], eng="pool")
        K.dma(wuq[:, kt, :, HD:HD + 32], wq_v[:, kt, :, 160:192], [], [b_wq], eng="pool")
        K.dma(wuq[:, kt, :, HD + 32:HD + 64], wq_v[:, kt, :, 128:160], [], [b_wq], eng="pool")
    st = [P.sb([128, NT], BF16, "stg") for _ in range(3)]
    b_st = P.bufs_n(3, "stg")
    sti = Ring([0, 1, 2])
    rtmp = [P.sb([128, NT], F32, "rtmp") for _ in range(2)]
    b_rtmp = P.bufs_n(2, "rtmp")
    MA = alloc_mem_attn(K, NM)
    x_v = x_d.rearrange("(kt p) t -> p kt t", p=128)
    W_v = W_in.rearrange("(kt p) c -> p kt c", p=128)
    b_x, b_qn, b_qr, b_md = P.buf("x"), P.buf("qn"), P.buf("qr"), P.buf("md")
    ps_mm = Ring([0, 1, 2])
    ps_s, ps_o, ps_r = Ring([0, 1, 2]), Ring([3, 4]), Ring([5, 7])
    MT = (R + MEMW) // 128
    for p in range(T // Tp):
        t0 = p * Tp
        norm_in(K, tm, x_v, b_x, t0, Tp, g2, b_g, hT, b_hT, D, 6)
        for mg in range(MT // 2):
            w, bw = ws.load(W_v, mg * 256)
            for mi in range(2):
                m = mg * 2 + mi
                for nt in range(Tp // NT):
                    psi = ps_mm.next()
                    hb = b_hT[nt * (NT // SUB):(nt + 1) * (NT // SUB)]
                    sl = slice(nt * NT, (nt + 1) * NT)
                    K.mm(psi, [(w[:, kt, mi * 128:(mi + 1) * 128], hT[:, kt, sl]) for kt in range(KT)], [bw] + hb)
                    if m < RK:
                        P.op("act", lambda h_, psi=psi, m=m, sl=sl: h_.activation(out=cq[:, m, sl], in_=K.ps[psi][:], func=AF.Copy),
                             reads=[K.b_ps[psi]], writes=b_cq)
                    else:
                        hh = m - RK
                        P.op("dve", lambda h_, psi=psi, hh=hh, sl=sl: h_.tensor_copy(out=qm[:, hh, sl], in_=K.ps[psi][:]),
                             reads=[K.b_ps[psi]], writes=[b_qm[hh]])
        mem_attn(K, MA, MKV, qm, b_qm, memo_d, b_md, t0, Tp, NM, MEMW, ps_s, ps_o, ps_r)
        sub_rmsnorm(K, cq, b_cq, cqn, b_cqn, gq, b_g, RK, Tp, sq, b_sq, rstd, b_rstd, 6)
        for nt in range(Tp // NT):
            sl = slice(nt * NT, (nt + 1) * NT)
            for hh in range(H):
                psi = ps_mm.next()
                K.mm(psi, [(wuq[:, kt, hh, 0:128], cqn[:, kt, sl]) for kt in range(RK)], [b_wq, b_cqn[nt]])
                si = sti.next()
                P.op("act", lambda h_, psi=psi, si=si: h_.activation(out=st[si][:], in_=K.ps[psi][:], func=AF.Copy),
                     reads=[K.b_ps[psi]], writes=[b_st[si]])
                K.dma(qn_d[hh, :, t0 + nt * NT: t0 + (nt + 1) * NT], st[si][:], [b_st[si]], [b_qn])
                pa, pb = ps_mm.next(), ps_mm.next()
                K.mm(pa, [(wuq[:, kt, hh, 128:192], cqn[:, kt, sl]) for kt in range(RK)], [b_wq, b_cqn[nt]], m=64)
                K.mm(pb, [(wuq[:, kt, hh, 192:256], cqn[:, kt, sl]) for kt in range(RK)], [b_wq, b_cqn[nt]], m=64)
                si = sti.next()
                apply_rope(K, RT, pa, pb, st[si][0:64, :], t0 + nt * NT, NT, rtmp, b_rtmp, b_st[si])
                K.dma(qr_d[hh, :, t0 + nt * NT: t0 + (nt + 1) * NT], st[si][0:64, :], [b_st[si]], [b_qr])
    P.barrier()
    P.release(mark)


def mla_attn(K, qn_d, qr_d, kn_d, kr_d, v_d, knp_d, krp_d, vp_d, pbias_d, mask_d, tok_d, H, T):
    P = K.P
    mark = P.mark()
    NKT = T // 128
    QB = T // NT
    scale = 192.0 ** -0.5
    kr = P.sb([64, 2 * T], BF16, "kr")
    b_krs = P.buf("kr")
    K.dma(kr[:, 0:T], krp_d, [], [b_krs])
    K.dma(kr[:, T:2 * T], kr_d, [], [b_krs])
    pb = P.sb([128, 1], F32, "pbias")
    b_pb = P.buf("pbias")
    K.dma(pb[:], pbias_d, [], [b_pb])
    msk = P.sb([128, 4, NT], BF16, "mask")
    b_msk = P.buf("mask")
    K.dma(msk[:], mask_d.rearrange("i p q -> p i q"), [], [b_msk], eng="pool")
    kn = [P.sb([128, 2 * T], BF16, "kn") for _ in range(2)]
    vv = [P.sb([128, 2 * NKT, 128], BF16, "vv") for _ in range(2)]
    qn = [P.sb([128, T], BF16, "qn") for _ in range(2)]
    qr = [P.sb([64, T], BF16, "qr") for _ in range(2)]
    b_hd = P.bufs_n(2, "headin")
    pt = [P.sb([128, NT], BF16, "pt") for _ in range(3)]
    b_pt = P.bufs_n(3, "pt")
    pti = Ring([0, 1, 2])
    rec = [P.sb([128, NT], F32, "rec") for _ in range(2)]
    b_rec = P.bufs_n(2, "rec")
    ob = [P.sb([128, NT], BF16, "ob") for _ in range(2)]
    b_ob = P.bufs_n(2, "ob")
    b_td = P.buf("tokd")
    ps_s, ps_o, ps_r = Ring([0, 1, 2]), Ring([3, 4]), Ring([5, 6])
    fin = 0
    def load_head(h):
        s_ = h % 2
        K.dma(kn[s_][:, 0:T], knp_d[h], [], [b_hd[s_]])
        K.dma(kn[s_][:, T:2 * T], kn_d[h], [], [b_hd[s_]])
        K.dma(vv[s_][:, 0:NKT, :], vp_d[:, h * 128:(h + 1) * 128].rearrange("(t p) d -> p t d", p=128), [], [b_hd[s_]])
        K.dma(vv[s_][:, NKT:2 * NKT, :], v_d[:, h * 128:(h + 1) * 128].rearrange("(t p) d -> p t d", p=128), [], [b_hd[s_]])
        K.dma(qn[s_][:], qn_d[h], [], [b_hd[s_]])
        K.dma(qr[s_][:], qr_d[h], [], [b_hd[s_]])
    load_head(0)
    for h in range(H):
        s_ = h % 2
        if h + 1 < H:
            load_head(h + 1)
        for qb in range(QB):
            qsl = slice(qb * NT, (qb + 1) * NT)
            tiles = list(range(NKT)) + [NKT + j for j in range(4 * qb + 4)]
            po, pr = ps_o.next(), ps_r.next()
            n = len(tiles)

            def score(kt):
                psi = ps_s.next()
                ksl = slice(kt * 128, (kt + 1) * 128)
                K.mm(psi, [(kn[s_][:, ksl], qn[s_][:, qsl]), (kr[0:64, ksl], qr[s_][0:64, qsl])], [b_hd[s_], b_krs])
                return psi
            pend = [score(tiles[0])]
            if n > 1:
                pend.append(score(tiles[1]))
            for i, kt in enumerate(tiles):
                psi = pend.pop(0)
                if i + 2 < n:
                    pend.append(score(tiles[i + 2]))
                pi_ = pti.next()
                prev = kt < NKT
                if prev:
                    P.op("act", lambda h_, psi=psi, pi_=pi_: h_.activation(out=pt[pi_][:], in_=K.ps[psi][:], func=AF.Exp, scale=scale, bias=pb[:, 0:1]),
                         reads=[K.b_ps[psi], b_pb], writes=[b_pt[pi_]])
                else:
                    P.op("act", lambda h_, psi=psi, pi_=pi_: h_.activation(out=pt[pi_][:], in_=K.ps[psi][:], func=AF.Exp, scale=scale),
                         reads=[K.b_ps[psi]], writes=[b_pt[pi_]])
                    di = kt - NKT - 4 * qb
                    if di >= 0:
                        P.op("pool", lambda h_, pi_=pi_, di=di: h_.tensor_tensor(out=pt[pi_][:], in0=pt[pi_][:], in1=msk[:, di, :], op=ALU.mult),
                             reads=[b_msk], writes=[b_pt[pi_]])
                ptap = pt[pi_][:]

                def mo(h_, po=po, pr=pr, kt=kt, ptap=ptap, i=i, n=n, s_=s_):
                    h_.matmul(K.ps[po][:], lhsT=vv[s_][:, kt, :], rhs=ptap, start=(i == 0), stop=(i == n - 1))
                    return h_.matmul(K.ps[pr][:], lhsT=K.ones[:], rhs=ptap, start=(i == 0), stop=(i == n - 1))
                P.op("pe", mo, reads=[b_pt[pi_], b_hd[s_], K.b_ones], writes=[K.b_ps[po], K.b_ps[pr]])
            fi = fin % 2
            fin += 1
            P.op("dve", lambda h_, pr=pr, fi=fi: h_.reciprocal(out=rec[fi][:], in_=K.ps[pr][:]), reads=[K.b_ps[pr]], writes=[b_rec[fi]])
            P.op("dve", lambda h_, po=po, fi=fi: h_.tensor_tensor(out=ob[fi][:], in0=K.ps[po][:], in1=rec[fi][:], op=ALU.mult),
                 reads=[K.b_ps[po], b_rec[fi]], writes=[b_ob[fi]])
            K.dma(tok_d[h * 128:(h + 1) * 128, qsl], ob[fi][:], [b_ob[fi]], [b_td])
    P.barrier()
    P.release(mark)


class Cfg:
    def __init__(self, D=2048, DFF=5632, TOKW=1536, MEMW=512, NM=256, G=96, R=512, H=12, SEQ=4096, B=4, L=4, Tp=1024):
        self.D, self.DFF, self.TOKW, self.MEMW, self.NM, self.G, self.R, self.H = D, DFF, TOKW, MEMW, NM, G, R, H
        self.SEQ, self.B, self.L, self.Tp = SEQ, B, L, Tp
        self.T = SEQ // 2
        self.KT = D // 128
        self.NA = L // 2
        self.NP = G // 2
        self.NQ = G // 8
        self.RK = R // 128
        self.NTK = TOKW // 128


def gain_layout(cfg, norms, mem_norm, kv_in_norm, kv_norm, mla_q_norm, s5_b_glu):
    cols, off = [], {}

    def add(name, v):
        v = np.asarray(v, np.float32)
        n = v.shape[-1] // 128
        a = v.reshape(-1, n, 128)
        a = np.transpose(a, (2, 0, 1)).reshape(128, -1)
        off[name] = (sum(c.shape[1] for c in cols), n)
        cols.append(a)
    add("norms", norms)
    add("mem_norm", mem_norm)
    add("kv_in", kv_in_norm)
    add("kv", kv_norm)
    add("q", mla_q_norm)
    add("bglu", s5_b_glu)
    return np.ascontiguousarray(np.concatenate(cols, 1)), off


def build_segment(cfg, seg, W):
    c = cfg
    nc = bass.Bass("TRN2", target_bir_lowering=False)
    D, T, Tp = c.D, c.T, c.Tp

    def din(name, shape, dt=F32):
        return nc.dram_tensor(name, list(shape), dt, kind="ExternalInput").ap()

    def dout(name, shape, dt=F32):
        return nc.dram_tensor(name, list(shape), dt, kind="ExternalOutput").ap()

    def dtmp(name, shape, dt=F32):
        return nc.dram_tensor(name, list(shape), dt, kind="Internal").ap()
    K = KB(nc)
    P = K.P
    gshape = W["gains"].shape
    gains_d = din("gains", gshape)
    goff = W["goff"]
    gains = P.sb(list(gshape), F32, "gains")
    b_g = P.buf("gains")
    K.dma(gains[:], gains_d, [], [b_g])

    def gn(l, i):
        o = goff["norms"][0] + (l * 6 + i) * c.KT
        return gains[:, o:o + c.KT]

    def gsl(name, idx, n):
        o = goff[name][0] + idx * n
        return gains[:, o:o + n]
    x_in = din("x_in", [D, T])
    x = dout("x_out", [D, T])
    b_xc = P.buf("xcopy")
    K.dma(x, x_in, [], [b_xc])
    P.barrier()
    wts = {}

    def wt(name, l=None, j=None):
        key = (name, l, j)
        if key not in wts:
            a = W[name]
            shp = a.shape
            if l is not None:
                shp = shp[1:]
            if j is not None:
                shp = shp[1:]
            nm_ = f"{name}_{l}_{j}".replace("None", "x")
            wts[key] = (nm_, din(nm_, shp))
        return wts[key][1]

    def ffn(l, i):
        ffn_stage(K, x, gn(l, 2 * i if i == 0 else 4), gn(l, 1 if i == 0 else 5), b_g,
                  wt("ffn_w_gate", l, i), wt("ffn_w_up", l, i), wt("ffn_w_down", l, i), D, c.DFF, T, Tp)

    def s5prm(l):
        return dict(lam_s=din(f"s5lam_s{l}", [128, 3, c.NP]), lam_r=din(f"s5lam_r{l}", [128, 3, c.NP * 128]),
                    bT=din(f"s5bT{l}", [2, 128, c.NP, 128]), cP=din(f"s5cP{l}", [2, 128, c.NP, 128]))
    mem_d = din("memT", [D, c.NM])

    def pre_A(l, u_d, memo_d):
        mk = P.mark()
        MKV = mem_kv_setup(K, mem_d, gsl("mem_norm", l, c.KT), b_g, wt("mem_w_kv", l), D, c.NM, c.MEMW)
        mixer_pre_A(K, x, gn(l, 2), b_g, wt("a_w_in", l), u_d, memo_d, MKV, D, c.TOKW, c.MEMW, c.NM, T, Tp)
        P.release(mk)

    def post_A(l, yg_d, memo_d):
        mixer_post(K, x, gn(l, 3), b_g, wt("w_out", l), yg_d, memo_d, D, c.TOKW, c.MEMW, T, Tp,
                   W_glu=wt("s5_w_glu", l), bglu=gsl("bglu", l, c.NTK))

    if seg in (1, 2, 3):
        l_scan1 = {1: 0, 2: 1, 3: None}[seg]
        l_scan2 = {1: None, 2: 0, 3: 1}[seg]
        if l_scan2 is not None:
            l = l_scan2
            u_d = din("u_in", [c.TOKW, T], BF16)
            memo_d = din("memo_in", [c.MEMW, T], BF16)
            carry_in = din("carry_in", [128, 2, c.NP])
            carry_dummy = dtmp("carry_dummy", [128, 2, c.NP])
            yg_d = dtmp("yg", [c.TOKW, T], BF16)
            mk = P.mark()
            S = s5_setup(K, s5prm(l), c.NP)
            s5_scan(K, S, c.NP, T, u_d, carry_in, carry_dummy, yg_d, din(f"s5d{l}", [128, c.NQ]), True)
            P.release(mk)
            post_A(l, yg_d, memo_d)
            ffn(l, 1)
        if l_scan1 is not None:
            l = l_scan1
            ffn(l, 0)
            u_o = dout("u_out", [c.TOKW, T], BF16)
            memo_o = dout("memo_out", [c.MEMW, T], BF16)
            carry_o = dout("carry_out", [128, 2, c.NP])
            zero_c = din("zero_carry", [128, 2, c.NP])
            pre_A(l, u_o, memo_o)
            mk = P.mark()
            S = s5_setup(K, s5prm(l), c.NP)
            s5_scan(K, S, c.NP, T, u_o, zero_c, carry_o, None, None, False)
            P.release(mk)
        if seg == 3:
            RT = rope_tables(K, din("pos", [T], I32), din("invf", [64, 1]), din("sgn", [64, 1]), T)
            kv_stage(K, x, gsl("kv_in", 0, c.KT), gsl("kv", 0, c.RK), b_g, wt("w_dkv"), wt("w_kr"), wt("w_uk"), wt("w_uv"), RT,
                     dout("kn_out", [c.H, 128, T], BF16), dout("kr_out", [64, T], BF16), dout("v_out", [T, c.H * 128], BF16),
                     D, c.R, c.H, T, Tp)
    else:
        RT = rope_tables(K, din("pos", [T], I32), din("invf", [64, 1]), din("sgn", [64, 1]), T)
        kn_d, kr_d, v_d = din("kn", [c.H, 128, T], BF16), din("kr", [64, T], BF16), din("v", [T, c.H * 128], BF16)
        knp_d, krp_d, vp_d = din("knp", [c.H, 128, T], BF16), din("krp", [64, T], BF16), din("vp", [T, c.H * 128], BF16)
        pbias_d = din("pbias", [128, 1])
        mask_d = din("cmask", [4, 128, NT])
        qn_d = dtmp("qn", [c.H, 128, T], BF16)
        qr_d = dtmp("qr", [c.H, 64, T], BF16)
        memo_d = dtmp("memo", [c.MEMW, T], BF16)
        tok_d = dtmp("tok", [c.TOKW, T], BF16)
        for l in range(c.NA, c.L):
            j = l - c.NA
            ffn(l, 0)
            mk = P.mark()
            MKV = mem_kv_setup(K, mem_d, gsl("mem_norm", l, c.KT), b_g, wt("mem_w_kv", l), D, c.NM, c.MEMW)
            mixer_pre_B(K, x, gn(l, 2), gsl("q", j, c.RK), b_g, wt("b_w_in", j), wt("mla_w_uq", j), RT, qn_d, qr_d, memo_d, MKV,
                        D, c.R, c.H, c.MEMW, c.NM, T, Tp)
            P.release(mk)
            mla_attn(K, qn_d, qr_d, kn_d, kr_d, v_d, knp_d, krp_d, vp_d, pbias_d, mask_d, tok_d, c.H, T)
            mixer_post(K, x, gn(l, 3), b_g, wt("w_out", l), tok_d, memo_d, D, c.TOKW, c.MEMW, T, Tp)
            ffn(l, 1)
    P.barrier()
    P.emit()
    return nc, wts


def run_model(cfg, inp, dbg=None):
    c = cfg
    T = c.T
    NCORE = 2 * c.B
    f32 = np.float32
    gains, goff = gain_layout(c, inp["norms"], inp["mem_norm"], inp["kv_in_norm"], inp["kv_norm"], inp["mla_q_norm"], inp["s5_b_glu"])
    W = dict(inp)
    W["gains"], W["goff"] = gains, goff
    s5l = [s5_host_layout(*(np.asarray(inp[k][l], f32) for k in ("s5_lambda_re", "s5_lambda_im", "s5_log_dt", "s5_b_re", "s5_b_im",
                                                                 "s5_c_re", "s5_c_im", "s5_d"))) for l in range(c.NA)]
    xT = [np.ascontiguousarray(np.asarray(inp["x"][cid // 2, (cid % 2) * T:(cid % 2 + 1) * T, :], f32).T) for cid in range(NCORE)]
    memT = [np.ascontiguousarray(np.asarray(inp["mem"][b], f32).T) for b in range(c.B)]
    inv_freq = (10000.0 ** (-np.arange(0, 64, 2, dtype=np.float32) / 64)).astype(f32)
    invf = np.concatenate([inv_freq, inv_freq])[:, None].astype(f32) / f32(2 * np.pi)
    sgn = np.concatenate([-np.ones(32, f32), np.ones(32, f32)])[:, None]
    kk, qq = np.arange(128)[:, None], np.arange(NT)[None, :]
    cmask = np.stack([(qq >= 128 * i + kk).astype(f32) for i in range(4)], 0)
    zero_carry = np.zeros((128, 2, c.NP), f32)

    def launch(seg, per_core):
        nc, wts = build_segment(c, seg, W)
        shared = {"gains": gains}
        for (name, l, j), (nm_, ap) in wts.items():
            a = inp[name]
            if l is not None:
                a = a[l]
            if j is not None:
                a = a[j]
            shared[nm_] = np.ascontiguousarray(np.asarray(a, f32))
        maps = []
        for cid in range(NCORE):
            m = dict(shared)
            m["memT"] = memT[cid // 2]
            m.update(per_core[cid])
            maps.append(m)
        res = run_bass_kernel_spmd(nc, maps, core_ids=list(range(NCORE)))
        if dbg is not None:
            dbg[seg] = res.results
        return res.results

    def s5in(l):
        return {f"s5lam_s{l}": s5l[l]["lam_s"], f"s5lam_r{l}": s5l[l]["lam_r"], f"s5bT{l}": s5l[l]["bT"], f"s5cP{l}": s5l[l]["cP"]}
    pos = [np.ascontiguousarray(np.asarray(inp["positions"][cid // 2, (cid % 2) * T:(cid % 2 + 1) * T], np.int32)) for cid in range(NCORE)]
    r = launch(1, [dict(x_in=xT[cid], zero_carry=zero_carry, **s5in(0)) for cid in range(NCORE)])
    for seg in (2, 3):
        l2 = seg - 2
        pc = []
        for cid in range(NCORE):
            cin = r[cid - 1]["carry_out"] if cid % 2 == 1 else zero_carry
            d = dict(x_in=r[cid]["x_out"], u_in=r[cid]["u_out"], memo_in=r[cid]["memo_out"], carry_in=cin,
                     zero_carry=zero_carry, **s5in(l2))
            d[f"s5d{l2}"] = s5l[l2]["d_s"]
            if seg == 2:
                d.update(s5in(1))
            else:
                d.update(pos=pos[cid], invf=invf, sgn=sgn)
            pc.append(d)
        r = launch(seg, pc)
    pc = []
    for cid in range(NCORE):
        prev = r[cid - 1] if cid % 2 == 1 else r[cid]
        pc.append(dict(x_in=r[cid]["x_out"], kn=r[cid]["kn_out"], kr=r[cid]["kr_out"], v=r[cid]["v_out"],
                       knp=prev["kn_out"], krp=prev["kr_out"], vp=prev["v_out"],
                       pbias=np.full((128, 1), 0.0 if cid % 2 == 1 else -30000.0, f32), cmask=cmask,
                       pos=pos[cid], invf=invf, sgn=sgn))
    r = launch(4, pc)
    out = np.empty((c.B, c.SEQ, c.D), f32)
    for cid in range(NCORE):
        out[cid // 2, (cid % 2) * T:(cid % 2 + 1) * T, :] = r[cid]["x_out"].T
    return out


def build_fused(cfg, W):
    c = cfg
    nc = bass.Bass("TRN2", target_bir_lowering=False)
    D, T, Tp = c.D, c.T, c.Tp

    def din(name, shape, dt=F32):
        return nc.dram_tensor(name, list(shape), dt, kind="ExternalInput").ap()

    def dout(name, shape, dt=F32):
        return nc.dram_tensor(name, list(shape), dt, kind="ExternalOutput").ap()

    def dtmp(name, shape, dt=F32):
        return nc.dram_tensor(name, list(shape), dt, kind="Internal").ap()
    K = KB(nc)
    P = K.P
    gshape = W["gains"].shape
    goff = W["goff"]
    gains = P.sb(list(gshape), F32, "gains")
    b_g = P.buf("gains")
    K.dma(gains[:], din("gains", gshape), [], [b_g])
    cflag = P.sb([128, 1], F32, "cflag")
    K.dma(cflag[:], din("cflag", [128, 1]), [], [b_g])

    def gn(l, i):
        o = goff["norms"][0] + (l * 6 + i) * c.KT
        return gains[:, o:o + c.KT]

    def gsl(name, idx, n):
        o = goff[name][0] + idx * n
        return gains[:, o:o + n]
    x = dout("x_out", [D, T])
    xp = dtmp("xp", [D, T])
    b_xc = P.buf("xcopy")
    K.dma(x, din("x_in", [D, T]), [], [b_xc])
    K.dma(xp, din("xp_in", [D, T]), [], [b_xc])
    P.barrier()
    wts = {}

    def wt(name, l=None, j=None):
        key = (name, l, j)
        if key not in wts:
            shp = W[name].shape
            if l is not None:
                shp = shp[1:]
            if j is not None:
                shp = shp[1:]
            nm_ = f"{name}_{l}_{j}".replace("None", "x")
            wts[key] = (nm_, din(nm_, shp))
        return wts[key][1]

    def fspec(l, i):
        return (gn(l, 0 if i == 0 else 4), gn(l, 1 if i == 0 else 5), wt("ffn_w_gate", l, i), wt("ffn_w_up", l, i), wt("ffn_w_down", l, i))

    def ffn(l, i, xx):
        ffn_multi(K, xx, [fspec(l, i)], b_g, D, c.DFF, T, Tp)

    def ffn2(l, xx):
        ffn_multi(K, xx, [fspec(l, 1), fspec(l + 1, 0)], b_g, D, c.DFF, T, Tp)
    s5p = [dict(lam_s=din(f"s5lam_s{l}", [128, 3, c.NP]), lam_r=din(f"s5lam_r{l}", [128, 3, c.NP * 128]),
                bT=din(f"s5bT{l}", [2, 128, c.NP, 128]), cP=din(f"s5cP{l}", [2, 128, c.NP, 128]),
                d=din(f"s5d{l}", [128, c.NQ])) for l in range(c.NA)]
    mem_d = din("memT", [D, c.NM])
    u_d = dtmp("u", [c.TOKW, T], BF16)
    memo_d = dtmp("memo", [c.MEMW, T], BF16)
    yg_d = dtmp("yg", [c.TOKW, T], BF16)
    zero_c = din("zero_carry", [128, 2, c.NP])
    carryA = [dtmp(f"carryA{l}", [128, 2, c.NP]) for l in range(c.NA)]
    carry_dummy = dtmp("carry_dummy", [128, 2, c.NP])
    kn_d, kr_d, v_d = dtmp("kn", [c.H, 128, T], BF16), dtmp("kr", [64, T], BF16), dtmp("v", [T, c.H * 128], BF16)
    knp_d, krp_d, vp_d = dtmp("knp", [c.H, 128, T], BF16), dtmp("krp", [64, T], BF16), dtmp("vp", [T, c.H * 128], BF16)
    invf_d, sgn_d = din("invf", [64, 1]), din("sgn", [64, 1])

    streams = [dict(x=xp, u=u_d, memo=memo_d, yg=yg_d),
               dict(x=x, u=dtmp("u2", [c.TOKW, T], BF16), memo=dtmp("memo2", [c.MEMW, T], BF16), yg=dtmp("yg2", [c.TOKW, T], BF16))]

    def pre_scan(l, st):
        mk = P.mark()
        MKV = mem_kv_setup(K, mem_d, gsl("mem_norm", l, c.KT), b_g, wt("mem_w_kv", l), D, c.NM, c.MEMW)
        mixer_pre_A(K, st["x"], gn(l, 2), b_g, wt("a_w_in", l), st["u"], st["memo"], MKV, D, c.TOKW, c.MEMW, c.NM, T, Tp)
        P.release(mk)

    def post_scan(l, st):
        mixer_post(K, st["x"], gn(l, 3), b_g, wt("w_out", l), st["yg"], st["memo"], D, c.TOKW, c.MEMW, T, Tp,
                   W_glu=wt("s5_w_glu", l), bglu=gsl("bglu", l, c.NTK))

    def kv(xx, pos_name, kn_, kr_, v_):
        mk = P.mark()
        RT = rope_tables(K, din(pos_name, [T], I32), invf_d, sgn_d, T)
        kv_stage(K, xx, gsl("kv_in", 0, c.KT), gsl("kv", 0, c.RK), b_g, wt("w_dkv"), wt("w_kr"), wt("w_uk"), wt("w_uv"), RT,
                 kn_, kr_, v_, D, c.R, c.H, T, Tp)
        return mk, RT
    for st in streams:
        ffn(0, 0, st["x"])
    for l in range(c.NA):
        for st in streams:
            pre_scan(l, st)
        mk = P.mark()
        S = s5_setup(K, s5p[l], c.NP)
        s5_scan(K, S, c.NP, T, streams[0]["u"], zero_c, carryA[l], streams[0]["yg"], s5p[l]["d"], True)
        s5_scan(K, S, c.NP, T, streams[1]["u"], carryA[l], carry_dummy, streams[1]["yg"], s5p[l]["d"], True, flag=cflag, b_flag=b_g)
        P.release(mk)
        for si, st in enumerate(streams):
            post_scan(l, st)
            if l < c.NA - 1:
                ffn2(l, st["x"])
            else:
                ffn(l, 1, st["x"])
                if si == 0:
                    mk, _ = kv(xp, "posp", knp_d, krp_d, vp_d)
                    P.release(mk)
    mk, RT = kv(x, "pos", kn_d, kr_d, v_d)
    pbias_d = din("pbias", [128, 1])
    mask_d = din("cmask", [4, 128, NT])
    qn_d = dtmp("qn", [c.H, 128, T], BF16)
    qr_d = dtmp("qr", [c.H, 64, T], BF16)
    tok_d = dtmp("tok", [c.TOKW, T], BF16)
    for l in range(c.NA, c.L):
        j = l - c.NA
        if l == c.NA:
            ffn(l, 0, x)
        mk2 = P.mark()
        MKV = mem_kv_setup(K, mem_d, gsl("mem_norm", l, c.KT), b_g, wt("mem_w_kv", l), D, c.NM, c.MEMW)
        mixer_pre_B(K, x, gn(l, 2), gsl("q", j, c.RK), b_g, wt("b_w_in", j), wt("mla_w_uq", j), RT, qn_d, qr_d, memo_d, MKV,
                    D, c.R, c.H, c.MEMW, c.NM, T, Tp)
        P.release(mk2)
        mla_attn(K, qn_d, qr_d, kn_d, kr_d, v_d, knp_d, krp_d, vp_d, pbias_d, mask_d, tok_d, c.H, T)
        mixer_post(K, x, gn(l, 3), b_g, wt("w_out", l), tok_d, memo_d, D, c.TOKW, c.MEMW, T, Tp)
        if l == c.L - 1:
            ffn(l, 1, x)
        else:
            ffn2(l, x)
    P.barrier()
    P.emit()
    return nc, wts


def run_fused(cfg, inp):
    c = cfg
    T = c.T
    NCORE = 2 * c.B
    f32 = np.float32
    gains, goff = gain_layout(c, inp["norms"], inp["mem_norm"], inp["kv_in_norm"], inp["kv_norm"], inp["mla_q_norm"], inp["s5_b_glu"])
    W = dict(inp)
    W["gains"], W["goff"] = gains, goff
    nc, wts = build_fused(c, W)
    shared = {"gains": gains}
    for (name, l, j), (nm_, ap) in wts.items():
        a = inp[name]
        if l is not None:
            a = a[l]
        if j is not None:
            a = a[j]
        shared[nm_] = np.ascontiguousarray(np.asarray(a, f32))
    for l in range(c.NA):
        lay = s5_host_layout(*(np.asarray(inp[k][l], f32) for k in ("s5_lambda_re", "s5_lambda_im", "s5_log_dt", "s5_b_re", "s5_b_im",
                                                                    "s5_c_re", "s5_c_im", "s5_d")))
        shared.update({f"s5lam_s{l}": lay["lam_s"], f"s5lam_r{l}": lay["lam_r"], f"s5bT{l}": lay["bT"], f"s5cP{l}": lay["cP"],
                       f"s5d{l}": lay["d_s"]})
    inv_freq = (10000.0 ** (-np.arange(0, 64, 2, dtype=np.float32) / 64)).astype(f32)
    shared["invf"] = np.concatenate([inv_freq, inv_freq])[:, None].astype(f32) / f32(2 * np.pi)
    shared["sgn"] = np.concatenate([-np.ones(32, f32), np.ones(32, f32)])[:, None]
    kk, qq = np.arange(128)[:, None], np.arange(NT)[None, :]
    shared["cmask"] = np.stack([(qq >= 128 * i + kk).astype(f32) for i in range(4)], 0)
    shared["zero_carry"] = np.zeros((128, 2, c.NP), f32)
    xa = np.asarray(inp["x"], f32)
    pa = np.asarray(inp["positions"], np.int32)
    maps = []
    for cid in range(NCORE):
        b, hh = divmod(cid, 2)
        m = dict(shared)
        m["memT"] = np.ascontiguousarray(np.asarray(inp["mem"][b], f32).T)
        m["x_in"] = np.ascontiguousarray(xa[b, hh * T:(hh + 1) * T, :].T)
        m["xp_in"] = np.ascontiguousarray(xa[b, 0:T, :].T)
        m["pos"] = np.ascontiguousarray(pa[b, hh * T:(hh + 1) * T])
        m["posp"] = np.ascontiguousarray(pa[b, 0:T])
        m["cflag"] = np.full((128, 1), float(hh), f32)
        m["pbias"] = np.full((128, 1), 0.0 if hh == 1 else -30000.0, f32)
        maps.append(m)
    res = run_bass_kernel_spmd(nc, maps, core_ids=list(range(NCORE)))
    out = np.empty((c.B, c.SEQ, c.D), f32)
    for cid in range(NCORE):
        b, hh = divmod(cid, 2)
        out[b, hh * T:(hh + 1) * T, :] = res.results[cid]["x_out"].T
    return out


def kernel(**inputs):
    return run_fused(Cfg(), inputs)
```

```python
import numpy as np
import concourse.bass as bass
import concourse.mybir as mybir
from concourse.bass_utils import run_bass_kernel_spmd

F32 = mybir.dt.float32
BF16 = mybir.dt.bfloat16
I32 = mybir.dt.int32
AF = mybir.ActivationFunctionType
ALU = mybir.AluOpType
AX = mybir.AxisListType

ENGS = ("pe", "act", "dve", "pool", "sp")
NDMASEM = 12


class Buf:
    __slots__ = ("name", "w", "r")

    def __init__(self, name):
        self.name = name
        self.w = None
        self.r = []


class Op:
    __slots__ = ("eng", "fn", "deps", "sig", "dma", "semi", "semv", "idx", "prev_semv")

    def __init__(self, eng, fn, dma):
        self.eng = eng
        self.fn = fn
        self.dma = dma
        self.deps = []
        self.sig = False
        self.semi = None
        self.semv = None
        self.prev_semv = None


class Prog:
    def __init__(self, nc):
        self.nc = nc
        self.ops = {e: [] for e in ENGS}
        self.all_ops = []
        self.sb_off = 16384 + 2048
        self.sb_cap = 16384 + 212000
        self.sb_hi = 0
        self.ntens = 0
        self.dma_rr = 0
        self.dma_sem_total = [0] * NDMASEM
        self.dma_sem_last = [None] * NDMASEM
        self.bufs = []

    def sb(self, shape, dtype, name=None):
        nbytes = int(np.prod(shape[1:])) * mybir.dt.size(dtype)
        nbytes = (nbytes + 63) // 64 * 64
        off = self.sb_off
        assert off + nbytes <= self.sb_cap, f"SBUF overflow {off}+{nbytes} ({name})"
        self.sb_off += nbytes
        self.sb_hi = max(self.sb_hi, self.sb_off)
        self.ntens += 1
        return self.nc.alloc_sbuf_tensor_at(f"t{self.ntens}_{name or ''}", list(shape), dtype, offset=off)

    def mark(self):
        return self.sb_off

    def release(self, mark):
        self.sb_off = mark

    def buf(self, name="b"):
        b = Buf(name)
        self.bufs.append(b)
        return b

    def bufs_n(self, n, name="b"):
        return [self.buf(name) for _ in range(n)]

    def op(self, eng, fn, reads=(), writes=(), dma=0):
        o = Op(eng, fn, dma)
        deps = set()
        for b in reads:
            if b.w is not None:
                deps.add(b.w)
        for b in writes:
            if b.w is not None:
                deps.add(b.w)
            for r in b.r:
                deps.add(r)
        for b in reads:
            b.r.append(o)
        for b in writes:
            b.w = o
            b.r = []
        deps.discard(o)
        for d in sorted(deps, key=lambda x: x.idx):
            if d.eng == "pe" and eng == "pe" and not d.dma and not dma:
                continue
            o.deps.append(d)
            d.sig = True
        if dma:
            o.sig = True
            k = self.dma_rr % NDMASEM
            self.dma_rr += 1
            o.semi = ("d", k)
            o.prev_semv = self.dma_sem_total[k]
            self.dma_sem_total[k] += 16 * dma
            o.semv = self.dma_sem_total[k]
            self.dma_sem_last[k] = o
        o.idx = len(self.all_ops)
        self.ops[eng].append(o)
        self.all_ops.append(o)
        return o

    def chain(self, eng, fns, reads=(), writes=()):
        c = self.buf("chain")
        o = None
        for f in fns:
            o = self.op(eng, f, reads=list(reads) + [c], writes=list(writes) + [c])
        return o

    def barrier(self):
        last = []
        for e in ENGS:
            if self.ops[e]:
                last.append(self.ops[e][-1])
        last += [o for o in self.dma_sem_last if o is not None]
        for d in last:
            d.sig = True
        for e in ENGS:
            o = Op(e, lambda h: None, 0)
            o.deps = list(last)
            o.idx = len(self.all_ops)
            self.ops[e].append(o)
            self.all_ops.append(o)
        for bb in self.bufs:
            bb.w = None
            bb.r = []

    def emit(self):
        nc = self.nc
        self.sems = {e: nc.alloc_semaphore(f"s_{e}") for e in ENGS}
        self.dsems = [nc.alloc_semaphore(f"s_dma{i}") for i in range(NDMASEM)]
        cnt = {e: 0 for e in ENGS}
        for o in self.all_ops:
            if not o.dma and o.sig:
                cnt[o.eng] += 1
                o.semi = ("e", o.eng)
                o.semv = cnt[o.eng]
        handles = {"pe": "tensor", "act": "scalar", "dve": "vector", "pool": "gpsimd", "sp": "sync"}

        def emit_engine(e, h):
            known = {}
            for o in self.ops[e]:
                waits = {}
                for d in o.deps:
                    waits[d.semi] = max(waits.get(d.semi, 0), d.semv)
                if o.dma and o.prev_semv:
                    waits[o.semi] = max(waits.get(o.semi, 0), o.prev_semv)
                for key, v in waits.items():
                    if known.get(key, 0) >= v:
                        continue
                    known[key] = v
                    sem = self.dsems[key[1]] if key[0] == "d" else self.sems[key[1]]
                    h.wait_ge(sem, v)
                r = o.fn(h)
                if o.dma:
                    sem = self.dsems[o.semi[1]]
                    assert len(r) == o.dma, (len(r), o.dma)
                    for ins in r:
                        ins.then_inc(sem, 16)
                elif o.sig:
                    if r is None:
                        r = h.nop()
                    r.then_inc(self.sems[e], 1)

        with nc.Block() as block:
            for e in ENGS:
                getattr(block, handles[e])(lambda h, e=e: emit_engine(e, h))


NT = 512
EPS = 1e-6


class KB:
    def __init__(self, nc):
        self.nc = nc
        self.P = Prog(nc)
        P = self.P
        self.ps = [nc.alloc_psum_tensor(f"psb{i}", [128, NT], F32) for i in range(8)]
        self.b_ps = P.bufs_n(8, "ps")
        self.ones = P.sb([128, 128], BF16, "ones")
        self.b_ones = P.buf("ones")
        P.op("dve", lambda h: h.memset(self.ones[:], 1.0), writes=[self.b_ones])
        self.init_consts()

    def mm(self, psi, pairs, reads, n=NT, m=128):
        ps = self.ps[psi]

        def f(h):
            L = len(pairs)
            for i, (a, b) in enumerate(pairs):
                r = h.matmul(ps[0:m, 0:n], lhsT=a, rhs=b, start=(i == 0), stop=(i == L - 1))
            return r
        return self.P.op("pe", f, reads=reads, writes=[self.b_ps[psi]])

    def dma(self, out, in_, reads, writes, eng="sp"):
        return self.P.op(eng, lambda h: [h.dma_start(out=out, in_=in_)], reads=reads, writes=writes, dma=1)

    def rstd_from_sq(self, sq, nk, n, psi, rstd, b_sq, b_rstd, dim):
        P = self.P
        ps = self.ps[psi]

        def mm(h):
            for k in range(nk):
                r = h.matmul(ps[:, 0:n], lhsT=self.ones[:], rhs=sq[:, k, 0:n], start=(k == 0), stop=(k == nk - 1))
            return r
        P.op("pe", mm, reads=[b_sq, self.b_ones], writes=[self.b_ps[psi]])
        P.op("act", lambda h: h.activation(out=rstd[:, 0:n], in_=ps[:, 0:n], func=AF.Sqrt, scale=1.0 / dim, bias=self.eps_ap()),
             reads=[self.b_ps[psi]], writes=[b_rstd])
        P.op("dve", lambda h: h.reciprocal(out=rstd[:, 0:n], in_=rstd[:, 0:n]), reads=[b_rstd], writes=[b_rstd])

    def eps_ap(self):
        return self.epsT[:, 0:1]

    def init_consts(self):
        P = self.P
        self.epsT = P.sb([128, 1], F32, "eps")
        self.b_eps = P.buf("eps")
        P.op("dve", lambda h: h.memset(self.epsT[:], EPS), writes=[self.b_eps])
        self.negpi = P.sb([128, 1], F32, "negpi")
        P.op("dve", lambda h: h.memset(self.negpi[:], -float(np.pi)), writes=[self.b_eps])


def ffn_stage(K, x_d, gin, gout, b_g, Wg, Wu, Wd, D, DFF, T, Tp, SUB=128, GW=256, res_scale=0.5):
    P = K.P
    KT = D // 128
    GC = GW // 128
    NTn = Tp // NT
    NG = DFF // GW
    mark = P.mark()
    hT = P.sb([128, KT, Tp], BF16, "hT")
    b_hT = P.bufs_n(Tp // SUB, "hT")
    acc = P.sb([128, KT, Tp], F32, "acc")
    b_acc = [P.bufs_n(NTn, "acc") for _ in range(KT)]
    xs = [P.sb([128, KT, SUB], F32, "xs") for _ in range(2)]
    b_xs = P.bufs_n(2, "xs")
    sq = P.sb([128, KT, SUB], BF16, "sq")
    b_sq = P.buf("sq")
    rstd = P.sb([128, SUB], F32, "rstd")
    b_rstd = P.buf("rstd")
    wg = [P.sb([128, KT, GW], BF16, "wg") for _ in range(2)]
    wu = [P.sb([128, KT, GW], BF16, "wu") for _ in range(2)]
    wd = [P.sb([128, GC, D], BF16, "wd") for _ in range(2)]
    b_wg = P.bufs_n(2, "wg")
    b_wu = P.bufs_n(2, "wu")
    b_wd = P.bufs_n(2, "wd")
    hid = [P.sb([128, GC, Tp], BF16, "hid") for _ in range(2)]
    b_hid = [[P.bufs_n(NTn, "hid") for _ in range(GC)] for _ in range(2)]
    sg = [P.sb([128, NT], F32, "sg") for _ in range(2)]
    b_sg = P.bufs_n(2, "sg")
    b_x = P.buf("xdram")
    gsc = P.sb([128, KT], F32, "gsc")
    b_gsc = P.buf("gsc")
    P.op("dve", lambda h: h.tensor_scalar(out=gsc[:], in0=gout, scalar1=float(res_scale), scalar2=None, op0=ALU.mult),
         reads=[b_g], writes=[b_gsc])
    Wg_v = Wg.rearrange("(kt p) c -> p kt c", p=128)
    Wu_v = Wu.rearrange("(kt p) c -> p kt c", p=128)
    Wd_v = Wd.rearrange("(c p) d -> p c d", p=128)
    x_v = x_d.rearrange("(kt p) t -> p kt t", p=128)
    PS_G, PS_U, PS_D, PS_M = (0, 1), (2, 3), (4, 5), 6
    cnt = {"gu": 0, "d": 0, "xs": 0}

    for p in range(T // Tp):
        t0 = p * Tp
        for s in range(Tp // SUB):
            xi = cnt["xs"] % 2
            cnt["xs"] += 1
            K.dma(xs[xi][:], x_v[:, :, t0 + s * SUB: t0 + (s + 1) * SUB], [b_x], [b_xs[xi]])
            P.op("act", lambda h, xi=xi: h.activation(out=sq[:], in_=xs[xi][:], func=AF.Square),
                 reads=[b_xs[xi]], writes=[b_sq])
            K.rstd_from_sq(sq, KT, SUB, PS_M, rstd, b_sq, b_rstd, D)

            def nrm(h, xi=xi, s=s):
                for kt in range(KT):
                    r = h.scalar_tensor_tensor(out=hT[:, kt, s * SUB:(s + 1) * SUB], in0=xs[xi][:, kt, :],
                                               scalar=gin[:, kt:kt + 1], in1=rstd[:], op0=ALU.mult, op1=ALU.mult)
                return r
            P.op("dve", nrm, reads=[b_xs[xi], b_rstd, b_g], writes=[b_hT[s]])

        def load_w(j):
            sl = j % 2
            K.dma(wg[sl][:], Wg_v[:, :, j * GW:(j + 1) * GW], [], [b_wg[sl]], eng="pool")
            K.dma(wu[sl][:], Wu_v[:, :, j * GW:(j + 1) * GW], [], [b_wu[sl]], eng="pool")
            K.dma(wd[sl][:], Wd_v[:, j * GC:(j + 1) * GC, :], [], [b_wd[sl]], eng="pool")

        def gateup(j):
            sl = j % 2
            for c in range(GC):
                for nt in range(NTn):
                    gi = cnt["gu"] % 2
                    cnt["gu"] += 1
                    pg, pu = PS_G[gi], PS_U[gi]
                    hbufs = b_hT[nt * (NT // SUB):(nt + 1) * (NT // SUB)]

                    K.mm(pg, [(wg[sl][:, kt, c * 128:(c + 1) * 128], hT[:, kt, nt * NT:(nt + 1) * NT]) for kt in range(KT)],
                         [b_wg[sl]] + hbufs)
                    K.mm(pu, [(wu[sl][:, kt, c * 128:(c + 1) * 128], hT[:, kt, nt * NT:(nt + 1) * NT]) for kt in range(KT)],
                         [b_wu[sl]] + hbufs)
                    P.op("act", lambda h, gi=gi, pg=pg: h.activation(out=sg[gi][:], in_=K.ps[pg][:], func=AF.Silu),
                         reads=[K.b_ps[pg]], writes=[b_sg[gi]])
                    P.op("dve", lambda h, gi=gi, pu=pu, sl=sl, c=c, nt=nt: h.tensor_tensor(
                        out=hid[sl][:, c, nt * NT:(nt + 1) * NT], in0=sg[gi][:], in1=K.ps[pu][:], op=ALU.mult),
                        reads=[b_sg[gi], K.b_ps[pu]], writes=[b_hid[sl][c][nt]])

        def down(j):
            sl = j % 2
            for m in range(KT):
                for nt in range(NTn):
                    di = cnt["d"] % 2
                    cnt["d"] += 1
                    pd = PS_D[di]

                    K.mm(pd, [(wd[sl][:, c, m * 128:(m + 1) * 128], hid[sl][:, c, nt * NT:(nt + 1) * NT]) for c in range(GC)],
                         [b_wd[sl]] + [b_hid[sl][c][nt] for c in range(GC)])
                    if j == 0:
                        P.op("dve", lambda h, m=m, nt=nt, pd=pd: h.tensor_copy(out=acc[:, m, nt * NT:(nt + 1) * NT], in_=K.ps[pd][:]),
                             reads=[K.b_ps[pd]], writes=[b_acc[m][nt]])
                    else:
                        P.op("dve", lambda h, m=m, nt=nt, pd=pd: h.tensor_tensor(
                            out=acc[:, m, nt * NT:(nt + 1) * NT], in0=acc[:, m, nt * NT:(nt + 1) * NT], in1=K.ps[pd][:], op=ALU.add),
                            reads=[K.b_ps[pd], b_acc[m][nt]], writes=[b_acc[m][nt]])

        load_w(0)
        gateup(0)
        for j in range(NG):
            if j + 1 < NG:
                load_w(j + 1)
                gateup(j + 1)
            down(j)

        for s in range(Tp // SUB):
            nt = (s * SUB) // NT
            accb = [b_acc[m][nt] for m in range(KT)]
            xi = cnt["xs"] % 2
            cnt["xs"] += 1
            K.dma(xs[xi][:], x_v[:, :, t0 + s * SUB: t0 + (s + 1) * SUB], [b_x], [b_xs[xi]])
            P.op("act", lambda h, s=s: h.activation(out=sq[:], in_=acc[:, :, s * SUB:(s + 1) * SUB], func=AF.Square),
                 reads=accb, writes=[b_sq])
            K.rstd_from_sq(sq, KT, SUB, PS_M, rstd, b_sq, b_rstd, D)

            def fin1(h, s=s):
                for kt in range(KT):
                    r = h.scalar_tensor_tensor(out=acc[:, kt, s * SUB:(s + 1) * SUB], in0=acc[:, kt, s * SUB:(s + 1) * SUB],
                                               scalar=gsc[:, kt:kt + 1], in1=rstd[:], op0=ALU.mult, op1=ALU.mult)
                return r
            P.op("dve", fin1, reads=accb + [b_rstd, b_gsc], writes=accb)
            P.op("pool", lambda h, xi=xi, s=s: h.tensor_tensor(
                out=xs[xi][:], in0=acc[:, :, s * SUB:(s + 1) * SUB], in1=xs[xi][:], op=ALU.add),
                reads=accb + [b_xs[xi]], writes=[b_xs[xi]])
            K.dma(x_v[:, :, t0 + s * SUB: t0 + (s + 1) * SUB], xs[xi][:], [b_xs[xi]], [b_x])
    P.barrier()
    P.release(mark)


def ffn_multi(K, x_d, specs, b_g, D, DFF, T, Tp, SUB=128, GW=256, res_scale=0.5):
    P = K.P
    KT = D // 128
    GC = GW // 128
    NTn = Tp // NT
    NG = DFF // GW
    NS = Tp // SUB
    mark = P.mark()
    hT = P.sb([128, KT, Tp], BF16, "hT")
    b_hT = P.bufs_n(NS, "hT")
    acc = P.sb([128, KT, Tp], F32, "acc")
    b_acc = [P.bufs_n(NTn, "acc") for _ in range(KT)]
    fixed = (KT * Tp * 6 + 2 * KT * SUB * 2 + 2 * SUB * 4 + 2 * (2 * KT * GW * 2 + GC * D * 2) + 2 * GC * Tp * 2 + 2 * NT * 4
             + len(specs) * KT * 4 + 2048)
    NXS = 4 if P.sb_cap - P.sb_off - fixed >= 4 * KT * SUB * 4 else 2
    xs = [P.sb([128, KT, SUB], F32, "xs") for _ in range(NXS)]
    b_xs = P.bufs_n(NXS, "xs")
    sq = [P.sb([128, KT, SUB], BF16, "sq") for _ in range(2)]
    b_sq = P.bufs_n(2, "sq")
    rstd = [P.sb([128, SUB], F32, "rstd") for _ in range(2)]
    b_rstd = P.bufs_n(2, "rstd")
    wg = [P.sb([128, KT, GW], BF16, "wg") for _ in range(2)]
    wu = [P.sb([128, KT, GW], BF16, "wu") for _ in range(2)]
    wd = [P.sb([128, GC, D], BF16, "wd") for _ in range(2)]
    b_wg, b_wu, b_wd = P.bufs_n(2, "wg"), P.bufs_n(2, "wu"), P.bufs_n(2, "wd")
    hid = [P.sb([128, GC, Tp], BF16, "hid") for _ in range(2)]
    b_hid = [[P.bufs_n(NTn, "hid") for _ in range(GC)] for _ in range(2)]
    sg = [P.sb([128, NT], F32, "sg") for _ in range(2)]
    b_sg = P.bufs_n(2, "sg")
    b_x = P.buf("xdram")
    gscs = []
    b_gsc = P.buf("gsc")
    for (gin, gout, _, _, _) in specs:
        g_ = P.sb([128, KT], F32, "gsc")
        P.op("dve", lambda h, g_=g_, gout=gout: h.tensor_scalar(out=g_[:], in0=gout, scalar1=float(res_scale), scalar2=None, op0=ALU.mult),
             reads=[b_g], writes=[b_gsc])
        gscs.append(g_)
    x_v = x_d.rearrange("(kt p) t -> p kt t", p=128)
    views = [(Wg.rearrange("(kt p) c -> p kt c", p=128), Wu.rearrange("(kt p) c -> p kt c", p=128), Wd.rearrange("(c p) d -> p c d", p=128))
             for (_, _, Wg, Wu, Wd) in specs]
    PS_G, PS_U, PS_D, PS_M = (0, 1), (2, 3), (4, 5), (6, 7)
    cnt = {"gu": 0, "d": 0, "xs": 0, "sq": 0}
    jobs = [(f, p) for f in range(len(specs)) for p in range(T // Tp)]

    def rstd_calc(src_ap, reads):
        qi = cnt["sq"] % 2
        cnt["sq"] += 1
        P.op("act", lambda h: h.activation(out=sq[qi][:], in_=src_ap, func=AF.Square), reads=reads, writes=[b_sq[qi]])
        K.rstd_from_sq(sq[qi], KT, SUB, PS_M[qi], rstd[qi], b_sq[qi], b_rstd[qi], D)
        return qi

    def norm(k):
        f, p = jobs[k]
        gin = specs[f][0]
        t0 = p * Tp
        for s_ in range(NS):
            xi = cnt["xs"] % 2
            cnt["xs"] += 1
            K.dma(xs[xi][:], x_v[:, :, t0 + s_ * SUB: t0 + (s_ + 1) * SUB], [b_x], [b_xs[xi]])
            qi = rstd_calc(xs[xi][:], [b_xs[xi]])

            def nrm(h, xi=xi, s_=s_, qi=qi):
                for kt in range(KT):
                    r = h.scalar_tensor_tensor(out=hT[:, kt, s_ * SUB:(s_ + 1) * SUB], in0=xs[xi][:, kt, :],
                                               scalar=gin[:, kt:kt + 1], in1=rstd[qi][:], op0=ALU.mult, op1=ALU.mult)
                return r
            P.op("dve", nrm, reads=[b_xs[xi], b_rstd[qi], b_g], writes=[b_hT[s_]])

    def load_w(k, j):
        f, _ = jobs[k]
        Wg_v, Wu_v, Wd_v = views[f]
        sl = j % 2
        K.dma(wg[sl][:], Wg_v[:, :, j * GW:(j + 1) * GW], [], [b_wg[sl]], eng="pool")
        K.dma(wu[sl][:], Wu_v[:, :, j * GW:(j + 1) * GW], [], [b_wu[sl]], eng="pool")
        K.dma(wd[sl][:], Wd_v[:, j * GC:(j + 1) * GC, :], [], [b_wd[sl]], eng="pool")

    def gateup(j):
        sl = j % 2
        for c in range(GC):
            for nt in range(NTn):
                gi = cnt["gu"] % 2
                cnt["gu"] += 1
                pg, pu = PS_G[gi], PS_U[gi]
                hbufs = b_hT[nt * (NT // SUB):(nt + 1) * (NT // SUB)]
                K.mm(pg, [(wg[sl][:, kt, c * 128:(c + 1) * 128], hT[:, kt, nt * NT:(nt + 1) * NT]) for kt in range(KT)], [b_wg[sl]] + hbufs)
                K.mm(pu, [(wu[sl][:, kt, c * 128:(c + 1) * 128], hT[:, kt, nt * NT:(nt + 1) * NT]) for kt in range(KT)], [b_wu[sl]] + hbufs)
                P.op("act", lambda h, gi=gi, pg=pg: h.activation(out=sg[gi][:], in_=K.ps[pg][:], func=AF.Silu),
                     reads=[K.b_ps[pg]], writes=[b_sg[gi]])
                P.op("dve", lambda h, gi=gi, pu=pu, sl=sl, c=c, nt=nt: h.tensor_tensor(
                    out=hid[sl][:, c, nt * NT:(nt + 1) * NT], in0=sg[gi][:], in1=K.ps[pu][:], op=ALU.mult),
                    reads=[b_sg[gi], K.b_ps[pu]], writes=[b_hid[sl][c][nt]])

    def down(j):
        sl = j % 2
        for m in range(KT):
            for nt in range(NTn):
                pd = PS_D[cnt["d"] % 2]
                cnt["d"] += 1
                K.mm(pd, [(wd[sl][:, c, m * 128:(m + 1) * 128], hid[sl][:, c, nt * NT:(nt + 1) * NT]) for c in range(GC)],
                     [b_wd[sl]] + [b_hid[sl][c][nt] for c in range(GC)])
                if j == 0:
                    P.op("dve", lambda h, m=m, nt=nt, pd=pd: h.tensor_copy(out=acc[:, m, nt * NT:(nt + 1) * NT], in_=K.ps[pd][:]),
                         reads=[K.b_ps[pd]], writes=[b_acc[m][nt]])
                else:
                    P.op("dve", lambda h, m=m, nt=nt, pd=pd: h.tensor_tensor(
                        out=acc[:, m, nt * NT:(nt + 1) * NT], in0=acc[:, m, nt * NT:(nt + 1) * NT], in1=K.ps[pd][:], op=ALU.add),
                        reads=[K.b_ps[pd], b_acc[m][nt]], writes=[b_acc[m][nt]])

    def finalize_job(k):
        f, p = jobs[k]
        gsc = gscs[f]
        t0 = p * Tp
        def xload(s_):
            xi_ = (cnt["xs"] + s_) % 2
            K.dma(xs[xi_][:], x_v[:, :, t0 + s_ * SUB: t0 + (s_ + 1) * SUB], [b_x], [b_xs[xi_]])
        xload(0)
        base = cnt["xs"]
        for s_ in range(NS):
            nt = (s_ * SUB) // NT
            accb = [b_acc[m][nt] for m in range(KT)]
            xi = (base + s_) % 2
            if s_ + 1 < NS:
                cnt["xs"] = base
                xload(s_ + 1)
            cnt["xs"] = base + s_ + 1
            qi = rstd_calc(acc[:, :, s_ * SUB:(s_ + 1) * SUB], accb)

            if NXS == 4:
                ti = 2 + cnt["xs"] % 2
                tmp, b_tmp = xs[ti], b_xs[ti]

                def fin1(h, s_=s_, qi=qi, tmp=tmp):
                    for kt in range(KT):
                        r = h.scalar_tensor_tensor(out=tmp[:, kt, :], in0=acc[:, kt, s_ * SUB:(s_ + 1) * SUB],
                                                   scalar=gsc[:, kt:kt + 1], in1=rstd[qi][:], op0=ALU.mult, op1=ALU.mult)
                    return r
                P.op("dve", fin1, reads=accb + [b_rstd[qi], b_gsc], writes=[b_tmp])
                P.op("pool", lambda h, xi=xi, tmp=tmp: h.tensor_tensor(out=xs[xi][:], in0=tmp[:], in1=xs[xi][:], op=ALU.add),
                     reads=[b_tmp, b_xs[xi]], writes=[b_xs[xi]])
            else:
                def fin1(h, s_=s_, qi=qi):
                    for kt in range(KT):
                        r = h.scalar_tensor_tensor(out=acc[:, kt, s_ * SUB:(s_ + 1) * SUB], in0=acc[:, kt, s_ * SUB:(s_ + 1) * SUB],
                                                   scalar=gsc[:, kt:kt + 1], in1=rstd[qi][:], op0=ALU.mult, op1=ALU.mult)
                    return r
                P.op("dve", fin1, reads=accb + [b_rstd[qi], b_gsc], writes=accb)
                P.op("pool", lambda h, xi=xi, s_=s_: h.tensor_tensor(
                    out=xs[xi][:], in0=acc[:, :, s_ * SUB:(s_ + 1) * SUB], in1=xs[xi][:], op=ALU.add),
                    reads=accb + [b_xs[xi]], writes=[b_xs[xi]])
            K.dma(x_v[:, :, t0 + s_ * SUB: t0 + (s_ + 1) * SUB], xs[xi][:], [b_xs[xi]], [b_x])

    assert NG % 2 == 0
    norm(0)
    load_w(0, 0)
    gateup(0)
    for k in range(len(jobs)):
        for j in range(NG):
            if j + 1 < NG:
                load_w(k, j + 1)
                gateup(j + 1)
                down(j)
            else:
                if k + 1 < len(jobs):
                    norm(k + 1)
                    load_w(k + 1, 0)
                    gateup(0)
                down(j)
        finalize_job(k)
    P.barrier()
    P.release(mark)


S5_L = 128


def s5_host_layout(lam_re, lam_im, log_dt, b_re, b_im, c_re, c_im, d):
    G, Pn, C = b_re.shape
    NP = G // 2

    def st(a):
        return np.ascontiguousarray(a.reshape(NP, 2 * Pn).T)
    ldt = np.repeat(log_dt[:, None], Pn, 1)
    lam_s = np.stack([st(lam_re), st(lam_im), st(ldt)], 1)
    row = np.stack([lam_re.reshape(-1), lam_im.reshape(-1), ldt.reshape(-1)], 0)
    lam_r = np.ascontiguousarray(np.broadcast_to(row[None], (128, 3, NP * 128)))
    bT = np.zeros((2, 128, NP, 128), np.float32)
    cP = np.zeros((2, 128, NP, 128), np.float32)
    for g in range(G):
        q, hh = g // 2, g % 2
        off = (g % 8) * 16
        bT[0, off:off + 16, q, hh * 64:(hh + 1) * 64] = b_re[g].T
        bT[1, off:off + 16, q, hh * 64:(hh + 1) * 64] = b_im[g].T
        cP[0, hh * 64:(hh + 1) * 64, q, off:off + 16] = c_re[g].T
        cP[1, hh * 64:(hh + 1) * 64, q, off:off + 16] = c_im[g].T
    d_s = np.ascontiguousarray(d.reshape(-1, 128).T)
    return dict(lam_s=lam_s, lam_r=lam_r, bT=bT, cP=cP, d_s=d_s)


def s5_setup(K, prm, NP):
    P = K.P
    L = S5_L
    PI = float(np.pi)
    S = {}
    lam_s = P.sb([128, 3, NP], F32, "lam_s")
    b_l = P.buf("lam_s")
    K.dma(lam_s[:], prm["lam_s"], [], [b_l])
    dt = P.sb([128, NP], F32, "dt")
    th = P.sb([128, NP], F32, "th")
    r = P.sb([128, NP], F32, "r")
    b_t = P.buf("s5tab")
    P.op("act", lambda h: h.activation(out=dt[:], in_=lam_s[:, 2, :], func=AF.Exp), reads=[b_l], writes=[b_t])
    P.op("dve", lambda h: h.tensor_tensor(out=th[:], in0=lam_s[:, 1, :], in1=dt[:], op=ALU.mult), reads=[b_l, b_t], writes=[b_t])
    P.op("dve", lambda h: h.tensor_tensor(out=r[:], in0=lam_s[:, 0, :], in1=dt[:], op=ALU.mult), reads=[b_l, b_t], writes=[b_t])
    P.op("act", lambda h: h.activation(out=r[:], in_=r[:], func=AF.Exp), reads=[b_t], writes=[b_t])
    jrow_i = P.sb([128, L], I32, "jrow_i")
    jrow = P.sb([128, L], F32, "jrow")
    P.op("pool", lambda h: h.iota(jrow_i[:], pattern=[[1, L]], base=0, channel_multiplier=0), writes=[b_t], reads=[b_t])
    P.op("dve", lambda h: h.tensor_copy(out=jrow[:], in_=jrow_i[:]), reads=[b_t], writes=[b_t])
    cosT = P.sb([128, NP, L], F32, "cosT")
    sinT = P.sb([128, NP, L], F32, "sinT")
    Rz = P.sb([128, NP, L], F32, "Rz")
    b_tab = P.buf("tabs")

    TWO_PI = 2 * PI
    MAGIC = 12582912.0
    thn = P.sb([128, NP], F32, "thn")
    P.op("dve", lambda h: h.tensor_scalar(out=thn[:], in0=th[:], scalar1=1.0 / TWO_PI, scalar2=None, op0=ALU.mult), reads=[b_t], writes=[b_t])
    Kre = P.sb([128, NP], F32, "Kre")
    Kim = P.sb([128, NP], F32, "Kim")
    mk0 = P.mark()
    tmpT = P.sb([128, NP, L], F32, "tmpT")

    def sin_cycles(tens, shift, b_r, b_w_):
        tv = tmpT_v(tens)
        steps = []
        if shift:
            steps.append(lambda h: h.tensor_scalar(out=tens, in0=tens, scalar1=float(shift), scalar2=None, op0=ALU.add))
        steps.append(lambda h: h.tensor_scalar(out=tv, in0=tens, scalar1=MAGIC, scalar2=None, op0=ALU.add))
        steps.append(lambda h: h.tensor_scalar(out=tv, in0=tv, scalar1=-MAGIC, scalar2=None, op0=ALU.add))
        steps.append(lambda h: h.tensor_tensor(out=tens, in0=tens, in1=tv, op=ALU.subtract))
        P.chain("dve", steps, reads=b_r, writes=b_w_)
        P.op("act", lambda h: h.activation(out=tens, in_=tens, func=AF.Sin, scale=TWO_PI), reads=b_w_, writes=b_w_)

    def tmpT_v(tens):
        shp = tens.shape
        if len(shp) == 3:
            return tmpT[:, 0:shp[1], 0:shp[2]]
        return tmpT[:, 0, 0:shp[1]]

    def angs(h):
        for q in range(NP):
            h.tensor_scalar(out=sinT[:, q, :], in0=jrow[:], scalar1=thn[:, q:q + 1], scalar2=None, op0=ALU.mult)
            r_ = h.tensor_scalar(out=cosT[:, q, :], in0=jrow[:], scalar1=thn[:, q:q + 1], scalar2=None, op0=ALU.mult)
        return r_
    P.op("dve", angs, reads=[b_t], writes=[b_tab])
    sin_cycles(sinT[:], 0.0, [b_tab], [b_tab])
    sin_cycles(cosT[:], 0.25, [b_tab], [b_tab])

    def rz(h):
        for q in range(NP):
            h.tensor_scalar(out=Rz[:, q, 1:L], in0=jrow[:, 1:L], scalar1=0.0, scalar2=r[:, q:q + 1], op0=ALU.mult, op1=ALU.add)
        return h.memset(Rz[:, :, 0:1], 0.0)
    P.op("dve", rz, reads=[b_t], writes=[b_tab])
    def kang(h):
        h.tensor_scalar(out=Kim[:], in0=thn[:], scalar1=float(L), scalar2=None, op0=ALU.mult)
        return h.tensor_scalar(out=Kre[:], in0=thn[:], scalar1=float(L), scalar2=None, op0=ALU.mult)
    P.op("dve", kang, reads=[b_t], writes=[b_tab])
    sin_cycles(Kim[:], 0.0, [b_tab], [b_tab])
    sin_cycles(Kre[:], 0.25, [b_tab], [b_tab])

    def kmul(h):
        h.tensor_tensor(out=Kim[:], in0=Kim[:], in1=r[:], op=ALU.mult)
        return h.tensor_tensor(out=Kre[:], in0=Kre[:], in1=r[:], op=ALU.mult)
    P.op("dve", kmul, reads=[b_tab, b_t], writes=[b_tab])

    P.barrier()
    P.release(mk0)
    BT = [P.sb([128, NP, 128], BF16, f"BT{i}") for i in range(2)]
    CT = [P.sb([128, NP, 128], BF16, f"CT{i}") for i in range(2)]
    b_BT = P.buf("BT")
    b_CT = P.buf("CT")
    K.dma(CT[0][:], prm["cP"][0], [], [b_CT], eng="pool")
    K.dma(CT[1][:], prm["cP"][1], [], [b_CT], eng="pool")
    P.op("dve", lambda h: h.tensor_scalar(out=CT[1][:], in0=CT[1][:], scalar1=-1.0, scalar2=None, op0=ALU.mult), reads=[b_CT], writes=[b_CT])
    mk = P.mark()
    QB = 4
    W = QB * 128
    lr = P.sb([128, 3, W], F32, "lr")
    tb = [P.sb([128, W], F32, f"tb{i}") for i in range(6)]
    bt = [P.sb([128, QB, 128], F32, f"bt{i}") for i in range(2)]
    b_lr = P.buf("lr")
    b_bt = P.buf("bt")
    b_w = P.buf("w")
    for blk in range(NP // QB):
        cs = slice(blk * W, (blk + 1) * W)
        K.dma(lr[:], prm["lam_r"][:, :, cs], [], [b_lr])
        K.dma(bt[0][:], prm["bT"][0][:, blk * QB:(blk + 1) * QB, :], [], [b_bt])
        K.dma(bt[1][:], prm["bT"][1][:, blk * QB:(blk + 1) * QB, :], [], [b_bt])
        dtr, thr, rr, ca, sa, den = tb
        P.op("act", lambda h: h.activation(out=dtr[:], in_=lr[:, 2, :], func=AF.Exp), reads=[b_lr], writes=[b_w])

        def c1(h):
            h.tensor_tensor(out=thr[:], in0=lr[:, 1, :], in1=dtr[:], op=ALU.mult)
            h.tensor_tensor(out=rr[:], in0=lr[:, 0, :], in1=dtr[:], op=ALU.mult)
            h.tensor_scalar(out=sa[:], in0=thr[:], scalar1=1.0 / TWO_PI, scalar2=None, op0=ALU.mult)
            h.tensor_scalar(out=ca[:], in0=thr[:], scalar1=1.0 / TWO_PI, scalar2=0.25, op0=ALU.mult, op1=ALU.add)
            for t_ in (sa, ca):
                h.tensor_scalar(out=den[:], in0=t_[:], scalar1=MAGIC, scalar2=None, op0=ALU.add)
                h.tensor_scalar(out=den[:], in0=den[:], scalar1=-MAGIC, scalar2=None, op0=ALU.add)
                r_ = h.tensor_tensor(out=t_[:], in0=t_[:], in1=den[:], op=ALU.subtract)
            return r_
        P.op("dve", c1, reads=[b_lr, b_w], writes=[b_w])
        P.op("act", lambda h: h.activation(out=rr[:], in_=rr[:], func=AF.Exp), reads=[b_w], writes=[b_w])
        P.op("act", lambda h: h.activation(out=sa[:], in_=sa[:], func=AF.Sin, scale=TWO_PI), reads=[b_w], writes=[b_w])
        P.op("act", lambda h: h.activation(out=ca[:], in_=ca[:], func=AF.Sin, scale=TWO_PI), reads=[b_w], writes=[b_w])

        def c2(h):
            h.tensor_tensor(out=ca[:], in0=ca[:], in1=rr[:], op=ALU.mult)
            h.tensor_scalar(out=ca[:], in0=ca[:], scalar1=-1.0, scalar2=None, op0=ALU.add)
            h.tensor_tensor(out=sa[:], in0=sa[:], in1=rr[:], op=ALU.mult)
            h.tensor_tensor(out=den[:], in0=lr[:, 0, :], in1=lr[:, 0, :], op=ALU.mult)
            h.tensor_tensor(out=dtr[:], in0=lr[:, 1, :], in1=lr[:, 1, :], op=ALU.mult)
            h.tensor_tensor(out=den[:], in0=den[:], in1=dtr[:], op=ALU.add)
            h.reciprocal(out=den[:], in_=den[:])
            h.tensor_tensor(out=thr[:], in0=ca[:], in1=lr[:, 0, :], op=ALU.mult)
            h.tensor_tensor(out=dtr[:], in0=sa[:], in1=lr[:, 1, :], op=ALU.mult)
            h.tensor_tensor(out=thr[:], in0=thr[:], in1=dtr[:], op=ALU.add)
            h.tensor_tensor(out=thr[:], in0=thr[:], in1=den[:], op=ALU.mult)
            h.tensor_tensor(out=rr[:], in0=sa[:], in1=lr[:, 0, :], op=ALU.mult)
            h.tensor_tensor(out=dtr[:], in0=ca[:], in1=lr[:, 1, :], op=ALU.mult)
            h.tensor_tensor(out=rr[:], in0=rr[:], in1=dtr[:], op=ALU.subtract)
            return h.tensor_tensor(out=rr[:], in0=rr[:], in1=den[:], op=ALU.mult)
        P.op("dve", c2, reads=[b_lr, b_w], writes=[b_w])

        def c3(h, blk=blk):
            qs = slice(blk * QB, (blk + 1) * QB)
            b0 = bt[0][:].rearrange("p q s -> p (q s)")
            b1 = bt[1][:].rearrange("p q s -> p (q s)")
            o0 = BT[0][:, qs, :].rearrange("p q s -> p (q s)")
            o1 = BT[1][:, qs, :].rearrange("p q s -> p (q s)")
            h.tensor_tensor(out=ca[:], in0=thr[:], in1=b0, op=ALU.mult)
            h.tensor_tensor(out=sa[:], in0=rr[:], in1=b1, op=ALU.mult)
            h.tensor_tensor(out=o0, in0=ca[:], in1=sa[:], op=ALU.subtract)
            h.tensor_tensor(out=ca[:], in0=thr[:], in1=b1, op=ALU.mult)
            h.tensor_tensor(out=sa[:], in0=rr[:], in1=b0, op=ALU.mult)
            return h.tensor_tensor(out=o1, in0=ca[:], in1=sa[:], op=ALU.add)
        P.op("dve", c3, reads=[b_w, b_bt], writes=[b_BT, b_w])
    P.barrier()
    P.release(mk)
    S.update(cosT=cosT, sinT=sinT, Rz=Rz, Kre=Kre, Kim=Kim, BT=BT, CT=CT, b_tab=b_tab, b_BT=b_BT, b_CT=b_CT)
    return S


def s5_scan(K, S, NP, T, u_d, carry_in_d, carry_out_d, yg_d, d_d, full, flag=None, b_flag=None):
    P = K.P
    L = S5_L
    NQ = NP // 4
    NG4 = (NQ + 3) // 4
    NCH = T // L
    mark = P.mark()
    cosT, sinT, Rz, Kre, Kim, BT, CT = (S[k] for k in ("cosT", "sinT", "Rz", "Kre", "Kim", "BT", "CT"))
    b_tab, b_BT, b_CT = S["b_tab"], S["b_BT"], S["b_CT"]
    u_v = u_d.rearrange("(q p) t -> p q t", p=128)
    SC = 4 * L
    uT = [P.sb([128, NQ, SC], BF16, "uT") for _ in range(2)]
    b_u = P.bufs_n(2, "uT")
    rc = P.sb([128, 2, NP], F32, "rc")
    b_rc = P.buf("rc")
    zl = P.sb([128, 2, NP], F32, "zl")
    b_zl = P.bufs_n(NQ, "zl")
    tmpc = [P.sb([128, NP], F32, f"tmpc{i}") for i in range(2)]
    K.dma(rc[:], carry_in_d, [], [b_rc])
    if flag is not None:
        P.op("dve", lambda h: h.tensor_scalar(out=rc[:].rearrange("p a q -> p (a q)"), in0=rc[:].rearrange("p a q -> p (a q)"),
                                              scalar1=flag[:, 0:1], scalar2=None, op0=ALU.mult), reads=[b_flag], writes=[b_rc])
    NB = 2
    v = [[P.sb([128, 4, L], F32, f"v{i}{j}") for j in range(2)] for i in range(NB)]
    z = [[P.sb([128, 4, L], F32, f"z{i}{j}") for j in range(2)] for i in range(NB)]
    b_v = P.bufs_n(NB, "v")
    xo = [[P.sb([128, 4, L], BF16, f"xo{i}{j}") for j in range(2)] for i in range(NB)]
    b_xo = P.bufs_n(NB, "xo")
    b_xr = P.bufs_n(NB, "xr")
    b_zz = P.bufs_n(NB, "zz")
    fl = lambda t: t[:].rearrange("p q l -> p (q l)")
    col = lambda ap: ap.rearrange("p (q o) -> p q o", o=1)
    if full:
        d_s = P.sb([128, NQ], F32, "d_s")
        b_d = P.buf("d")
        K.dma(d_s[:], d_d, [], [b_d])
        du = [P.sb([128, NQ, L], F32, f"du{i}") for i in range(2)]
        b_du = P.bufs_n(2, "du")
        yt = [P.sb([128, 4, L], F32, f"yt{i}") for i in range(2)]
        y2 = [P.sb([128, 4, L], F32, f"y2{i}") for i in range(2)]
        yo = [P.sb([128, 4, L], BF16, f"yo{i}") for i in range(2)]
        b_yt = P.bufs_n(2, "yt")
        b_yo = P.bufs_n(2, "yo")
        b_ygd = P.buf("ygd")
        yg_v = yg_d.rearrange("(q p) t -> p q t", p=128)
    b_ud = P.buf("ud")
    PS_B = [(0, 1), (2, 3)]
    PS_Y = [4, 5, 6, 7]
    yit = [0]
    items = [(c, qd) for c in range(NCH) for qd in range(NQ)]
    usl = {}

    def emit_mmb(i):
        c, qd = items[i]
        sc, cc = divmod(c, 4)
        us = sc % 2
        if cc == 0 and qd == 0:
            K.dma(uT[us][:], u_v[:, :, sc * SC:(sc + 1) * SC], [b_ud], [b_u[us]])
        pr, pi_ = PS_B[i % 2]
        prs = list(range(qd * 4, qd * 4 + 4))
        rhs = uT[us][:, qd, cc * L:(cc + 1) * L]

        def mmb(h, pr=pr, pi_=pi_, prs=prs, rhs=rhs):
            for k, q in enumerate(prs):
                h.matmul(K.ps[pr][:, k * L:(k + 1) * L], lhsT=BT[0][:, q, :], rhs=rhs, start=True, stop=True)
            for k, q in enumerate(prs):
                r_ = h.matmul(K.ps[pi_][:, k * L:(k + 1) * L], lhsT=BT[1][:, q, :], rhs=rhs, start=True, stop=True)
            return r_
        P.op("pe", mmb, reads=[b_BT, b_u[us]], writes=[K.b_ps[pr], K.b_ps[pi_]])

    def emit_rest(i):
        c, qd = items[i]
        sc, cc = divmod(c, 4)
        us = sc % 2
        dui = c % 2
        if full and qd == 0:
            def mkdu(h, dui=dui, us=us, cc=cc):
                for q_ in range(NQ):
                    r_ = h.tensor_scalar(out=du[dui][:, q_, :], in0=uT[us][:, q_, cc * L:(cc + 1) * L], scalar1=d_s[:, q_:q_ + 1],
                                         scalar2=None, op0=ALU.mult)
                return r_
            P.op("pool", mkdu, reads=[b_u[us], b_d], writes=[b_du[dui]])
        vi = i % NB
        pr, pi_ = PS_B[i % 2]
        prs = list(range(qd * 4, qd * 4 + 4))
        vr, vim = v[vi]
        zr, zi = z[vi]
        qs = slice(qd * 4, qd * 4 + 4)
        cs_ = cosT[:, qs, :].rearrange("p q l -> p (q l)")
        sn_ = sinT[:, qs, :].rearrange("p q l -> p (q l)")
        rz_ = Rz[:, qs, :].rearrange("p q l -> p (q l)")

        def rot_in(h):
            h.tensor_tensor(out=fl(zr), in0=K.ps[pr][:], in1=cs_, op=ALU.mult)
            h.tensor_tensor(out=fl(zi), in0=K.ps[pi_][:], in1=sn_, op=ALU.mult)
            h.tensor_tensor(out=fl(vr), in0=fl(zr), in1=fl(zi), op=ALU.add)
            h.tensor_tensor(out=fl(zr), in0=K.ps[pi_][:], in1=cs_, op=ALU.mult)
            h.tensor_tensor(out=fl(zi), in0=K.ps[pr][:], in1=sn_, op=ALU.mult)
            return h.tensor_tensor(out=fl(vim), in0=fl(zr), in1=fl(zi), op=ALU.subtract)
        P.op("dve", rot_in, reads=[K.b_ps[pr], K.b_ps[pi_], b_tab], writes=[b_v[vi], b_zz[vi]])

        def sc1(h):
            h.tensor_tensor(out=vr[:, :, 0:1], in0=vr[:, :, 0:1], in1=col(rc[:, 0, qs]), op=ALU.add)
            return h.tensor_tensor(out=vim[:, :, 0:1], in0=vim[:, :, 0:1], in1=col(rc[:, 1, qs]), op=ALU.add)

        def sc2(h):
            h.tensor_tensor_scan(out=fl(zr), data0=rz_, data1=fl(vr), initial=0.0, op0=ALU.mult, op1=ALU.add)
            return h.tensor_tensor_scan(out=fl(zi), data0=rz_, data1=fl(vim), initial=0.0, op0=ALU.mult, op1=ALU.add)

        def sc3(h):
            h.tensor_copy(out=col(zl[:, 0, qs]), in_=zr[:, :, L - 1:L])
            return h.tensor_copy(out=col(zl[:, 1, qs]), in_=zi[:, :, L - 1:L])
        P.chain("dve", [sc1, sc2, sc3], reads=[b_rc, b_tab], writes=[b_v[vi], b_zz[vi], b_zl[qd]])
        if full:
            xr, xi = xo[vi]


            def rot_re(h):
                h.tensor_tensor(out=fl(vr), in0=fl(zr), in1=cs_, op=ALU.mult)
                h.tensor_tensor(out=fl(vim), in0=fl(zi), in1=sn_, op=ALU.mult)
                return h.tensor_tensor(out=fl(xr), in0=fl(vr), in1=fl(vim), op=ALU.subtract)
            P.op("dve", rot_re, reads=[b_tab, b_zz[vi]], writes=[b_xr[vi], b_v[vi]])

            def rot_im(h):
                h.tensor_tensor(out=fl(vr), in0=fl(zr), in1=sn_, op=ALU.mult)
                h.tensor_tensor(out=fl(vim), in0=fl(zi), in1=cs_, op=ALU.mult)
                return h.tensor_tensor(out=fl(xi), in0=fl(vr), in1=fl(vim), op=ALU.add)
            P.op("dve", rot_im, reads=[b_tab, b_zz[vi]], writes=[b_xo[vi], b_v[vi]])
        if i + 2 < len(items):
            emit_mmb(i + 2)
        if full:
            g4, q4 = divmod(qd, 4)
            py = PS_Y[(c * NG4 + g4) % 4]

            def mmy(h):
                for k, q in enumerate(prs):
                    h.matmul(K.ps[py][:, q4 * L:(q4 + 1) * L], lhsT=CT[0][:, q, :], rhs=xr[:, k, :], start=(k == 0), stop=False)
                for k, q in enumerate(prs):
                    r_ = h.matmul(K.ps[py][:, q4 * L:(q4 + 1) * L], lhsT=CT[1][:, q, :], rhs=xi[:, k, :], start=False, stop=(k == 3))
                return r_
            P.op("pe", mmy, reads=[b_xo[vi], b_xr[vi], b_CT], writes=[K.b_ps[py]])
            if q4 == 3 or qd == NQ - 1:
                nq4 = q4 + 1
                yi = yit[0] % 2
                yit[0] += 1
                W4 = nq4 * L
                P.op("dve", lambda h: h.tensor_tensor(
                    out=yt[yi][:, 0:nq4, :].rearrange("p q l -> p (q l)"), in0=K.ps[py][:, 0:W4],
                    in1=du[dui][:, g4 * 4:g4 * 4 + nq4, :].rearrange("p q l -> p (q l)"), op=ALU.add),
                    reads=[K.b_ps[py], b_du[dui], b_yo[yi]], writes=[b_yt[yi]])
                P.op("act", lambda h: h.activation(out=y2[yi][:, 0:nq4, :], in_=yt[yi][:, 0:nq4, :], func=AF.Square),
                     reads=[b_yt[yi]], writes=[b_yt[yi]])

                def g2(h):
                    h.tensor_scalar(out=y2[yi][:, 0:nq4, :], in0=y2[yi][:, 0:nq4, :], scalar1=0.044715, scalar2=1.0, op0=ALU.mult, op1=ALU.add)
                    return h.tensor_tensor(out=y2[yi][:, 0:nq4, :], in0=y2[yi][:, 0:nq4, :], in1=yt[yi][:, 0:nq4, :], op=ALU.mult)
                P.op("dve", g2, reads=[b_yt[yi]], writes=[b_yt[yi]])
                P.op("act", lambda h: h.activation(out=y2[yi][:, 0:nq4, :], in_=y2[yi][:, 0:nq4, :], func=AF.Sigmoid, scale=1.5957691216),
                     reads=[b_yt[yi]], writes=[b_yt[yi]])
                P.op("dve", lambda h: h.tensor_tensor(out=yo[yi][:, 0:nq4, :], in0=y2[yi][:, 0:nq4, :], in1=yt[yi][:, 0:nq4, :], op=ALU.mult),
                     reads=[b_yt[yi]], writes=[b_yo[yi]])
                K.dma(yg_v[:, g4 * 4:g4 * 4 + nq4, c * L:(c + 1) * L], yo[yi][:, 0:nq4, :], [b_yo[yi]], [b_ygd])

    emit_mmb(0)
    if len(items) > 1:
        emit_mmb(1)
    for c in range(NCH):
        for qd in range(NQ):
            emit_rest(c * NQ + qd)

        def cu1(h):
            h.tensor_tensor(out=tmpc[0][:], in0=zl[:, 0, :], in1=Kre[:], op=ALU.mult)
            return h.tensor_tensor(out=tmpc[1][:], in0=zl[:, 1, :], in1=Kim[:], op=ALU.mult)

        def cu2(h):
            return h.tensor_tensor(out=rc[:, 0, :], in0=tmpc[0][:], in1=tmpc[1][:], op=ALU.subtract)

        def cu3(h):
            h.tensor_tensor(out=tmpc[0][:], in0=zl[:, 0, :], in1=Kim[:], op=ALU.mult)
            return h.tensor_tensor(out=tmpc[1][:], in0=zl[:, 1, :], in1=Kre[:], op=ALU.mult)

        def cu4(h):
            return h.tensor_tensor(out=rc[:, 1, :], in0=tmpc[0][:], in1=tmpc[1][:], op=ALU.add)
        P.chain("dve", [cu1, cu2, cu3, cu4], reads=b_zl + [b_tab], writes=[b_rc])
    b_co = P.buf("carry_out")
    K.dma(carry_out_d, rc[:], [b_rc], [b_co])
    P.barrier()
    P.release(mark)


def alloc_norm_tmp(K, KT, SUB):
    P = K.P
    return dict(xs=[P.sb([128, KT, SUB], F32, "xs") for _ in range(2)], b_xs=P.bufs_n(2, "xs"),
                sq=P.sb([128, KT, SUB], BF16, "sq"), b_sq=P.buf("sq"),
                rstd=P.sb([128, SUB], F32, "rstd"), b_rstd=P.buf("rstd"), n=0, SUB=SUB, KT=KT)


def norm_in(K, tm, x_v, b_x, t0, Tp, g_ap, b_g, hT, b_hT, D, PS_M):
    P = K.P
    SUB, KT = tm["SUB"], tm["KT"]
    for s in range(Tp // SUB):
        xi = tm["n"] % 2
        tm["n"] += 1
        xs, sq, rstd = tm["xs"][xi], tm["sq"], tm["rstd"]
        K.dma(xs[:], x_v[:, :, t0 + s * SUB: t0 + (s + 1) * SUB], [b_x], [tm["b_xs"][xi]])
        P.op("act", lambda h, xs=xs: h.activation(out=sq[:], in_=xs[:], func=AF.Square), reads=[tm["b_xs"][xi]], writes=[tm["b_sq"]])
        K.rstd_from_sq(sq, KT, SUB, PS_M, rstd, tm["b_sq"], tm["b_rstd"], D)

        def nrm(h, xs=xs, s=s):
            for kt in range(KT):
                r = h.scalar_tensor_tensor(out=hT[:, kt, s * SUB:(s + 1) * SUB], in0=xs[:, kt, :],
                                           scalar=g_ap[:, kt:kt + 1], in1=rstd[:], op0=ALU.mult, op1=ALU.mult)
            return r
        P.op("dve", nrm, reads=[tm["b_xs"][xi], tm["b_rstd"], b_g], writes=[b_hT[s]])


def finalize(K, tm, acc, b_acc, x_v, b_x, t0, Tp, gsc, b_gsc, D, PS_M):
    P = K.P
    SUB, KT = tm["SUB"], tm["KT"]
    if "tmp" not in tm and P.sb_cap - P.sb_off >= KT * SUB * 4 + 1024:
        tm["tmp"] = P.sb([128, KT, SUB], F32, "fintmp")
        tm["b_tmp"] = P.buf("fintmp")
    tmp, b_tmp = tm.get("tmp"), tm.get("b_tmp")
    base = tm["n"]

    def xload(s):
        xi_ = (base + s) % 2
        K.dma(tm["xs"][xi_][:], x_v[:, :, t0 + s * SUB: t0 + (s + 1) * SUB], [b_x], [tm["b_xs"][xi_]])
    xload(0)
    for s in range(Tp // SUB):
        nt = (s * SUB) // NT
        accb = [b_acc[m][nt] for m in range(KT)]
        xi = (base + s) % 2
        tm["n"] = base + s + 1
        xs, sq, rstd = tm["xs"][xi], tm["sq"], tm["rstd"]
        if s + 1 < Tp // SUB:
            xload(s + 1)
        P.op("act", lambda h, s=s: h.activation(out=sq[:], in_=acc[:, :, s * SUB:(s + 1) * SUB], func=AF.Square),
             reads=accb, writes=[tm["b_sq"]])
        K.rstd_from_sq(sq, KT, SUB, PS_M, rstd, tm["b_sq"], tm["b_rstd"], D)
        if tmp is not None:
            def fin1(h, s=s):
                for kt in range(KT):
                    r = h.scalar_tensor_tensor(out=tmp[:, kt, :], in0=acc[:, kt, s * SUB:(s + 1) * SUB],
                                               scalar=gsc[:, kt:kt + 1], in1=rstd[:], op0=ALU.mult, op1=ALU.mult)
                return r
            P.op("dve", fin1, reads=accb + [tm["b_rstd"], b_gsc], writes=[b_tmp])
            P.op("pool", lambda h, xs=xs: h.tensor_tensor(out=xs[:], in0=tmp[:], in1=xs[:], op=ALU.add),
                 reads=[b_tmp, tm["b_xs"][xi]], writes=[tm["b_xs"][xi]])
        else:
            def fin1(h, s=s):
                for kt in range(KT):
                    r = h.scalar_tensor_tensor(out=acc[:, kt, s * SUB:(s + 1) * SUB], in0=acc[:, kt, s * SUB:(s + 1) * SUB],
                                               scalar=gsc[:, kt:kt + 1], in1=rstd[:], op0=ALU.mult, op1=ALU.mult)
                return r
            P.op("dve", fin1, reads=accb + [tm["b_rstd"], b_gsc], writes=accb)
            P.op("pool", lambda h, xs=xs, s=s: h.tensor_tensor(out=xs[:], in0=acc[:, :, s * SUB:(s + 1) * SUB], in1=xs[:], op=ALU.add),
                 reads=accb + [tm["b_xs"][xi]], writes=[tm["b_xs"][xi]])
        K.dma(x_v[:, :, t0 + s * SUB: t0 + (s + 1) * SUB], xs[:], [tm["b_xs"][xi]], [b_x])


class WStream:
    def __init__(self, K, KT, MW, name="w"):
        P = K.P
        self.K, self.KT, self.MW = K, KT, MW
        self.w = [P.sb([128, KT, MW], BF16, name) for _ in range(2)]
        self.b = P.bufs_n(2, name)
        self.n = 0

    def load(self, W_v, c0, ncols=None):
        ncols = ncols or self.MW
        sl = self.n % 2
        self.n += 1
        self.K.dma(self.w[sl][:, :, 0:ncols], W_v[:, :, c0:c0 + ncols], [], [self.b[sl]], eng="pool")
        return self.w[sl], self.b[sl]


class Ring:
    def __init__(self, items):
        self.items = items
        self.n = 0

    def next(self):
        r = self.items[self.n % len(self.items)]
        self.n += 1
        return r


def mem_kv_setup(K, mem_d, gm, b_gm, Wkv, D, NM, MEMW):
    P = K.P
    KT = D // 128
    memK = P.sb([128, MEMW // 128, NM], BF16, "memK")
    memV = P.sb([128, NM // 128, MEMW], BF16, "memV")
    b_mk = P.buf("memK")
    b_mv = P.buf("memV")
    mark = P.mark()
    tm = alloc_norm_tmp(K, KT, NM)
    nm = P.sb([128, KT, NM], BF16, "nmem")
    b_nm = [P.buf("nmem")]
    wkv = P.sb([128, KT, 2 * MEMW], BF16, "wkv")
    b_w = P.buf("wkv")
    K.dma(wkv[:], Wkv.rearrange("(kt p) c -> p kt c", p=128), [], [b_w], eng="pool")
    mem_v = mem_d.rearrange("(kt p) t -> p kt t", p=128)
    norm_in(K, tm, mem_v, P.buf("memd"), 0, NM, gm, b_gm, nm, b_nm, D, 6)
    for h in range(MEMW // 128):
        psi = h % 2
        K.mm(psi, [(wkv[:, kt, h * 128:(h + 1) * 128], nm[:, kt, :]) for kt in range(KT)], [b_w] + b_nm, n=NM)
        P.op("act", lambda h_, h=h, psi=psi: h_.activation(out=memK[:, h, :], in_=K.ps[psi][:, 0:NM], func=AF.Copy),
             reads=[K.b_ps[psi]], writes=[b_mk])
    for kt_ in range(NM // 128):
        psi = 2 + kt_ % 2
        K.mm(psi, [(nm[:, kt, kt_ * 128:(kt_ + 1) * 128], wkv[:, kt, MEMW:2 * MEMW]) for kt in range(KT)], [b_w] + b_nm, n=MEMW)
        P.op("dve", lambda h_, kt_=kt_, psi=psi: h_.tensor_copy(out=memV[:, kt_, :], in_=K.ps[psi][:, 0:MEMW]),
             reads=[K.b_ps[psi]], writes=[b_mv])
    P.barrier()
    P.release(mark)
    return dict(memK=memK, memV=memV, b_mk=b_mk, b_mv=b_mv)


def alloc_mem_attn(K, NM):
    P = K.P
    return dict(pt=[P.sb([128, NM // 128, NT], BF16, "mpt") for _ in range(2)], b_pt=P.bufs_n(2, "mpt"),
                rec=[P.sb([128, NT], F32, "mrec") for _ in range(2)], b_rec=P.bufs_n(2, "mrec"),
                mo=[P.sb([128, NT], BF16, "mo") for _ in range(2)], b_mo=P.bufs_n(2, "mo"), n=0)


def mem_attn(K, MA, MKV, qm, b_qm, memo_d, b_md, t0, Tp, NM, MEMW, ps_s, ps_o, ps_r):
    P = K.P
    H = MEMW // 128
    NKT = NM // 128
    scale = 128.0 ** -0.5
    for h in range(H):
        for nt in range(Tp // NT):
            i = MA["n"] % 2
            MA["n"] += 1
            pt, rec, mo = MA["pt"][i], MA["rec"][i], MA["mo"][i]
            for kt in range(NKT):
                psi = ps_s.next()
                K.mm(psi, [(MKV["memK"][:, h, kt * 128:(kt + 1) * 128], qm[:, h, nt * NT:(nt + 1) * NT])], [MKV["b_mk"]] + b_qm)
                P.op("act", lambda h_, psi=psi, pt=pt, kt=kt: h_.activation(out=pt[:, kt, :], in_=K.ps[psi][:], func=AF.Exp, scale=scale),
                     reads=[K.b_ps[psi]], writes=[MA["b_pt"][i]])
            po, pr = ps_o.next(), ps_r.next()
            K.mm(po, [(MKV["memV"][:, kt, h * 128:(h + 1) * 128], pt[:, kt, :]) for kt in range(NKT)], [MKV["b_mv"], MA["b_pt"][i]])
            K.mm(pr, [(K.ones[:], pt[:, kt, :]) for kt in range(NKT)], [K.b_ones, MA["b_pt"][i]])
            P.op("dve", lambda h_, pr=pr, rec=rec: h_.reciprocal(out=rec[:], in_=K.ps[pr][:]), reads=[K.b_ps[pr]], writes=[MA["b_rec"][i]])
            P.op("dve", lambda h_, po=po, rec=rec, mo=mo: h_.tensor_tensor(out=mo[:], in0=K.ps[po][:], in1=rec[:], op=ALU.mult),
                 reads=[K.b_ps[po], MA["b_rec"][i]], writes=[MA["b_mo"][i]])
            K.dma(memo_d[h * 128:(h + 1) * 128, t0 + nt * NT: t0 + (nt + 1) * NT], mo[:], [MA["b_mo"][i]], [b_md])


def mixer_pre_A(K, x_d, g2, b_g, W_in, u_d, memo_d, MKV, D, TOKW, MEMW, NM, T, Tp, SUB=256):
    P = K.P
    KT = D // 128
    mark = P.mark()
    tm = alloc_norm_tmp(K, KT, SUB)
    hT = P.sb([128, KT, Tp], BF16, "hT")
    b_hT = P.bufs_n(Tp // SUB, "hT")
    qm = P.sb([128, MEMW // 128, Tp], BF16, "qm")
    b_qm = P.bufs_n(MEMW // 128, "qm")
    ws = WStream(K, KT, 256, "win")
    st = [P.sb([128, NT], BF16, "stg") for _ in range(3)]
    b_st = P.bufs_n(3, "stg")
    sti = Ring([0, 1, 2])
    MA = alloc_mem_attn(K, NM)
    x_v = x_d.rearrange("(kt p) t -> p kt t", p=128)
    W_v = W_in.rearrange("(kt p) c -> p kt c", p=128)
    b_x, b_ud, b_md = P.buf("x"), P.buf("ud"), P.buf("md")
    ps_mm = Ring([0, 1, 2])
    ps_s, ps_o, ps_r = Ring([0, 1, 2]), Ring([3, 4]), Ring([5, 7])
    MT = (TOKW + MEMW) // 128
    for p in range(T // Tp):
        t0 = p * Tp
        norm_in(K, tm, x_v, b_x, t0, Tp, g2, b_g, hT, b_hT, D, 6)
        for mg in range(MT // 2):
            w, bw = ws.load(W_v, mg * 256)
            for mi in range(2):
                m = mg * 2 + mi
                for nt in range(Tp // NT):
                    psi = ps_mm.next()
                    hb = b_hT[nt * (NT // SUB):(nt + 1) * (NT // SUB)]
                    K.mm(psi, [(w[:, kt, mi * 128:(mi + 1) * 128], hT[:, kt, nt * NT:(nt + 1) * NT]) for kt in range(KT)], [bw] + hb)
                    if m < TOKW // 128:
                        si = sti.next()
                        P.op("act", lambda h_, psi=psi, si=si: h_.activation(out=st[si][:], in_=K.ps[psi][:], func=AF.Copy),
                             reads=[K.b_ps[psi]], writes=[b_st[si]])
                        K.dma(u_d[m * 128:(m + 1) * 128, t0 + nt * NT: t0 + (nt + 1) * NT], st[si][:], [b_st[si]], [b_ud])
                    else:
                        hh = m - TOKW // 128
                        P.op("dve", lambda h_, psi=psi, hh=hh, nt=nt: h_.tensor_copy(out=qm[:, hh, nt * NT:(nt + 1) * NT], in_=K.ps[psi][:]),
                             reads=[K.b_ps[psi]], writes=[b_qm[hh]])
        mem_attn(K, MA, MKV, qm, b_qm, memo_d, b_md, t0, Tp, NM, MEMW, ps_s, ps_o, ps_r)
    P.barrier()
    P.release(mark)


def mixer_post(K, x_d, g3, b_g, W_out, tok_d, memo_d, D, TOKW, MEMW, T, Tp, W_glu=None, bglu=None, SUB=256):
    P = K.P
    KT = D // 128
    NTK, NMK = TOKW // 128, MEMW // 128
    mark = P.mark()
    tm = alloc_norm_tmp(K, KT, SUB)
    acc = P.sb([128, KT, Tp], F32, "acc")
    b_acc = [P.bufs_n(Tp // NT, "acc") for _ in range(KT)]
    tk = P.sb([128, NTK, Tp], BF16, "tk")
    b_tk = P.bufs_n(NTK, "tk")
    mo = P.sb([128, NMK, Tp], BF16, "mo")
    b_mo = P.buf("mo")
    b_x, b_td, b_md = P.buf("x"), P.buf("td"), P.buf("md")
    x_v = x_d.rearrange("(kt p) t -> p kt t", p=128)
    tok_v = tok_d.rearrange("(kt p) t -> p kt t", p=128)
    memo_v = memo_d.rearrange("(kt p) t -> p kt t", p=128)
    Wo_v = W_out.rearrange("(kt p) c -> p kt c", p=128)
    wso = WStream(K, KT, 256, "wout")
    ps_mm = Ring([0, 1, 2, 3])
    if W_glu is not None:
        yg = P.sb([128, NTK, Tp], BF16, "yg")
        b_yg = P.buf("yg")
        wsg = WStream(K, NTK, 256, "wglu")
        Wg_v = W_glu.rearrange("(kt p) c -> p kt c", p=128)
        gt = [P.sb([128, NT], F32, "gt") for _ in range(2)]
        b_gt = P.bufs_n(2, "gt")
        gti = Ring([0, 1])
    def loads(p_):
        t0_ = p_ * Tp
        K.dma(mo[:], memo_v[:, :, t0_:t0_ + Tp], [b_md], [b_mo])
        if W_glu is None:
            K.dma(tk[:], tok_v[:, :, t0_:t0_ + Tp], [b_td], b_tk)
        else:
            K.dma(yg[:], tok_v[:, :, t0_:t0_ + Tp], [b_td], [b_yg])
    loads(0)
    for p in range(T // Tp):
        t0 = p * Tp
        if W_glu is not None:
            for mg in range((NTK + 1) // 2):
                nm_ = min(2, NTK - mg * 2)
                w, bw = wsg.load(Wg_v, mg * 256, nm_ * 128)
                for mi in range(nm_):
                    m = mg * 2 + mi
                    for nt in range(Tp // NT):
                        psi = ps_mm.next()
                        gi = gti.next()
                        K.mm(psi, [(w[:, kt, mi * 128:(mi + 1) * 128], yg[:, kt, nt * NT:(nt + 1) * NT]) for kt in range(NTK)], [bw, b_yg])
                        P.op("act", lambda h_, psi=psi, gi=gi, m=m: h_.activation(out=gt[gi][:], in_=K.ps[psi][:], func=AF.Sigmoid, bias=bglu[:, m:m + 1]),
                             reads=[K.b_ps[psi], b_g], writes=[b_gt[gi]])
                        P.op("dve", lambda h_, gi=gi, m=m, nt=nt: h_.tensor_tensor(out=tk[:, m, nt * NT:(nt + 1) * NT], in0=gt[gi][:],
                                                                                 in1=yg[:, m, nt * NT:(nt + 1) * NT], op=ALU.mult),
                             reads=[b_gt[gi], b_yg], writes=[b_tk[m]])
        for mg in range(KT // 2):
            w, bw = wso.load(Wo_v, mg * 256)
            for mi in range(2):
                m = mg * 2 + mi
                for nt in range(Tp // NT):
                    psi = ps_mm.next()
                    pairs = [(w[:, kt, mi * 128:(mi + 1) * 128], tk[:, kt, nt * NT:(nt + 1) * NT]) for kt in range(NTK)]
                    pairs += [(w[:, NTK + kt, mi * 128:(mi + 1) * 128], mo[:, kt, nt * NT:(nt + 1) * NT]) for kt in range(NMK)]
                    K.mm(psi, pairs, [bw, b_mo] + b_tk)
                    P.op("act", lambda h_, psi=psi, m=m, nt=nt: h_.activation(out=acc[:, m, nt * NT:(nt + 1) * NT], in_=K.ps[psi][:], func=AF.Copy),
                         reads=[K.b_ps[psi]], writes=[b_acc[m][nt]])
        if p + 1 < T // Tp:
            loads(p + 1)
        finalize(K, tm, acc, b_acc, x_v, b_x, t0, Tp, g3, b_g, D, 6)
    P.barrier()
    P.release(mark)


def rope_tables(K, pos_d, invf_d, sgn_d, T):
    P = K.P
    cs = P.sb([64, T], F32, "rope_cs")
    sn = P.sb([64, T], F32, "rope_sn")
    b_r = P.buf("rope")
    mark = P.mark()
    pi_ = P.sb([64, T], I32, "pos_i")
    tmp = P.sb([64, T], F32, "rtmp")
    cf = P.sb([64, 2], F32, "rcf")
    K.dma(pi_[:], pos_d.rearrange("(o t) -> o t", o=1).broadcast_to([64, T]), [], [b_r])
    K.dma(cf[:, 0:1], invf_d, [], [b_r])
    K.dma(cf[:, 1:2], sgn_d, [], [b_r])
    MAGIC = 12582912.0
    steps = [
        lambda h: h.tensor_copy(out=sn[:], in_=pi_[:]),
        lambda h: h.tensor_scalar(out=sn[:], in0=sn[:], scalar1=cf[:, 0:1], scalar2=None, op0=ALU.mult),
        lambda h: h.tensor_scalar(out=cs[:], in0=sn[:], scalar1=0.25, scalar2=None, op0=ALU.add),
        lambda h: h.tensor_scalar(out=tmp[:], in0=sn[:], scalar1=MAGIC, scalar2=None, op0=ALU.add),
        lambda h: h.tensor_scalar(out=tmp[:], in0=tmp[:], scalar1=-MAGIC, scalar2=None, op0=ALU.add),
        lambda h: h.tensor_tensor(out=sn[:], in0=sn[:], in1=tmp[:], op=ALU.subtract),
        lambda h: h.tensor_scalar(out=tmp[:], in0=cs[:], scalar1=MAGIC, scalar2=None, op0=ALU.add),
        lambda h: h.tensor_scalar(out=tmp[:], in0=tmp[:], scalar1=-MAGIC, scalar2=None, op0=ALU.add),
        lambda h: h.tensor_tensor(out=cs[:], in0=cs[:], in1=tmp[:], op=ALU.subtract),
    ]
    P.chain("dve", steps, reads=[b_r], writes=[b_r])
    P.op("act", lambda h: h.activation(out=sn[:], in_=sn[:], func=AF.Sin, scale=2 * float(np.pi)), reads=[b_r], writes=[b_r])
    P.op("act", lambda h: h.activation(out=cs[:], in_=cs[:], func=AF.Sin, scale=2 * float(np.pi)), reads=[b_r], writes=[b_r])
    P.op("dve", lambda h: h.tensor_scalar(out=sn[:], in0=sn[:], scalar1=cf[:, 1:2], scalar2=None, op0=ALU.mult), reads=[b_r], writes=[b_r])
    P.barrier()
    P.release(mark)
    return dict(cs=cs, sn=sn, b=b_r)


def apply_rope(K, RT, psa, psb, out, t0, n, tmp, b_tmp, b_out):
    P = K.P
    P.op("dve", lambda h: h.tensor_tensor(out=tmp[0][0:64, 0:n], in0=K.ps[psa][0:64, 0:n], in1=RT["cs"][:, t0:t0 + n], op=ALU.mult),
         reads=[K.b_ps[psa], RT["b"]], writes=[b_tmp[0]])
    P.op("dve", lambda h: h.tensor_tensor(out=tmp[1][0:64, 0:n], in0=K.ps[psb][0:64, 0:n], in1=RT["sn"][:, t0:t0 + n], op=ALU.mult),
         reads=[K.b_ps[psb], RT["b"]], writes=[b_tmp[1]])
    P.op("pool", lambda h: h.tensor_tensor(out=out, in0=tmp[0][0:64, 0:n], in1=tmp[1][0:64, 0:n], op=ALU.add),
         reads=[b_tmp[0], b_tmp[1]], writes=[b_out])


def sub_rmsnorm(K, src, b_src, dst, b_dst, g_ap, b_g, nk, Tp, sq, b_sq, rstd, b_rstd, PS_M):
    P = K.P
    for nt in range(Tp // NT):
        sl = slice(nt * NT, (nt + 1) * NT)
        P.op("act", lambda h, sl=sl: h.activation(out=sq[:, :, :], in_=src[:, :, sl], func=AF.Square), reads=b_src, writes=[b_sq])
        K.rstd_from_sq(sq, nk, NT, PS_M, rstd, b_sq, b_rstd, nk * 128)

        def f(h, sl=sl):
            for kt in range(nk):
                r = h.scalar_tensor_tensor(out=dst[:, kt, sl], in0=src[:, kt, sl], scalar=g_ap[:, kt:kt + 1], in1=rstd[:],
                                           op0=ALU.mult, op1=ALU.mult)
            return r
        P.op("dve", f, reads=b_src + [b_rstd, b_g], writes=[b_dst[nt]])


def kv_stage(K, x_d, gkv_in, gkv, b_g, W_dkv, W_kr, W_uk, W_uv, RT, kn_d, kr_d, v_d, D, R, H, T, Tp, SUB=256):
    P = K.P
    KT = D // 128
    RK = R // 128
    mark = P.mark()
    tm = alloc_norm_tmp(K, KT, SUB)
    hT = P.sb([128, KT, Tp], BF16, "hT")
    b_hT = P.bufs_n(Tp // SUB, "hT")
    ck = P.sb([128, RK, Tp], F32, "ck")
    b_ck = P.bufs_n(1, "ck")
    ckn = P.sb([128, RK, Tp], BF16, "ckn")
    b_ckn = P.bufs_n(Tp // NT, "ckn")
    sq = P.sb([128, RK, NT], BF16, "sq2")
    rstd = P.sb([128, NT], F32, "rstd2")
    b_sq, b_rstd = P.buf("sq2"), P.buf("rstd2")
    wd = P.sb([128, KT, R], BF16, "wdkv")
    wkr = P.sb([128, KT, 128], BF16, "wkr")
    wuk = P.sb([128, RK, H * 128], BF16, "wuk")
    wuv = P.sb([128, RK, H * 128], BF16, "wuv")
    b_w = P.buf("kvw")
    K.dma(wd[:], W_dkv.rearrange("(kt p) c -> p kt c", p=128), [], [b_w], eng="pool")
    wkr_v = W_kr.rearrange("(kt p) c -> p kt c", p=128)
    K.dma(wkr[:, :, 0:64], wkr_v, [], [b_w], eng="pool")
    K.dma(wkr[:, :, 64:96], wkr_v[:, :, 32:64], [], [b_w], eng="pool")
    K.dma(wkr[:, :, 96:128], wkr_v[:, :, 0:32], [], [b_w], eng="pool")
    K.dma(wuk[:], W_uk.rearrange("(kt p) c -> p kt c", p=128), [], [b_w], eng="pool")
    K.dma(wuv[:], W_uv.rearrange("(kt p) c -> p kt c", p=128), [], [b_w], eng="pool")
    st = [P.sb([128, NT], BF16, "stg") for _ in range(3)]
    b_st = P.bufs_n(3, "stg")
    sti = Ring([0, 1, 2])
    rtmp = [P.sb([128, NT], F32, "rtmp") for _ in range(2)]
    b_rtmp = P.bufs_n(2, "rtmp")
    x_v = x_d.rearrange("(kt p) t -> p kt t", p=128)
    b_x, b_kn, b_kr, b_v = P.buf("x"), P.buf("kn"), P.buf("kr"), P.buf("v")
    ps_mm = Ring([0, 1, 2, 3])
    for p in range(T // Tp):
        t0 = p * Tp
        norm_in(K, tm, x_v, b_x, t0, Tp, gkv_in, b_g, hT, b_hT, D, 6)
        for nt in range(Tp // NT):
            hb = b_hT[nt * (NT // SUB):(nt + 1) * (NT // SUB)]
            sl = slice(nt * NT, (nt + 1) * NT)
            for m in range(RK):
                psi = ps_mm.next()
                K.mm(psi, [(wd[:, kt, m * 128:(m + 1) * 128], hT[:, kt, sl]) for kt in range(KT)], [b_w] + hb)
                P.op("act", lambda h_, psi=psi, m=m, sl=sl: h_.activation(out=ck[:, m, sl], in_=K.ps[psi][:], func=AF.Copy),
                     reads=[K.b_ps[psi]], writes=b_ck)
            pa, pb = ps_mm.next(), ps_mm.next()
            K.mm(pa, [(wkr[:, kt, 0:64], hT[:, kt, sl]) for kt in range(KT)], [b_w] + hb, m=64)
            K.mm(pb, [(wkr[:, kt, 64:128], hT[:, kt, sl]) for kt in range(KT)], [b_w] + hb, m=64)
            si = sti.next()
            apply_rope(K, RT, pa, pb, st[si][0:64, :], t0 + nt * NT, NT, rtmp, b_rtmp, b_st[si])
            K.dma(kr_d[:, t0 + nt * NT: t0 + (nt + 1) * NT], st[si][0:64, :], [b_st[si]], [b_kr])
        sub_rmsnorm(K, ck, b_ck, ckn, b_ckn, gkv, b_g, RK, Tp, sq, b_sq, rstd, b_rstd, 6)
        for nt in range(Tp // NT):
            sl = slice(nt * NT, (nt + 1) * NT)
            for hh in range(H):
                psi = ps_mm.next()
                K.mm(psi, [(wuk[:, kt, hh * 128:(hh + 1) * 128], ckn[:, kt, sl]) for kt in range(RK)], [b_w, b_ckn[nt]])
                si = sti.next()
                P.op("act", lambda h_, psi=psi, si=si: h_.activation(out=st[si][:], in_=K.ps[psi][:], func=AF.Copy),
                     reads=[K.b_ps[psi]], writes=[b_st[si]])
                K.dma(kn_d[hh, :, t0 + nt * NT: t0 + (nt + 1) * NT], st[si][:], [b_st[si]], [b_kn])
            for tt in range(NT // 128):
                tsl = slice(nt * NT + tt * 128, nt * NT + (tt + 1) * 128)
                CW = min(NT, H * 128)
                for cc in range(H * 128 // CW):
                    psi = ps_mm.next()
                    K.mm(psi, [(ckn[:, kt, tsl], wuv[:, kt, cc * CW:(cc + 1) * CW]) for kt in range(RK)], [b_w, b_ckn[nt]], n=CW)
                    si = sti.next()
                    P.op("dve", lambda h_, psi=psi, si=si, CW=CW: h_.tensor_copy(out=st[si][:, 0:CW], in_=K.ps[psi][:, 0:CW]),
                         reads=[K.b_ps[psi]], writes=[b_st[si]])
                    K.dma(v_d[t0 + nt * NT + tt * 128: t0 + nt * NT + (tt + 1) * 128, cc * CW:(cc + 1) * CW], st[si][:, 0:CW], [b_st[si]], [b_v])
    P.barrier()
    P.release(mark)


def mixer_pre_B(K, x_d, g2, gq, b_g, W_in, W_uq, RT, qn_d, qr_d, memo_d, MKV, D, R, H, MEMW, NM, T, Tp, SUB=256):
    P = K.P
    KT = D // 128
    RK = R // 128
    mark = P.mark()
    tm = alloc_norm_tmp(K, KT, SUB)
    hT = P.sb([128, KT, Tp], BF16, "hT")
    b_hT = P.bufs_n(Tp // SUB, "hT")
    cq = P.sb([128, RK, Tp], F32, "cq")
    b_cq = P.bufs_n(1, "cq")
    cqn = P.sb([128, RK, Tp], BF16, "cqn")
    b_cqn = P.bufs_n(Tp // NT, "cqn")
    sq = P.sb([128, RK, NT], BF16, "sq2")
    rstd = P.sb([128, NT], F32, "rstd2")
    b_sq, b_rstd = P.buf("sq2"), P.buf("rstd2")
    qm = P.sb([128, MEMW // 128, Tp], BF16, "qm")
    b_qm = P.bufs_n(MEMW // 128, "qm")
    ws = WStream(K, KT, 256, "win")
    HD = 192
    wuq = P.sb([128, RK, H, HD + 64], BF16, "wuq")
    b_wq = P.buf("wuq")
    wq_v = W_uq.rearrange("(kt p) (h e) -> p kt h e", p=128, e=HD)
    for kt in range(RK):
        K.dma(wuq[:, kt, :, 0:HD], wq_v[:, kt, :, :], [], [b_wq], eng="pool")
        K.dma(wuq[:, kt, :, HD:HD + 32], wq_v[:, kt, :, 160:192], [], [b_wq], eng="pool")
        K.dma(wuq[:, kt, :, HD + 32:HD + 64], wq_v[:, kt, :, 128:160], [], [b_wq], eng="pool")
    st = [P.sb([128, NT], BF16, "stg") for _ in range(3)]
    b_st = P.bufs_n(3, "stg")
    sti = Ring([0, 1, 2])
    rtmp = [P.sb([128, NT], F32, "rtmp") for _ in range(2)]
    b_rtmp = P.bufs_n(2, "rtmp")
    MA = alloc_mem_attn(K, NM)
    x_v = x_d.rearrange("(kt p) t -> p kt t", p=128)
    W_v = W_in.rearrange("(kt p) c -> p kt c", p=128)
    b_x, b_qn, b_qr, b_md = P.buf("x"), P.buf("qn"), P.buf("qr"), P.buf("md")
    ps_mm = Ring([0, 1, 2])
    ps_s, ps_o, ps_r = Ring([0, 1, 2]), Ring([3, 4]), Ring([5, 7])
    MT = (R + MEMW) // 128
    for p in range(T // Tp):
        t0 = p * Tp
        norm_in(K, tm, x_v, b_x, t0, Tp, g2, b_g, hT, b_hT, D, 6)
        for mg in range(MT // 2):
            w, bw = ws.load(W_v, mg * 256)
            for mi in range(2):
                m = mg * 2 + mi
                for nt in range(Tp // NT):
                    psi = ps_mm.next()
                    hb = b_hT[nt * (NT // SUB):(nt + 1) * (NT // SUB)]
                    sl = slice(nt * NT, (nt + 1) * NT)
                    K.mm(psi, [(w[:, kt, mi * 128:(mi + 1) * 128], hT[:, kt, sl]) for kt in range(KT)], [bw] + hb)
                    if m < RK:
                        P.op("act", lambda h_, psi=psi, m=m, sl=sl: h_.activation(out=cq[:, m, sl], in_=K.ps[psi][:], func=AF.Copy),
                             reads=[K.b_ps[psi]], writes=b_cq)
                    else:
                        hh = m - RK
                        P.op("dve", lambda h_, psi=psi, hh=hh, sl=sl: h_.tensor_copy(out=qm[:, hh, sl], in_=K.ps[psi][:]),
                             reads=[K.b_ps[psi]], writes=[b_qm[hh]])
        mem_attn(K, MA, MKV, qm, b_qm, memo_d, b_md, t0, Tp, NM, MEMW, ps_s, ps_o, ps_r)
        sub_rmsnorm(K, cq, b_cq, cqn, b_cqn, gq, b_g, RK, Tp, sq, b_sq, rstd, b_rstd, 6)
        for nt in range(Tp // NT):
            sl = slice(nt * NT, (nt + 1) * NT)
            for hh in range(H):
                psi = ps_mm.next()
                K.mm(psi, [(wuq[:, kt, hh, 0:128], cqn[:, kt, sl]) for kt in range(RK)], [b_wq, b_cqn[nt]])
                si = sti.next()
                P.op("act", lambda h_, psi=psi, si=si: h_.activation(out=st[si][:], in_=K.ps[psi][:], func=AF.Copy),
                     reads=[K.b_ps[psi]], writes=[b_st[si]])
                K.dma(qn_d[hh, :, t0 + nt * NT: t0 + (nt + 1) * NT], st[si][:], [b_st[si]], [b_qn])
                pa, pb = ps_mm.next(), ps_mm.next()
                K.mm(pa, [(wuq[:, kt, hh, 128:192], cqn[:, kt, sl]) for kt in range(RK)], [b_wq, b_cqn[nt]], m=64)
                K.mm(pb, [(wuq[:, kt, hh, 192:256], cqn[:, kt, sl]) for kt in range(RK)], [b_wq, b_cqn[nt]], m=64)
                si = sti.next()
                apply_rope(K, RT, pa, pb, st[si][0:64, :], t0 + nt * NT, NT, rtmp, b_rtmp, b_st[si])
                K.dma(qr_d[hh, :, t0 + nt * NT: t0 + (nt + 1) * NT], st[si][0:64, :], [b_st[si]], [b_qr])
    P.barrier()
    P.release(mark)


def mla_attn(K, qn_d, qr_d, kn_d, kr_d, v_d, knp_d, krp_d, vp_d, pbias_d, mask_d, tok_d, H, T):
    P = K.P
    mark = P.mark()
    NKT = T // 128
    QB = T // NT
    scale = 192.0 ** -0.5
    kr = P.sb([64, 2 * T], BF16, "kr")
    b_krs = P.buf("kr")
    K.dma(kr[:, 0:T], krp_d, [], [b_krs])
    K.dma(kr[:, T:2 * T], kr_d, [], [b_krs])
    pb = P.sb([128, 1], F32, "pbias")
    b_pb = P.buf("pbias")
    K.dma(pb[:], pbias_d, [], [b_pb])
    msk = P.sb([128, 4, NT], BF16, "mask")
    b_msk = P.buf("mask")
    K.dma(msk[:], mask_d.rearrange("i p q -> p i q"), [], [b_msk], eng="pool")
    kn = [P.sb([128, 2 * T], BF16, "kn") for _ in range(2)]
    vv = [P.sb([128, 2 * NKT, 128], BF16, "vv") for _ in range(2)]
    qn = [P.sb([128, T], BF16, "qn") for _ in range(2)]
    qr = [P.sb([64, T], BF16, "qr") for _ in range(2)]
    b_hd = P.bufs_n(2, "headin")
    pt = [P.sb([128, NT], BF16, "pt") for _ in range(3)]
    b_pt = P.bufs_n(3, "pt")
    pti = Ring([0, 1, 2])
    rec = [P.sb([128, NT], F32, "rec") for _ in range(2)]
    b_rec = P.bufs_n(2, "rec")
    ob = [P.sb([128, NT], BF16, "ob") for _ in range(2)]
    b_ob = P.bufs_n(2, "ob")
    b_td = P.buf("tokd")
    ps_s, ps_o, ps_r = Ring([0, 1, 2]), Ring([3, 4]), Ring([5, 6])
    fin = 0
    def load_head(h):
        s_ = h % 2
        K.dma(kn[s_][:, 0:T], knp_d[h], [], [b_hd[s_]])
        K.dma(kn[s_][:, T:2 * T], kn_d[h], [], [b_hd[s_]])
        K.dma(vv[s_][:, 0:NKT, :], vp_d[:, h * 128:(h + 1) * 128].rearrange("(t p) d -> p t d", p=128), [], [b_hd[s_]])
        K.dma(vv[s_][:, NKT:2 * NKT, :], v_d[:, h * 128:(h + 1) * 128].rearrange("(t p) d -> p t d", p=128), [], [b_hd[s_]])
        K.dma(qn[s_][:], qn_d[h], [], [b_hd[s_]])
        K.dma(qr[s_][:], qr_d[h], [], [b_hd[s_]])
    load_head(0)
    for h in range(H):
        s_ = h % 2
        if h + 1 < H:
            load_head(h + 1)
        for qb in range(QB):
            qsl = slice(qb * NT, (qb + 1) * NT)
            tiles = list(range(NKT)) + [NKT + j for j in range(4 * qb + 4)]
            po, pr = ps_o.next(), ps_r.next()
            n = len(tiles)

            def score(kt):
                psi = ps_s.next()
                ksl = slice(kt * 128, (kt + 1) * 128)
                K.mm(psi, [(kn[s_][:, ksl], qn[s_][:, qsl]), (kr[0:64, ksl], qr[s_][0:64, qsl])], [b_hd[s_], b_krs])
                return psi
            pend = [score(tiles[0])]
            if n > 1:
                pend.append(score(tiles[1]))
            for i, kt in enumerate(tiles):
                psi = pend.pop(0)
                if i + 2 < n:
                    pend.append(score(tiles[i + 2]))
                pi_ = pti.next()
                prev = kt < NKT
                if prev:
                    P.op("act", lambda h_, psi=psi, pi_=pi_: h_.activation(out=pt[pi_][:], in_=K.ps[psi][:], func=AF.Exp, scale=scale, bias=pb[:, 0:1]),
                         reads=[K.b_ps[psi], b_pb], writes=[b_pt[pi_]])
                else:
                    P.op("act", lambda h_, psi=psi, pi_=pi_: h_.activation(out=pt[pi_][:], in_=K.ps[psi][:], func=AF.Exp, scale=scale),
                         reads=[K.b_ps[psi]], writes=[b_pt[pi_]])
                    di = kt - NKT - 4 * qb
                    if di >= 0:
                        P.op("pool", lambda h_, pi_=pi_, di=di: h_.tensor_tensor(out=pt[pi_][:], in0=pt[pi_][:], in1=msk[:, di, :], op=ALU.mult),
                             reads=[b_msk], writes=[b_pt[pi_]])
                ptap = pt[pi_][:]

                def mo(h_, po=po, pr=pr, kt=kt, ptap=ptap, i=i, n=n, s_=s_):
                    h_.matmul(K.ps[po][:], lhsT=vv[s_][:, kt, :], rhs=ptap, start=(i == 0), stop=(i == n - 1))
                    return h_.matmul(K.ps[pr][:], lhsT=K.ones[:], rhs=ptap, start=(i == 0), stop=(i == n - 1))
                P.op("pe", mo, reads=[b_pt[pi_], b_hd[s_], K.b_ones], writes=[K.b_ps[po], K.b_ps[pr]])
            fi = fin % 2
            fin += 1
            P.op("dve", lambda h_, pr=pr, fi=fi: h_.reciprocal(out=rec[fi][:], in_=K.ps[pr][:]), reads=[K.b_ps[pr]], writes=[b_rec[fi]])
            P.op("dve", lambda h_, po=po, fi=fi: h_.tensor_tensor(out=ob[fi][:], in0=K.ps[po][:], in1=rec[fi][:], op=ALU.mult),
                 reads=[K.b_ps[po], b_rec[fi]], writes=[b_ob[fi]])
            K.dma(tok_d[h * 128:(h + 1) * 128, qsl], ob[fi][:], [b_ob[fi]], [b_td])
    P.barrier()
    P.release(mark)


class Cfg:
    def __init__(self, D=2048, DFF=5632, TOKW=1536, MEMW=512, NM=256, G=96, R=512, H=12, SEQ=4096, B=4, L=4, Tp=1024):
        self.D, self.DFF, self.TOKW, self.MEMW, self.NM, self.G, self.R, self.H = D, DFF, TOKW, MEMW, NM, G, R, H
        self.SEQ, self.B, self.L, self.Tp = SEQ, B, L, Tp
        self.T = SEQ // 2
        self.KT = D // 128
        self.NA = L // 2
        self.NP = G // 2
        self.NQ = G // 8
        self.RK = R // 128
        self.NTK = TOKW // 128


def gain_layout(cfg, norms, mem_norm, kv_in_norm, kv_norm, mla_q_norm, s5_b_glu):
    cols, off = [], {}

    def add(name, v):
        v = np.asarray(v, np.float32)
        n = v.shape[-1] // 128
        a = v.reshape(-1, n, 128)
        a = np.transpose(a, (2, 0, 1)).reshape(128, -1)
        off[name] = (sum(c.shape[1] for c in cols), n)
        cols.append(a)
    add("norms", norms)
    add("mem_norm", mem_norm)
    add("kv_in", kv_in_norm)
    add("kv", kv_norm)
    add("q", mla_q_norm)
    add("bglu", s5_b_glu)
    return np.ascontiguousarray(np.concatenate(cols, 1)), off


def build_segment(cfg, seg, W):
    c = cfg
    nc = bass.Bass("TRN2", target_bir_lowering=False)
    D, T, Tp = c.D, c.T, c.Tp

    def din(name, shape, dt=F32):
        return nc.dram_tensor(name, list(shape), dt, kind="ExternalInput").ap()

    def dout(name, shape, dt=F32):
        return nc.dram_tensor(name, list(shape), dt, kind="ExternalOutput").ap()

    def dtmp(name, shape, dt=F32):
        return nc.dram_tensor(name, list(shape), dt, kind="Internal").ap()
    K = KB(nc)
    P = K.P
    gshape = W["gains"].shape
    gains_d = din("gains", gshape)
    goff = W["goff"]
    gains = P.sb(list(gshape), F32, "gains")
    b_g = P.buf("gains")
    K.dma(gains[:], gains_d, [], [b_g])

    def gn(l, i):
        o = goff["norms"][0] + (l * 6 + i) * c.KT
        return gains[:, o:o + c.KT]

    def gsl(name, idx, n):
        o = goff[name][0] + idx * n
        return gains[:, o:o + n]
    x_in = din("x_in", [D, T])
    x = dout("x_out", [D, T])
    b_xc = P.buf("xcopy")
    K.dma(x, x_in, [], [b_xc])
    P.barrier()
    wts = {}

    def wt(name, l=None, j=None):
        key = (name, l, j)
        if key not in wts:
            a = W[name]
            shp = a.shape
            if l is not None:
                shp = shp[1:]
            if j is not None:
                shp = shp[1:]
            nm_ = f"{name}_{l}_{j}".replace("None", "x")
            wts[key] = (nm_, din(nm_, shp))
        return wts[key][1]

    def ffn(l, i):
        ffn_stage(K, x, gn(l, 2 * i if i == 0 else 4), gn(l, 1 if i == 0 else 5), b_g,
                  wt("ffn_w_gate", l, i), wt("ffn_w_up", l, i), wt("ffn_w_down", l, i), D, c.DFF, T, Tp)

    def s5prm(l):
        return dict(lam_s=din(f"s5lam_s{l}", [128, 3, c.NP]), lam_r=din(f"s5lam_r{l}", [128, 3, c.NP * 128]),
                    bT=din(f"s5bT{l}", [2, 128, c.NP, 128]), cP=din(f"s5cP{l}", [2, 128, c.NP, 128]))
    mem_d = din("memT", [D, c.NM])

    def pre_A(l, u_d, memo_d):
        mk = P.mark()
        MKV = mem_kv_setup(K, mem_d, gsl("mem_norm", l, c.KT), b_g, wt("mem_w_kv", l), D, c.NM, c.MEMW)
        mixer_pre_A(K, x, gn(l, 2), b_g, wt("a_w_in", l), u_d, memo_d, MKV, D, c.TOKW, c.MEMW, c.NM, T, Tp)
        P.release(mk)

    def post_A(l, yg_d, memo_d):
        mixer_post(K, x, gn(l, 3), b_g, wt("w_out", l), yg_d, memo_d, D, c.TOKW, c.MEMW, T, Tp,
                   W_glu=wt("s5_w_glu", l), bglu=gsl("bglu", l, c.NTK))

    if seg in (1, 2, 3):
        l_scan1 = {1: 0, 2: 1, 3: None}[seg]
        l_scan2 = {1: None, 2: 0, 3: 1}[seg]
        if l_scan2 is not None:
            l = l_scan2
            u_d = din("u_in", [c.TOKW, T], BF16)
            memo_d = din("memo_in", [c.MEMW, T], BF16)
            carry_in = din("carry_in", [128, 2, c.NP])
            carry_dummy = dtmp("carry_dummy", [128, 2, c.NP])
            yg_d = dtmp("yg", [c.TOKW, T], BF16)
            mk = P.mark()
            S = s5_setup(K, s5prm(l), c.NP)
            s5_scan(K, S, c.NP, T, u_d, carry_in, carry_dummy, yg_d, din(f"s5d{l}", [128, c.NQ]), True)
            P.release(mk)
            post_A(l, yg_d, memo_d)
            ffn(l, 1)
        if l_scan1 is not None:
            l = l_scan1
            ffn(l, 0)
            u_o = dout("u_out", [c.TOKW, T], BF16)
            memo_o = dout("memo_out", [c.MEMW, T], BF16)
            carry_o = dout("carry_out", [128, 2, c.NP])
            zero_c = din("zero_carry", [128, 2, c.NP])
            pre_A(l, u_o, memo_o)
            mk = P.mark()
            S = s5_setup(K, s5prm(l), c.NP)
            s5_scan(K, S, c.NP, T, u_o, zero_c, carry_o, None, None, False)
            P.release(mk)
        if seg == 3:
            RT = rope_tables(K, din("pos", [T], I32), din("invf", [64, 1]), din("sgn", [64, 1]), T)
            kv_stage(K, x, gsl("kv_in", 0, c.KT), gsl("kv", 0, c.RK), b_g, wt("w_dkv"), wt("w_kr"), wt("w_uk"), wt("w_uv"), RT,
                     dout("kn_out", [c.H, 128, T], BF16), dout("kr_out", [64, T], BF16), dout("v_out", [T, c.H * 128], BF16),
                     D, c.R, c.H, T, Tp)
    else:
        RT = rope_tables(K, din("pos", [T], I32), din("invf", [64, 1]), din("sgn", [64, 1]), T)
        kn_d, kr_d, v_d = din("kn", [c.H, 128, T], BF16), din("kr", [64, T], BF16), din("v", [T, c.H * 128], BF16)
        knp_d, krp_d, vp_d = din("knp", [c.H, 128, T], BF16), din("krp", [64, T], BF16), din("vp", [T, c.H * 128], BF16)
        pbias_d = din("pbias", [128, 1])
        mask_d = din("cmask", [4, 128, NT])
        qn_d = dtmp("qn", [c.H, 128, T], BF16)
        qr_d = dtmp("qr", [c.H, 64, T], BF16)
        memo_d = dtmp("memo", [c.MEMW, T], BF16)
        tok_d = dtmp("tok", [c.TOKW, T], BF16)
        for l in range(c.NA, c.L):
            j = l - c.NA
            ffn(l, 0)
            mk = P.mark()
            MKV = mem_kv_setup(K, mem_d, gsl("mem_norm", l, c.KT), b_g, wt("mem_w_kv", l), D, c.NM, c.MEMW)
            mixer_pre_B(K, x, gn(l, 2), gsl("q", j, c.RK), b_g, wt("b_w_in", j), wt("mla_w_uq", j), RT, qn_d, qr_d, memo_d, MKV,
                        D, c.R, c.H, c.MEMW, c.NM, T, Tp)
            P.release(mk)
            mla_attn(K, qn_d, qr_d, kn_d, kr_d, v_d, knp_d, krp_d, vp_d, pbias_d, mask_d, tok_d, c.H, T)
            mixer_post(K, x, gn(l, 3), b_g, wt("w_out", l), tok_d, memo_d, D, c.TOKW, c.MEMW, T, Tp)
            ffn(l, 1)
    P.barrier()
    P.emit()
    return nc, wts


def run_model(cfg, inp, dbg=None):
    c = cfg
    T = c.T
    NCORE = 2 * c.B
    f32 = np.float32
    gains, goff = gain_layout(c, inp["norms"], inp["mem_norm"], inp["kv_in_norm"], inp["kv_norm"], inp["mla_q_norm"], inp["s5_b_glu"])
    W = dict(inp)
    W["gains"], W["goff"] = gains, goff
    s5l = [s5_host_layout(*(np.asarray(inp[k][l], f32) for k in ("s5_lambda_re", "s5_lambda_im", "s5_log_dt", "s5_b_re", "s5_b_im",
                                                                 "s5_c_re", "s5_c_im", "s5_d"))) for l in range(c.NA)]
    xT = [np.ascontiguousarray(np.asarray(inp["x"][cid // 2, (cid % 2) * T:(cid % 2 + 1) * T, :], f32).T) for cid in range(NCORE)]
    memT = [np.ascontiguousarray(np.asarray(inp["mem"][b], f32).T) for b in range(c.B)]
    inv_freq = (10000.0 ** (-np.arange(0, 64, 2, dtype=np.float32) / 64)).astype(f32)
    invf = np.concatenate([inv_freq, inv_freq])[:, None].astype(f32) / f32(2 * np.pi)
    sgn = np.concatenate([-np.ones(32, f32), np.ones(32, f32)])[:, None]
    kk, qq = np.arange(128)[:, None], np.arange(NT)[None, :]
    cmask = np.stack([(qq >= 128 * i + kk).astype(f32) for i in range(4)], 0)
    zero_carry = np.zeros((128, 2, c.NP), f32)

    def launch(seg, per_core):
        nc, wts = build_segment(c, seg, W)
        shared = {"gains": gains}
        for (name, l, j), (nm_, ap) in wts.items():
            a = inp[name]
            if l is not None:
                a = a[l]
            if j is not None:
                a = a[j]
            shared[nm_] = np.ascontiguousarray(np.asarray(a, f32))
        maps = []
        for cid in range(NCORE):
            m = dict(shared)
            m["memT"] = memT[cid // 2]
            m.update(per_core[cid])
            maps.append(m)
        res = run_bass_kernel_spmd(nc, maps, core_ids=list(range(NCORE)))
        if dbg is not None:
            dbg[seg] = res.results
        return res.results

    def s5in(l):
        return {f"s5lam_s{l}": s5l[l]["lam_s"], f"s5lam_r{l}": s5l[l]["lam_r"], f"s5bT{l}": s5l[l]["bT"], f"s5cP{l}": s5l[l]["cP"]}
    pos = [np.ascontiguousarray(np.asarray(inp["positions"][cid // 2, (cid % 2) * T:(cid % 2 + 1) * T], np.int32)) for cid in range(NCORE)]
    r = launch(1, [dict(x_in=xT[cid], zero_carry=zero_carry, **s5in(0)) for cid in range(NCORE)])
    for seg in (2, 3):
        l2 = seg - 2
        pc = []
        for cid in range(NCORE):
            cin = r[cid - 1]["carry_out"] if cid % 2 == 1 else zero_carry
            d = dict(x_in=r[cid]["x_out"], u_in=r[cid]["u_out"], memo_in=r[cid]["memo_out"], carry_in=cin,
                     zero_carry=zero_carry, **s5in(l2))
            d[f"s5d{l2}"] = s5l[l2]["d_s"]
            if seg == 2:
                d.update(s5in(1))
            else:
                d.update(pos=pos[cid], invf=invf, sgn=sgn)
            pc.append(d)
        r = launch(seg, pc)
    pc = []
    for cid in range(NCORE):
        prev = r[cid - 1] if cid % 2 == 1 else r[cid]
        pc.append(dict(x_in=r[cid]["x_out"], kn=r[cid]["kn_out"], kr=r[cid]["kr_out"], v=r[cid]["v_out"],
                       knp=prev["kn_out"], krp=prev["kr_out"], vp=prev["v_out"],
                       pbias=np.full((128, 1), 0.0 if cid % 2 == 1 else -30000.0, f32), cmask=cmask,
                       pos=pos[cid], invf=invf, sgn=sgn))
    r = launch(4, pc)
    out = np.empty((c.B, c.SEQ, c.D), f32)
    for cid in range(NCORE):
        out[cid // 2, (cid % 2) * T:(cid % 2 + 1) * T, :] = r[cid]["x_out"].T
    return out


def build_fused(cfg, W):
    c = cfg
    nc = bass.Bass("TRN2", target_bir_lowering=False)
    D, T, Tp = c.D, c.T, c.Tp

    def din(name, shape, dt=F32):
        return nc.dram_tensor(name, list(shape), dt, kind="ExternalInput").ap()

    def dout(name, shape, dt=F32):
        return nc.dram_tensor(name, list(shape), dt, kind="ExternalOutput").ap()

    def dtmp(name, shape, dt=F32):
        return nc.dram_tensor(name, list(shape), dt, kind="Internal").ap()
    K = KB(nc)
    P = K.P
    gshape = W["gains"].shape
    goff = W["goff"]
    gains = P.sb(list(gshape), F32, "gains")
    b_g = P.buf("gains")
    K.dma(gains[:], din("gains", gshape), [], [b_g])
    cflag = P.sb([128, 1], F32, "cflag")
    K.dma(cflag[:], din("cflag", [128, 1]), [], [b_g])

    def gn(l, i):
        o = goff["norms"][0] + (l * 6 + i) * c.KT
        return gains[:, o:o + c.KT]

    def gsl(name, idx, n):
        o = goff[name][0] + idx * n
        return gains[:, o:o + n]
    x = dout("x_out", [D, T])
    xp = dtmp("xp", [D, T])
    b_xc = P.buf("xcopy")
    K.dma(x, din("x_in", [D, T]), [], [b_xc])
    K.dma(xp, din("xp_in", [D, T]), [], [b_xc])
    P.barrier()
    wts = {}

    def wt(name, l=None, j=None):
        key = (name, l, j)
        if key not in wts:
            shp = W[name].shape
            if l is not None:
                shp = shp[1:]
            if j is not None:
                shp = shp[1:]
            nm_ = f"{name}_{l}_{j}".replace("None", "x")
            wts[key] = (nm_, din(nm_, shp))
        return wts[key][1]

    def fspec(l, i):
        return (gn(l, 0 if i == 0 else 4), gn(l, 1 if i == 0 else 5), wt("ffn_w_gate", l, i), wt("ffn_w_up", l, i), wt("ffn_w_down", l, i))

    def ffn(l, i, xx):
        ffn_multi(K, xx, [fspec(l, i)], b_g, D, c.DFF, T, Tp)

    def ffn2(l, xx):
        ffn_multi(K, xx, [fspec(l, 1), fspec(l + 1, 0)], b_g, D, c.DFF, T, Tp)
    s5p = [dict(lam_s=din(f"s5lam_s{l}", [128, 3, c.NP]), lam_r=din(f"s5lam_r{l}", [128, 3, c.NP * 128]),
                bT=din(f"s5bT{l}", [2, 128, c.NP, 128]), cP=din(f"s5cP{l}", [2, 128, c.NP, 128]),
                d=din(f"s5d{l}", [128, c.NQ])) for l in range(c.NA)]
    mem_d = din("memT", [D, c.NM])
    u_d = dtmp("u", [c.TOKW, T], BF16)
    memo_d = dtmp("memo", [c.MEMW, T], BF16)
    yg_d = dtmp("yg", [c.TOKW, T], BF16)
    zero_c = din("zero_carry", [128, 2, c.NP])
    carryA = [dtmp(f"carryA{l}", [128, 2, c.NP]) for l in range(c.NA)]
    carry_dummy = dtmp("carry_dummy", [128, 2, c.NP])
    kn_d, kr_d, v_d = dtmp("kn", [c.H, 128, T], BF16), dtmp("kr", [64, T], BF16), dtmp("v", [T, c.H * 128], BF16)
    knp_d, krp_d, vp_d = dtmp("knp", [c.H, 128, T], BF16), dtmp("krp", [64, T], BF16), dtmp("vp", [T, c.H * 128], BF16)
    invf_d, sgn_d = din("invf", [64, 1]), din("sgn", [64, 1])

    streams = [dict(x=xp, u=u_d, memo=memo_d, yg=yg_d),
               dict(x=x, u=dtmp("u2", [c.TOKW, T], BF16), memo=dtmp("memo2", [c.MEMW, T], BF16), yg=dtmp("yg2", [c.TOKW, T], BF16))]

    def pre_scan(l, st, MKV):
        mixer_pre_A(K, st["x"], gn(l, 2), b_g, wt("a_w_in", l), st["u"], st["memo"], MKV, D, c.TOKW, c.MEMW, c.NM, T, Tp)

    def post_scan(l, st):
        mixer_post(K, st["x"], gn(l, 3), b_g, wt("w_out", l), st["yg"], st["memo"], D, c.TOKW, c.MEMW, T, Tp,
                   W_glu=wt("s5_w_glu", l), bglu=gsl("bglu", l, c.NTK))

    def kv(xx, pos_name, kn_, kr_, v_):
        mk = P.mark()
        RT = rope_tables(K, din(pos_name, [T], I32), invf_d, sgn_d, T)
        kv_stage(K, xx, gsl("kv_in", 0, c.KT), gsl("kv", 0, c.RK), b_g, wt("w_dkv"), wt("w_kr"), wt("w_uk"), wt("w_uv"), RT,
                 kn_, kr_, v_, D, c.R, c.H, T, Tp)
        return mk, RT
    for st in streams:
        ffn(0, 0, st["x"])
    for l in range(c.NA):
        mk = P.mark()
        MKV = mem_kv_setup(K, mem_d, gsl("mem_norm", l, c.KT), b_g, wt("mem_w_kv", l), D, c.NM, c.MEMW)
        for st in streams:
            pre_scan(l, st, MKV)
        P.release(mk)
        mk = P.mark()
        S = s5_setup(K, s5p[l], c.NP)
        s5_scan(K, S, c.NP, T, streams[0]["u"], zero_c, carryA[l], streams[0]["yg"], s5p[l]["d"], True)
        s5_scan(K, S, c.NP, T, streams[1]["u"], carryA[l], carry_dummy, streams[1]["yg"], s5p[l]["d"], True, flag=cflag, b_flag=b_g)
        P.release(mk)
        for si, st in enumerate(streams):
            post_scan(l, st)
            if l < c.NA - 1:
                ffn2(l, st["x"])
            else:
                ffn(l, 1, st["x"])
                if si == 0:
                    mk, _ = kv(xp, "posp", knp_d, krp_d, vp_d)
                    P.release(mk)
    mk, RT = kv(x, "pos", kn_d, kr_d, v_d)
    pbias_d = din("pbias", [128, 1])
    mask_d = din("cmask", [4, 128, NT])
    qn_d = dtmp("qn", [c.H, 128, T], BF16)
    qr_d = dtmp("qr", [c.H, 64, T], BF16)
    tok_d = dtmp("tok", [c.TOKW, T], BF16)
    for l in range(c.NA, c.L):
        j = l - c.NA
        if l == c.NA:
            ffn(l, 0, x)
        mk2 = P.mark()
        MKV = mem_kv_setup(K, mem_d, gsl("mem_norm", l, c.KT), b_g, wt("mem_w_kv", l), D, c.NM, c.MEMW)
        mixer_pre_B(K, x, gn(l, 2), gsl("q", j, c.RK), b_g, wt("b_w_in", j), wt("mla_w_uq", j), RT, qn_d, qr_d, memo_d, MKV,
                    D, c.R, c.H, c.MEMW, c.NM, T, Tp)
        P.release(mk2)
        mla_attn(K, qn_d, qr_d, kn_d, kr_d, v_d, knp_d, krp_d, vp_d, pbias_d, mask_d, tok_d, c.H, T)
        mixer_post(K, x, gn(l, 3), b_g, wt("w_out", l), tok_d, memo_d, D, c.TOKW, c.MEMW, T, Tp)
        if l == c.L - 1:
            ffn(l, 1, x)
        else:
            ffn2(l, x)
    P.barrier()
    P.emit()
    return nc, wts


def run_fused(cfg, inp):
    c = cfg
    T = c.T
    NCORE = 2 * c.B
    f32 = np.float32
    gains, goff = gain_layout(c, inp["norms"], inp["mem_norm"], inp["kv_in_norm"], inp["kv_norm"], inp["mla_q_norm"], inp["s5_b_glu"])
    W = dict(inp)
    W["gains"], W["goff"] = gains, goff
    nc, wts = build_fused(c, W)
    shared = {"gains": gains}
    for (name, l, j), (nm_, ap) in wts.items():
        a = inp[name]
        if l is not None:
            a = a[l]
        if j is not None:
            a = a[j]
        shared[nm_] = np.ascontiguousarray(np.asarray(a, f32))
    for l in range(c.NA):
        lay = s5_host_layout(*(np.asarray(inp[k][l], f32) for k in ("s5_lambda_re", "s5_lambda_im", "s5_log_dt", "s5_b_re", "s5_b_im",
                                                                    "s5_c_re", "s5_c_im", "s5_d")))
        shared.update({f"s5lam_s{l}": lay["lam_s"], f"s5lam_r{l}": lay["lam_r"], f"s5bT{l}": lay["bT"], f"s5cP{l}": lay["cP"],
                       f"s5d{l}": lay["d_s"]})
    inv_freq = (10000.0 ** (-np.arange(0, 64, 2, dtype=np.float32) / 64)).astype(f32)
    shared["invf"] = np.concatenate([inv_freq, inv_freq])[:, None].astype(f32) / f32(2 * np.pi)
    shared["sgn"] = np.concatenate([-np.ones(32, f32), np.ones(32, f32)])[:, None]
    kk, qq = np.arange(128)[:, None], np.arange(NT)[None, :]
    shared["cmask"] = np.stack([(qq >= 128 * i + kk).astype(f32) for i in range(4)], 0)
    shared["zero_carry"] = np.zeros((128, 2, c.NP), f32)
    xa = np.asarray(inp["x"], f32)
    pa = np.asarray(inp["positions"], np.int32)
    maps = []
    for cid in range(NCORE):
        b, hh = divmod(cid, 2)
        m = dict(shared)
        m["memT"] = np.ascontiguousarray(np.asarray(inp["mem"][b], f32).T)
        m["x_in"] = np.ascontiguousarray(xa[b, hh * T:(hh + 1) * T, :].T)
        m["xp_in"] = np.ascontiguousarray(xa[b, 0:T, :].T)
        m["pos"] = np.ascontiguousarray(pa[b, hh * T:(hh + 1) * T])
        m["posp"] = np.ascontiguousarray(pa[b, 0:T])
        m["cflag"] = np.full((128, 1), float(hh), f32)
        m["pbias"] = np.full((128, 1), 0.0 if hh == 1 else -30000.0, f32)
        maps.append(m)
    res = run_bass_kernel_spmd(nc, maps, core_ids=list(range(NCORE)))
    out = np.empty((c.B, c.SEQ, c.D), f32)
    for cid in range(NCORE):
        b, hh = divmod(cid, 2)
        out[b, hh * T:(hh + 1) * T, :] = res.results[cid]["x_out"].T
    return out


def kernel(**inputs):
    return run_fused(Cfg(), inputs)
```

```python
import numpy as np
import concourse.bass as bass
import concourse.mybir as mybir
from concourse.bass_utils import run_bass_kernel_spmd

F32 = mybir.dt.float32
BF16 = mybir.dt.bfloat16
I32 = mybir.dt.int32
AF = mybir.ActivationFunctionType
ALU = mybir.AluOpType
AX = mybir.AxisListType

ENGS = ("pe", "act", "dve", "pool", "sp")
NDMASEM = 12


class Buf:
    __slots__ = ("name", "w", "r")

    def __init__(self, name):
        self.name = name
        self.w = None
        self.r = []


class Op:
    __slots__ = ("eng", "fn", "deps", "sig", "dma", "semi", "semv", "idx", "prev_semv")

    def __init__(self, eng, fn, dma):
        self.eng = eng
        self.fn = fn
        self.dma = dma
        self.deps = []
        self.sig = False
        self.semi = None
        self.semv = None
        self.prev_semv = None


class Prog:
    def __init__(self, nc):
        self.nc = nc
        self.ops = {e: [] for e in ENGS}
        self.all_ops = []
        self.sb_off = 16384 + 2048
        self.sb_cap = 16384 + 212000
        self.sb_hi = 0
        self.ntens = 0
        self.dma_rr = 0
        self.dma_sem_total = [0] * NDMASEM
        self.dma_sem_last = [None] * NDMASEM
        self.bufs = []

    def sb(self, shape, dtype, name=None):
        nbytes = int(np.prod(shape[1:])) * mybir.dt.size(dtype)
        nbytes = (nbytes + 63) // 64 * 64
        off = self.sb_off
        assert off + nbytes <= self.sb_cap, f"SBUF overflow {off}+{nbytes} ({name})"
        self.sb_off += nbytes
        self.sb_hi = max(self.sb_hi, self.sb_off)
        self.ntens += 1
        return self.nc.alloc_sbuf_tensor_at(f"t{self.ntens}_{name or ''}", list(shape), dtype, offset=off)

    def mark(self):
        return self.sb_off

    def release(self, mark):
        self.sb_off = mark

    def buf(self, name="b"):
        b = Buf(name)
        self.bufs.append(b)
        return b

    def bufs_n(self, n, name="b"):
        return [self.buf(name) for _ in range(n)]

    def op(self, eng, fn, reads=(), writes=(), dma=0):
        o = Op(eng, fn, dma)
        deps = set()
        for b in reads:
            if b.w is not None:
                deps.add(b.w)
        for b in writes:
            if b.w is not None:
                deps.add(b.w)
            for r in b.r:
                deps.add(r)
        for b in reads:
            b.r.append(o)
        for b in writes:
            b.w = o
            b.r = []
        deps.discard(o)
        for d in sorted(deps, key=lambda x: x.idx):
            if d.eng == "pe" and eng == "pe" and not d.dma and not dma:
                continue
            o.deps.append(d)
            d.sig = True
        if dma:
            o.sig = True
            k = self.dma_rr % NDMASEM
            self.dma_rr += 1
            o.semi = ("d", k)
            o.prev_semv = self.dma_sem_total[k]
            self.dma_sem_total[k] += 16 * dma
            o.semv = self.dma_sem_total[k]
            self.dma_sem_last[k] = o
        o.idx = len(self.all_ops)
        self.ops[eng].append(o)
        self.all_ops.append(o)
        return o

    def chain(self, eng, fns, reads=(), writes=()):
        c = self.buf("chain")
        o = None
        for f in fns:
            o = self.op(eng, f, reads=list(reads) + [c], writes=list(writes) + [c])
        return o

    def barrier(self):
        last = []
        for e in ENGS:
            if self.ops[e]:
                last.append(self.ops[e][-1])
        last += [o for o in self.dma_sem_last if o is not None]
        for d in last:
            d.sig = True
        for e in ENGS:
            o = Op(e, lambda h: None, 0)
            o.deps = list(last)
            o.idx = len(self.all_ops)
            self.ops[e].append(o)
            self.all_ops.append(o)
        for bb in self.bufs:
            bb.w = None
            bb.r = []

    def emit(self):
        nc = self.nc
        self.sems = {e: nc.alloc_semaphore(f"s_{e}") for e in ENGS}
        self.dsems = [nc.alloc_semaphore(f"s_dma{i}") for i in range(NDMASEM)]
        cnt = {e: 0 for e in ENGS}
        for o in self.all_ops:
            if not o.dma and o.sig:
                cnt[o.eng] += 1
                o.semi = ("e", o.eng)
                o.semv = cnt[o.eng]
        handles = {"pe": "tensor", "act": "scalar", "dve": "vector", "pool": "gpsimd", "sp": "sync"}

        def emit_engine(e, h):
            known = {}
            for o in self.ops[e]:
                waits = {}
                for d in o.deps:
                    waits[d.semi] = max(waits.get(d.semi, 0), d.semv)
                if o.dma and o.prev_semv:
                    waits[o.semi] = max(waits.get(o.semi, 0), o.prev_semv)
                for key, v in waits.items():
                    if known.get(key, 0) >= v:
                        continue
                    known[key] = v
                    sem = self.dsems[key[1]] if key[0] == "d" else self.sems[key[1]]
                    h.wait_ge(sem, v)
                r = o.fn(h)
                if o.dma:
                    sem = self.dsems[o.semi[1]]
                    assert len(r) == o.dma, (len(r), o.dma)
                    for ins in r:
                        ins.then_inc(sem, 16)
                elif o.sig:
                    if r is None:
                        r = h.nop()
                    r.then_inc(self.sems[e], 1)

        with nc.Block() as block:
            for e in ENGS:
                getattr(block, handles[e])(lambda h, e=e: emit_engine(e, h))


NT = 512
EPS = 1e-6


class KB:
    def __init__(self, nc):
        self.nc = nc
        self.P = Prog(nc)
        P = self.P
        self.ps = [nc.alloc_psum_tensor(f"psb{i}", [128, NT], F32) for i in range(8)]
        self.b_ps = P.bufs_n(8, "ps")
        self.ones = P.sb([128, 128], BF16, "ones")
        self.b_ones = P.buf("ones")
        P.op("dve", lambda h: h.memset(self.ones[:], 1.0), writes=[self.b_ones])
        self.init_consts()

    def mm(self, psi, pairs, reads, n=NT, m=128):
        ps = self.ps[psi]

        def f(h):
            L = len(pairs)
            for i, (a, b) in enumerate(pairs):
                r = h.matmul(ps[0:m, 0:n], lhsT=a, rhs=b, start=(i == 0), stop=(i == L - 1))
            return r
        return self.P.op("pe", f, reads=reads, writes=[self.b_ps[psi]])

    def dma(self, out, in_, reads, writes, eng="sp"):
        return self.P.op(eng, lambda h: [h.dma_start(out=out, in_=in_)], reads=reads, writes=writes, dma=1)

    def rstd_from_sq(self, sq, nk, n, psi, rstd, b_sq, b_rstd, dim):
        P = self.P
        ps = self.ps[psi]

        def mm(h):
            for k in range(nk):
                r = h.matmul(ps[:, 0:n], lhsT=self.ones[:], rhs=sq[:, k, 0:n], start=(k == 0), stop=(k == nk - 1))
            return r
        P.op("pe", mm, reads=[b_sq, self.b_ones], writes=[self.b_ps[psi]])
        P.op("act", lambda h: h.activation(out=rstd[:, 0:n], in_=ps[:, 0:n], func=AF.Sqrt, scale=1.0 / dim, bias=self.eps_ap()),
             reads=[self.b_ps[psi]], writes=[b_rstd])
        P.op("dve", lambda h: h.reciprocal(out=rstd[:, 0:n], in_=rstd[:, 0:n]), reads=[b_rstd], writes=[b_rstd])

    def eps_ap(self):
        return self.epsT[:, 0:1]

    def init_consts(self):
        P = self.P
        self.epsT = P.sb([128, 1], F32, "eps")
        self.b_eps = P.buf("eps")
        P.op("dve", lambda h: h.memset(self.epsT[:], EPS), writes=[self.b_eps])
        self.negpi = P.sb([128, 1], F32, "negpi")
        P.op("dve", lambda h: h.memset(self.negpi[:], -float(np.pi)), writes=[self.b_eps])


def ffn_stage(K, x_d, gin, gout, b_g, Wg, Wu, Wd, D, DFF, T, Tp, SUB=128, GW=256, res_scale=0.5):
    P = K.P
    KT = D // 128
    GC = GW // 128
    NTn = Tp // NT
    NG = DFF // GW
    mark = P.mark()
    hT = P.sb([128, KT, Tp], BF16, "hT")
    b_hT = P.bufs_n(Tp // SUB, "hT")
    acc = P.sb([128, KT, Tp], F32, "acc")
    b_acc = [P.bufs_n(NTn, "acc") for _ in range(KT)]
    xs = [P.sb([128, KT, SUB], F32, "xs") for _ in range(2)]
    b_xs = P.bufs_n(2, "xs")
    sq = P.sb([128, KT, SUB], BF16, "sq")
    b_sq = P.buf("sq")
    rstd = P.sb([128, SUB], F32, "rstd")
    b_rstd = P.buf("rstd")
    wg = [P.sb([128, KT, GW], BF16, "wg") for _ in range(2)]
    wu = [P.sb([128, KT, GW], BF16, "wu") for _ in range(2)]
    wd = [P.sb([128, GC, D], BF16, "wd") for _ in range(2)]
    b_wg = P.bufs_n(2, "wg")
    b_wu = P.bufs_n(2, "wu")
    b_wd = P.bufs_n(2, "wd")
    hid = [P.sb([128, GC, Tp], BF16, "hid") for _ in range(2)]
    b_hid = [[P.bufs_n(NTn, "hid") for _ in range(GC)] for _ in range(2)]
    sg = [P.sb([128, NT], F32, "sg") for _ in range(2)]
    b_sg = P.bufs_n(2, "sg")
    b_x = P.buf("xdram")
    gsc = P.sb([128, KT], F32, "gsc")
    b_gsc = P.buf("gsc")
    P.op("dve", lambda h: h.tensor_scalar(out=gsc[:], in0=gout, scalar1=float(res_scale), scalar2=None, op0=ALU.mult),
         reads=[b_g], writes=[b_gsc])
    Wg_v = Wg.rearrange("(kt p) c -> p kt c", p=128)
    Wu_v = Wu.rearrange("(kt p) c -> p kt c", p=128)
    Wd_v = Wd.rearrange("(c p) d -> p c d", p=128)
    x_v = x_d.rearrange("(kt p) t -> p kt t", p=128)
    PS_G, PS_U, PS_D, PS_M = (0, 1), (2, 3), (4, 5), 6
    cnt = {"gu": 0, "d": 0, "xs": 0}

    for p in range(T // Tp):
        t0 = p * Tp
        for s in range(Tp // SUB):
            xi = cnt["xs"] % 2
            cnt["xs"] += 1
            K.dma(xs[xi][:], x_v[:, :, t0 + s * SUB: t0 + (s + 1) * SUB], [b_x], [b_xs[xi]])
            P.op("act", lambda h, xi=xi: h.activation(out=sq[:], in_=xs[xi][:], func=AF.Square),
                 reads=[b_xs[xi]], writes=[b_sq])
            K.rstd_from_sq(sq, KT, SUB, PS_M, rstd, b_sq, b_rstd, D)

            def nrm(h, xi=xi, s=s):
                for kt in range(KT):
                    r = h.scalar_tensor_tensor(out=hT[:, kt, s * SUB:(s + 1) * SUB], in0=xs[xi][:, kt, :],
                                               scalar=gin[:, kt:kt + 1], in1=rstd[:], op0=ALU.mult, op1=ALU.mult)
                return r
            P.op("dve", nrm, reads=[b_xs[xi], b_rstd, b_g], writes=[b_hT[s]])

        def load_w(j):
            sl = j % 2
            K.dma(wg[sl][:], Wg_v[:, :, j * GW:(j + 1) * GW], [], [b_wg[sl]], eng="pool")
            K.dma(wu[sl][:], Wu_v[:, :, j * GW:(j + 1) * GW], [], [b_wu[sl]], eng="pool")
            K.dma(wd[sl][:], Wd_v[:, j * GC:(j + 1) * GC, :], [], [b_wd[sl]], eng="pool")

        def gateup(j):
            sl = j % 2
            for c in range(GC):
                for nt in range(NTn):
                    gi = cnt["gu"] % 2
                    cnt["gu"] += 1
                    pg, pu = PS_G[gi], PS_U[gi]
                    hbufs = b_hT[nt * (NT // SUB):(nt + 1) * (NT // SUB)]

                    K.mm(pg, [(wg[sl][:, kt, c * 128:(c + 1) * 128], hT[:, kt, nt * NT:(nt + 1) * NT]) for kt in range(KT)],
                         [b_wg[sl]] + hbufs)
                    K.mm(pu, [(wu[sl][:, kt, c * 128:(c + 1) * 128], hT[:, kt, nt * NT:(nt + 1) * NT]) for kt in range(KT)],
                         [b_wu[sl]] + hbufs)
                    P.op("act", lambda h, gi=gi, pg=pg: h.activation(out=sg[gi][:], in_=K.ps[pg][:], func=AF.Silu),
                         reads=[K.b_ps[pg]], writes=[b_sg[gi]])
                    P.op("dve", lambda h, gi=gi, pu=pu, sl=sl, c=c, nt=nt: h.tensor_tensor(
                        out=hid[sl][:, c, nt * NT:(nt + 1) * NT], in0=sg[gi][:], in1=K.ps[pu][:], op=ALU.mult),
                        reads=[b_sg[gi], K.b_ps[pu]], writes=[b_hid[sl][c][nt]])

        def down(j):
            sl = j % 2
            for m in range(KT):
                for nt in range(NTn):
                    di = cnt["d"] % 2
                    cnt["d"] += 1
                    pd = PS_D[di]

                    K.mm(pd, [(wd[sl][:, c, m * 128:(m + 1) * 128], hid[sl][:, c, nt * NT:(nt + 1) * NT]) for c in range(GC)],
                         [b_wd[sl]] + [b_hid[sl][c][nt] for c in range(GC)])
                    if j == 0:
                        P.op("dve", lambda h, m=m, nt=nt, pd=pd: h.tensor_copy(out=acc[:, m, nt * NT:(nt + 1) * NT], in_=K.ps[pd][:]),
                             reads=[K.b_ps[pd]], writes=[b_acc[m][nt]])
                    else:
                        P.op("dve", lambda h, m=m, nt=nt, pd=pd: h.tensor_tensor(
                            out=acc[:, m, nt * NT:(nt + 1) * NT], in0=acc[:, m, nt * NT:(nt + 1) * NT], in1=K.ps[pd][:], op=ALU.add),
                            reads=[K.b_ps[pd], b_acc[m][nt]], writes=[b_acc[m][nt]])

        load_w(0)
        gateup(0)
        for j in range(NG):
            if j + 1 < NG:
                load_w(j + 1)
                gateup(j + 1)
            down(j)

        for s in range(Tp // SUB):
            nt = (s * SUB) // NT
            accb = [b_acc[m][nt] for m in range(KT)]
            xi = cnt["xs"] % 2
            cnt["xs"] += 1
            K.dma(xs[xi][:], x_v[:, :, t0 + s * SUB: t0 + (s + 1) * SUB], [b_x], [b_xs[xi]])
            P.op("act", lambda h, s=s: h.activation(out=sq[:], in_=acc[:, :, s * SUB:(s + 1) * SUB], func=AF.Square),
                 reads=accb, writes=[b_sq])
            K.rstd_from_sq(sq, KT, SUB, PS_M, rstd, b_sq, b_rstd, D)

            def fin1(h, s=s):
                for kt in range(KT):
                    r = h.scalar_tensor_tensor(out=acc[:, kt, s * SUB:(s + 1) * SUB], in0=acc[:, kt, s * SUB:(s + 1) * SUB],
                                               scalar=gsc[:, kt:kt + 1], in1=rstd[:], op0=ALU.mult, op1=ALU.mult)
                return r
            P.op("dve", fin1, reads=accb + [b_rstd, b_gsc], writes=accb)
            P.op("pool", lambda h, xi=xi, s=s: h.tensor_tensor(
                out=xs[xi][:], in0=acc[:, :, s * SUB:(s + 1) * SUB], in1=xs[xi][:], op=ALU.add),
                reads=accb + [b_xs[xi]], writes=[b_xs[xi]])
            K.dma(x_v[:, :, t0 + s * SUB: t0 + (s + 1) * SUB], xs[xi][:], [b_xs[xi]], [b_x])
    P.barrier()
    P.release(mark)


def ffn_multi(K, x_d, specs, b_g, D, DFF, T, Tp, SUB=128, GW=256, res_scale=0.5):
    P = K.P
    KT = D // 128
    GC = GW // 128
    NTn = Tp // NT
    NG = DFF // GW
    NS = Tp // SUB
    mark = P.mark()
    hT = P.sb([128, KT, Tp], BF16, "hT")
    b_hT = P.bufs_n(NS, "hT")
    acc = P.sb([128, KT, Tp], F32, "acc")
    b_acc = [P.bufs_n(NTn, "acc") for _ in range(KT)]
    fixed = (KT * Tp * 6 + 2 * KT * SUB * 2 + 2 * SUB * 4 + 2 * (2 * KT * GW * 2 + GC * D * 2) + 2 * GC * Tp * 2 + 2 * NT * 4
             + len(specs) * KT * 4 + 2048)
    NXS = 4 if P.sb_cap - P.sb_off - fixed >= 4 * KT * SUB * 4 else 2
    xs = [P.sb([128, KT, SUB], F32, "xs") for _ in range(NXS)]
    b_xs = P.bufs_n(NXS, "xs")
    sq = [P.sb([128, KT, SUB], BF16, "sq") for _ in range(2)]
    b_sq = P.bufs_n(2, "sq")
    rstd = [P.sb([128, SUB], F32, "rstd") for _ in range(2)]
    b_rstd = P.bufs_n(2, "rstd")
    wg = [P.sb([128, KT, GW], BF16, "wg") for _ in range(2)]
    wu = [P.sb([128, KT, GW], BF16, "wu") for _ in range(2)]
    wd = [P.sb([128, GC, D], BF16, "wd") for _ in range(2)]
    b_wg, b_wu, b_wd = P.bufs_n(2, "wg"), P.bufs_n(2, "wu"), P.bufs_n(2, "wd")
    hid = [P.sb([128, GC, Tp], BF16, "hid") for _ in range(2)]
    b_hid = [[P.bufs_n(NTn, "hid") for _ in range(GC)] for _ in range(2)]
    sg = [P.sb([128, NT], F32, "sg") for _ in range(2)]
    b_sg = P.bufs_n(2, "sg")
    b_x = P.buf("xdram")
    gscs = []
    b_gsc = P.buf("gsc")
    for (gin, gout, _, _, _) in specs:
        g_ = P.sb([128, KT], F32, "gsc")
        P.op("dve", lambda h, g_=g_, gout=gout: h.tensor_scalar(out=g_[:], in0=gout, scalar1=float(res_scale), scalar2=None, op0=ALU.mult),
             reads=[b_g], writes=[b_gsc])
        gscs.append(g_)
    x_v = x_d.rearrange("(kt p) t -> p kt t", p=128)
    views = [(Wg.rearrange("(kt p) c -> p kt c", p=128), Wu.rearrange("(kt p) c -> p kt c", p=128), Wd.rearrange("(c p) d -> p c d", p=128))
             for (_, _, Wg, Wu, Wd) in specs]
    PS_G, PS_U, PS_D, PS_M = (0, 1), (2, 3), (4, 5), (6, 7)
    cnt = {"gu": 0, "d": 0, "xs": 0, "sq": 0}
    jobs = [(f, p) for f in range(len(specs)) for p in range(T // Tp)]

    def rstd_calc(src_ap, reads):
        qi = cnt["sq"] % 2
        cnt["sq"] += 1
        P.op("act", lambda h: h.activation(out=sq[qi][:], in_=src_ap, func=AF.Square), reads=reads, writes=[b_sq[qi]])
        K.rstd_from_sq(sq[qi], KT, SUB, PS_M[qi], rstd[qi], b_sq[qi], b_rstd[qi], D)
        return qi

    def norm(k):
        f, p = jobs[k]
        gin = specs[f][0]
        t0 = p * Tp
        for s_ in range(NS):
            xi = cnt["xs"] % 2
            cnt["xs"] += 1
            K.dma(xs[xi][:], x_v[:, :, t0 + s_ * SUB: t0 + (s_ + 1) * SUB], [b_x], [b_xs[xi]])
            qi = rstd_calc(xs[xi][:], [b_xs[xi]])

            def nrm(h, xi=xi, s_=s_, qi=qi):
                for kt in range(KT):
                    r = h.scalar_tensor_tensor(out=hT[:, kt, s_ * SUB:(s_ + 1) * SUB], in0=xs[xi][:, kt, :],
                                               scalar=gin[:, kt:kt + 1], in1=rstd[qi][:], op0=ALU.mult, op1=ALU.mult)
                return r
            P.op("dve", nrm, reads=[b_xs[xi], b_rstd[qi], b_g], writes=[b_hT[s_]])

    def load_w(k, j):
        f, _ = jobs[k]
        Wg_v, Wu_v, Wd_v = views[f]
        sl = j % 2
        K.dma(wg[sl][:], Wg_v[:, :, j * GW:(j + 1) * GW], [], [b_wg[sl]], eng="pool")
        K.dma(wu[sl][:], Wu_v[:, :, j * GW:(j + 1) * GW], [], [b_wu[sl]], eng="pool")
        K.dma(wd[sl][:], Wd_v[:, j * GC:(j + 1) * GC, :], [], [b_wd[sl]], eng="pool")

    def gateup(j):
        sl = j % 2
        for c in range(GC):
            for nt in range(NTn):
                gi = cnt["gu"] % 2
                cnt["gu"] += 1
                pg, pu = PS_G[gi], PS_U[gi]
                hbufs = b_hT[nt * (NT // SUB):(nt + 1) * (NT // SUB)]
                K.mm(pg, [(wg[sl][:, kt, c * 128:(c + 1) * 128], hT[:, kt, nt * NT:(nt + 1) * NT]) for kt in range(KT)], [b_wg[sl]] + hbufs)
                K.mm(pu, [(wu[sl][:, kt, c * 128:(c + 1) * 128], hT[:, kt, nt * NT:(nt + 1) * NT]) for kt in range(KT)], [b_wu[sl]] + hbufs)
                P.op("act", lambda h, gi=gi, pg=pg: h.activation(out=sg[gi][:], in_=K.ps[pg][:], func=AF.Silu),
                     reads=[K.b_ps[pg]], writes=[b_sg[gi]])
                P.op("dve", lambda h, gi=gi, pu=pu, sl=sl, c=c, nt=nt: h.tensor_tensor(
                    out=hid[sl][:, c, nt * NT:(nt + 1) * NT], in0=sg[gi][:], in1=K.ps[pu][:], op=ALU.mult),
                    reads=[b_sg[gi], K.b_ps[pu]], writes=[b_hid[sl][c][nt]])

    def down(j):
        sl = j % 2
        for m in range(KT):
            for nt in range(NTn):
                pd = PS_D[cnt["d"] % 2]
                cnt["d"] += 1
                K.mm(pd, [(wd[sl][:, c, m * 128:(m + 1) * 128], hid[sl][:, c, nt * NT:(nt + 1) * NT]) for c in range(GC)],
                     [b_wd[sl]] + [b_hid[sl][c][nt] for c in range(GC)])
                if j == 0:
                    P.op("dve", lambda h, m=m, nt=nt, pd=pd: h.tensor_copy(out=acc[:, m, nt * NT:(nt + 1) * NT], in_=K.ps[pd][:]),
                         reads=[K.b_ps[pd]], writes=[b_acc[m][nt]])
                else:
                    P.op("dve", lambda h, m=m, nt=nt, pd=pd: h.tensor_tensor(
                        out=acc[:, m, nt * NT:(nt + 1) * NT], in0=acc[:, m, nt * NT:(nt + 1) * NT], in1=K.ps[pd][:], op=ALU.add),
                        reads=[K.b_ps[pd], b_acc[m][nt]], writes=[b_acc[m][nt]])

    def finalize_job(k):
        f, p = jobs[k]
        gsc = gscs[f]
        t0 = p * Tp
        def xload(s_):
            xi_ = (cnt["xs"] + s_) % 2
            K.dma(xs[xi_][:], x_v[:, :, t0 + s_ * SUB: t0 + (s_ + 1) * SUB], [b_x], [b_xs[xi_]])
        xload(0)
        base = cnt["xs"]
        for s_ in range(NS):
            nt = (s_ * SUB) // NT
            accb = [b_acc[m][nt] for m in range(KT)]
            xi = (base + s_) % 2
            if s_ + 1 < NS:
                cnt["xs"] = base
                xload(s_ + 1)
            cnt["xs"] = base + s_ + 1
            qi = rstd_calc(acc[:, :, s_ * SUB:(s_ + 1) * SUB], accb)

            if NXS == 4:
                ti = 2 + cnt["xs"] % 2
                tmp, b_tmp = xs[ti], b_xs[ti]

                def fin1(h, s_=s_, qi=qi, tmp=tmp):
                    for kt in range(KT):
                        r = h.scalar_tensor_tensor(out=tmp[:, kt, :], in0=acc[:, kt, s_ * SUB:(s_ + 1) * SUB],
                                                   scalar=gsc[:, kt:kt + 1], in1=rstd[qi][:], op0=ALU.mult, op1=ALU.mult)
                    return r
                P.op("dve", fin1, reads=accb + [b_rstd[qi], b_gsc], writes=[b_tmp])
                P.op("pool", lambda h, xi=xi, tmp=tmp: h.tensor_tensor(out=xs[xi][:], in0=tmp[:], in1=xs[xi][:], op=ALU.add),
                     reads=[b_tmp, b_xs[xi]], writes=[b_xs[xi]])
            else:
                def fin1(h, s_=s_, qi=qi):
                    for kt in range(KT):
                        r = h.scalar_tensor_tensor(out=acc[:, kt, s_ * SUB:(s_ + 1) * SUB], in0=acc[:, kt, s_ * SUB:(s_ + 1) * SUB],
                                                   scalar=gsc[:, kt:kt + 1], in1=rstd[qi][:], op0=ALU.mult, op1=ALU.mult)
                    return r
                P.op("dve", fin1, reads=accb + [b_rstd[qi], b_gsc], writes=accb)
                P.op("pool", lambda h, xi=xi, s_=s_: h.tensor_tensor(
                    out=xs[xi][:], in0=acc[:, :, s_ * SUB:(s_ + 1) * SUB], in1=xs[xi][:], op=ALU.add),
                    reads=accb + [b_xs[xi]], writes=[b_xs[xi]])
            K.dma(x_v[:, :, t0 + s_ * SUB: t0 + (s_ + 1) * SUB], xs[xi][:], [b_xs[xi]], [b_x])

    assert NG % 2 == 0
    norm(0)
    load_w(0, 0)
    gateup(0)
    for k in range(len(jobs)):
        for j in range(NG):
            if j + 1 < NG:
                load_w(k, j + 1)
                gateup(j + 1)
                down(j)
            else:
                if k + 1 < len(jobs):
                    norm(k + 1)
                    load_w(k + 1, 0)
                    gateup(0)
                down(j)
        finalize_job(k)
    P.barrier()
    P.release(mark)


S5_L = 128


def s5_host_layout(lam_re, lam_im, log_dt, b_re, b_im, c_re, c_im, d):
    G, Pn, C = b_re.shape
    NP = G // 2

    def st(a):
        return np.ascontiguousarray(a.reshape(NP, 2 * Pn).T)
    ldt = np.repeat(log_dt[:, None], Pn, 1)
    lam_s = np.stack([st(lam_re), st(lam_im), st(ldt)], 1)
    row = np.stack([lam_re.reshape(-1), lam_im.reshape(-1), ldt.reshape(-1)], 0)
    lam_r = np.ascontiguousarray(np.broadcast_to(row[None], (128, 3, NP * 128)))
    bT = np.zeros((2, 128, NP, 128), np.float32)
    cP = np.zeros((2, 128, NP, 128), np.float32)
    for g in range(G):
        q, hh = g // 2, g % 2
        off = (g % 8) * 16
        bT[0, off:off + 16, q, hh * 64:(hh + 1) * 64] = b_re[g].T
        bT[1, off:off + 16, q, hh * 64:(hh + 1) * 64] = b_im[g].T
        cP[0, hh * 64:(hh + 1) * 64, q, off:off + 16] = c_re[g].T
        cP[1, hh * 64:(hh + 1) * 64, q, off:off + 16] = c_im[g].T
    d_s = np.ascontiguousarray(d.reshape(-1, 128).T)
    return dict(lam_s=lam_s, lam_r=lam_r, bT=bT, cP=cP, d_s=d_s)


def s5_setup(K, prm, NP):
    P = K.P
    L = S5_L
    PI = float(np.pi)
    S = {}
    lam_s = P.sb([128, 3, NP], F32, "lam_s")
    b_l = P.buf("lam_s")
    K.dma(lam_s[:], prm["lam_s"], [], [b_l])
    dt = P.sb([128, NP], F32, "dt")
    th = P.sb([128, NP], F32, "th")
    r = P.sb([128, NP], F32, "r")
    b_t = P.buf("s5tab")
    P.op("act", lambda h: h.activation(out=dt[:], in_=lam_s[:, 2, :], func=AF.Exp), reads=[b_l], writes=[b_t])
    P.op("dve", lambda h: h.tensor_tensor(out=th[:], in0=lam_s[:, 1, :], in1=dt[:], op=ALU.mult), reads=[b_l, b_t], writes=[b_t])
    P.op("dve", lambda h: h.tensor_tensor(out=r[:], in0=lam_s[:, 0, :], in1=dt[:], op=ALU.mult), reads=[b_l, b_t], writes=[b_t])
    P.op("act", lambda h: h.activation(out=r[:], in_=r[:], func=AF.Exp), reads=[b_t], writes=[b_t])
    jrow_i = P.sb([128, L], I32, "jrow_i")
    jrow = P.sb([128, L], F32, "jrow")
    P.op("pool", lambda h: h.iota(jrow_i[:], pattern=[[1, L]], base=0, channel_multiplier=0), writes=[b_t], reads=[b_t])
    P.op("dve", lambda h: h.tensor_copy(out=jrow[:], in_=jrow_i[:]), reads=[b_t], writes=[b_t])
    cosT = P.sb([128, NP, L], F32, "cosT")
    sinT = P.sb([128, NP, L], F32, "sinT")
    Rz = P.sb([128, NP, L], F32, "Rz")
    b_tab = P.buf("tabs")

    TWO_PI = 2 * PI
    MAGIC = 12582912.0
    thn = P.sb([128, NP], F32, "thn")
    P.op("dve", lambda h: h.tensor_scalar(out=thn[:], in0=th[:], scalar1=1.0 / TWO_PI, scalar2=None, op0=ALU.mult), reads=[b_t], writes=[b_t])
    Kre = P.sb([128, NP], F32, "Kre")
    Kim = P.sb([128, NP], F32, "Kim")
    mk0 = P.mark()
    tmpT = P.sb([128, NP, L], F32, "tmpT")

    def sin_cycles(tens, shift, b_r, b_w_):
        tv = tmpT_v(tens)
        steps = []
        if shift:
            steps.append(lambda h: h.tensor_scalar(out=tens, in0=tens, scalar1=float(shift), scalar2=None, op0=ALU.add))
        steps.append(lambda h: h.tensor_scalar(out=tv, in0=tens, scalar1=MAGIC, scalar2=None, op0=ALU.add))
        steps.append(lambda h: h.tensor_scalar(out=tv, in0=tv, scalar1=-MAGIC, scalar2=None, op0=ALU.add))
        steps.append(lambda h: h.tensor_tensor(out=tens, in0=tens, in1=tv, op=ALU.subtract))
        P.chain("dve", steps, reads=b_r, writes=b_w_)
        P.op("act", lambda h: h.activation(out=tens, in_=tens, func=AF.Sin, scale=TWO_PI), reads=b_w_, writes=b_w_)

    def tmpT_v(tens):
        shp = tens.shape
        if len(shp) == 3:
            return tmpT[:, 0:shp[1], 0:shp[2]]
        return tmpT[:, 0, 0:shp[1]]

    def angs(h):
        for q in range(NP):
            h.tensor_scalar(out=sinT[:, q, :], in0=jrow[:], scalar1=thn[:, q:q + 1], scalar2=None, op0=ALU.mult)
            r_ = h.tensor_scalar(out=cosT[:, q, :], in0=jrow[:], scalar1=thn[:, q:q + 1], scalar2=None, op0=ALU.mult)
        return r_
    P.op("dve", angs, reads=[b_t], writes=[b_tab])
    sin_cycles(sinT[:], 0.0, [b_tab], [b_tab])
    sin_cycles(cosT[:], 0.25, [b_tab], [b_tab])

    def rz(h):
        for q in range(NP):
            h.tensor_scalar(out=Rz[:, q, 1:L], in0=jrow[:, 1:L], scalar1=0.0, scalar2=r[:, q:q + 1], op0=ALU.mult, op1=ALU.add)
        return h.memset(Rz[:, :, 0:1], 0.0)
    P.op("dve", rz, reads=[b_t], writes=[b_tab])
    def kang(h):
        h.tensor_scalar(out=Kim[:], in0=thn[:], scalar1=float(L), scalar2=None, op0=ALU.mult)
        return h.tensor_scalar(out=Kre[:], in0=thn[:], scalar1=float(L), scalar2=None, op0=ALU.mult)
    P.op("dve", kang, reads=[b_t], writes=[b_tab])
    sin_cycles(Kim[:], 0.0, [b_tab], [b_tab])
    sin_cycles(Kre[:], 0.25, [b_tab], [b_tab])

    def kmul(h):
        h.tensor_tensor(out=Kim[:], in0=Kim[:], in1=r[:], op=ALU.mult)
        return h.tensor_tensor(out=Kre[:], in0=Kre[:], in1=r[:], op=ALU.mult)
    P.op("dve", kmul, reads=[b_tab, b_t], writes=[b_tab])

    P.barrier()
    P.release(mk0)
    BT = [P.sb([128, NP, 128], BF16, f"BT{i}") for i in range(2)]
    CT = [P.sb([128, NP, 128], BF16, f"CT{i}") for i in range(2)]
    b_BT = P.buf("BT")
    b_CT = P.buf("CT")
    K.dma(CT[0][:], prm["cP"][0], [], [b_CT], eng="pool")
    K.dma(CT[1][:], prm["cP"][1], [], [b_CT], eng="pool")
    P.op("dve", lambda h: h.tensor_scalar(out=CT[1][:], in0=CT[1][:], scalar1=-1.0, scalar2=None, op0=ALU.mult), reads=[b_CT], writes=[b_CT])
    mk = P.mark()
    QB = 4
    W = QB * 128
    lr = P.sb([128, 3, W], F32, "lr")
    tb = [P.sb([128, W], F32, f"tb{i}") for i in range(6)]
    bt = [P.sb([128, QB, 128], F32, f"bt{i}") for i in range(2)]
    b_lr = P.buf("lr")
    b_bt = P.buf("bt")
    b_w = P.buf("w")
    for blk in range(NP // QB):
        cs = slice(blk * W, (blk + 1) * W)
        K.dma(lr[:], prm["lam_r"][:, :, cs], [], [b_lr])
        K.dma(bt[0][:], prm["bT"][0][:, blk * QB:(blk + 1) * QB, :], [], [b_bt])
        K.dma(bt[1][:], prm["bT"][1][:, blk * QB:(blk + 1) * QB, :], [], [b_bt])
        dtr, thr, rr, ca, sa, den = tb
        P.op("act", lambda h: h.activation(out=dtr[:], in_=lr[:, 2, :], func=AF.Exp), reads=[b_lr], writes=[b_w])

        def c1(h):
            h.tensor_tensor(out=thr[:], in0=lr[:, 1, :], in1=dtr[:], op=ALU.mult)
            h.tensor_tensor(out=rr[:], in0=lr[:, 0, :], in1=dtr[:], op=ALU.mult)
            h.tensor_scalar(out=sa[:], in0=thr[:], scalar1=1.0 / TWO_PI, scalar2=None, op0=ALU.mult)
            h.tensor_scalar(out=ca[:], in0=thr[:], scalar1=1.0 / TWO_PI, scalar2=0.25, op0=ALU.mult, op1=ALU.add)
            for t_ in (sa, ca):
                h.tensor_scalar(out=den[:], in0=t_[:], scalar1=MAGIC, scalar2=None, op0=ALU.add)
                h.tensor_scalar(out=den[:], in0=den[:], scalar1=-MAGIC, scalar2=None, op0=ALU.add)
                r_ = h.tensor_tensor(out=t_[:], in0=t_[:], in1=den[:], op=ALU.subtract)
            return r_
        P.op("dve", c1, reads=[b_lr, b_w], writes=[b_w])
        P.op("act", lambda h: h.activation(out=rr[:], in_=rr[:], func=AF.Exp), reads=[b_w], writes=[b_w])
        P.op("act", lambda h: h.activation(out=sa[:], in_=sa[:], func=AF.Sin, scale=TWO_PI), reads=[b_w], writes=[b_w])
        P.op("act", lambda h: h.activation(out=ca[:], in_=ca[:], func=AF.Sin, scale=TWO_PI), reads=[b_w], writes=[b_w])

        def c2(h):
            h.tensor_tensor(out=ca[:], in0=ca[:], in1=rr[:], op=ALU.mult)
            h.tensor_scalar(out=ca[:], in0=ca[:], scalar1=-1.0, scalar2=None, op0=ALU.add)
            h.tensor_tensor(out=sa[:], in0=sa[:], in1=rr[:], op=ALU.mult)
            h.tensor_tensor(out=den[:], in0=lr[:, 0, :], in1=lr[:, 0, :], op=ALU.mult)
            h.tensor_tensor(out=dtr[:], in0=lr[:, 1, :], in1=lr[:, 1, :], op=ALU.mult)
            h.tensor_tensor(out=den[:], in0=den[:], in1=dtr[:], op=ALU.add)
            h.reciprocal(out=den[:], in_=den[:])
            h.tensor_tensor(out=thr[:], in0=ca[:], in1=lr[:, 0, :], op=ALU.mult)
            h.tensor_tensor(out=dtr[:], in0=sa[:], in1=lr[:, 1, :], op=ALU.mult)
            h.tensor_tensor(out=thr[:], in0=thr[:], in1=dtr[:], op=ALU.add)
            h.tensor_tensor(out=thr[:], in0=thr[:], in1=den[:], op=ALU.mult)
            h.tensor_tensor(out=rr[:], in0=sa[:], in1=lr[:, 0, :], op=ALU.mult)
            h.tensor_tensor(out=dtr[:], in0=ca[:], in1=lr[:, 1, :], op=ALU.mult)
            h.tensor_tensor(out=rr[:], in0=rr[:], in1=dtr[:], op=ALU.subtract)
            return h.tensor_tensor(out=rr[:], in0=rr[:], in1=den[:], op=ALU.mult)
        P.op("dve", c2, reads=[b_lr, b_w], writes=[b_w])

        def c3(h, blk=blk):
            qs = slice(blk * QB, (blk + 1) * QB)
            b0 = bt[0][:].rearrange("p q s -> p (q s)")
            b1 = bt[1][:].rearrange("p q s -> p (q s)")
            o0 = BT[0][:, qs, :].rearrange("p q s -> p (q s)")
            o1 = BT[1][:, qs, :].rearrange("p q s -> p (q s)")
            h.tensor_tensor(out=ca[:], in0=thr[:], in1=b0, op=ALU.mult)
            h.tensor_tensor(out=sa[:], in0=rr[:], in1=b1, op=ALU.mult)
            h.tensor_tensor(out=o0, in0=ca[:], in1=sa[:], op=ALU.subtract)
            h.tensor_tensor(out=ca[:], in0=thr[:], in1=b1, op=ALU.mult)
            h.tensor_tensor(out=sa[:], in0=rr[:], in1=b0, op=ALU.mult)
            return h.tensor_tensor(out=o1, in0=ca[:], in1=sa[:], op=ALU.add)
        P.op("dve", c3, reads=[b_w, b_bt], writes=[b_BT, b_w])
    P.barrier()
    P.release(mk)
    S.update(cosT=cosT, sinT=sinT, Rz=Rz, Kre=Kre, Kim=Kim, BT=BT, CT=CT, b_tab=b_tab, b_BT=b_BT, b_CT=b_CT)
    return S


def s5_scan(K, S, NP, T, u_d, carry_in_d, carry_out_d, yg_d, d_d, full, flag=None, b_flag=None):
    P = K.P
    L = S5_L
    NQ = NP // 4
    NG4 = (NQ + 3) // 4
    NCH = T // L
    mark = P.mark()
    cosT, sinT, Rz, Kre, Kim, BT, CT = (S[k] for k in ("cosT", "sinT", "Rz", "Kre", "Kim", "BT", "CT"))
    b_tab, b_BT, b_CT = S["b_tab"], S["b_BT"], S["b_CT"]
    u_v = u_d.rearrange("(q p) t -> p q t", p=128)
    SC = 4 * L
    uT = [P.sb([128, NQ, SC], BF16, "uT") for _ in range(2)]
    b_u = P.bufs_n(2, "uT")
    rc = P.sb([128, 2, NP], F32, "rc")
    b_rc = P.buf("rc")
    zl = P.sb([128, 2, NP], F32, "zl")
    b_zl = P.bufs_n(NQ, "zl")
    tmpc = [P.sb([128, NP], F32, f"tmpc{i}") for i in range(2)]
    K.dma(rc[:], carry_in_d, [], [b_rc])
    if flag is not None:
        P.op("dve", lambda h: h.tensor_scalar(out=rc[:].rearrange("p a q -> p (a q)"), in0=rc[:].rearrange("p a q -> p (a q)"),
                                              scalar1=flag[:, 0:1], scalar2=None, op0=ALU.mult), reads=[b_flag], writes=[b_rc])
    NB = 2
    v = [[P.sb([128, 4, L], F32, f"v{i}{j}") for j in range(2)] for i in range(NB)]
    z = [[P.sb([128, 4, L], F32, f"z{i}{j}") for j in range(2)] for i in range(NB)]
    b_v = P.bufs_n(NB, "v")
    xo = [[P.sb([128, 4, L], BF16, f"xo{i}{j}") for j in range(2)] for i in range(NB)]
    b_xo = P.bufs_n(NB, "xo")
    b_xr = P.bufs_n(NB, "xr")
    xo2 = [[P.sb([128, 4, L], BF16, f"xo2{i}{j}") for j in range(2)] for i in range(NB)] if full else None
    b_zz = P.bufs_n(NB, "zz")
    fl = lambda t: t[:].rearrange("p q l -> p (q l)")
    col = lambda ap: ap.rearrange("p (q o) -> p q o", o=1)
    if full:
        d_s = P.sb([128, NQ], F32, "d_s")
        b_d = P.buf("d")
        K.dma(d_s[:], d_d, [], [b_d])
        du = [P.sb([128, NQ, L], F32, f"du{i}") for i in range(2)]
        b_du = P.bufs_n(2, "du")
        yt = [P.sb([128, 4, L], F32, f"yt{i}") for i in range(2)]
        y2 = [P.sb([128, 4, L], F32, f"y2{i}") for i in range(2)]
        yo = [P.sb([128, 4, L], BF16, f"yo{i}") for i in range(2)]
        b_yt = P.bufs_n(2, "yt")
        b_yo = P.bufs_n(2, "yo")
        b_ygd = P.buf("ygd")
        yg_v = yg_d.rearrange("(q p) t -> p q t", p=128)
    b_ud = P.buf("ud")
    PS_B = [(0, 1), (2, 3)]
    PS_Y = [4, 5, 6, 7]
    yit = [0]
    items = [(c, qd) for c in range(NCH) for qd in range(NQ)]
    usl = {}

    def emit_mmb(i):
        c, qd = items[i]
        sc, cc = divmod(c, 4)
        us = sc % 2
        if cc == 0 and qd == 0:
            K.dma(uT[us][:], u_v[:, :, sc * SC:(sc + 1) * SC], [b_ud], [b_u[us]])
        pr, pi_ = PS_B[i % 2]
        prs = list(range(qd * 4, qd * 4 + 4))
        rhs = uT[us][:, qd, cc * L:(cc + 1) * L]

        def mmb(h, pr=pr, pi_=pi_, prs=prs, rhs=rhs):
            for k, q in enumerate(prs):
                h.matmul(K.ps[pr][:, k * L:(k + 1) * L], lhsT=BT[0][:, q, :], rhs=rhs, start=True, stop=True)
            for k, q in enumerate(prs):
                r_ = h.matmul(K.ps[pi_][:, k * L:(k + 1) * L], lhsT=BT[1][:, q, :], rhs=rhs, start=True, stop=True)
            return r_
        P.op("pe", mmb, reads=[b_BT, b_u[us]], writes=[K.b_ps[pr], K.b_ps[pi_]])

    def emit_rest(i):
        c, qd = items[i]
        sc, cc = divmod(c, 4)
        us = sc % 2
        dui = c % 2
        if full and qd == 0:
            def mkdu(h, dui=dui, us=us, cc=cc):
                for q_ in range(NQ):
                    r_ = h.tensor_scalar(out=du[dui][:, q_, :], in0=uT[us][:, q_, cc * L:(cc + 1) * L], scalar1=d_s[:, q_:q_ + 1],
                                         scalar2=None, op0=ALU.mult)
                return r_
            P.op("pool", mkdu, reads=[b_u[us], b_d], writes=[b_du[dui]])
        vi = i % NB
        pr, pi_ = PS_B[i % 2]
        prs = list(range(qd * 4, qd * 4 + 4))
        vr, vim = v[vi]
        zr, zi = z[vi]
        qs = slice(qd * 4, qd * 4 + 4)
        cs_ = cosT[:, qs, :].rearrange("p q l -> p (q l)")
        sn_ = sinT[:, qs, :].rearrange("p q l -> p (q l)")
        rz_ = Rz[:, qs, :].rearrange("p q l -> p (q l)")

        def rot_in(h):
            h.tensor_tensor(out=fl(zr), in0=K.ps[pr][:], in1=cs_, op=ALU.mult)
            h.tensor_tensor(out=fl(zi), in0=K.ps[pi_][:], in1=sn_, op=ALU.mult)
            h.tensor_tensor(out=fl(vr), in0=fl(zr), in1=fl(zi), op=ALU.add)
            h.tensor_tensor(out=fl(zr), in0=K.ps[pi_][:], in1=cs_, op=ALU.mult)
            h.tensor_tensor(out=fl(zi), in0=K.ps[pr][:], in1=sn_, op=ALU.mult)
            return h.tensor_tensor(out=fl(vim), in0=fl(zr), in1=fl(zi), op=ALU.subtract)
        P.op("dve", rot_in, reads=[K.b_ps[pr], K.b_ps[pi_], b_tab], writes=[b_v[vi], b_zz[vi]])

        def sc1(h):
            h.tensor_tensor(out=vr[:, :, 0:1], in0=vr[:, :, 0:1], in1=col(rc[:, 0, qs]), op=ALU.add)
            return h.tensor_tensor(out=vim[:, :, 0:1], in0=vim[:, :, 0:1], in1=col(rc[:, 1, qs]), op=ALU.add)

        def sc2(h):
            h.tensor_tensor_scan(out=fl(zr), data0=rz_, data1=fl(vr), initial=0.0, op0=ALU.mult, op1=ALU.add)
            return h.tensor_tensor_scan(out=fl(zi), data0=rz_, data1=fl(vim), initial=0.0, op0=ALU.mult, op1=ALU.add)

        def sc3(h):
            h.tensor_copy(out=col(zl[:, 0, qs]), in_=zr[:, :, L - 1:L])
            return h.tensor_copy(out=col(zl[:, 1, qs]), in_=zi[:, :, L - 1:L])
        P.chain("dve", [sc1, sc2, sc3], reads=[b_rc, b_tab], writes=[b_v[vi], b_zz[vi], b_zl[qd]])
        if full:
            xr, xi = xo[vi]


            xr2, xi2 = xo2[vi]

            def rot_prod(h):
                h.tensor_tensor(out=fl(xr), in0=fl(zr), in1=cs_, op=ALU.mult)
                h.scalar_tensor_tensor(out=fl(xr2), in0=fl(zi), scalar=-1.0, in1=sn_, op0=ALU.mult, op1=ALU.mult)
                h.tensor_tensor(out=fl(xi), in0=fl(zr), in1=sn_, op=ALU.mult)
                return h.tensor_tensor(out=fl(xi2), in0=fl(zi), in1=cs_, op=ALU.mult)
            P.op("dve", rot_prod, reads=[b_tab, b_zz[vi]], writes=[b_xo[vi], b_xr[vi]])
        if i + 2 < len(items):
            emit_mmb(i + 2)
        if full:
            g4, q4 = divmod(qd, 4)
            py = PS_Y[(c * NG4 + g4) % 4]

            def mmy(h):
                ops_ = [(CT[0], xr), (CT[0], xr2), (CT[1], xi), (CT[1], xi2)]
                n_ = len(ops_) * 4
                t_ = 0
                for (ct, xx_) in ops_:
                    for k, q in enumerate(prs):
                        r_ = h.matmul(K.ps[py][:, q4 * L:(q4 + 1) * L], lhsT=ct[:, q, :], rhs=xx_[:, k, :], start=(t_ == 0), stop=(t_ == n_ - 1))
                        t_ += 1
                return r_
            P.op("pe", mmy, reads=[b_xo[vi], b_xr[vi], b_CT], writes=[K.b_ps[py]])
            if q4 == 3 or qd == NQ - 1:
                nq4 = q4 + 1
                yi = yit[0] % 2
                yit[0] += 1
                W4 = nq4 * L
                P.op("dve", lambda h: h.tensor_tensor(
                    out=yt[yi][:, 0:nq4, :].rearrange("p q l -> p (q l)"), in0=K.ps[py][:, 0:W4],
                    in1=du[dui][:, g4 * 4:g4 * 4 + nq4, :].rearrange("p q l -> p (q l)"), op=ALU.add),
                    reads=[K.b_ps[py], b_du[dui], b_yo[yi]], writes=[b_yt[yi]])
                P.op("act", lambda h: h.activation(out=y2[yi][:, 0:nq4, :], in_=yt[yi][:, 0:nq4, :], func=AF.Square),
                     reads=[b_yt[yi]], writes=[b_yt[yi]])

                def g2(h):
                    h.tensor_scalar(out=y2[yi][:, 0:nq4, :], in0=y2[yi][:, 0:nq4, :], scalar1=0.044715, scalar2=1.0, op0=ALU.mult, op1=ALU.add)
                    return h.tensor_tensor(out=y2[yi][:, 0:nq4, :], in0=y2[yi][:, 0:nq4, :], in1=yt[yi][:, 0:nq4, :], op=ALU.mult)
                P.op("dve", g2, reads=[b_yt[yi]], writes=[b_yt[yi]])
                P.op("act", lambda h: h.activation(out=y2[yi][:, 0:nq4, :], in_=y2[yi][:, 0:nq4, :], func=AF.Sigmoid, scale=1.5957691216),
                     reads=[b_yt[yi]], writes=[b_yt[yi]])
                P.op("dve", lambda h: h.tensor_tensor(out=yo[yi][:, 0:nq4, :], in0=y2[yi][:, 0:nq4, :], in1=yt[yi][:, 0:nq4, :], op=ALU.mult),
                     reads=[b_yt[yi]], writes=[b_yo[yi]])
                K.dma(yg_v[:, g4 * 4:g4 * 4 + nq4, c * L:(c + 1) * L], yo[yi][:, 0:nq4, :], [b_yo[yi]], [b_ygd])

    emit_mmb(0)
    if len(items) > 1:
        emit_mmb(1)
    for c in range(NCH):
        for qd in range(NQ):
            emit_rest(c * NQ + qd)

        def cu1(h):
            h.tensor_tensor(out=tmpc[0][:], in0=zl[:, 0, :], in1=Kre[:], op=ALU.mult)
            return h.tensor_tensor(out=tmpc[1][:], in0=zl[:, 1, :], in1=Kim[:], op=ALU.mult)

        def cu2(h):
            return h.tensor_tensor(out=rc[:, 0, :], in0=tmpc[0][:], in1=tmpc[1][:], op=ALU.subtract)

        def cu3(h):
            h.tensor_tensor(out=tmpc[0][:], in0=zl[:, 0, :], in1=Kim[:], op=ALU.mult)
            return h.tensor_tensor(out=tmpc[1][:], in0=zl[:, 1, :], in1=Kre[:], op=ALU.mult)

        def cu4(h):
            return h.tensor_tensor(out=rc[:, 1, :], in0=tmpc[0][:], in1=tmpc[1][:], op=ALU.add)
        P.chain("dve", [cu1, cu2, cu3, cu4], reads=b_zl + [b_tab], writes=[b_rc])
    b_co = P.buf("carry_out")
    K.dma(carry_out_d, rc[:], [b_rc], [b_co])
    P.barrier()
    P.release(mark)


def alloc_norm_tmp(K, KT, SUB):
    P = K.P
    return dict(xs=[P.sb([128, KT, SUB], F32, "xs") for _ in range(2)], b_xs=P.bufs_n(2, "xs"),
                sq=P.sb([128, KT, SUB], BF16, "sq"), b_sq=P.buf("sq"),
                rstd=P.sb([128, SUB], F32, "rstd"), b_rstd=P.buf("rstd"), n=0, SUB=SUB, KT=KT)


def norm_in(K, tm, x_v, b_x, t0, Tp, g_ap, b_g, hT, b_hT, D, PS_M):
    P = K.P
    SUB, KT = tm["SUB"], tm["KT"]
    for s in range(Tp // SUB):
        xi = tm["n"] % 2
        tm["n"] += 1
        xs, sq, rstd = tm["xs"][xi], tm["sq"], tm["rstd"]
        K.dma(xs[:], x_v[:, :, t0 + s * SUB: t0 + (s + 1) * SUB], [b_x], [tm["b_xs"][xi]])
        P.op("act", lambda h, xs=xs: h.activation(out=sq[:], in_=xs[:], func=AF.Square), reads=[tm["b_xs"][xi]], writes=[tm["b_sq"]])
        K.rstd_from_sq(sq, KT, SUB, PS_M, rstd, tm["b_sq"], tm["b_rstd"], D)

        def nrm(h, xs=xs, s=s):
            for kt in range(KT):
                r = h.scalar_tensor_tensor(out=hT[:, kt, s * SUB:(s + 1) * SUB], in0=xs[:, kt, :],
                                           scalar=g_ap[:, kt:kt + 1], in1=rstd[:], op0=ALU.mult, op1=ALU.mult)
            return r
        P.op("dve", nrm, reads=[tm["b_xs"][xi], tm["b_rstd"], b_g], writes=[b_hT[s]])


def finalize(K, tm, acc, b_acc, x_v, b_x, t0, Tp, gsc, b_gsc, D, PS_M):
    P = K.P
    SUB, KT = tm["SUB"], tm["KT"]
    if "tmp" not in tm and P.sb_cap - P.sb_off >= KT * SUB * 4 + 1024:
        tm["tmp"] = P.sb([128, KT, SUB], F32, "fintmp")
        tm["b_tmp"] = P.buf("fintmp")
    tmp, b_tmp = tm.get("tmp"), tm.get("b_tmp")
    base = tm["n"]

    def xload(s):
        xi_ = (base + s) % 2
        K.dma(tm["xs"][xi_][:], x_v[:, :, t0 + s * SUB: t0 + (s + 1) * SUB], [b_x], [tm["b_xs"][xi_]])
    xload(0)
    for s in range(Tp // SUB):
        nt = (s * SUB) // NT
        accb = [b_acc[m][nt] for m in range(KT)]
        xi = (base + s) % 2
        tm["n"] = base + s + 1
        xs, sq, rstd = tm["xs"][xi], tm["sq"], tm["rstd"]
        if s + 1 < Tp // SUB:
            xload(s + 1)
        P.op("act", lambda h, s=s: h.activation(out=sq[:], in_=acc[:, :, s * SUB:(s + 1) * SUB], func=AF.Square),
             reads=accb, writes=[tm["b_sq"]])
        K.rstd_from_sq(sq, KT, SUB, PS_M, rstd, tm["b_sq"], tm["b_rstd"], D)
        if tmp is not None:
            def fin1(h, s=s):
                for kt in range(KT):
                    r = h.scalar_tensor_tensor(out=tmp[:, kt, :], in0=acc[:, kt, s * SUB:(s + 1) * SUB],
                                               scalar=gsc[:, kt:kt + 1], in1=rstd[:], op0=ALU.mult, op1=ALU.mult)
                return r
            P.op("dve", fin1, reads=accb + [tm["b_rstd"], b_gsc], writes=[b_tmp])
            P.op("pool", lambda h, xs=xs: h.tensor_tensor(out=xs[:], in0=tmp[:], in1=xs[:], op=ALU.add),
                 reads=[b_tmp, tm["b_xs"][xi]], writes=[tm["b_xs"][xi]])
        else:
            def fin1(h, s=s):
                for kt in range(KT):
                    r = h.scalar_tensor_tensor(out=acc[:, kt, s * SUB:(s + 1) * SUB], in0=acc[:, kt, s * SUB:(s + 1) * SUB],
                                               scalar=gsc[:, kt:kt + 1], in1=rstd[:], op0=ALU.mult, op1=ALU.mult)
                return r
            P.op("dve", fin1, reads=accb + [tm["b_rstd"], b_gsc], writes=accb)
            P.op("pool", lambda h, xs=xs, s=s: h.tensor_tensor(out=xs[:], in0=acc[:, :, s * SUB:(s + 1) * SUB], in1=xs[:], op=ALU.add),
                 reads=accb + [tm["b_xs"][xi]], writes=[tm["b_xs"][xi]])
        K.dma(x_v[:, :, t0 + s * SUB: t0 + (s + 1) * SUB], xs[:], [tm["b_xs"][xi]], [b_x])


class WStream:
    def __init__(self, K, KT, MW, name="w"):
        P = K.P
        self.K, self.KT, self.MW = K, KT, MW
        self.w = [P.sb([128, KT, MW], BF16, name) for _ in range(2)]
        self.b = P.bufs_n(2, name)
        self.n = 0

    def load(self, W_v, c0, ncols=None):
        ncols = ncols or self.MW
        sl = self.n % 2
        self.n += 1
        self.K.dma(self.w[sl][:, :, 0:ncols], W_v[:, :, c0:c0 + ncols], [], [self.b[sl]], eng="pool")
        return self.w[sl], self.b[sl]


class Ring:
    def __init__(self, items):
        self.items = items
        self.n = 0

    def next(self):
        r = self.items[self.n % len(self.items)]
        self.n += 1
        return r


def mem_kv_setup(K, mem_d, gm, b_gm, Wkv, D, NM, MEMW):
    P = K.P
    KT = D // 128
    memK = P.sb([128, MEMW // 128, NM], BF16, "memK")
    memV = P.sb([128, NM // 128, MEMW], BF16, "memV")
    b_mk = P.buf("memK")
    b_mv = P.buf("memV")
    mark = P.mark()
    tm = alloc_norm_tmp(K, KT, NM)
    nm = P.sb([128, KT, NM], BF16, "nmem")
    b_nm = [P.buf("nmem")]
    wkv = P.sb([128, KT, 2 * MEMW], BF16, "wkv")
    b_w = P.buf("wkv")
    K.dma(wkv[:], Wkv.rearrange("(kt p) c -> p kt c", p=128), [], [b_w], eng="pool")
    mem_v = mem_d.rearrange("(kt p) t -> p kt t", p=128)
    norm_in(K, tm, mem_v, P.buf("memd"), 0, NM, gm, b_gm, nm, b_nm, D, 6)
    for h in range(MEMW // 128):
        psi = h % 2
        K.mm(psi, [(wkv[:, kt, h * 128:(h + 1) * 128], nm[:, kt, :]) for kt in range(KT)], [b_w] + b_nm, n=NM)
        P.op("act", lambda h_, h=h, psi=psi: h_.activation(out=memK[:, h, :], in_=K.ps[psi][:, 0:NM], func=AF.Copy),
             reads=[K.b_ps[psi]], writes=[b_mk])
    for kt_ in range(NM // 128):
        psi = 2 + kt_ % 2
        K.mm(psi, [(nm[:, kt, kt_ * 128:(kt_ + 1) * 128], wkv[:, kt, MEMW:2 * MEMW]) for kt in range(KT)], [b_w] + b_nm, n=MEMW)
        P.op("dve", lambda h_, kt_=kt_, psi=psi: h_.tensor_copy(out=memV[:, kt_, :], in_=K.ps[psi][:, 0:MEMW]),
             reads=[K.b_ps[psi]], writes=[b_mv])
    P.barrier()
    P.release(mark)
    return dict(memK=memK, memV=memV, b_mk=b_mk, b_mv=b_mv)


def alloc_mem_attn(K, NM):
    P = K.P
    return dict(pt=[P.sb([128, NM // 128, NT], BF16, "mpt") for _ in range(2)], b_pt=P.bufs_n(2, "mpt"),
                rec=[P.sb([128, NT], F32, "mrec") for _ in range(2)], b_rec=P.bufs_n(2, "mrec"),
                mo=[P.sb([128, NT], BF16, "mo") for _ in range(2)], b_mo=P.bufs_n(2, "mo"), n=0)


def mem_attn(K, MA, MKV, qm, b_qm, memo_d, b_md, t0, Tp, NM, MEMW, ps_s, ps_o, ps_r):
    P = K.P
    H = MEMW // 128
    NKT = NM // 128
    scale = 128.0 ** -0.5
    for h in range(H):
        for nt in range(Tp // NT):
            i = MA["n"] % 2
            MA["n"] += 1
            pt, rec, mo = MA["pt"][i], MA["rec"][i], MA["mo"][i]
            for kt in range(NKT):
                psi = ps_s.next()
                K.mm(psi, [(MKV["memK"][:, h, kt * 128:(kt + 1) * 128], qm[:, h, nt * NT:(nt + 1) * NT])], [MKV["b_mk"]] + b_qm)
                P.op("act", lambda h_, psi=psi, pt=pt, kt=kt: h_.activation(out=pt[:, kt, :], in_=K.ps[psi][:], func=AF.Exp, scale=scale),
                     reads=[K.b_ps[psi]], writes=[MA["b_pt"][i]])
            po, pr = ps_o.next(), ps_r.next()
            K.mm(po, [(MKV["memV"][:, kt, h * 128:(h + 1) * 128], pt[:, kt, :]) for kt in range(NKT)], [MKV["b_mv"], MA["b_pt"][i]])
            K.mm(pr, [(K.ones[:], pt[:, kt, :]) for kt in range(NKT)], [K.b_ones, MA["b_pt"][i]])
            P.op("dve", lambda h_, pr=pr, rec=rec: h_.reciprocal(out=rec[:], in_=K.ps[pr][:]), reads=[K.b_ps[pr]], writes=[MA["b_rec"][i]])
            P.op("dve", lambda h_, po=po, rec=rec, mo=mo: h_.tensor_tensor(out=mo[:], in0=K.ps[po][:], in1=rec[:], op=ALU.mult),
                 reads=[K.b_ps[po], MA["b_rec"][i]], writes=[MA["b_mo"][i]])
            K.dma(memo_d[h * 128:(h + 1) * 128, t0 + nt * NT: t0 + (nt + 1) * NT], mo[:], [MA["b_mo"][i]], [b_md])


def mixer_pre_A(K, x_d, g2, b_g, W_in, u_d, memo_d, MKV, D, TOKW, MEMW, NM, T, Tp, SUB=256):
    P = K.P
    KT = D // 128
    mark = P.mark()
    tm = alloc_norm_tmp(K, KT, SUB)
    hT = P.sb([128, KT, Tp], BF16, "hT")
    b_hT = P.bufs_n(Tp // SUB, "hT")
    qm = P.sb([128, MEMW // 128, Tp], BF16, "qm")
    b_qm = P.bufs_n(MEMW // 128, "qm")
    ws = WStream(K, KT, 256, "win")
    st = [P.sb([128, NT], BF16, "stg") for _ in range(3)]
    b_st = P.bufs_n(3, "stg")
    sti = Ring([0, 1, 2])
    MA = alloc_mem_attn(K, NM)
    x_v = x_d.rearrange("(kt p) t -> p kt t", p=128)
    W_v = W_in.rearrange("(kt p) c -> p kt c", p=128)
    b_x, b_ud, b_md = P.buf("x"), P.buf("ud"), P.buf("md")
    ps_mm = Ring([0, 1, 2])
    ps_s, ps_o, ps_r = Ring([0, 1, 2]), Ring([3, 4]), Ring([5, 7])
    MT = (TOKW + MEMW) // 128
    for p in range(T // Tp):
        t0 = p * Tp
        norm_in(K, tm, x_v, b_x, t0, Tp, g2, b_g, hT, b_hT, D, 6)
        for mg in range(MT // 2):
            w, bw = ws.load(W_v, mg * 256)
            for mi in range(2):
                m = mg * 2 + mi
                for nt in range(Tp // NT):
                    psi = ps_mm.next()
                    hb = b_hT[nt * (NT // SUB):(nt + 1) * (NT // SUB)]
                    K.mm(psi, [(w[:, kt, mi * 128:(mi + 1) * 128], hT[:, kt, nt * NT:(nt + 1) * NT]) for kt in range(KT)], [bw] + hb)
                    if m < TOKW // 128:
                        si = sti.next()
                        P.op("act", lambda h_, psi=psi, si=si: h_.activation(out=st[si][:], in_=K.ps[psi][:], func=AF.Copy),
                             reads=[K.b_ps[psi]], writes=[b_st[si]])
                        K.dma(u_d[m * 128:(m + 1) * 128, t0 + nt * NT: t0 + (nt + 1) * NT], st[si][:], [b_st[si]], [b_ud])
                    else:
                        hh = m - TOKW // 128
                        P.op("dve", lambda h_, psi=psi, hh=hh, nt=nt: h_.tensor_copy(out=qm[:, hh, nt * NT:(nt + 1) * NT], in_=K.ps[psi][:]),
                             reads=[K.b_ps[psi]], writes=[b_qm[hh]])
        mem_attn(K, MA, MKV, qm, b_qm, memo_d, b_md, t0, Tp, NM, MEMW, ps_s, ps_o, ps_r)
    P.barrier()
    P.release(mark)


def mixer_post(K, x_d, g3, b_g, W_out, tok_d, memo_d, D, TOKW, MEMW, T, Tp, W_glu=None, bglu=None, SUB=256):
    P = K.P
    KT = D // 128
    NTK, NMK = TOKW // 128, MEMW // 128
    mark = P.mark()
    tm = alloc_norm_tmp(K, KT, SUB)
    acc = P.sb([128, KT, Tp], F32, "acc")
    b_acc = [P.bufs_n(Tp // NT, "acc") for _ in range(KT)]
    tk = P.sb([128, NTK, Tp], BF16, "tk")
    b_tk = P.bufs_n(NTK, "tk")
    mo = P.sb([128, NMK, Tp], BF16, "mo")
    b_mo = P.buf("mo")
    b_x, b_td, b_md = P.buf("x"), P.buf("td"), P.buf("md")
    x_v = x_d.rearrange("(kt p) t -> p kt t", p=128)
    tok_v = tok_d.rearrange("(kt p) t -> p kt t", p=128)
    memo_v = memo_d.rearrange("(kt p) t -> p kt t", p=128)
    Wo_v = W_out.rearrange("(kt p) c -> p kt c", p=128)
    wso = WStream(K, KT, 256, "wout")
    ps_mm = Ring([0, 1, 2, 3])
    if W_glu is not None:
        yg = P.sb([128, NTK, Tp], BF16, "yg")
        b_yg = P.buf("yg")
        wsg = WStream(K, NTK, 256, "wglu")
        Wg_v = W_glu.rearrange("(kt p) c -> p kt c", p=128)
        gt = [P.sb([128, NT], F32, "gt") for _ in range(2)]
        b_gt = P.bufs_n(2, "gt")
        gti = Ring([0, 1])
    def loads(p_):
        t0_ = p_ * Tp
        K.dma(mo[:], memo_v[:, :, t0_:t0_ + Tp], [b_md], [b_mo])
        if W_glu is None:
            K.dma(tk[:], tok_v[:, :, t0_:t0_ + Tp], [b_td], b_tk)
        else:
            K.dma(yg[:], tok_v[:, :, t0_:t0_ + Tp], [b_td], [b_yg])
    loads(0)
    for p in range(T // Tp):
        t0 = p * Tp
        if W_glu is not None:
            for mg in range((NTK + 1) // 2):
                nm_ = min(2, NTK - mg * 2)
                w, bw = wsg.load(Wg_v, mg * 256, nm_ * 128)
                for mi in range(nm_):
                    m = mg * 2 + mi
                    for nt in range(Tp // NT):
                        psi = ps_mm.next()
                        gi = gti.next()
                        K.mm(psi, [(w[:, kt, mi * 128:(mi + 1) * 128], yg[:, kt, nt * NT:(nt + 1) * NT]) for kt in range(NTK)], [bw, b_yg])
                        P.op("act", lambda h_, psi=psi, gi=gi, m=m: h_.activation(out=gt[gi][:], in_=K.ps[psi][:], func=AF.Sigmoid, bias=bglu[:, m:m + 1]),
                             reads=[K.b_ps[psi], b_g], writes=[b_gt[gi]])
                        P.op("dve", lambda h_, gi=gi, m=m, nt=nt: h_.tensor_tensor(out=tk[:, m, nt * NT:(nt + 1) * NT], in0=gt[gi][:],
                                                                                 in1=yg[:, m, nt * NT:(nt + 1) * NT], op=ALU.mult),
                             reads=[b_gt[gi], b_yg], writes=[b_tk[m]])
        for mg in range(KT // 2):
            w, bw = wso.load(Wo_v, mg * 256)
            for mi in range(2):
                m = mg * 2 + mi
                for nt in range(Tp // NT):
                    psi = ps_mm.next()
                    pairs = [(w[:, kt, mi * 128:(mi + 1) * 128], tk[:, kt, nt * NT:(nt + 1) * NT]) for kt in range(NTK)]
                    pairs += [(w[:, NTK + kt, mi * 128:(mi + 1) * 128], mo[:, kt, nt * NT:(nt + 1) * NT]) for kt in range(NMK)]
                    K.mm(psi, pairs, [bw, b_mo] + b_tk)
                    P.op("act", lambda h_, psi=psi, m=m, nt=nt: h_.activation(out=acc[:, m, nt * NT:(nt + 1) * NT], in_=K.ps[psi][:], func=AF.Copy),
                         reads=[K.b_ps[psi]], writes=[b_acc[m][nt]])
        if p + 1 < T // Tp:
            loads(p + 1)
        finalize(K, tm, acc, b_acc, x_v, b_x, t0, Tp, g3, b_g, D, 6)
    P.barrier()
    P.release(mark)


def rope_tables(K, pos_d, invf_d, sgn_d, T):
    P = K.P
    cs = P.sb([64, T], F32, "rope_cs")
    sn = P.sb([64, T], F32, "rope_sn")
    b_r = P.buf("rope")
    mark = P.mark()
    pi_ = P.sb([64, T], I32, "pos_i")
    tmp = P.sb([64, T], F32, "rtmp")
    cf = P.sb([64, 2], F32, "rcf")
    K.dma(pi_[:], pos_d.rearrange("(o t) -> o t", o=1).broadcast_to([64, T]), [], [b_r])
    K.dma(cf[:, 0:1], invf_d, [], [b_r])
    K.dma(cf[:, 1:2], sgn_d, [], [b_r])
    MAGIC = 12582912.0
    steps = [
        lambda h: h.tensor_copy(out=sn[:], in_=pi_[:]),
        lambda h: h.tensor_scalar(out=sn[:], in0=sn[:], scalar1=cf[:, 0:1], scalar2=None, op0=ALU.mult),
        lambda h: h.tensor_scalar(out=cs[:], in0=sn[:], scalar1=0.25, scalar2=None, op0=ALU.add),
        lambda h: h.tensor_scalar(out=tmp[:], in0=sn[:], scalar1=MAGIC, scalar2=None, op0=ALU.add),
        lambda h: h.tensor_scalar(out=tmp[:], in0=tmp[:], scalar1=-MAGIC, scalar2=None, op0=ALU.add),
        lambda h: h.tensor_tensor(out=sn[:], in0=sn[:], in1=tmp[:], op=ALU.subtract),
        lambda h: h.tensor_scalar(out=tmp[:], in0=cs[:], scalar1=MAGIC, scalar2=None, op0=ALU.add),
        lambda h: h.tensor_scalar(out=tmp[:], in0=tmp[:], scalar1=-MAGIC, scalar2=None, op0=ALU.add),
        lambda h: h.tensor_tensor(out=cs[:], in0=cs[:], in1=tmp[:], op=ALU.subtract),
    ]
    P.chain("dve", steps, reads=[b_r], writes=[b_r])
    P.op("act", lambda h: h.activation(out=sn[:], in_=sn[:], func=AF.Sin, scale=2 * float(np.pi)), reads=[b_r], writes=[b_r])
    P.op("act", lambda h: h.activation(out=cs[:], in_=cs[:], func=AF.Sin, scale=2 * float(np.pi)), reads=[b_r], writes=[b_r])
    P.op("dve", lambda h: h.tensor_scalar(out=sn[:], in0=sn[:], scalar1=cf[:, 1:2], scalar2=None, op0=ALU.mult), reads=[b_r], writes=[b_r])
    P.barrier()
    P.release(mark)
    return dict(cs=cs, sn=sn, b=b_r)


def apply_rope(K, RT, psa, psb, out, t0, n, tmp, b_tmp, b_out):
    P = K.P
    P.op("dve", lambda h: h.tensor_tensor(out=tmp[0][0:64, 0:n], in0=K.ps[psa][0:64, 0:n], in1=RT["cs"][:, t0:t0 + n], op=ALU.mult),
         reads=[K.b_ps[psa], RT["b"]], writes=[b_tmp[0]])
    P.op("dve", lambda h: h.tensor_tensor(out=tmp[1][0:64, 0:n], in0=K.ps[psb][0:64, 0:n], in1=RT["sn"][:, t0:t0 + n], op=ALU.mult),
         reads=[K.b_ps[psb], RT["b"]], writes=[b_tmp[1]])
    P.op("pool", lambda h: h.tensor_tensor(out=out, in0=tmp[0][0:64, 0:n], in1=tmp[1][0:64, 0:n], op=ALU.add),
         reads=[b_tmp[0], b_tmp[1]], writes=[b_out])


def sub_rmsnorm(K, src, b_src, dst, b_dst, g_ap, b_g, nk, Tp, sq, b_sq, rstd, b_rstd, PS_M):
    P = K.P
    for nt in range(Tp // NT):
        sl = slice(nt * NT, (nt + 1) * NT)
        P.op("act", lambda h, sl=sl: h.activation(out=sq[:, :, :], in_=src[:, :, sl], func=AF.Square), reads=b_src, writes=[b_sq])
        K.rstd_from_sq(sq, nk, NT, PS_M, rstd, b_sq, b_rstd, nk * 128)

        def f(h, sl=sl):
            for kt in range(nk):
                r = h.scalar_tensor_tensor(out=dst[:, kt, sl], in0=src[:, kt, sl], scalar=g_ap[:, kt:kt + 1], in1=rstd[:],
                                           op0=ALU.mult, op1=ALU.mult)
            return r
        P.op("dve", f, reads=b_src + [b_rstd, b_g], writes=[b_dst[nt]])


def kv_stage(K, x_d, gkv_in, gkv, b_g, W_dkv, W_kr, W_uk, W_uv, RT, kn_d, kr_d, v_d, D, R, H, T, Tp, SUB=256):
    P = K.P
    KT = D // 128
    RK = R // 128
    mark = P.mark()
    tm = alloc_norm_tmp(K, KT, SUB)
    hT = P.sb([128, KT, Tp], BF16, "hT")
    b_hT = P.bufs_n(Tp // SUB, "hT")
    ck = P.sb([128, RK, Tp], F32, "ck")
    b_ck = P.bufs_n(1, "ck")
    ckn = P.sb([128, RK, Tp], BF16, "ckn")
    b_ckn = P.bufs_n(Tp // NT, "ckn")
    sq = P.sb([128, RK, NT], BF16, "sq2")
    rstd = P.sb([128, NT], F32, "rstd2")
    b_sq, b_rstd = P.buf("sq2"), P.buf("rstd2")
    wd = P.sb([128, KT, R], BF16, "wdkv")
    wkr = P.sb([128, KT, 128], BF16, "wkr")
    wuk = P.sb([128, RK, H * 128], BF16, "wuk")
    wuv = P.sb([128, RK, H * 128], BF16, "wuv")
    b_w = P.buf("kvw")
    K.dma(wd[:], W_dkv.rearrange("(kt p) c -> p kt c", p=128), [], [b_w], eng="pool")
    wkr_v = W_kr.rearrange("(kt p) c -> p kt c", p=128)
    K.dma(wkr[:, :, 0:64], wkr_v, [], [b_w], eng="pool")
    K.dma(wkr[:, :, 64:96], wkr_v[:, :, 32:64], [], [b_w], eng="pool")
    K.dma(wkr[:, :, 96:128], wkr_v[:, :, 0:32], [], [b_w], eng="pool")
    K.dma(wuk[:], W_uk.rearrange("(kt p) c -> p kt c", p=128), [], [b_w], eng="pool")
    K.dma(wuv[:], W_uv.rearrange("(kt p) c -> p kt c", p=128), [], [b_w], eng="pool")
    st = [P.sb([128, NT], BF16, "stg") for _ in range(3)]
    b_st = P.bufs_n(3, "stg")
    sti = Ring([0, 1, 2])
    rtmp = [P.sb([128, NT], F32, "rtmp") for _ in range(2)]
    b_rtmp = P.bufs_n(2, "rtmp")
    x_v = x_d.rearrange("(kt p) t -> p kt t", p=128)
    b_x, b_kn, b_kr, b_v = P.buf("x"), P.buf("kn"), P.buf("kr"), P.buf("v")
    ps_mm = Ring([0, 1, 2, 3])
    for p in range(T // Tp):
        t0 = p * Tp
        norm_in(K, tm, x_v, b_x, t0, Tp, gkv_in, b_g, hT, b_hT, D, 6)
        for nt in range(Tp // NT):
            hb = b_hT[nt * (NT // SUB):(nt + 1) * (NT // SUB)]
            sl = slice(nt * NT, (nt + 1) * NT)
            for m in range(RK):
                psi = ps_mm.next()
                K.mm(psi, [(wd[:, kt, m * 128:(m + 1) * 128], hT[:, kt, sl]) for kt in range(KT)], [b_w] + hb)
                P.op("act", lambda h_, psi=psi, m=m, sl=sl: h_.activation(out=ck[:, m, sl], in_=K.ps[psi][:], func=AF.Copy),
                     reads=[K.b_ps[psi]], writes=b_ck)
            pa, pb = ps_mm.next(), ps_mm.next()
            K.mm(pa, [(wkr[:, kt, 0:64], hT[:, kt, sl]) for kt in range(KT)], [b_w] + hb, m=64)
            K.mm(pb, [(wkr[:, kt, 64:128], hT[:, kt, sl]) for kt in range(KT)], [b_w] + hb, m=64)
            si = sti.next()
            apply_rope(K, RT, pa, pb, st[si][0:64, :], t0 + nt * NT, NT, rtmp, b_rtmp, b_st[si])
            K.dma(kr_d[:, t0 + nt * NT: t0 + (nt + 1) * NT], st[si][0:64, :], [b_st[si]], [b_kr])
        sub_rmsnorm(K, ck, b_ck, ckn, b_ckn, gkv, b_g, RK, Tp, sq, b_sq, rstd, b_rstd, 6)
        for nt in range(Tp // NT):
            sl = slice(nt * NT, (nt + 1) * NT)
            for hh in range(H):
                psi = ps_mm.next()
                K.mm(psi, [(wuk[:, kt, hh * 128:(hh + 1) * 128], ckn[:, kt, sl]) for kt in range(RK)], [b_w, b_ckn[nt]])
                si = sti.next()
                P.op("act", lambda h_, psi=psi, si=si: h_.activation(out=st[si][:], in_=K.ps[psi][:], func=AF.Copy),
                     reads=[K.b_ps[psi]], writes=[b_st[si]])
                K.dma(kn_d[hh, :, t0 + nt * NT: t0 + (nt + 1) * NT], st[si][:], [b_st[si]], [b_kn])
            for tt in range(NT // 128):
                tsl = slice(nt * NT + tt * 128, nt * NT + (tt + 1) * 128)
                CW = min(NT, H * 128)
                for cc in range(H * 128 // CW):
                    psi = ps_mm.next()
                    K.mm(psi, [(ckn[:, kt, tsl], wuv[:, kt, cc * CW:(cc + 1) * CW]) for kt in range(RK)], [b_w, b_ckn[nt]], n=CW)
                    si = sti.next()
                    P.op("dve", lambda h_, psi=psi, si=si, CW=CW: h_.tensor_copy(out=st[si][:, 0:CW], in_=K.ps[psi][:, 0:CW]),
                         reads=[K.b_ps[psi]], writes=[b_st[si]])
                    K.dma(v_d[t0 + nt * NT + tt * 128: t0 + nt * NT + (tt + 1) * 128, cc * CW:(cc + 1) * CW], st[si][:, 0:CW], [b_st[si]], [b_v])
    P.barrier()
    P.release(mark)


def mixer_pre_B(K, x_d, g2, gq, b_g, W_in, W_uq, RT, qn_d, qr_d, memo_d, MKV, D, R, H, MEMW, NM, T, Tp, SUB=256):
    P = K.P
    KT = D // 128
    RK = R // 128
    mark = P.mark()
    tm = alloc_norm_tmp(K, KT, SUB)
    hT = P.sb([128, KT, Tp], BF16, "hT")
    b_hT = P.bufs_n(Tp // SUB, "hT")
    cq = P.sb([128, RK, Tp], F32, "cq")
    b_cq = P.bufs_n(1, "cq")
    cqn = P.sb([128, RK, Tp], BF16, "cqn")
    b_cqn = P.bufs_n(Tp // NT, "cqn")
    sq = P.sb([128, RK, NT], BF16, "sq2")
    rstd = P.sb([128, NT], F32, "rstd2")
    b_sq, b_rstd = P.buf("sq2"), P.buf("rstd2")
    qm = P.sb([128, MEMW // 128, Tp], BF16, "qm")
    b_qm = P.bufs_n(MEMW // 128, "qm")
    ws = WStream(K, KT, 256, "win")
    HD = 192
    wuq = P.sb([128, RK, H, HD + 64], BF16, "wuq")
    b_wq = P.buf("wuq")
    wq_v = W_uq.rearrange("(kt p) (h e) -> p kt h e", p=128, e=HD)
    for kt in range(RK):
        K.dma(wuq[:, kt, :, 0:HD], wq_v[:, kt, :, :], [], [b_wq], eng="pool")
        K.dma(wuq[:, kt, :, HD:HD + 32], wq_v[:, kt, :, 160:192], [], [b_wq], eng="pool")
        K.dma(wuq[:, kt, :, HD + 32:HD + 64], wq_v[:, kt, :, 128:160], [], [b_wq], eng="pool")
    st = [P.sb([128, NT], BF16, "stg") for _ in range(3)]
    b_st = P.bufs_n(3, "stg")
    sti = Ring([0, 1, 2])
    rtmp = [P.sb([128, NT], F32, "rtmp") for _ in range(2)]
    b_rtmp = P.bufs_n(2, "rtmp")
    MA = alloc_mem_attn(K, NM)
    x_v = x_d.rearrange("(kt p) t -> p kt t", p=128)
    W_v = W_in.rearrange("(kt p) c -> p kt c", p=128)
    b_x, b_qn, b_qr, b_md = P.buf("x"), P.buf("qn"), P.buf("qr"), P.buf("md")
    ps_mm = Ring([0, 1, 2])
    ps_s, ps_o, ps_r = Ring([0, 1, 2]), Ring([3, 4]), Ring([5, 7])
    MT = (R + MEMW) // 128
    for p in range(T // Tp):
        t0 = p * Tp
        norm_in(K, tm, x_v, b_x, t0, Tp, g2, b_g, hT, b_hT, D, 6)
        for mg in range(MT // 2):
            w, bw = ws.load(W_v, mg * 256)
            for mi in range(2):
                m = mg * 2 + mi
                for nt in range(Tp // NT):
                    psi = ps_mm.next()
                    hb = b_hT[nt * (NT // SUB):(nt + 1) * (NT // SUB)]
                    sl = slice(nt * NT, (nt + 1) * NT)
                    K.mm(psi, [(w[:, kt, mi * 128:(mi + 1) * 128], hT[:, kt, sl]) for kt in range(KT)], [bw] + hb)
                    if m < RK:
                        P.op("act", lambda h_, psi=psi, m=m, sl=sl: h_.activation(out=cq[:, m, sl], in_=K.ps[psi][:], func=AF.Copy),
                             reads=[K.b_ps[psi]], writes=b_cq)
                    else:
                        hh = m - RK
                        P.op("dve", lambda h_, psi=psi, hh=hh, sl=sl: h_.tensor_copy(out=qm[:, hh, sl], in_=K.ps[psi][:]),
                             reads=[K.b_ps[psi]], writes=[b_qm[hh]])
        mem_attn(K, MA, MKV, qm, b_qm, memo_d, b_md, t0, Tp, NM, MEMW, ps_s, ps_o, ps_r)
        sub_rmsnorm(K, cq, b_cq, cqn, b_cqn, gq, b_g, RK, Tp, sq, b_sq, rstd, b_rstd, 6)
        for nt in range(Tp // NT):
            sl = slice(nt * NT, (nt + 1) * NT)
            for hh in range(H):
                psi = ps_mm.next()
                K.mm(psi, [(wuq[:, kt, hh, 0:128], cqn[:, kt, sl]) for kt in range(RK)], [b_wq, b_cqn[nt]])
                si = sti.next()
                P.op("act", lambda h_, psi=psi, si=si: h_.activation(out=st[si][:], in_=K.ps[psi][:], func=AF.Copy),
                     reads=[K.b_ps[psi]], writes=[b_st[si]])
                K.dma(qn_d[hh, :, t0 + nt * NT: t0 + (nt + 1) * NT], st[si][:], [b_st[si]], [b_qn])
                pa, pb = ps_mm.next(), ps_mm.next()
                K.mm(pa, [(wuq[:, kt, hh, 128:192], cqn[:, kt, sl]) for kt in range(RK)], [b_wq, b_cqn[nt]], m=64)
                K.mm(pb, [(wuq[:, kt, hh, 192:256], cqn[:, kt, sl]) for kt in range(RK)], [b_wq, b_cqn[nt]], m=64)
                si = sti.next()
                apply_rope(K, RT, pa, pb, st[si][0:64, :], t0 + nt * NT, NT, rtmp, b_rtmp, b_st[si])
                K.dma(qr_d[hh, :, t0 + nt * NT: t0 + (nt + 1) * NT], st[si][0:64, :], [b_st[si]], [b_qr])
    P.barrier()
    P.release(mark)


def mla_attn(K, qn_d, qr_d, kn_d, kr_d, v_d, knp_d, krp_d, vp_d, pbias_d, mask_d, tok_d, H, T):
    P = K.P
    mark = P.mark()
    NKT = T // 128
    QB = T // NT
    scale = 192.0 ** -0.5
    kr = P.sb([64, 2 * T], BF16, "kr")
    b_krs = P.buf("kr")
    K.dma(kr[:, 0:T], krp_d, [], [b_krs])
    K.dma(kr[:, T:2 * T], kr_d, [], [b_krs])
    pb = P.sb([128, 1], F32, "pbias")
    b_pb = P.buf("pbias")
    K.dma(pb[:], pbias_d, [], [b_pb])
    msk = P.sb([128, 4, NT], BF16, "mask")
    b_msk = P.buf("mask")
    K.dma(msk[:], mask_d.rearrange("i p q -> p i q"), [], [b_msk], eng="pool")
    kn = [P.sb([128, 2 * T], BF16, "kn") for _ in range(2)]
    vv = [P.sb([128, 2 * NKT, 128], BF16, "vv") for _ in range(2)]
    qn = [P.sb([128, T], BF16, "qn") for _ in range(2)]
    qr = [P.sb([64, T], BF16, "qr") for _ in range(2)]
    b_hd = P.bufs_n(2, "headin")
    pt = [P.sb([128, NT], BF16, "pt") for _ in range(3)]
    b_pt = P.bufs_n(3, "pt")
    pti = Ring([0, 1, 2])
    rec = [P.sb([128, NT], F32, "rec") for _ in range(2)]
    b_rec = P.bufs_n(2, "rec")
    ob = [P.sb([128, NT], BF16, "ob") for _ in range(2)]
    b_ob = P.bufs_n(2, "ob")
    b_td = P.buf("tokd")
    ps_s, ps_o, ps_r = Ring([0, 1, 2]), Ring([3, 4]), Ring([5, 6])
    fin = 0
    def load_head(h):
        s_ = h % 2
        K.dma(kn[s_][:, 0:T], knp_d[h], [], [b_hd[s_]])
        K.dma(kn[s_][:, T:2 * T], kn_d[h], [], [b_hd[s_]])
        K.dma(vv[s_][:, 0:NKT, :], vp_d[:, h * 128:(h + 1) * 128].rearrange("(t p) d -> p t d", p=128), [], [b_hd[s_]])
        K.dma(vv[s_][:, NKT:2 * NKT, :], v_d[:, h * 128:(h + 1) * 128].rearrange("(t p) d -> p t d", p=128), [], [b_hd[s_]])
        K.dma(qn[s_][:], qn_d[h], [], [b_hd[s_]])
        K.dma(qr[s_][:], qr_d[h], [], [b_hd[s_]])
    load_head(0)
    for h in range(H):
        s_ = h % 2
        if h + 1 < H:
            load_head(h + 1)
        for qb in range(QB):
            qsl = slice(qb * NT, (qb + 1) * NT)
            tiles = list(range(NKT)) + [NKT + j for j in range(4 * qb + 4)]
            po, pr = ps_o.next(), ps_r.next()
            n = len(tiles)

            def score(kt):
                psi = ps_s.next()
                ksl = slice(kt * 128, (kt + 1) * 128)
                K.mm(psi, [(kn[s_][:, ksl], qn[s_][:, qsl]), (kr[0:64, ksl], qr[s_][0:64, qsl])], [b_hd[s_], b_krs])
                return psi
            pend = [score(tiles[0])]
            if n > 1:
                pend.append(score(tiles[1]))
            for i, kt in enumerate(tiles):
                psi = pend.pop(0)
                if i + 2 < n:
                    pend.append(score(tiles[i + 2]))
                pi_ = pti.next()
                prev = kt < NKT
                if prev:
                    P.op("act", lambda h_, psi=psi, pi_=pi_: h_.activation(out=pt[pi_][:], in_=K.ps[psi][:], func=AF.Exp, scale=scale, bias=pb[:, 0:1]),
                         reads=[K.b_ps[psi], b_pb], writes=[b_pt[pi_]])
                else:
                    P.op("act", lambda h_, psi=psi, pi_=pi_: h_.activation(out=pt[pi_][:], in_=K.ps[psi][:], func=AF.Exp, scale=scale),
                         reads=[K.b_ps[psi]], writes=[b_pt[pi_]])
                    di = kt - NKT - 4 * qb
                    if di >= 0:
                        P.op("pool", lambda h_, pi_=pi_, di=di: h_.tensor_tensor(out=pt[pi_][:], in0=pt[pi_][:], in1=msk[:, di, :], op=ALU.mult),
                             reads=[b_msk], writes=[b_pt[pi_]])
                ptap = pt[pi_][:]

                def mo(h_, po=po, pr=pr, kt=kt, ptap=ptap, i=i, n=n, s_=s_):
                    h_.matmul(K.ps[po][:], lhsT=vv[s_][:, kt, :], rhs=ptap, start=(i == 0), stop=(i == n - 1))
                    return h_.matmul(K.ps[pr][:], lhsT=K.ones[:], rhs=ptap, start=(i == 0), stop=(i == n - 1))
                P.op("pe", mo, reads=[b_pt[pi_], b_hd[s_], K.b_ones], writes=[K.b_ps[po], K.b_ps[pr]])
            fi = fin % 2
            fin += 1
            P.op("dve", lambda h_, pr=pr, fi=fi: h_.reciprocal(out=rec[fi][:], in_=K.ps[pr][:]), reads=[K.b_ps[pr]], writes=[b_rec[fi]])
            P.op("dve", lambda h_, po=po, fi=fi: h_.tensor_tensor(out=ob[fi][:], in0=K.ps[po][:], in1=rec[fi][:], op=ALU.mult),
                 reads=[K.b_ps[po], b_rec[fi]], writes=[b_ob[fi]])
            K.dma(tok_d[h * 128:(h + 1) * 128, qsl], ob[fi][:], [b_ob[fi]], [b_td])
    P.barrier()
    P.release(mark)


class Cfg:
    def __init__(self, D=2048, DFF=5632, TOKW=1536, MEMW=512, NM=256, G=96, R=512, H=12, SEQ=4096, B=4, L=4, Tp=1024):
        self.D, self.DFF, self.TOKW, self.MEMW, self.NM, self.G, self.R, self.H = D, DFF, TOKW, MEMW, NM, G, R, H
        self.SEQ, self.B, self.L, self.Tp = SEQ, B, L, Tp
        self.T = SEQ // 2
        self.KT = D // 128
        self.NA = L // 2
        self.NP = G // 2
        self.NQ = G // 8
        self.RK = R // 128
        self.NTK = TOKW // 128


def gain_layout(cfg, norms, mem_norm, kv_in_norm, kv_norm, mla_q_norm, s5_b_glu):
    cols, off = [], {}

    def add(name, v):
        v = np.asarray(v, np.float32)
        n = v.shape[-1] // 128
        a = v.reshape(-1, n, 128)
        a = np.transpose(a, (2, 0, 1)).reshape(128, -1)
        off[name] = (sum(c.shape[1] for c in cols), n)
        cols.append(a)
    add("norms", norms)
    add("mem_norm", mem_norm)
    add("kv_in", kv_in_norm)
    add("kv", kv_norm)
    add("q", mla_q_norm)
    add("bglu", s5_b_glu)
    return np.ascontiguousarray(np.concatenate(cols, 1)), off


def build_segment(cfg, seg, W):
    c = cfg
    nc = bass.Bass("TRN2", target_bir_lowering=False)
    D, T, Tp = c.D, c.T, c.Tp

    def din(name, shape, dt=F32):
        return nc.dram_tensor(name, list(shape), dt, kind="ExternalInput").ap()

    def dout(name, shape, dt=F32):
        return nc.dram_tensor(name, list(shape), dt, kind="ExternalOutput").ap()

    def dtmp(name, shape, dt=F32):
        return nc.dram_tensor(name, list(shape), dt, kind="Internal").ap()
    K = KB(nc)
    P = K.P
    gshape = W["gains"].shape
    gains_d = din("gains", gshape)
    goff = W["goff"]
    gains = P.sb(list(gshape), F32, "gains")
    b_g = P.buf("gains")
    K.dma(gains[:], gains_d, [], [b_g])

    def gn(l, i):
        o = goff["norms"][0] + (l * 6 + i) * c.KT
        return gains[:, o:o + c.KT]

    def gsl(name, idx, n):
        o = goff[name][0] + idx * n
        return gains[:, o:o + n]
    x_in = din("x_in", [D, T])
    x = dout("x_out", [D, T])
    b_xc = P.buf("xcopy")
    K.dma(x, x_in, [], [b_xc])
    P.barrier()
    wts = {}

    def wt(name, l=None, j=None):
        key = (name, l, j)
        if key not in wts:
            a = W[name]
            shp = a.shape
            if l is not None:
                shp = shp[1:]
            if j is not None:
                shp = shp[1:]
            nm_ = f"{name}_{l}_{j}".replace("None", "x")
            wts[key] = (nm_, din(nm_, shp))
        return wts[key][1]

    def ffn(l, i):
        ffn_stage(K, x, gn(l, 2 * i if i == 0 else 4), gn(l, 1 if i == 0 else 5), b_g,
                  wt("ffn_w_gate", l, i), wt("ffn_w_up", l, i), wt("ffn_w_down", l, i), D, c.DFF, T, Tp)

    def s5prm(l):
        return dict(lam_s=din(f"s5lam_s{l}", [128, 3, c.NP]), lam_r=din(f"s5lam_r{l}", [128, 3, c.NP * 128]),
                    bT=din(f"s5bT{l}", [2, 128, c.NP, 128]), cP=din(f"s5cP{l}", [2, 128, c.NP, 128]))
    mem_d = din("memT", [D, c.NM])

    def pre_A(l, u_d, memo_d):
        mk = P.mark()
        MKV = mem_kv_setup(K, mem_d, gsl("mem_norm", l, c.KT), b_g, wt("mem_w_kv", l), D, c.NM, c.MEMW)
        mixer_pre_A(K, x, gn(l, 2), b_g, wt("a_w_in", l), u_d, memo_d, MKV, D, c.TOKW, c.MEMW, c.NM, T, Tp)
        P.release(mk)

    def post_A(l, yg_d, memo_d):
        mixer_post(K, x, gn(l, 3), b_g, wt("w_out", l), yg_d, memo_d, D, c.TOKW, c.MEMW, T, Tp,
                   W_glu=wt("s5_w_glu", l), bglu=gsl("bglu", l, c.NTK))

    if seg in (1, 2, 3):
        l_scan1 = {1: 0, 2: 1, 3: None}[seg]
        l_scan2 = {1: None, 2: 0, 3: 1}[seg]
        if l_scan2 is not None:
            l = l_scan2
            u_d = din("u_in", [c.TOKW, T], BF16)
            memo_d = din("memo_in", [c.MEMW, T], BF16)
            carry_in = din("carry_in", [128, 2, c.NP])
            carry_dummy = dtmp("carry_dummy", [128, 2, c.NP])
            yg_d = dtmp("yg", [c.TOKW, T], BF16)
            mk = P.mark()
            S = s5_setup(K, s5prm(l), c.NP)
            s5_scan(K, S, c.NP, T, u_d, carry_in, carry_dummy, yg_d, din(f"s5d{l}", [128, c.NQ]), True)
            P.release(mk)
            post_A(l, yg_d, memo_d)
            ffn(l, 1)
        if l_scan1 is not None:
            l = l_scan1
            ffn(l, 0)
            u_o = dout("u_out", [c.TOKW, T], BF16)
            memo_o = dout("memo_out", [c.MEMW, T], BF16)
            carry_o = dout("carry_out", [128, 2, c.NP])
            zero_c = din("zero_carry", [128, 2, c.NP])
            pre_A(l, u_o, memo_o)
            mk = P.mark()
            S = s5_setup(K, s5prm(l), c.NP)
            s5_scan(K, S, c.NP, T, u_o, zero_c, carry_o, None, None, False)
            P.release(mk)
        if seg == 3:
            RT = rope_tables(K, din("pos", [T], I32), din("invf", [64, 1]), din("sgn", [64, 1]), T)
            kv_stage(K, x, gsl("kv_in", 0, c.KT), gsl("kv", 0, c.RK), b_g, wt("w_dkv"), wt("w_kr"), wt("w_uk"), wt("w_uv"), RT,
                     dout("kn_out", [c.H, 128, T], BF16), dout("kr_out", [64, T], BF16), dout("v_out", [T, c.H * 128], BF16),
                     D, c.R, c.H, T, Tp)
    else:
        RT = rope_tables(K, din("pos", [T], I32), din("invf", [64, 1]), din("sgn", [64, 1]), T)
        kn_d, kr_d, v_d = din("kn", [c.H, 128, T], BF16), din("kr", [64, T], BF16), din("v", [T, c.H * 128], BF16)
        knp_d, krp_d, vp_d = din("knp", [c.H, 128, T], BF16), din("krp", [64, T], BF16), din("vp", [T, c.H * 128], BF16)
        pbias_d = din("pbias", [128, 1])
        mask_d = din("cmask", [4, 128, NT])
        qn_d = dtmp("qn", [c.H, 128, T], BF16)
        qr_d = dtmp("qr", [c.H, 64, T], BF16)
        memo_d = dtmp("memo", [c.MEMW, T], BF16)
        tok_d = dtmp("tok", [c.TOKW, T], BF16)
        for l in range(c.NA, c.L):
            j = l - c.NA
            ffn(l, 0)
            mk = P.mark()
            MKV = mem_kv_setup(K, mem_d, gsl("mem_norm", l, c.KT), b_g, wt("mem_w_kv", l), D, c.NM, c.MEMW)
            mixer_pre_B(K, x, gn(l, 2), gsl("q", j, c.RK), b_g, wt("b_w_in", j), wt("mla_w_uq", j), RT, qn_d, qr_d, memo_d, MKV,
                        D, c.R, c.H, c.MEMW, c.NM, T, Tp)
            P.release(mk)
            mla_attn(K, qn_d, qr_d, kn_d, kr_d, v_d, knp_d, krp_d, vp_d, pbias_d, mask_d, tok_d, c.H, T)
            mixer_post(K, x, gn(l, 3), b_g, wt("w_out", l), tok_d, memo_d, D, c.TOKW, c.MEMW, T, Tp)
            ffn(l, 1)
    P.barrier()
    P.emit()
    return nc, wts


def run_model(cfg, inp, dbg=None):
    c = cfg
    T = c.T
    NCORE = 2 * c.B
    f32 = np.float32
    gains, goff = gain_layout(c, inp["norms"], inp["mem_norm"], inp["kv_in_norm"], inp["kv_norm"], inp["mla_q_norm"], inp["s5_b_glu"])
    W = dict(inp)
    W["gains"], W["goff"] = gains, goff
    s5l = [s5_host_layout(*(np.asarray(inp[k][l], f32) for k in ("s5_lambda_re", "s5_lambda_im", "s5_log_dt", "s5_b_re", "s5_b_im",
                                                                 "s5_c_re", "s5_c_im", "s5_d"))) for l in range(c.NA)]
    xT = [np.ascontiguousarray(np.asarray(inp["x"][cid // 2, (cid % 2) * T:(cid % 2 + 1) * T, :], f32).T) for cid in range(NCORE)]
    memT = [np.ascontiguousarray(np.asarray(inp["mem"][b], f32).T) for b in range(c.B)]
    inv_freq = (10000.0 ** (-np.arange(0, 64, 2, dtype=np.float32) / 64)).astype(f32)
    invf = np.concatenate([inv_freq, inv_freq])[:, None].astype(f32) / f32(2 * np.pi)
    sgn = np.concatenate([-np.ones(32, f32), np.ones(32, f32)])[:, None]
    kk, qq = np.arange(128)[:, None], np.arange(NT)[None, :]
    cmask = np.stack([(qq >= 128 * i + kk).astype(f32) for i in range(4)], 0)
    zero_carry = np.zeros((128, 2, c.NP), f32)

    def launch(seg, per_core):
        nc, wts = build_segment(c, seg, W)
        shared = {"gains": gains}
        for (name, l, j), (nm_, ap) in wts.items():
            a = inp[name]
            if l is not None:
                a = a[l]
            if j is not None:
                a = a[j]
            shared[nm_] = np.ascontiguousarray(np.asarray(a, f32))
        maps = []
        for cid in range(NCORE):
            m = dict(shared)
            m["memT"] = memT[cid // 2]
            m.update(per_core[cid])
            maps.append(m)
        res = run_bass_kernel_spmd(nc, maps, core_ids=list(range(NCORE)))
        if dbg is not None:
            dbg[seg] = res.results
        return res.results

    def s5in(l):
        return {f"s5lam_s{l}": s5l[l]["lam_s"], f"s5lam_r{l}": s5l[l]["lam_r"], f"s5bT{l}": s5l[l]["bT"], f"s5cP{l}": s5l[l]["cP"]}
    pos = [np.ascontiguousarray(np.asarray(inp["positions"][cid // 2, (cid % 2) * T:(cid % 2 + 1) * T], np.int32)) for cid in range(NCORE)]
    r = launch(1, [dict(x_in=xT[cid], zero_carry=zero_carry, **s5in(0)) for cid in range(NCORE)])
    for seg in (2, 3):
        l2 = seg - 2
        pc = []
        for cid in range(NCORE):
            cin = r[cid - 1]["carry_out"] if cid % 2 == 1 else zero_carry
            d = dict(x_in=r[cid]["x_out"], u_in=r[cid]["u_out"], memo_in=r[cid]["memo_out"], carry_in=cin,
                     zero_carry=zero_carry, **s5in(l2))
            d[f"s5d{l2}"] = s5l[l2]["d_s"]
            if seg == 2:
                d.update(s5in(1))
            else:
                d.update(pos=pos[cid], invf=invf, sgn=sgn)
            pc.append(d)
        r = launch(seg, pc)
    pc = []
    for cid in range(NCORE):
        prev = r[cid - 1] if cid % 2 == 1 else r[cid]
        pc.append(dict(x_in=r[cid]["x_out"], kn=r[cid]["kn_out"], kr=r[cid]["kr_out"], v=r[cid]["v_out"],
                       knp=prev["kn_out"], krp=prev["kr_out"], vp=prev["v_out"],
                       pbias=np.full((128, 1), 0.0 if cid % 2 == 1 else -30000.0, f32), cmask=cmask,
                       pos=pos[cid], invf=invf, sgn=sgn))
    r = launch(4, pc)
    out = np.empty((c.B, c.SEQ, c.D), f32)
    for cid in range(NCORE):
        out[cid // 2, (cid % 2) * T:(cid % 2 + 1) * T, :] = r[cid]["x_out"].T
    return out


def build_fused(cfg, W):
    c = cfg
    nc = bass.Bass("TRN2", target_bir_lowering=False)
    D, T, Tp = c.D, c.T, c.Tp

    def din(name, shape, dt=F32):
        return nc.dram_tensor(name, list(shape), dt, kind="ExternalInput").ap()

    def dout(name, shape, dt=F32):
        return nc.dram_tensor(name, list(shape), dt, kind="ExternalOutput").ap()

    def dtmp(name, shape, dt=F32):
        return nc.dram_tensor(name, list(shape), dt, kind="Internal").ap()
    K = KB(nc)
    P = K.P
    gshape = W["gains"].shape
    goff = W["goff"]
    gains = P.sb(list(gshape), F32, "gains")
    b_g = P.buf("gains")
    K.dma(gains[:], din("gains", gshape), [], [b_g])
    cflag = P.sb([128, 1], F32, "cflag")
    K.dma(cflag[:], din("cflag", [128, 1]), [], [b_g])

    def gn(l, i):
        o = goff["norms"][0] + (l * 6 + i) * c.KT
        return gains[:, o:o + c.KT]

    def gsl(name, idx, n):
        o = goff[name][0] + idx * n
        return gains[:, o:o + n]
    x = dout("x_out", [D, T])
    xp = dtmp("xp", [D, T])
    b_xc = P.buf("xcopy")
    K.dma(x, din("x_in", [D, T]), [], [b_xc])
    K.dma(xp, din("xp_in", [D, T]), [], [b_xc])
    P.barrier()
    wts = {}

    def wt(name, l=None, j=None):
        key = (name, l, j)
        if key not in wts:
            shp = W[name].shape
            if l is not None:
                shp = shp[1:]
            if j is not None:
                shp = shp[1:]
            nm_ = f"{name}_{l}_{j}".replace("None", "x")
            wts[key] = (nm_, din(nm_, shp))
        return wts[key][1]

    def fspec(l, i):
        return (gn(l, 0 if i == 0 else 4), gn(l, 1 if i == 0 else 5), wt("ffn_w_gate", l, i), wt("ffn_w_up", l, i), wt("ffn_w_down", l, i))

    def ffn(l, i, xx):
        ffn_multi(K, xx, [fspec(l, i)], b_g, D, c.DFF, T, Tp)

    def ffn2(l, xx):
        ffn_multi(K, xx, [fspec(l, 1), fspec(l + 1, 0)], b_g, D, c.DFF, T, Tp)
    s5p = [dict(lam_s=din(f"s5lam_s{l}", [128, 3, c.NP]), lam_r=din(f"s5lam_r{l}", [128, 3, c.NP * 128]),
                bT=din(f"s5bT{l}", [2, 128, c.NP, 128]), cP=din(f"s5cP{l}", [2, 128, c.NP, 128]),
                d=din(f"s5d{l}", [128, c.NQ])) for l in range(c.NA)]
    mem_d = din("memT", [D, c.NM])
    u_d = dtmp("u", [c.TOKW, T], BF16)
    memo_d = dtmp("memo", [c.MEMW, T], BF16)
    yg_d = dtmp("yg", [c.TOKW, T], BF16)
    zero_c = din("zero_carry", [128, 2, c.NP])
    carryA = [dtmp(f"carryA{l}", [128, 2, c.NP]) for l in range(c.NA)]
    carry_dummy = dtmp("carry_dummy", [128, 2, c.NP])
    kn_d, kr_d, v_d = dtmp("kn", [c.H, 128, T], BF16), dtmp("kr", [64, T], BF16), dtmp("v", [T, c.H * 128], BF16)
    knp_d, krp_d, vp_d = dtmp("knp", [c.H, 128, T], BF16), dtmp("krp", [64, T], BF16), dtmp("vp", [T, c.H * 128], BF16)
    invf_d, sgn_d = din("invf", [64, 1]), din("sgn", [64, 1])

    streams = [dict(x=xp, u=u_d, memo=memo_d, yg=yg_d),
               dict(x=x, u=dtmp("u2", [c.TOKW, T], BF16), memo=dtmp("memo2", [c.MEMW, T], BF16), yg=dtmp("yg2", [c.TOKW, T], BF16))]

    def pre_scan(l, st, MKV):
        mixer_pre_A(K, st["x"], gn(l, 2), b_g, wt("a_w_in", l), st["u"], st["memo"], MKV, D, c.TOKW, c.MEMW, c.NM, T, Tp)

    def post_scan(l, st):
        mixer_post(K, st["x"], gn(l, 3), b_g, wt("w_out", l), st["yg"], st["memo"], D, c.TOKW, c.MEMW, T, Tp,
                   W_glu=wt("s5_w_glu", l), bglu=gsl("bglu", l, c.NTK))

    def kv(xx, pos_name, kn_, kr_, v_):
        mk = P.mark()
        RT = rope_tables(K, din(pos_name, [T], I32), invf_d, sgn_d, T)
        kv_stage(K, xx, gsl("kv_in", 0, c.KT), gsl("kv", 0, c.RK), b_g, wt("w_dkv"), wt("w_kr"), wt("w_uk"), wt("w_uv"), RT,
                 kn_, kr_, v_, D, c.R, c.H, T, Tp)
        return mk, RT
    for st in streams:
        ffn(0, 0, st["x"])
    for l in range(c.NA):
        mk = P.mark()
        MKV = mem_kv_setup(K, mem_d, gsl("mem_norm", l, c.KT), b_g, wt("mem_w_kv", l), D, c.NM, c.MEMW)
        for st in streams:
            pre_scan(l, st, MKV)
        P.release(mk)
        mk = P.mark()
        S = s5_setup(K, s5p[l], c.NP)
        s5_scan(K, S, c.NP, T, streams[0]["u"], zero_c, carryA[l], streams[0]["yg"], s5p[l]["d"], True)
        s5_scan(K, S, c.NP, T, streams[1]["u"], carryA[l], carry_dummy, streams[1]["yg"], s5p[l]["d"], True, flag=cflag, b_flag=b_g)
        P.release(mk)
        for si, st in enumerate(streams):
            post_scan(l, st)
            if l < c.NA - 1:
                ffn2(l, st["x"])
            else:
                ffn(l, 1, st["x"])
                if si == 0:
                    mk, _ = kv(xp, "posp", knp_d, krp_d, vp_d)
                    P.release(mk)
    mk, RT = kv(x, "pos", kn_d, kr_d, v_d)
    pbias_d = din("pbias", [128, 1])
    mask_d = din("cmask", [4, 128, NT])
    qn_d = dtmp("qn", [c.H, 128, T], BF16)
    qr_d = dtmp("qr", [c.H, 64, T], BF16)
    tok_d = dtmp("tok", [c.TOKW, T], BF16)
    for l in range(c.NA, c.L):
        j = l - c.NA
        if l == c.NA:
            ffn(l, 0, x)
        mk2 = P.mark()
        MKV = mem_kv_setup(K, mem_d, gsl("mem_norm", l, c.KT), b_g, wt("mem_w_kv", l), D, c.NM, c.MEMW)
        mixer_pre_B(K, x, gn(l, 2), gsl("q", j, c.RK), b_g, wt("b_w_in", j), wt("mla_w_uq", j), RT, qn_d, qr_d, memo_d, MKV,
                    D, c.R, c.H, c.MEMW, c.NM, T, Tp)
        P.release(mk2)
        mla_attn(K, qn_d, qr_d, kn_d, kr_d, v_d, knp_d, krp_d, vp_d, pbias_d, mask_d, tok_d, c.H, T)
        mixer_post(K, x, gn(l, 3), b_g, wt("w_out", l), tok_d, memo_d, D, c.TOKW, c.MEMW, T, Tp)
        if l == c.L - 1:
            ffn(l, 1, x)
        else:
            ffn2(l, x)
    P.barrier()
    P.emit()
    return nc, wts


def run_fused(cfg, inp):
    c = cfg
    T = c.T
    NCORE = 2 * c.B
    f32 = np.float32
    gains, goff = gain_layout(c, inp["norms"], inp["mem_norm"], inp["kv_in_norm"], inp["kv_norm"], inp["mla_q_norm"], inp["s5_b_glu"])
    W = dict(inp)
    W["gains"], W["goff"] = gains, goff
    nc, wts = build_fused(c, W)
    shared = {"gains": gains}
    for (name, l, j), (nm_, ap) in wts.items():
        a = inp[name]
        if l is not None:
            a = a[l]
        if j is not None:
            a = a[j]
        shared[nm_] = np.ascontiguousarray(np.asarray(a, f32))
    for l in range(c.NA):
        lay = s5_host_layout(*(np.asarray(inp[k][l], f32) for k in ("s5_lambda_re", "s5_lambda_im", "s5_log_dt", "s5_b_re", "s5_b_im",
                                                                    "s5_c_re", "s5_c_im", "s5_d")))
        shared.update({f"s5lam_s{l}": lay["lam_s"], f"s5lam_r{l}": lay["lam_r"], f"s5bT{l}": lay["bT"], f"s5cP{l}": lay["cP"],
                       f"s5d{l}": lay["d_s"]})
    inv_freq = (10000.0 ** (-np.arange(0, 64, 2, dtype=np.float32) / 64)).astype(f32)
    shared["invf"] = np.concatenate([inv_freq, inv_freq])[:, None].astype(f32) / f32(2 * np.pi)
    shared["sgn"] = np.concatenate([-np.ones(32, f32), np.ones(32, f32)])[:, None]
    kk, qq = np.arange(128)[:, None], np.arange(NT)[None, :]
    shared["cmask"] = np.stack([(qq >= 128 * i + kk).astype(f32) for i in range(4)], 0)
    shared["zero_carry"] = np.zeros((128, 2, c.NP), f32)
    xa = np.asarray(inp["x"], f32)
    pa = np.asarray(inp["positions"], np.int32)
    maps = []
    for cid in range(NCORE):
        b, hh = divmod(cid, 2)
        m = dict(shared)
        m["memT"] = np.ascontiguousarray(np.asarray(inp["mem"][b], f32).T)
        m["x_in"] = np.ascontiguousarray(xa[b, hh * T:(hh + 1) * T, :].T)
        m["xp_in"] = np.ascontiguousarray(xa[b, 0:T, :].T)
        m["pos"] = np.ascontiguousarray(pa[b, hh * T:(hh + 1) * T])
        m["posp"] = np.ascontiguousarray(pa[b, 0:T])
        m["cflag"] = np.full((128, 1), float(hh), f32)
        m["pbias"] = np.full((128, 1), 0.0 if hh == 1 else -30000.0, f32)
        maps.append(m)
    res = run_bass_kernel_spmd(nc, maps, core_ids=list(range(NCORE)))
    out = np.empty((c.B, c.SEQ, c.D), f32)
    for cid in range(NCORE):
        b, hh = divmod(cid, 2)
        out[b, hh * T:(hh + 1) * T, :] = res.results[cid]["x_out"].T
    return out


def kernel(**inputs):
    return run_fused(Cfg(), inputs)
```

```python
import numpy as np
import concourse.bass as bass
import concourse.mybir as mybir
from concourse.bass_utils import run_bass_kernel_spmd

F32 = mybir.dt.float32
BF16 = mybir.dt.bfloat16
I32 = mybir.dt.int32
AF = mybir.ActivationFunctionType
ALU = mybir.AluOpType
AX = mybir.AxisListType

ENGS = ("pe", "act", "dve", "pool", "sp")
NDMASEM = 12


class Buf:
    __slots__ = ("name", "w", "r")

    def __init__(self, name):
        self.name = name
        self.w = None
        self.r = []


class Op:
    __slots__ = ("eng", "fn", "deps", "sig", "dma", "semi", "semv", "idx", "prev_semv")

    def __init__(self, eng, fn, dma):
        self.eng = eng
        self.fn = fn
        self.dma = dma
        self.deps = []
        self.sig = False
        self.semi = None
        self.semv = None
        self.prev_semv = None


class Prog:
    def __init__(self, nc):
        self.nc = nc
        self.ops = {e: [] for e in ENGS}
        self.all_ops = []
        self.sb_off = 16384 + 2048
        self.sb_cap = 16384 + 212000
        self.sb_hi = 0
        self.ntens = 0
        self.dma_rr = 0
        self.dma_sem_total = [0] * NDMASEM
        self.dma_sem_last = [None] * NDMASEM
        self.bufs = []

    def sb(self, shape, dtype, name=None):
        nbytes = int(np.prod(shape[1:])) * mybir.dt.size(dtype)
        nbytes = (nbytes + 63) // 64 * 64
        off = self.sb_off
        assert off + nbytes <= self.sb_cap, f"SBUF overflow {off}+{nbytes} ({name})"
        self.sb_off += nbytes
        self.sb_hi = max(self.sb_hi, self.sb_off)
        self.ntens += 1
        return self.nc.alloc_sbuf_tensor_at(f"t{self.ntens}_{name or ''}", list(shape), dtype, offset=off)

    def mark(self):
        return self.sb_off

    def release(self, mark):
        self.sb_off = mark

    def buf(self, name="b"):
        b = Buf(name)
        self.bufs.append(b)
        return b

    def bufs_n(self, n, name="b"):
        return [self.buf(name) for _ in range(n)]

    def op(self, eng, fn, reads=(), writes=(), dma=0):
        o = Op(eng, fn, dma)
        deps = set()
        for b in reads:
            if b.w is not None:
                deps.add(b.w)
        for b in writes:
            if b.w is not None:
                deps.add(b.w)
            for r in b.r:
                deps.add(r)
        for b in reads:
            b.r.append(o)
        for b in writes:
            b.w = o
            b.r = []
        deps.discard(o)
        for d in sorted(deps, key=lambda x: x.idx):
            if d.eng == "pe" and eng == "pe" and not d.dma and not dma:
                continue
            o.deps.append(d)
            d.sig = True
        if dma:
            o.sig = True
            k = self.dma_rr % NDMASEM
            self.dma_rr += 1
            o.semi = ("d", k)
            o.prev_semv = self.dma_sem_total[k]
            self.dma_sem_total[k] += 16 * dma
            o.semv = self.dma_sem_total[k]
            self.dma_sem_last[k] = o
        o.idx = len(self.all_ops)
        self.ops[eng].append(o)
        self.all_ops.append(o)
        return o

    def chain(self, eng, fns, reads=(), writes=()):
        c = self.buf("chain")
        o = None
        for f in fns:
            o = self.op(eng, f, reads=list(reads) + [c], writes=list(writes) + [c])
        return o

    def barrier(self):
        last = []
        for e in ENGS:
            if self.ops[e]:
                last.append(self.ops[e][-1])
        last += [o for o in self.dma_sem_last if o is not None]
        for d in last:
            d.sig = True
        for e in ENGS:
            o = Op(e, lambda h: None, 0)
            o.deps = list(last)
            o.idx = len(self.all_ops)
            self.ops[e].append(o)
            self.all_ops.append(o)
        for bb in self.bufs:
            bb.w = None
            bb.r = []

    def emit(self):
        nc = self.nc
        self.sems = {e: nc.alloc_semaphore(f"s_{e}") for e in ENGS}
        self.dsems = [nc.alloc_semaphore(f"s_dma{i}") for i in range(NDMASEM)]
        cnt = {e: 0 for e in ENGS}
        for o in self.all_ops:
            if not o.dma and o.sig:
                cnt[o.eng] += 1
                o.semi = ("e", o.eng)
                o.semv = cnt[o.eng]
        handles = {"pe": "tensor", "act": "scalar", "dve": "vector", "pool": "gpsimd", "sp": "sync"}

        def emit_engine(e, h):
            known = {}
            for o in self.ops[e]:
                waits = {}
                for d in o.deps:
                    waits[d.semi] = max(waits.get(d.semi, 0), d.semv)
                if o.dma and o.prev_semv:
                    waits[o.semi] = max(waits.get(o.semi, 0), o.prev_semv)
                for key, v in waits.items():
                    if known.get(key, 0) >= v:
                        continue
                    known[key] = v
                    sem = self.dsems[key[1]] if key[0] == "d" else self.sems[key[1]]
                    h.wait_ge(sem, v)
                r = o.fn(h)
                if o.dma:
                    sem = self.dsems[o.semi[1]]
                    assert len(r) == o.dma, (len(r), o.dma)
                    for ins in r:
                        ins.then_inc(sem, 16)
                elif o.sig:
                    if r is None:
                        r = h.nop()
                    r.then_inc(self.sems[e], 1)

        with nc.Block() as block:
            for e in ENGS:
                getattr(block, handles[e])(lambda h, e=e: emit_engine(e, h))


NT = 512
EPS = 1e-6


class KB:
    def __init__(self, nc):
        self.nc = nc
        self.P = Prog(nc)
        P = self.P
        self.ps = [nc.alloc_psum_tensor(f"psb{i}", [128, NT], F32) for i in range(8)]
        self.b_ps = P.bufs_n(8, "ps")
        self.ones = P.sb([128, 128], BF16, "ones")
        self.b_ones = P.buf("ones")
        P.op("dve", lambda h: h.memset(self.ones[:], 1.0), writes=[self.b_ones])
        self.init_consts()

    def mm(self, psi, pairs, reads, n=NT, m=128):
        ps = self.ps[psi]

        def f(h):
            L = len(pairs)
            for i, (a, b) in enumerate(pairs):
                r = h.matmul(ps[0:m, 0:n], lhsT=a, rhs=b, start=(i == 0), stop=(i == L - 1))
            return r
        return self.P.op("pe", f, reads=reads, writes=[self.b_ps[psi]])

    def dma(self, out, in_, reads, writes, eng="sp"):
        return self.P.op(eng, lambda h: [h.dma_start(out=out, in_=in_)], reads=reads, writes=writes, dma=1)

    def rstd_from_sq(self, sq, nk, n, psi, rstd, b_sq, b_rstd, dim):
        P = self.P
        ps = self.ps[psi]

        def mm(h):
            for k in range(nk):
                r = h.matmul(ps[:, 0:n], lhsT=self.ones[:], rhs=sq[:, k, 0:n], start=(k == 0), stop=(k == nk - 1))
            return r
        P.op("pe", mm, reads=[b_sq, self.b_ones], writes=[self.b_ps[psi]])
        P.op("act", lambda h: h.activation(out=rstd[:, 0:n], in_=ps[:, 0:n], func=AF.Sqrt, scale=1.0 / dim, bias=self.eps_ap()),
             reads=[self.b_ps[psi]], writes=[b_rstd])
        P.op("dve", lambda h: h.reciprocal(out=rstd[:, 0:n], in_=rstd[:, 0:n]), reads=[b_rstd], writes=[b_rstd])

    def eps_ap(self):
        return self.epsT[:, 0:1]

    def init_consts(self):
        P = self.P
        self.epsT = P.sb([128, 1], F32, "eps")
        self.b_eps = P.buf("eps")
        P.op("dve", lambda h: h.memset(self.epsT[:], EPS), writes=[self.b_eps])
        self.negpi = P.sb([128, 1], F32, "negpi")
        P.op("dve", lambda h: h.memset(self.negpi[:], -float(np.pi)), writes=[self.b_eps])


def ffn_stage(K, x_d, gin, gout, b_g, Wg, Wu, Wd, D, DFF, T, Tp, SUB=128, GW=256, res_scale=0.5):
    P = K.P
    KT = D // 128
    GC = GW // 128
    NTn = Tp // NT
    NG = DFF // GW
    mark = P.mark()
    hT = P.sb([128, KT, Tp], BF16, "hT")
    b_hT = P.bufs_n(Tp // SUB, "hT")
    acc = P.sb([128, KT, Tp], F32, "acc")
    b_acc = [P.bufs_n(NTn, "acc") for _ in range(KT)]
    xs = [P.sb([128, KT, SUB], F32, "xs") for _ in range(2)]
    b_xs = P.bufs_n(2, "xs")
    sq = P.sb([128, KT, SUB], BF16, "sq")
    b_sq = P.buf("sq")
    rstd = P.sb([128, SUB], F32, "rstd")
    b_rstd = P.buf("rstd")
    wg = [P.sb([128, KT, GW], BF16, "wg") for _ in range(2)]
    wu = [P.sb([128, KT, GW], BF16, "wu") for _ in range(2)]
    wd = [P.sb([128, GC, D], BF16, "wd") for _ in range(2)]
    b_wg = P.bufs_n(2, "wg")
    b_wu = P.bufs_n(2, "wu")
    b_wd = P.bufs_n(2, "wd")
    hid = [P.sb([128, GC, Tp], BF16, "hid") for _ in range(2)]
    b_hid = [[P.bufs_n(NTn, "hid") for _ in range(GC)] for _ in range(2)]
    sg = [P.sb([128, NT], F32, "sg") for _ in range(2)]
    b_sg = P.bufs_n(2, "sg")
    b_x = P.buf("xdram")
    gsc = P.sb([128, KT], F32, "gsc")
    b_gsc = P.buf("gsc")
    P.op("dve", lambda h: h.tensor_scalar(out=gsc[:], in0=gout, scalar1=float(res_scale), scalar2=None, op0=ALU.mult),
         reads=[b_g], writes=[b_gsc])
    Wg_v = Wg.rearrange("(kt p) c -> p kt c", p=128)
    Wu_v = Wu.rearrange("(kt p) c -> p kt c", p=128)
    Wd_v = Wd.rearrange("(c p) d -> p c d", p=128)
    x_v = x_d.rearrange("(kt p) t -> p kt t", p=128)
    PS_G, PS_U, PS_D, PS_M = (0, 1), (2, 3), (4, 5), 6
    cnt = {"gu": 0, "d": 0, "xs": 0}

    for p in range(T // Tp):
        t0 = p * Tp
        for s in range(Tp // SUB):
            xi = cnt["xs"] % 2
            cnt["xs"] += 1
            K.dma(xs[xi][:], x_v[:, :, t0 + s * SUB: t0 + (s + 1) * SUB], [b_x], [b_xs[xi]])
            P.op("act", lambda h, xi=xi: h.activation(out=sq[:], in_=xs[xi][:], func=AF.Square),
                 reads=[b_xs[xi]], writes=[b_sq])
            K.rstd_from_sq(sq, KT, SUB, PS_M, rstd, b_sq, b_rstd, D)

            def nrm(h, xi=xi, s=s):
                for kt in range(KT):
                    r = h.scalar_tensor_tensor(out=hT[:, kt, s * SUB:(s + 1) * SUB], in0=xs[xi][:, kt, :],
                                               scalar=gin[:, kt:kt + 1], in1=rstd[:], op0=ALU.mult, op1=ALU.mult)
                return r
            P.op("dve", nrm, reads=[b_xs[xi], b_rstd, b_g], writes=[b_hT[s]])

        def load_w(j):
            sl = j % 2
            K.dma(wg[sl][:], Wg_v[:, :, j * GW:(j + 1) * GW], [], [b_wg[sl]], eng="pool")
            K.dma(wu[sl][:], Wu_v[:, :, j * GW:(j + 1) * GW], [], [b_wu[sl]], eng="pool")
            K.dma(wd[sl][:], Wd_v[:, j * GC:(j + 1) * GC, :], [], [b_wd[sl]], eng="pool")

        def gateup(j):
            sl = j % 2
            for c in range(GC):
                for nt in range(NTn):
                    gi = cnt["gu"] % 2
                    cnt["gu"] += 1
                    pg, pu = PS_G[gi], PS_U[gi]
                    hbufs = b_hT[nt * (NT // SUB):(nt + 1) * (NT // SUB)]

                    K.mm(pg, [(wg[sl][:, kt, c * 128:(c + 1) * 128], hT[:, kt, nt * NT:(nt + 1) * NT]) for kt in range(KT)],
                         [b_wg[sl]] + hbufs)
                    K.mm(pu, [(wu[sl][:, kt, c * 128:(c + 1) * 128], hT[:, kt, nt * NT:(nt + 1) * NT]) for kt in range(KT)],
                         [b_wu[sl]] + hbufs)
                    P.op("act", lambda h, gi=gi, pg=pg: h.activation(out=sg[gi][:], in_=K.ps[pg][:], func=AF.Silu),
                         reads=[K.b_ps[pg]], writes=[b_sg[gi]])
                    P.op("dve", lambda h, gi=gi, pu=pu, sl=sl, c=c, nt=nt: h.tensor_tensor(
                        out=hid[sl][:, c, nt * NT:(nt + 1) * NT], in0=sg[gi][:], in1=K.ps[pu][:], op=ALU.mult),
                        reads=[b_sg[gi], K.b_ps[pu]], writes=[b_hid[sl][c][nt]])

        def down(j):
            sl = j % 2
            for m in range(KT):
                for nt in range(NTn):
                    di = cnt["d"] % 2
                    cnt["d"] += 1
                    pd = PS_D[di]

                    K.mm(pd, [(wd[sl][:, c, m * 128:(m + 1) * 128], hid[sl][:, c, nt * NT:(nt + 1) * NT]) for c in range(GC)],
                         [b_wd[sl]] + [b_hid[sl][c][nt] for c in range(GC)])
                    if j == 0:
                        P.op("dve", lambda h, m=m, nt=nt, pd=pd: h.tensor_copy(out=acc[:, m, nt * NT:(nt + 1) * NT], in_=K.ps[pd][:]),
                             reads=[K.b_ps[pd]], writes=[b_acc[m][nt]])
                    else:
                        P.op("dve", lambda h, m=m, nt=nt, pd=pd: h.tensor_tensor(
                            out=acc[:, m, nt * NT:(nt + 1) * NT], in0=acc[:, m, nt * NT:(nt + 1) * NT], in1=K.ps[pd][:], op=ALU.add),
                            reads=[K.b_ps[pd], b_acc[m][nt]], writes=[b_acc[m][nt]])

        load_w(0)
        gateup(0)
        for j in range(NG):
            if j + 1 < NG:
                load_w(j + 1)
                gateup(j + 1)
            down(j)

        for s in range(Tp // SUB):
            nt = (s * SUB) // NT
            accb = [b_acc[m][nt] for m in range(KT)]
            xi = cnt["xs"] % 2
            cnt["xs"] += 1
            K.dma(xs[xi][:], x_v[:, :, t0 + s * SUB: t0 + (s + 1) * SUB], [b_x], [b_xs[xi]])
            P.op("act", lambda h, s=s: h.activation(out=sq[:], in_=acc[:, :, s * SUB:(s + 1) * SUB], func=AF.Square),
                 reads=accb, writes=[b_sq])
            K.rstd_from_sq(sq, KT, SUB, PS_M, rstd, b_sq, b_rstd, D)

            def fin1(h, s=s):
                for kt in range(KT):
                    r = h.scalar_tensor_tensor(out=acc[:, kt, s * SUB:(s + 1) * SUB], in0=acc[:, kt, s * SUB:(s + 1) * SUB],
                                               scalar=gsc[:, kt:kt + 1], in1=rstd[:], op0=ALU.mult, op1=ALU.mult)
                return r
            P.op("dve", fin1, reads=accb + [b_rstd, b_gsc], writes=accb)
            P.op("pool", lambda h, xi=xi, s=s: h.tensor_tensor(
                out=xs[xi][:], in0=acc[:, :, s * SUB:(s + 1) * SUB], in1=xs[xi][:], op=ALU.add),
                reads=accb + [b_xs[xi]], writes=[b_xs[xi]])
            K.dma(x_v[:, :, t0 + s * SUB: t0 + (s + 1) * SUB], xs[xi][:], [b_xs[xi]], [b_x])
    P.barrier()
    P.release(mark)


def ffn_multi(K, x_d, specs, b_g, D, DFF, T, Tp, SUB=128, GW=256, res_scale=0.5):
    P = K.P
    KT = D // 128
    GC = GW // 128
    NTn = Tp // NT
    NG = DFF // GW
    NS = Tp // SUB
    mark = P.mark()
    hT = P.sb([128, KT, Tp], BF16, "hT")
    b_hT = P.bufs_n(NS, "hT")
    acc = P.sb([128, KT, Tp], F32, "acc")
    b_acc = [P.bufs_n(NTn, "acc") for _ in range(KT)]
    fixed = (KT * Tp * 6 + 2 * KT * SUB * 2 + 2 * SUB * 4 + 2 * (2 * KT * GW * 2 + GC * D * 2) + 2 * GC * Tp * 2 + 2 * NT * 4
             + len(specs) * KT * 4 + 2048)
    NXS = 4 if P.sb_cap - P.sb_off - fixed >= 4 * KT * SUB * 4 else 2
    xs = [P.sb([128, KT, SUB], F32, "xs") for _ in range(NXS)]
    b_xs = P.bufs_n(NXS, "xs")
    sq = [P.sb([128, KT, SUB], BF16, "sq") for _ in range(2)]
    b_sq = P.bufs_n(2, "sq")
    rstd = [P.sb([128, SUB], F32, "rstd") for _ in range(2)]
    b_rstd = P.bufs_n(2, "rstd")
    wg = [P.sb([128, KT, GW], BF16, "wg") for _ in range(2)]
    wu = [P.sb([128, KT, GW], BF16, "wu") for _ in range(2)]
    wd = [P.sb([128, GC, D], BF16, "wd") for _ in range(2)]
    b_wg, b_wu, b_wd = P.bufs_n(2, "wg"), P.bufs_n(2, "wu"), P.bufs_n(2, "wd")
    hid = [P.sb([128, GC, Tp], BF16, "hid") for _ in range(2)]
    b_hid = [[P.bufs_n(NTn, "hid") for _ in range(GC)] for _ in range(2)]
    sg = [P.sb([128, NT], F32, "sg") for _ in range(2)]
    b_sg = P.bufs_n(2, "sg")
    b_x = P.buf("xdram")
    gscs = []
    b_gsc = P.buf("gsc")
    for (gin, gout, _, _, _) in specs:
        g_ = P.sb([128, KT], F32, "gsc")
        P.op("dve", lambda h, g_=g_, gout=gout: h.tensor_scalar(out=g_[:], in0=gout, scalar1=float(res_scale), scalar2=None, op0=ALU.mult),
             reads=[b_g], writes=[b_gsc])
        gscs.append(g_)
    x_v = x_d.rearrange("(kt p) t -> p kt t", p=128)
    views = [(Wg.rearrange("(kt p) c -> p kt c", p=128), Wu.rearrange("(kt p) c -> p kt c", p=128), Wd.rearrange("(c p) d -> p c d", p=128))
             for (_, _, Wg, Wu, Wd) in specs]
    PS_G, PS_U, PS_D, PS_M = (0, 1), (2, 3), (4, 5, 7), (6, 6)
    cnt = {"gu": 0, "d": 0, "xs": 0, "sq": 0}
    jobs = [(f, p) for f in range(len(specs)) for p in range(T // Tp)]

    def rstd_calc(src_ap, reads):
        qi = cnt["sq"] % 2
        cnt["sq"] += 1
        P.op("act", lambda h: h.activation(out=sq[qi][:], in_=src_ap, func=AF.Square), reads=reads, writes=[b_sq[qi]])
        K.rstd_from_sq(sq[qi], KT, SUB, PS_M[qi], rstd[qi], b_sq[qi], b_rstd[qi], D)
        return qi

    def norm(k):
        f, p = jobs[k]
        gin = specs[f][0]
        t0 = p * Tp
        for s_ in range(NS):
            xi = cnt["xs"] % 2
            cnt["xs"] += 1
            K.dma(xs[xi][:], x_v[:, :, t0 + s_ * SUB: t0 + (s_ + 1) * SUB], [b_x], [b_xs[xi]])
            qi = rstd_calc(xs[xi][:], [b_xs[xi]])

            def nrm(h, xi=xi, s_=s_, qi=qi):
                for kt in range(KT):
                    r = h.scalar_tensor_tensor(out=hT[:, kt, s_ * SUB:(s_ + 1) * SUB], in0=xs[xi][:, kt, :],
                                               scalar=gin[:, kt:kt + 1], in1=rstd[qi][:], op0=ALU.mult, op1=ALU.mult)
                return r
            P.op("dve", nrm, reads=[b_xs[xi], b_rstd[qi], b_g], writes=[b_hT[s_]])

    def load_w(k, j):
        f, _ = jobs[k]
        Wg_v, Wu_v, Wd_v = views[f]
        sl = j % 2
        K.dma(wg[sl][:], Wg_v[:, :, j * GW:(j + 1) * GW], [], [b_wg[sl]], eng="pool")
        K.dma(wu[sl][:], Wu_v[:, :, j * GW:(j + 1) * GW], [], [b_wu[sl]], eng="pool")
        K.dma(wd[sl][:], Wd_v[:, j * GC:(j + 1) * GC, :], [], [b_wd[sl]], eng="pool")

    def gateup(j):
        sl = j % 2
        for c in range(GC):
            for nt in range(NTn):
                gi = cnt["gu"] % 2
                cnt["gu"] += 1
                pg, pu = PS_G[gi], PS_U[gi]
                hbufs = b_hT[nt * (NT // SUB):(nt + 1) * (NT // SUB)]
                K.mm(pg, [(wg[sl][:, kt, c * 128:(c + 1) * 128], hT[:, kt, nt * NT:(nt + 1) * NT]) for kt in range(KT)], [b_wg[sl]] + hbufs)
                K.mm(pu, [(wu[sl][:, kt, c * 128:(c + 1) * 128], hT[:, kt, nt * NT:(nt + 1) * NT]) for kt in range(KT)], [b_wu[sl]] + hbufs)
                P.op("act", lambda h, gi=gi, pg=pg: h.activation(out=sg[gi][:], in_=K.ps[pg][:], func=AF.Silu),
                     reads=[K.b_ps[pg]], writes=[b_sg[gi]])
                P.op("dve", lambda h, gi=gi, pu=pu, sl=sl, c=c, nt=nt: h.tensor_tensor(
                    out=hid[sl][:, c, nt * NT:(nt + 1) * NT], in0=sg[gi][:], in1=K.ps[pu][:], op=ALU.mult),
                    reads=[b_sg[gi], K.b_ps[pu]], writes=[b_hid[sl][c][nt]])

    def down(j):
        sl = j % 2
        for m in range(KT):
            for nt in range(NTn):
                pd = PS_D[cnt["d"] % 3]
                cnt["d"] += 1
                K.mm(pd, [(wd[sl][:, c, m * 128:(m + 1) * 128], hid[sl][:, c, nt * NT:(nt + 1) * NT]) for c in range(GC)],
                     [b_wd[sl]] + [b_hid[sl][c][nt] for c in range(GC)])
                if j == 0:
                    P.op("dve", lambda h, m=m, nt=nt, pd=pd: h.tensor_copy(out=acc[:, m, nt * NT:(nt + 1) * NT], in_=K.ps[pd][:]),
                         reads=[K.b_ps[pd]], writes=[b_acc[m][nt]])
                else:
                    P.op("dve", lambda h, m=m, nt=nt, pd=pd: h.tensor_tensor(
                        out=acc[:, m, nt * NT:(nt + 1) * NT], in0=acc[:, m, nt * NT:(nt + 1) * NT], in1=K.ps[pd][:], op=ALU.add),
                        reads=[K.b_ps[pd], b_acc[m][nt]], writes=[b_acc[m][nt]])

    def finalize_job(k):
        f, p = jobs[k]
        gsc = gscs[f]
        t0 = p * Tp
        def xload(s_):
            xi_ = (cnt["xs"] + s_) % 2
            K.dma(xs[xi_][:], x_v[:, :, t0 + s_ * SUB: t0 + (s_ + 1) * SUB], [b_x], [b_xs[xi_]])
        xload(0)
        base = cnt["xs"]
        for s_ in range(NS):
            nt = (s_ * SUB) // NT
            accb = [b_acc[m][nt] for m in range(KT)]
            xi = (base + s_) % 2
            if s_ + 1 < NS:
                cnt["xs"] = base
                xload(s_ + 1)
            cnt["xs"] = base + s_ + 1
            qi = rstd_calc(acc[:, :, s_ * SUB:(s_ + 1) * SUB], accb)

            if NXS == 4:
                ti = 2 + cnt["xs"] % 2
                tmp, b_tmp = xs[ti], b_xs[ti]

                def fin1(h, s_=s_, qi=qi, tmp=tmp):
                    for kt in range(KT):
                        r = h.scalar_tensor_tensor(out=tmp[:, kt, :], in0=acc[:, kt, s_ * SUB:(s_ + 1) * SUB],
                                                   scalar=gsc[:, kt:kt + 1], in1=rstd[qi][:], op0=ALU.mult, op1=ALU.mult)
                    return r
                P.op("dve", fin1, reads=accb + [b_rstd[qi], b_gsc], writes=[b_tmp])
                P.op("pool", lambda h, xi=xi, tmp=tmp: h.tensor_tensor(out=xs[xi][:], in0=tmp[:], in1=xs[xi][:], op=ALU.add),
                     reads=[b_tmp, b_xs[xi]], writes=[b_xs[xi]])
            else:
                def fin1(h, s_=s_, qi=qi):
                    for kt in range(KT):
                        r = h.scalar_tensor_tensor(out=acc[:, kt, s_ * SUB:(s_ + 1) * SUB], in0=acc[:, kt, s_ * SUB:(s_ + 1) * SUB],
                                                   scalar=gsc[:, kt:kt + 1], in1=rstd[qi][:], op0=ALU.mult, op1=ALU.mult)
                    return r
                P.op("dve", fin1, reads=accb + [b_rstd[qi], b_gsc], writes=accb)
                P.op("pool", lambda h, xi=xi, s_=s_: h.tensor_tensor(
                    out=xs[xi][:], in0=acc[:, :, s_ * SUB:(s_ + 1) * SUB], in1=xs[xi][:], op=ALU.add),
                    reads=accb + [b_xs[xi]], writes=[b_xs[xi]])
            K.dma(x_v[:, :, t0 + s_ * SUB: t0 + (s_ + 1) * SUB], xs[xi][:], [b_xs[xi]], [b_x])

    assert NG % 2 == 0
    norm(0)
    load_w(0, 0)
    gateup(0)
    for k in range(len(jobs)):
        for j in range(NG):
            if j + 1 < NG:
                load_w(k, j + 1)
                gateup(j + 1)
                down(j)
            else:
                if k + 1 < len(jobs):
                    norm(k + 1)
                    load_w(k + 1, 0)
                    gateup(0)
                down(j)
        finalize_job(k)
    P.barrier()
    P.release(mark)


S5_L = 128


def s5_host_layout(lam_re, lam_im, log_dt, b_re, b_im, c_re, c_im, d):
    G, Pn, C = b_re.shape
    NP = G // 2

    def st(a):
        return np.ascontiguousarray(a.reshape(NP, 2 * Pn).T)
    ldt = np.repeat(log_dt[:, None], Pn, 1)
    lam_s = np.stack([st(lam_re), st(lam_im), st(ldt)], 1)
    row = np.stack([lam_re.reshape(-1), lam_im.reshape(-1), ldt.reshape(-1)], 0)
    lam_r = np.ascontiguousarray(np.broadcast_to(row[None], (128, 3, NP * 128)))
    bT = np.zeros((2, 128, NP, 128), np.float32)
    cP = np.zeros((2, 128, NP, 128), np.float32)
    for g in range(G):
        q, hh = g // 2, g % 2
        off = (g % 8) * 16
        bT[0, off:off + 16, q, hh * 64:(hh + 1) * 64] = b_re[g].T
        bT[1, off:off + 16, q, hh * 64:(hh + 1) * 64] = b_im[g].T
        cP[0, hh * 64:(hh + 1) * 64, q, off:off + 16] = c_re[g].T
        cP[1, hh * 64:(hh + 1) * 64, q, off:off + 16] = c_im[g].T
    d_s = np.ascontiguousarray(d.reshape(-1, 128).T)
    return dict(lam_s=lam_s, lam_r=lam_r, bT=bT, cP=cP, d_s=d_s)


def s5_setup(K, prm, NP):
    P = K.P
    L = S5_L
    PI = float(np.pi)
    S = {}
    lam_s = P.sb([128, 3, NP], F32, "lam_s")
    b_l = P.buf("lam_s")
    K.dma(lam_s[:], prm["lam_s"], [], [b_l])
    dt = P.sb([128, NP], F32, "dt")
    th = P.sb([128, NP], F32, "th")
    r = P.sb([128, NP], F32, "r")
    b_t = P.buf("s5tab")
    P.op("act", lambda h: h.activation(out=dt[:], in_=lam_s[:, 2, :], func=AF.Exp), reads=[b_l], writes=[b_t])
    P.op("dve", lambda h: h.tensor_tensor(out=th[:], in0=lam_s[:, 1, :], in1=dt[:], op=ALU.mult), reads=[b_l, b_t], writes=[b_t])
    P.op("dve", lambda h: h.tensor_tensor(out=r[:], in0=lam_s[:, 0, :], in1=dt[:], op=ALU.mult), reads=[b_l, b_t], writes=[b_t])
    P.op("act", lambda h: h.activation(out=r[:], in_=r[:], func=AF.Exp), reads=[b_t], writes=[b_t])
    jrow_i = P.sb([128, L], I32, "jrow_i")
    jrow = P.sb([128, L], F32, "jrow")
    P.op("pool", lambda h: h.iota(jrow_i[:], pattern=[[1, L]], base=0, channel_multiplier=0), writes=[b_t], reads=[b_t])
    P.op("dve", lambda h: h.tensor_copy(out=jrow[:], in_=jrow_i[:]), reads=[b_t], writes=[b_t])
    cosT = P.sb([128, NP, L], F32, "cosT")
    sinT = P.sb([128, NP, L], F32, "sinT")
    Rz = P.sb([128, NP, L], F32, "Rz")
    b_tab = P.buf("tabs")

    TWO_PI = 2 * PI
    MAGIC = 12582912.0
    thn = P.sb([128, NP], F32, "thn")
    P.op("dve", lambda h: h.tensor_scalar(out=thn[:], in0=th[:], scalar1=1.0 / TWO_PI, scalar2=None, op0=ALU.mult), reads=[b_t], writes=[b_t])
    Kre = P.sb([128, NP], F32, "Kre")
    Kim = P.sb([128, NP], F32, "Kim")
    mk0 = P.mark()
    tmpT = P.sb([128, NP, L], F32, "tmpT")

    def sin_cycles(tens, shift, b_r, b_w_):
        tv = tmpT_v(tens)
        steps = []
        if shift:
            steps.append(lambda h: h.tensor_scalar(out=tens, in0=tens, scalar1=float(shift), scalar2=None, op0=ALU.add))
        steps.append(lambda h: h.tensor_scalar(out=tv, in0=tens, scalar1=MAGIC, scalar2=None, op0=ALU.add))
        steps.append(lambda h: h.tensor_scalar(out=tv, in0=tv, scalar1=-MAGIC, scalar2=None, op0=ALU.add))
        steps.append(lambda h: h.tensor_tensor(out=tens, in0=tens, in1=tv, op=ALU.subtract))
        P.chain("dve", steps, reads=b_r, writes=b_w_)
        P.op("act", lambda h: h.activation(out=tens, in_=tens, func=AF.Sin, scale=TWO_PI), reads=b_w_, writes=b_w_)

    def tmpT_v(tens):
        shp = tens.shape
        if len(shp) == 3:
            return tmpT[:, 0:shp[1], 0:shp[2]]
        return tmpT[:, 0, 0:shp[1]]

    def angs(h):
        for q in range(NP):
            h.tensor_scalar(out=sinT[:, q, :], in0=jrow[:], scalar1=thn[:, q:q + 1], scalar2=None, op0=ALU.mult)
            r_ = h.tensor_scalar(out=cosT[:, q, :], in0=jrow[:], scalar1=thn[:, q:q + 1], scalar2=None, op0=ALU.mult)
        return r_
    P.op("dve", angs, reads=[b_t], writes=[b_tab])
    sin_cycles(sinT[:], 0.0, [b_tab], [b_tab])
    sin_cycles(cosT[:], 0.25, [b_tab], [b_tab])

    def rz(h):
        for q in range(NP):
            h.tensor_scalar(out=Rz[:, q, 1:L], in0=jrow[:, 1:L], scalar1=0.0, scalar2=r[:, q:q + 1], op0=ALU.mult, op1=ALU.add)
        return h.memset(Rz[:, :, 0:1], 0.0)
    P.op("dve", rz, reads=[b_t], writes=[b_tab])
    def kang(h):
        h.tensor_scalar(out=Kim[:], in0=thn[:], scalar1=float(L), scalar2=None, op0=ALU.mult)
        return h.tensor_scalar(out=Kre[:], in0=thn[:], scalar1=float(L), scalar2=None, op0=ALU.mult)
    P.op("dve", kang, reads=[b_t], writes=[b_tab])
    sin_cycles(Kim[:], 0.0, [b_tab], [b_tab])
    sin_cycles(Kre[:], 0.25, [b_tab], [b_tab])

    def kmul(h):
        h.tensor_tensor(out=Kim[:], in0=Kim[:], in1=r[:], op=ALU.mult)
        return h.tensor_tensor(out=Kre[:], in0=Kre[:], in1=r[:], op=ALU.mult)
    P.op("dve", kmul, reads=[b_tab, b_t], writes=[b_tab])

    P.barrier()
    P.release(mk0)
    BT = [P.sb([128, NP, 128], BF16, f"BT{i}") for i in range(2)]
    CT = [P.sb([128, NP, 128], BF16, f"CT{i}") for i in range(2)]
    b_BT = P.buf("BT")
    b_CT = P.buf("CT")
    K.dma(CT[0][:], prm["cP"][0], [], [b_CT], eng="pool")
    K.dma(CT[1][:], prm["cP"][1], [], [b_CT], eng="pool")
    P.op("dve", lambda h: h.tensor_scalar(out=CT[1][:], in0=CT[1][:], scalar1=-1.0, scalar2=None, op0=ALU.mult), reads=[b_CT], writes=[b_CT])
    mk = P.mark()
    QB = 4
    W = QB * 128
    lr = P.sb([128, 3, W], F32, "lr")
    tb = [P.sb([128, W], F32, f"tb{i}") for i in range(6)]
    bt = [P.sb([128, QB, 128], F32, f"bt{i}") for i in range(2)]
    b_lr = P.buf("lr")
    b_bt = P.buf("bt")
    b_w = P.buf("w")
    for blk in range(NP // QB):
        cs = slice(blk * W, (blk + 1) * W)
        K.dma(lr[:], prm["lam_r"][:, :, cs], [], [b_lr])
        K.dma(bt[0][:], prm["bT"][0][:, blk * QB:(blk + 1) * QB, :], [], [b_bt])
        K.dma(bt[1][:], prm["bT"][1][:, blk * QB:(blk + 1) * QB, :], [], [b_bt])
        dtr, thr, rr, ca, sa, den = tb
        P.op("act", lambda h: h.activation(out=dtr[:], in_=lr[:, 2, :], func=AF.Exp), reads=[b_lr], writes=[b_w])

        def c1(h):
            h.tensor_tensor(out=thr[:], in0=lr[:, 1, :], in1=dtr[:], op=ALU.mult)
            h.tensor_tensor(out=rr[:], in0=lr[:, 0, :], in1=dtr[:], op=ALU.mult)
            h.tensor_scalar(out=sa[:], in0=thr[:], scalar1=1.0 / TWO_PI, scalar2=None, op0=ALU.mult)
            h.tensor_scalar(out=ca[:], in0=thr[:], scalar1=1.0 / TWO_PI, scalar2=0.25, op0=ALU.mult, op1=ALU.add)
            for t_ in (sa, ca):
                h.tensor_scalar(out=den[:], in0=t_[:], scalar1=MAGIC, scalar2=None, op0=ALU.add)
                h.tensor_scalar(out=den[:], in0=den[:], scalar1=-MAGIC, scalar2=None, op0=ALU.add)
                r_ = h.tensor_tensor(out=t_[:], in0=t_[:], in1=den[:], op=ALU.subtract)
            return r_
        P.op("dve", c1, reads=[b_lr, b_w], writes=[b_w])
        P.op("act", lambda h: h.activation(out=rr[:], in_=rr[:], func=AF.Exp), reads=[b_w], writes=[b_w])
        P.op("act", lambda h: h.activation(out=sa[:], in_=sa[:], func=AF.Sin, scale=TWO_PI), reads=[b_w], writes=[b_w])
        P.op("act", lambda h: h.activation(out=ca[:], in_=ca[:], func=AF.Sin, scale=TWO_PI), reads=[b_w], writes=[b_w])

        def c2(h):
            h.tensor_tensor(out=ca[:], in0=ca[:], in1=rr[:], op=ALU.mult)
            h.tensor_scalar(out=ca[:], in0=ca[:], scalar1=-1.0, scalar2=None, op0=ALU.add)
            h.tensor_tensor(out=sa[:], in0=sa[:], in1=rr[:], op=ALU.mult)
            h.tensor_tensor(out=den[:], in0=lr[:, 0, :], in1=lr[:, 0, :], op=ALU.mult)
            h.tensor_tensor(out=dtr[:], in0=lr[:, 1, :], in1=lr[:, 1, :], op=ALU.mult)
            h.tensor_tensor(out=den[:], in0=den[:], in1=dtr[:], op=ALU.add)
            h.reciprocal(out=den[:], in_=den[:])
            h.tensor_tensor(out=thr[:], in0=ca[:], in1=lr[:, 0, :], op=ALU.mult)
            h.tensor_tensor(out=dtr[:], in0=sa[:], in1=lr[:, 1, :], op=ALU.mult)
            h.tensor_tensor(out=thr[:], in0=thr[:], in1=dtr[:], op=ALU.add)
            h.tensor_tensor(out=thr[:], in0=thr[:], in1=den[:], op=ALU.mult)
            h.tensor_tensor(out=rr[:], in0=sa[:], in1=lr[:, 0, :], op=ALU.mult)
            h.tensor_tensor(out=dtr[:], in0=ca[:], in1=lr[:, 1, :], op=ALU.mult)
            h.tensor_tensor(out=rr[:], in0=rr[:], in1=dtr[:], op=ALU.subtract)
            return h.tensor_tensor(out=rr[:], in0=rr[:], in1=den[:], op=ALU.mult)
        P.op("dve", c2, reads=[b_lr, b_w], writes=[b_w])

        def c3(h, blk=blk):
            qs = slice(blk * QB, (blk + 1) * QB)
            b0 = bt[0][:].rearrange("p q s -> p (q s)")
            b1 = bt[1][:].rearrange("p q s -> p (q s)")
            o0 = BT[0][:, qs, :].rearrange("p q s -> p (q s)")
            o1 = BT[1][:, qs, :].rearrange("p q s -> p (q s)")
            h.tensor_tensor(out=ca[:], in0=thr[:], in1=b0, op=ALU.mult)
            h.tensor_tensor(out=sa[:], in0=rr[:], in1=b1, op=ALU.mult)
            h.tensor_tensor(out=o0, in0=ca[:], in1=sa[:], op=ALU.subtract)
            h.tensor_tensor(out=ca[:], in0=thr[:], in1=b1, op=ALU.mult)
            h.tensor_tensor(out=sa[:], in0=rr[:], in1=b0, op=ALU.mult)
            return h.tensor_tensor(out=o1, in0=ca[:], in1=sa[:], op=ALU.add)
        P.op("dve", c3, reads=[b_w, b_bt], writes=[b_BT, b_w])
    P.barrier()
    P.release(mk)
    S.update(cosT=cosT, sinT=sinT, Rz=Rz, Kre=Kre, Kim=Kim, BT=BT, CT=CT, b_tab=b_tab, b_BT=b_BT, b_CT=b_CT)
    return S


def s5_scan(K, S, NP, T, u_d, carry_in_d, carry_out_d, yg_d, d_d, full, flag=None, b_flag=None):
    P = K.P
    L = S5_L
    NQ = NP // 4
    NG4 = (NQ + 3) // 4
    NCH = T // L
    mark = P.mark()
    cosT, sinT, Rz, Kre, Kim, BT, CT = (S[k] for k in ("cosT", "sinT", "Rz", "Kre", "Kim", "BT", "CT"))
    b_tab, b_BT, b_CT = S["b_tab"], S["b_BT"], S["b_CT"]
    u_v = u_d.rearrange("(q p) t -> p q t", p=128)
    SC = 4 * L
    uT = [P.sb([128, NQ, SC], BF16, "uT") for _ in range(2)]
    b_u = P.bufs_n(2, "uT")
    rc = P.sb([128, 2, NP], F32, "rc")
    b_rc = P.buf("rc")
    zl = P.sb([128, 2, NP], F32, "zl")
    b_zl = P.bufs_n(NQ, "zl")
    tmpc = [P.sb([128, NP], F32, f"tmpc{i}") for i in range(2)]
    K.dma(rc[:], carry_in_d, [], [b_rc])
    if flag is not None:
        P.op("dve", lambda h: h.tensor_scalar(out=rc[:].rearrange("p a q -> p (a q)"), in0=rc[:].rearrange("p a q -> p (a q)"),
                                              scalar1=flag[:, 0:1], scalar2=None, op0=ALU.mult), reads=[b_flag], writes=[b_rc])
    NB = 2
    v = [[P.sb([128, 4, L], F32, f"v{i}{j}") for j in range(2)] for i in range(NB)]
    z = [[P.sb([128, 4, L], F32, f"z{i}{j}") for j in range(2)] for i in range(NB)]
    b_v = P.bufs_n(NB, "v")
    xo = [[P.sb([128, 4, L], BF16, f"xo{i}{j}") for j in range(2)] for i in range(NB)]
    b_xo = P.bufs_n(NB, "xo")
    b_xr = P.bufs_n(NB, "xr")
    xo2 = [[P.sb([128, 4, L], BF16, f"xo2{i}{j}") for j in range(2)] for i in range(NB)] if full else None
    b_zz = P.bufs_n(NB, "zz")
    fl = lambda t: t[:].rearrange("p q l -> p (q l)")
    col = lambda ap: ap.rearrange("p (q o) -> p q o", o=1)
    if full:
        d_s = P.sb([128, NQ], F32, "d_s")
        b_d = P.buf("d")
        K.dma(d_s[:], d_d, [], [b_d])
        du = [P.sb([128, NQ, L], F32, f"du{i}") for i in range(2)]
        b_du = P.bufs_n(2, "du")
        yt = [P.sb([128, 4, L], F32, f"yt{i}") for i in range(2)]
        y2 = [P.sb([128, 4, L], F32, f"y2{i}") for i in range(2)]
        yo = [P.sb([128, 4, L], BF16, f"yo{i}") for i in range(2)]
        b_yt = P.bufs_n(2, "yt")
        b_yo = P.bufs_n(2, "yo")
        b_ygd = P.buf("ygd")
        yg_v = yg_d.rearrange("(q p) t -> p q t", p=128)
    b_ud = P.buf("ud")
    PS_B = [(0, 1), (2, 3)]
    PS_Y = [4, 5, 6, 7]
    yit = [0]
    items = [(c, qd) for c in range(NCH) for qd in range(NQ)]
    usl = {}

    def emit_mmb(i):
        c, qd = items[i]
        sc, cc = divmod(c, 4)
        us = sc % 2
        if cc == 0 and qd == 0:
            K.dma(uT[us][:], u_v[:, :, sc * SC:(sc + 1) * SC], [b_ud], [b_u[us]])
        pr, pi_ = PS_B[i % 2]
        prs = list(range(qd * 4, qd * 4 + 4))
        rhs = uT[us][:, qd, cc * L:(cc + 1) * L]

        def mmb(h, pr=pr, pi_=pi_, prs=prs, rhs=rhs):
            for k, q in enumerate(prs):
                h.matmul(K.ps[pr][:, k * L:(k + 1) * L], lhsT=BT[0][:, q, :], rhs=rhs, start=True, stop=True)
            for k, q in enumerate(prs):
                r_ = h.matmul(K.ps[pi_][:, k * L:(k + 1) * L], lhsT=BT[1][:, q, :], rhs=rhs, start=True, stop=True)
            return r_
        P.op("pe", mmb, reads=[b_BT, b_u[us]], writes=[K.b_ps[pr], K.b_ps[pi_]])

    def emit_rest(i):
        c, qd = items[i]
        sc, cc = divmod(c, 4)
        us = sc % 2
        dui = c % 2
        if full and qd == 0:
            def mkdu(h, dui=dui, us=us, cc=cc):
                for q_ in range(NQ):
                    r_ = h.tensor_scalar(out=du[dui][:, q_, :], in0=uT[us][:, q_, cc * L:(cc + 1) * L], scalar1=d_s[:, q_:q_ + 1],
                                         scalar2=None, op0=ALU.mult)
                return r_
            P.op("pool", mkdu, reads=[b_u[us], b_d], writes=[b_du[dui]])
        vi = i % NB
        pr, pi_ = PS_B[i % 2]
        prs = list(range(qd * 4, qd * 4 + 4))
        vr, vim = v[vi]
        zr, zi = z[vi]
        qs = slice(qd * 4, qd * 4 + 4)
        cs_ = cosT[:, qs, :].rearrange("p q l -> p (q l)")
        sn_ = sinT[:, qs, :].rearrange("p q l -> p (q l)")
        rz_ = Rz[:, qs, :].rearrange("p q l -> p (q l)")

        def rot_in(h):
            h.tensor_tensor(out=fl(zr), in0=K.ps[pr][:], in1=cs_, op=ALU.mult)
            h.tensor_tensor(out=fl(zi), in0=K.ps[pi_][:], in1=sn_, op=ALU.mult)
            h.tensor_tensor(out=fl(vr), in0=fl(zr), in1=fl(zi), op=ALU.add)
            h.tensor_tensor(out=fl(zr), in0=K.ps[pi_][:], in1=cs_, op=ALU.mult)
            h.tensor_tensor(out=fl(zi), in0=K.ps[pr][:], in1=sn_, op=ALU.mult)
            return h.tensor_tensor(out=fl(vim), in0=fl(zr), in1=fl(zi), op=ALU.subtract)
        P.op("dve", rot_in, reads=[K.b_ps[pr], K.b_ps[pi_], b_tab], writes=[b_v[vi], b_zz[vi]])

        def sc1(h):
            h.tensor_tensor(out=vr[:, :, 0:1], in0=vr[:, :, 0:1], in1=col(rc[:, 0, qs]), op=ALU.add)
            return h.tensor_tensor(out=vim[:, :, 0:1], in0=vim[:, :, 0:1], in1=col(rc[:, 1, qs]), op=ALU.add)

        def sc2(h):
            h.tensor_tensor_scan(out=fl(zr), data0=rz_, data1=fl(vr), initial=0.0, op0=ALU.mult, op1=ALU.add)
            return h.tensor_tensor_scan(out=fl(zi), data0=rz_, data1=fl(vim), initial=0.0, op0=ALU.mult, op1=ALU.add)

        def sc3(h):
            h.tensor_copy(out=col(zl[:, 0, qs]), in_=zr[:, :, L - 1:L])
            return h.tensor_copy(out=col(zl[:, 1, qs]), in_=zi[:, :, L - 1:L])
        P.chain("dve", [sc1, sc2, sc3], reads=[b_rc, b_tab], writes=[b_v[vi], b_zz[vi], b_zl[qd]])
        if full:
            xr, xi = xo[vi]


            xr2, xi2 = xo2[vi]

            def rot_prod(h):
                h.tensor_tensor(out=fl(xr), in0=fl(zr), in1=cs_, op=ALU.mult)
                h.scalar_tensor_tensor(out=fl(xr2), in0=fl(zi), scalar=-1.0, in1=sn_, op0=ALU.mult, op1=ALU.mult)
                h.tensor_tensor(out=fl(xi), in0=fl(zr), in1=sn_, op=ALU.mult)
                return h.tensor_tensor(out=fl(xi2), in0=fl(zi), in1=cs_, op=ALU.mult)
            P.op("dve", rot_prod, reads=[b_tab, b_zz[vi]], writes=[b_xo[vi], b_xr[vi]])
        if i + 2 < len(items):
            emit_mmb(i + 2)
        if full:
            g4, q4 = divmod(qd, 4)
            py = PS_Y[(c * NG4 + g4) % 4]

            def mmy(h):
                ops_ = [(CT[0], xr), (CT[0], xr2), (CT[1], xi), (CT[1], xi2)]
                n_ = len(ops_) * 4
                t_ = 0
                for (ct, xx_) in ops_:
                    for k, q in enumerate(prs):
                        r_ = h.matmul(K.ps[py][:, q4 * L:(q4 + 1) * L], lhsT=ct[:, q, :], rhs=xx_[:, k, :], start=(t_ == 0), stop=(t_ == n_ - 1))
                        t_ += 1
                return r_
            P.op("pe", mmy, reads=[b_xo[vi], b_xr[vi], b_CT], writes=[K.b_ps[py]])
            if q4 == 3 or qd == NQ - 1:
                nq4 = q4 + 1
                yi = yit[0] % 2
                yit[0] += 1
                W4 = nq4 * L
                P.op("dve", lambda h: h.tensor_tensor(
                    out=yt[yi][:, 0:nq4, :].rearrange("p q l -> p (q l)"), in0=K.ps[py][:, 0:W4],
                    in1=du[dui][:, g4 * 4:g4 * 4 + nq4, :].rearrange("p q l -> p (q l)"), op=ALU.add),
                    reads=[K.b_ps[py], b_du[dui], b_yo[yi]], writes=[b_yt[yi]])
                P.op("act", lambda h: h.activation(out=y2[yi][:, 0:nq4, :], in_=yt[yi][:, 0:nq4, :], func=AF.Square),
                     reads=[b_yt[yi]], writes=[b_yt[yi]])

                def g2(h):
                    h.tensor_scalar(out=y2[yi][:, 0:nq4, :], in0=y2[yi][:, 0:nq4, :], scalar1=0.044715, scalar2=1.0, op0=ALU.mult, op1=ALU.add)
                    return h.tensor_tensor(out=y2[yi][:, 0:nq4, :], in0=y2[yi][:, 0:nq4, :], in1=yt[yi][:, 0:nq4, :], op=ALU.mult)
                P.op("dve", g2, reads=[b_yt[yi]], writes=[b_yt[yi]])
                P.op("act", lambda h: h.activation(out=y2[yi][:, 0:nq4, :], in_=y2[yi][:, 0:nq4, :], func=AF.Sigmoid, scale=1.5957691216),
                     reads=[b_yt[yi]], writes=[b_yt[yi]])
                P.op("dve", lambda h: h.tensor_tensor(out=yo[yi][:, 0:nq4, :], in0=y2[yi][:, 0:nq4, :], in1=yt[yi][:, 0:nq4, :], op=ALU.mult),
                     reads=[b_yt[yi]], writes=[b_yo[yi]])
                K.dma(yg_v[:, g4 * 4:g4 * 4 + nq4, c * L:(c + 1) * L], yo[yi][:, 0:nq4, :], [b_yo[yi]], [b_ygd])

    emit_mmb(0)
    if len(items) > 1:
        emit_mmb(1)
    for c in range(NCH):
        for qd in range(NQ):
            emit_rest(c * NQ + qd)

        def cu1(h):
            h.tensor_tensor(out=tmpc[0][:], in0=zl[:, 0, :], in1=Kre[:], op=ALU.mult)
            return h.tensor_tensor(out=tmpc[1][:], in0=zl[:, 1, :], in1=Kim[:], op=ALU.mult)

        def cu2(h):
            return h.tensor_tensor(out=rc[:, 0, :], in0=tmpc[0][:], in1=tmpc[1][:], op=ALU.subtract)

        def cu3(h):
            h.tensor_tensor(out=tmpc[0][:], in0=zl[:, 0, :], in1=Kim[:], op=ALU.mult)
            return h.tensor_tensor(out=tmpc[1][:], in0=zl[:, 1, :], in1=Kre[:], op=ALU.mult)

        def cu4(h):
            return h.tensor_tensor(out=rc[:, 1, :], in0=tmpc[0][:], in1=tmpc[1][:], op=ALU.add)
        P.chain("dve", [cu1, cu2, cu3, cu4], reads=b_zl + [b_tab], writes=[b_rc])
    b_co = P.buf("carry_out")
    K.dma(carry_out_d, rc[:], [b_rc], [b_co])
    P.barrier()
    P.release(mark)


def alloc_norm_tmp(K, KT, SUB):
    P = K.P
    return dict(xs=[P.sb([128, KT, SUB], F32, "xs") for _ in range(2)], b_xs=P.bufs_n(2, "xs"),
                sq=P.sb([128, KT, SUB], BF16, "sq"), b_sq=P.buf("sq"),
                rstd=P.sb([128, SUB], F32, "rstd"), b_rstd=P.buf("rstd"), n=0, SUB=SUB, KT=KT)


def norm_in(K, tm, x_v, b_x, t0, Tp, g_ap, b_g, hT, b_hT, D, PS_M):
    P = K.P
    SUB, KT = tm["SUB"], tm["KT"]
    for s in range(Tp // SUB):
        xi = tm["n"] % 2
        tm["n"] += 1
        xs, sq, rstd = tm["xs"][xi], tm["sq"], tm["rstd"]
        K.dma(xs[:], x_v[:, :, t0 + s * SUB: t0 + (s + 1) * SUB], [b_x], [tm["b_xs"][xi]])
        P.op("act", lambda h, xs=xs: h.activation(out=sq[:], in_=xs[:], func=AF.Square), reads=[tm["b_xs"][xi]], writes=[tm["b_sq"]])
        K.rstd_from_sq(sq, KT, SUB, PS_M, rstd, tm["b_sq"], tm["b_rstd"], D)

        def nrm(h, xs=xs, s=s):
            for kt in range(KT):
                r = h.scalar_tensor_tensor(out=hT[:, kt, s * SUB:(s + 1) * SUB], in0=xs[:, kt, :],
                                           scalar=g_ap[:, kt:kt + 1], in1=rstd[:], op0=ALU.mult, op1=ALU.mult)
            return r
        P.op("dve", nrm, reads=[tm["b_xs"][xi], tm["b_rstd"], b_g], writes=[b_hT[s]])


def finalize(K, tm, acc, b_acc, x_v, b_x, t0, Tp, gsc, b_gsc, D, PS_M):
    P = K.P
    SUB, KT = tm["SUB"], tm["KT"]
    if "tmp" not in tm and P.sb_cap - P.sb_off >= KT * SUB * 4 + 1024:
        tm["tmp"] = P.sb([128, KT, SUB], F32, "fintmp")
        tm["b_tmp"] = P.buf("fintmp")
    tmp, b_tmp = tm.get("tmp"), tm.get("b_tmp")
    base = tm["n"]

    def xload(s):
        xi_ = (base + s) % 2
        K.dma(tm["xs"][xi_][:], x_v[:, :, t0 + s * SUB: t0 + (s + 1) * SUB], [b_x], [tm["b_xs"][xi_]])
    xload(0)
    for s in range(Tp // SUB):
        nt = (s * SUB) // NT
        accb = [b_acc[m][nt] for m in range(KT)]
        xi = (base + s) % 2
        tm["n"] = base + s + 1
        xs, sq, rstd = tm["xs"][xi], tm["sq"], tm["rstd"]
        if s + 1 < Tp // SUB:
            xload(s + 1)
        P.op("act", lambda h, s=s: h.activation(out=sq[:], in_=acc[:, :, s * SUB:(s + 1) * SUB], func=AF.Square),
             reads=accb, writes=[tm["b_sq"]])
        K.rstd_from_sq(sq, KT, SUB, PS_M, rstd, tm["b_sq"], tm["b_rstd"], D)
        if tmp is not None:
            def fin1(h, s=s):
                for kt in range(KT):
                    r = h.scalar_tensor_tensor(out=tmp[:, kt, :], in0=acc[:, kt, s * SUB:(s + 1) * SUB],
                                               scalar=gsc[:, kt:kt + 1], in1=rstd[:], op0=ALU.mult, op1=ALU.mult)
                return r
            P.op("dve", fin1, reads=accb + [tm["b_rstd"], b_gsc], writes=[b_tmp])
            P.op("pool", lambda h, xs=xs: h.tensor_tensor(out=xs[:], in0=tmp[:], in1=xs[:], op=ALU.add),
                 reads=[b_tmp, tm["b_xs"][xi]], writes=[tm["b_xs"][xi]])
        else:
            def fin1(h, s=s):
                for kt in range(KT):
                    r = h.scalar_tensor_tensor(out=acc[:, kt, s * SUB:(s + 1) * SUB], in0=acc[:, kt, s * SUB:(s + 1) * SUB],
                                               scalar=gsc[:, kt:kt + 1], in1=rstd[:], op0=ALU.mult, op1=ALU.mult)
                return r
            P.op("dve", fin1, reads=accb + [tm["b_rstd"], b_gsc], writes=accb)
            P.op("pool", lambda h, xs=xs, s=s: h.tensor_tensor(out=xs[:], in0=acc[:, :, s * SUB:(s + 1) * SUB], in1=xs[:], op=ALU.add),
                 reads=accb + [tm["b_xs"][xi]], writes=[tm["b_xs"][xi]])
        K.dma(x_v[:, :, t0 + s * SUB: t0 + (s + 1) * SUB], xs[:], [tm["b_xs"][xi]], [b_x])


class WStream:
    def __init__(self, K, KT, MW, name="w"):
        P = K.P
        self.K, self.KT, self.MW = K, KT, MW
        self.w = [P.sb([128, KT, MW], BF16, name) for _ in range(2)]
        self.b = P.bufs_n(2, name)
        self.n = 0

    def load(self, W_v, c0, ncols=None):
        ncols = ncols or self.MW
        sl = self.n % 2
        self.n += 1
        self.K.dma(self.w[sl][:, :, 0:ncols], W_v[:, :, c0:c0 + ncols], [], [self.b[sl]], eng="pool")
        return self.w[sl], self.b[sl]


class Ring:
    def __init__(self, items):
        self.items = items
        self.n = 0

    def next(self):
        r = self.items[self.n % len(self.items)]
        self.n += 1
        return r


def mem_kv_setup(K, mem_d, gm, b_gm, Wkv, D, NM, MEMW):
    P = K.P
    KT = D // 128
    memK = P.sb([128, MEMW // 128, NM], BF16, "memK")
    memV = P.sb([128, NM // 128, MEMW], BF16, "memV")
    b_mk = P.buf("memK")
    b_mv = P.buf("memV")
    mark = P.mark()
    tm = alloc_norm_tmp(K, KT, NM)
    nm = P.sb([128, KT, NM], BF16, "nmem")
    b_nm = [P.buf("nmem")]
    wkv = P.sb([128, KT, 2 * MEMW], BF16, "wkv")
    b_w = P.buf("wkv")
    K.dma(wkv[:], Wkv.rearrange("(kt p) c -> p kt c", p=128), [], [b_w], eng="pool")
    mem_v = mem_d.rearrange("(kt p) t -> p kt t", p=128)
    norm_in(K, tm, mem_v, P.buf("memd"), 0, NM, gm, b_gm, nm, b_nm, D, 6)
    for h in range(MEMW // 128):
        psi = h % 2
        K.mm(psi, [(wkv[:, kt, h * 128:(h + 1) * 128], nm[:, kt, :]) for kt in range(KT)], [b_w] + b_nm, n=NM)
        P.op("act", lambda h_, h=h, psi=psi: h_.activation(out=memK[:, h, :], in_=K.ps[psi][:, 0:NM], func=AF.Copy),
             reads=[K.b_ps[psi]], writes=[b_mk])
    for kt_ in range(NM // 128):
        psi = 2 + kt_ % 2
        K.mm(psi, [(nm[:, kt, kt_ * 128:(kt_ + 1) * 128], wkv[:, kt, MEMW:2 * MEMW]) for kt in range(KT)], [b_w] + b_nm, n=MEMW)
        P.op("dve", lambda h_, kt_=kt_, psi=psi: h_.tensor_copy(out=memV[:, kt_, :], in_=K.ps[psi][:, 0:MEMW]),
             reads=[K.b_ps[psi]], writes=[b_mv])
    P.barrier()
    P.release(mark)
    return dict(memK=memK, memV=memV, b_mk=b_mk, b_mv=b_mv)


def alloc_mem_attn(K, NM):
    P = K.P
    return dict(pt=[P.sb([128, NM // 128, NT], BF16, "mpt") for _ in range(2)], b_pt=P.bufs_n(2, "mpt"),
                rec=[P.sb([128, NT], F32, "mrec") for _ in range(2)], b_rec=P.bufs_n(2, "mrec"),
                mo=[P.sb([128, NT], BF16, "mo") for _ in range(2)], b_mo=P.bufs_n(2, "mo"), n=0)


def mem_attn(K, MA, MKV, qm, b_qm, memo_d, b_md, t0, Tp, NM, MEMW, ps_s, ps_o, ps_r):
    P = K.P
    H = MEMW // 128
    NKT = NM // 128
    scale = 128.0 ** -0.5
    for h in range(H):
        for nt in range(Tp // NT):
            i = MA["n"] % 2
            MA["n"] += 1
            pt, rec, mo = MA["pt"][i], MA["rec"][i], MA["mo"][i]
            for kt in range(NKT):
                psi = ps_s.next()
                K.mm(psi, [(MKV["memK"][:, h, kt * 128:(kt + 1) * 128], qm[:, h, nt * NT:(nt + 1) * NT])], [MKV["b_mk"]] + b_qm)
                P.op("act", lambda h_, psi=psi, pt=pt, kt=kt: h_.activation(out=pt[:, kt, :], in_=K.ps[psi][:], func=AF.Exp, scale=scale),
                     reads=[K.b_ps[psi]], writes=[MA["b_pt"][i]])
            po, pr = ps_o.next(), ps_r.next()
            K.mm(po, [(MKV["memV"][:, kt, h * 128:(h + 1) * 128], pt[:, kt, :]) for kt in range(NKT)], [MKV["b_mv"], MA["b_pt"][i]])
            K.mm(pr, [(K.ones[:], pt[:, kt, :]) for kt in range(NKT)], [K.b_ones, MA["b_pt"][i]])
            P.op("dve", lambda h_, pr=pr, rec=rec: h_.reciprocal(out=rec[:], in_=K.ps[pr][:]), reads=[K.b_ps[pr]], writes=[MA["b_rec"][i]])
            P.op("dve", lambda h_, po=po, rec=rec, mo=mo: h_.tensor_tensor(out=mo[:], in0=K.ps[po][:], in1=rec[:], op=ALU.mult),
                 reads=[K.b_ps[po], MA["b_rec"][i]], writes=[MA["b_mo"][i]])
            K.dma(memo_d[h * 128:(h + 1) * 128, t0 + nt * NT: t0 + (nt + 1) * NT], mo[:], [MA["b_mo"][i]], [b_md])


def mixer_pre_A(K, x_d, g2, b_g, W_in, u_d, memo_d, MKV, D, TOKW, MEMW, NM, T, Tp, SUB=256):
    P = K.P
    KT = D // 128
    mark = P.mark()
    tm = alloc_norm_tmp(K, KT, SUB)
    hT = P.sb([128, KT, Tp], BF16, "hT")
    b_hT = P.bufs_n(Tp // SUB, "hT")
    qm = P.sb([128, MEMW // 128, Tp], BF16, "qm")
    b_qm = P.bufs_n(MEMW // 128, "qm")
    ws = WStream(K, KT, 256, "win")
    st = [P.sb([128, NT], BF16, "stg") for _ in range(3)]
    b_st = P.bufs_n(3, "stg")
    sti = Ring([0, 1, 2])
    MA = alloc_mem_attn(K, NM)
    x_v = x_d.rearrange("(kt p) t -> p kt t", p=128)
    W_v = W_in.rearrange("(kt p) c -> p kt c", p=128)
    b_x, b_ud, b_md = P.buf("x"), P.buf("ud"), P.buf("md")
    ps_mm = Ring([0, 1, 2])
    ps_s, ps_o, ps_r = Ring([0, 1, 2]), Ring([3, 4]), Ring([5, 7])
    MT = (TOKW + MEMW) // 128
    for p in range(T // Tp):
        t0 = p * Tp
        norm_in(K, tm, x_v, b_x, t0, Tp, g2, b_g, hT, b_hT, D, 6)
        for mg in range(MT // 2):
            w, bw = ws.load(W_v, mg * 256)
            for mi in range(2):
                m = mg * 2 + mi
                for nt in range(Tp // NT):
                    psi = ps_mm.next()
                    hb = b_hT[nt * (NT // SUB):(nt + 1) * (NT // SUB)]
                    K.mm(psi, [(w[:, kt, mi * 128:(mi + 1) * 128], hT[:, kt, nt * NT:(nt + 1) * NT]) for kt in range(KT)], [bw] + hb)
                    if m < TOKW // 128:
                        si = sti.next()
                        P.op("act", lambda h_, psi=psi, si=si: h_.activation(out=st[si][:], in_=K.ps[psi][:], func=AF.Copy),
                             reads=[K.b_ps[psi]], writes=[b_st[si]])
                        K.dma(u_d[m * 128:(m + 1) * 128, t0 + nt * NT: t0 + (nt + 1) * NT], st[si][:], [b_st[si]], [b_ud])
                    else:
                        hh = m - TOKW // 128
                        P.op("dve", lambda h_, psi=psi, hh=hh, nt=nt: h_.tensor_copy(out=qm[:, hh, nt * NT:(nt + 1) * NT], in_=K.ps[psi][:]),
                             reads=[K.b_ps[psi]], writes=[b_qm[hh]])
        mem_attn(K, MA, MKV, qm, b_qm, memo_d, b_md, t0, Tp, NM, MEMW, ps_s, ps_o, ps_r)
    P.barrier()
    P.release(mark)


def mixer_post(K, x_d, g3, b_g, W_out, tok_d, memo_d, D, TOKW, MEMW, T, Tp, W_glu=None, bglu=None, SUB=256):
    P = K.P
    KT = D // 128
    NTK, NMK = TOKW // 128, MEMW // 128
    mark = P.mark()
    tm = alloc_norm_tmp(K, KT, SUB)
    acc = P.sb([128, KT, Tp], F32, "acc")
    b_acc = [P.bufs_n(Tp // NT, "acc") for _ in range(KT)]
    tk = P.sb([128, NTK, Tp], BF16, "tk")
    b_tk = P.bufs_n(NTK, "tk")
    mo = P.sb([128, NMK, Tp], BF16, "mo")
    b_mo = P.buf("mo")
    b_x, b_td, b_md = P.buf("x"), P.buf("td"), P.buf("md")
    x_v = x_d.rearrange("(kt p) t -> p kt t", p=128)
    tok_v = tok_d.rearrange("(kt p) t -> p kt t", p=128)
    memo_v = memo_d.rearrange("(kt p) t -> p kt t", p=128)
    Wo_v = W_out.rearrange("(kt p) c -> p kt c", p=128)
    wso = WStream(K, KT, 256, "wout")
    ps_mm = Ring([0, 1, 2, 3])
    if W_glu is not None:
        yg = P.sb([128, NTK, Tp], BF16, "yg")
        b_yg = P.buf("yg")
        wsg = WStream(K, NTK, 256, "wglu")
        Wg_v = W_glu.rearrange("(kt p) c -> p kt c", p=128)
        gt = [P.sb([128, NT], F32, "gt") for _ in range(2)]
        b_gt = P.bufs_n(2, "gt")
        gti = Ring([0, 1])
    def loads(p_):
        t0_ = p_ * Tp
        K.dma(mo[:], memo_v[:, :, t0_:t0_ + Tp], [b_md], [b_mo])
        if W_glu is None:
            K.dma(tk[:], tok_v[:, :, t0_:t0_ + Tp], [b_td], b_tk)
        else:
            K.dma(yg[:], tok_v[:, :, t0_:t0_ + Tp], [b_td], [b_yg])
    loads(0)
    for p in range(T // Tp):
        t0 = p * Tp
        if W_glu is not None:
            for mg in range((NTK + 1) // 2):
                nm_ = min(2, NTK - mg * 2)
                w, bw = wsg.load(Wg_v, mg * 256, nm_ * 128)
                for mi in range(nm_):
                    m = mg * 2 + mi
                    for nt in range(Tp // NT):
                        psi = ps_mm.next()
                        gi = gti.next()
                        K.mm(psi, [(w[:, kt, mi * 128:(mi + 1) * 128], yg[:, kt, nt * NT:(nt + 1) * NT]) for kt in range(NTK)], [bw, b_yg])
                        P.op("act", lambda h_, psi=psi, gi=gi, m=m: h_.activation(out=gt[gi][:], in_=K.ps[psi][:], func=AF.Sigmoid, bias=bglu[:, m:m + 1]),
                             reads=[K.b_ps[psi], b_g], writes=[b_gt[gi]])
                        P.op("dve", lambda h_, gi=gi, m=m, nt=nt: h_.tensor_tensor(out=tk[:, m, nt * NT:(nt + 1) * NT], in0=gt[gi][:],
                                                                                 in1=yg[:, m, nt * NT:(nt + 1) * NT], op=ALU.mult),
                             reads=[b_gt[gi], b_yg], writes=[b_tk[m]])
        for mg in range(KT // 2):
            w, bw = wso.load(Wo_v, mg * 256)
            for mi in range(2):
                m = mg * 2 + mi
                for nt in range(Tp // NT):
                    psi = ps_mm.next()
                    pairs = [(w[:, kt, mi * 128:(mi + 1) * 128], tk[:, kt, nt * NT:(nt + 1) * NT]) for kt in range(NTK)]
                    pairs += [(w[:, NTK + kt, mi * 128:(mi + 1) * 128], mo[:, kt, nt * NT:(nt + 1) * NT]) for kt in range(NMK)]
                    K.mm(psi, pairs, [bw, b_mo] + b_tk)
                    P.op("act", lambda h_, psi=psi, m=m, nt=nt: h_.activation(out=acc[:, m, nt * NT:(nt + 1) * NT], in_=K.ps[psi][:], func=AF.Copy),
                         reads=[K.b_ps[psi]], writes=[b_acc[m][nt]])
        if p + 1 < T // Tp:
            loads(p + 1)
        finalize(K, tm, acc, b_acc, x_v, b_x, t0, Tp, g3, b_g, D, 6)
    P.barrier()
    P.release(mark)


def rope_tables(K, pos_d, invf_d, sgn_d, T):
    P = K.P
    cs = P.sb([64, T], F32, "rope_cs")
    sn = P.sb([64, T], F32, "rope_sn")
    b_r = P.buf("rope")
    mark = P.mark()
    pi_ = P.sb([64, T], I32, "pos_i")
    tmp = P.sb([64, T], F32, "rtmp")
    cf = P.sb([64, 2], F32, "rcf")
    K.dma(pi_[:], pos_d.rearrange("(o t) -> o t", o=1).broadcast_to([64, T]), [], [b_r])
    K.dma(cf[:, 0:1], invf_d, [], [b_r])
    K.dma(cf[:, 1:2], sgn_d, [], [b_r])
    MAGIC = 12582912.0
    steps = [
        lambda h: h.tensor_copy(out=sn[:], in_=pi_[:]),
        lambda h: h.tensor_scalar(out=sn[:], in0=sn[:], scalar1=cf[:, 0:1], scalar2=None, op0=ALU.mult),
        lambda h: h.tensor_scalar(out=cs[:], in0=sn[:], scalar1=0.25, scalar2=None, op0=ALU.add),
        lambda h: h.tensor_scalar(out=tmp[:], in0=sn[:], scalar1=MAGIC, scalar2=None, op0=ALU.add),
        lambda h: h.tensor_scalar(out=tmp[:], in0=tmp[:], scalar1=-MAGIC, scalar2=None, op0=ALU.add),
        lambda h: h.tensor_tensor(out=sn[:], in0=sn[:], in1=tmp[:], op=ALU.subtract),
        lambda h: h.tensor_scalar(out=tmp[:], in0=cs[:], scalar1=MAGIC, scalar2=None, op0=ALU.add),
        lambda h: h.tensor_scalar(out=tmp[:], in0=tmp[:], scalar1=-MAGIC, scalar2=None, op0=ALU.add),
        lambda h: h.tensor_tensor(out=cs[:], in0=cs[:], in1=tmp[:], op=ALU.subtract),
    ]
    P.chain("dve", steps, reads=[b_r], writes=[b_r])
    P.op("act", lambda h: h.activation(out=sn[:], in_=sn[:], func=AF.Sin, scale=2 * float(np.pi)), reads=[b_r], writes=[b_r])
    P.op("act", lambda h: h.activation(out=cs[:], in_=cs[:], func=AF.Sin, scale=2 * float(np.pi)), reads=[b_r], writes=[b_r])
    P.op("dve", lambda h: h.tensor_scalar(out=sn[:], in0=sn[:], scalar1=cf[:, 1:2], scalar2=None, op0=ALU.mult), reads=[b_r], writes=[b_r])
    P.barrier()
    P.release(mark)
    return dict(cs=cs, sn=sn, b=b_r)


def apply_rope(K, RT, psa, psb, out, t0, n, tmp, b_tmp, b_out):
    P = K.P
    P.op("dve", lambda h: h.tensor_tensor(out=tmp[0][0:64, 0:n], in0=K.ps[psa][0:64, 0:n], in1=RT["cs"][:, t0:t0 + n], op=ALU.mult),
         reads=[K.b_ps[psa], RT["b"]], writes=[b_tmp[0]])
    P.op("dve", lambda h: h.tensor_tensor(out=tmp[1][0:64, 0:n], in0=K.ps[psb][0:64, 0:n], in1=RT["sn"][:, t0:t0 + n], op=ALU.mult),
         reads=[K.b_ps[psb], RT["b"]], writes=[b_tmp[1]])
    P.op("pool", lambda h: h.tensor_tensor(out=out, in0=tmp[0][0:64, 0:n], in1=tmp[1][0:64, 0:n], op=ALU.add),
         reads=[b_tmp[0], b_tmp[1]], writes=[b_out])


def sub_rmsnorm(K, src, b_src, dst, b_dst, g_ap, b_g, nk, Tp, sq, b_sq, rstd, b_rstd, PS_M):
    P = K.P
    for nt in range(Tp // NT):
        sl = slice(nt * NT, (nt + 1) * NT)
        P.op("act", lambda h, sl=sl: h.activation(out=sq[:, :, :], in_=src[:, :, sl], func=AF.Square), reads=b_src, writes=[b_sq])
        K.rstd_from_sq(sq, nk, NT, PS_M, rstd, b_sq, b_rstd, nk * 128)

        def f(h, sl=sl):
            for kt in range(nk):
                r = h.scalar_tensor_tensor(out=dst[:, kt, sl], in0=src[:, kt, sl], scalar=g_ap[:, kt:kt + 1], in1=rstd[:],
                                           op0=ALU.mult, op1=ALU.mult)
            return r
        P.op("dve", f, reads=b_src + [b_rstd, b_g], writes=[b_dst[nt]])


def kv_stage(K, x_d, gkv_in, gkv, b_g, W_dkv, W_kr, W_uk, W_uv, RT, kn_d, kr_d, v_d, D, R, H, T, Tp, SUB=256):
    P = K.P
    KT = D // 128
    RK = R // 128
    mark = P.mark()
    tm = alloc_norm_tmp(K, KT, SUB)
    hT = P.sb([128, KT, Tp], BF16, "hT")
    b_hT = P.bufs_n(Tp // SUB, "hT")
    ck = P.sb([128, RK, Tp], F32, "ck")
    b_ck = P.bufs_n(1, "ck")
    ckn = P.sb([128, RK, Tp], BF16, "ckn")
    b_ckn = P.bufs_n(Tp // NT, "ckn")
    sq = P.sb([128, RK, NT], BF16, "sq2")
    rstd = P.sb([128, NT], F32, "rstd2")
    b_sq, b_rstd = P.buf("sq2"), P.buf("rstd2")
    wd = P.sb([128, KT, R], BF16, "wdkv")
    wkr = P.sb([128, KT, 128], BF16, "wkr")
    wuk = P.sb([128, RK, H * 128], BF16, "wuk")
    wuv = P.sb([128, RK, H * 128], BF16, "wuv")
    b_w = P.buf("kvw")
    K.dma(wd[:], W_dkv.rearrange("(kt p) c -> p kt c", p=128), [], [b_w], eng="pool")
    wkr_v = W_kr.rearrange("(kt p) c -> p kt c", p=128)
    K.dma(wkr[:, :, 0:64], wkr_v, [], [b_w], eng="pool")
    K.dma(wkr[:, :, 64:96], wkr_v[:, :, 32:64], [], [b_w], eng="pool")
    K.dma(wkr[:, :, 96:128], wkr_v[:, :, 0:32], [], [b_w], eng="pool")
    K.dma(wuk[:], W_uk.rearrange("(kt p) c -> p kt c", p=128), [], [b_w], eng="pool")
    K.dma(wuv[:], W_uv.rearrange("(kt p) c -> p kt c", p=128), [], [b_w], eng="pool")
    st = [P.sb([128, NT], BF16, "stg") for _ in range(3)]
    b_st = P.bufs_n(3, "stg")
    sti = Ring([0, 1, 2])
    rtmp = [P.sb([128, NT], F32, "rtmp") for _ in range(2)]
    b_rtmp = P.bufs_n(2, "rtmp")
    x_v = x_d.rearrange("(kt p) t -> p kt t", p=128)
    b_x, b_kn, b_kr, b_v = P.buf("x"), P.buf("kn"), P.buf("kr"), P.buf("v")
    ps_mm = Ring([0, 1, 2, 3])
    for p in range(T // Tp):
        t0 = p * Tp
        norm_in(K, tm, x_v, b_x, t0, Tp, gkv_in, b_g, hT, b_hT, D, 6)
        for nt in range(Tp // NT):
            hb = b_hT[nt * (NT // SUB):(nt + 1) * (NT // SUB)]
            sl = slice(nt * NT, (nt + 1) * NT)
            for m in range(RK):
                psi = ps_mm.next()
                K.mm(psi, [(wd[:, kt, m * 128:(m + 1) * 128], hT[:, kt, sl]) for kt in range(KT)], [b_w] + hb)
                P.op("act", lambda h_, psi=psi, m=m, sl=sl: h_.activation(out=ck[:, m, sl], in_=K.ps[psi][:], func=AF.Copy),
                     reads=[K.b_ps[psi]], writes=b_ck)
            pa, pb = ps_mm.next(), ps_mm.next()
            K.mm(pa, [(wkr[:, kt, 0:64], hT[:, kt, sl]) for kt in range(KT)], [b_w] + hb, m=64)
            K.mm(pb, [(wkr[:, kt, 64:128], hT[:, kt, sl]) for kt in range(KT)], [b_w] + hb, m=64)
            si = sti.next()
            apply_rope(K, RT, pa, pb, st[si][0:64, :], t0 + nt * NT, NT, rtmp, b_rtmp, b_st[si])
            K.dma(kr_d[:, t0 + nt * NT: t0 + (nt + 1) * NT], st[si][0:64, :], [b_st[si]], [b_kr])
        sub_rmsnorm(K, ck, b_ck, ckn, b_ckn, gkv, b_g, RK, Tp, sq, b_sq, rstd, b_rstd, 6)
        for nt in range(Tp // NT):
            sl = slice(nt * NT, (nt + 1) * NT)
            for hh in range(H):
                psi = ps_mm.next()
                K.mm(psi, [(wuk[:, kt, hh * 128:(hh + 1) * 128], ckn[:, kt, sl]) for kt in range(RK)], [b_w, b_ckn[nt]])
                si = sti.next()
                P.op("act", lambda h_, psi=psi, si=si: h_.activation(out=st[si][:], in_=K.ps[psi][:], func=AF.Copy),
                     reads=[K.b_ps[psi]], writes=[b_st[si]])
                K.dma(kn_d[hh, :, t0 + nt * NT: t0 + (nt + 1) * NT], st[si][:], [b_st[si]], [b_kn])
            for tt in range(NT // 128):
                tsl = slice(nt * NT + tt * 128, nt * NT + (tt + 1) * 128)
                CW = min(NT, H * 128)
                for cc in range(H * 128 // CW):
                    psi = ps_mm.next()
                    K.mm(psi, [(ckn[:, kt, tsl], wuv[:, kt, cc * CW:(cc + 1) * CW]) for kt in range(RK)], [b_w, b_ckn[nt]], n=CW)
                    si = sti.next()
                    P.op("dve", lambda h_, psi=psi, si=si, CW=CW: h_.tensor_copy(out=st[si][:, 0:CW], in_=K.ps[psi][:, 0:CW]),
                         reads=[K.b_ps[psi]], writes=[b_st[si]])
                    K.dma(v_d[t0 + nt * NT + tt * 128: t0 + nt * NT + (tt + 1) * 128, cc * CW:(cc + 1) * CW], st[si][:, 0:CW], [b_st[si]], [b_v])
    P.barrier()
    P.release(mark)


def mixer_pre_B(K, x_d, g2, gq, b_g, W_in, W_uq, RT, qn_d, qr_d, memo_d, MKV, D, R, H, MEMW, NM, T, Tp, SUB=256):
    P = K.P
    KT = D // 128
    RK = R // 128
    mark = P.mark()
    tm = alloc_norm_tmp(K, KT, SUB)
    hT = P.sb([128, KT, Tp], BF16, "hT")
    b_hT = P.bufs_n(Tp // SUB, "hT")
    cq = P.sb([128, RK, Tp], F32, "cq")
    b_cq = P.bufs_n(1, "cq")
    cqn = P.sb([128, RK, Tp], BF16, "cqn")
    b_cqn = P.bufs_n(Tp // NT, "cqn")
    sq = P.sb([128, RK, NT], BF16, "sq2")
    rstd = P.sb([128, NT], F32, "rstd2")
    b_sq, b_rstd = P.buf("sq2"), P.buf("rstd2")
    qm = P.sb([128, MEMW // 128, Tp], BF16, "qm")
    b_qm = P.bufs_n(MEMW // 128, "qm")
    ws = WStream(K, KT, 256, "win")
    HD = 192
    wuq = P.sb([128, RK, H, HD + 64], BF16, "wuq")
    b_wq = P.buf("wuq")
    wq_v = W_uq.rearrange("(kt p) (h e) -> p kt h e", p=128, e=HD)
    for kt in range(RK):
        K.dma(wuq[:, kt, :, 0:HD], wq_v[:, kt, :, :], [], [b_wq], eng="pool")
        K.dma(wuq[:, kt, :, HD:HD + 32], wq_v[:, kt, :, 160:192], [], [b_wq], eng="pool")
        K.dma(wuq[:, kt, :, HD + 32:HD + 64], wq_v[:, kt, :, 128:160], [], [b_wq], eng="pool")
    st = [P.sb([128, NT], BF16, "stg") for _ in range(3)]
    b_st = P.bufs_n(3, "stg")
    sti = Ring([0, 1, 2])
    rtmp = [P.sb([128, NT], F32, "rtmp") for _ in range(2)]
    b_rtmp = P.bufs_n(2, "rtmp")
    MA = alloc_mem_attn(K, NM)
    x_v = x_d.rearrange("(kt p) t -> p kt t", p=128)
    W_v = W_in.rearrange("(kt p) c -> p kt c", p=128)
    b_x, b_qn, b_qr, b_md = P.buf("x"), P.buf("qn"), P.buf("qr"), P.buf("md")
    ps_mm = Ring([0, 1, 2])
    ps_s, ps_o, ps_r = Ring([0, 1, 2]), Ring([3, 4]), Ring([5, 7])
    MT = (R + MEMW) // 128
    for p in range(T // Tp):
        t0 = p * Tp
        norm_in(K, tm, x_v, b_x, t0, Tp, g2, b_g, hT, b_hT, D, 6)
        for mg in range(MT // 2):
            w, bw = ws.load(W_v, mg * 256)
            for mi in range(2):
                m = mg * 2 + mi
                for nt in range(Tp // NT):
                    psi = ps_mm.next()
                    hb = b_hT[nt * (NT // SUB):(nt + 1) * (NT // SUB)]
                    sl = slice(nt * NT, (nt + 1) * NT)
                    K.mm(psi, [(w[:, kt, mi * 128:(mi + 1) * 128], hT[:, kt, sl]) for kt in range(KT)], [bw] + hb)
                    if m < RK:
                        P.op("act", lambda h_, psi=psi, m=m, sl=sl: h_.activation(out=cq[:, m, sl], in_=K.ps[psi][:], func=AF.Copy),
                             reads=[K.b_ps[psi]], writes=b_cq)
                    else:
                        hh = m - RK
                        P.op("dve", lambda h_, psi=psi, hh=hh, sl=sl: h_.tensor_copy(out=qm[:, hh, sl], in_=K.ps[psi][:]),
                             reads=[K.b_ps[psi]], writes=[b_qm[hh]])
        mem_attn(K, MA, MKV, qm, b_qm, memo_d, b_md, t0, Tp, NM, MEMW, ps_s, ps_o, ps_r)
        sub_rmsnorm(K, cq, b_cq, cqn, b_cqn, gq, b_g, RK, Tp, sq, b_sq, rstd, b_rstd, 6)
        for nt in range(Tp // NT):
            sl = slice(nt * NT, (nt + 1) * NT)
            for hh in range(H):
                psi = ps_mm.next()
                K.mm(psi, [(wuq[:, kt, hh, 0:128], cqn[:, kt, sl]) for kt in range(RK)], [b_wq, b_cqn[nt]])
                si = sti.next()
                P.op("act", lambda h_, psi=psi, si=si: h_.activation(out=st[si][:], in_=K.ps[psi][:], func=AF.Copy),
                     reads=[K.b_ps[psi]], writes=[b_st[si]])
                K.dma(qn_d[hh, :, t0 + nt * NT: t0 + (nt + 1) * NT], st[si][:], [b_st[si]], [b_qn])
                pa, pb = ps_mm.next(), ps_mm.next()
                K.mm(pa, [(wuq[:, kt, hh, 128:192], cqn[:, kt, sl]) for kt in range(RK)], [b_wq, b_cqn[nt]], m=64)
                K.mm(pb, [(wuq[:, kt, hh, 192:256], cqn[:, kt, sl]) for kt in range(RK)], [b_wq, b_cqn[nt]], m=64)
                si = sti.next()
                apply_rope(K, RT, pa, pb, st[si][0:64, :], t0 + nt * NT, NT, rtmp, b_rtmp, b_st[si])
                K.dma(qr_d[hh, :, t0 + nt * NT: t0 + (nt + 1) * NT], st[si][0:64, :], [b_st[si]], [b_qr])
    P.barrier()
    P.release(mark)


def mla_attn(K, qn_d, qr_d, kn_d, kr_d, v_d, knp_d, krp_d, vp_d, pbias_d, mask_d, tok_d, H, T):
    P = K.P
    mark = P.mark()
    NKT = T // 128
    QB = T // NT
    scale = 192.0 ** -0.5
    kr = P.sb([64, 2 * T], BF16, "kr")
    b_krs = P.buf("kr")
    K.dma(kr[:, 0:T], krp_d, [], [b_krs])
    K.dma(kr[:, T:2 * T], kr_d, [], [b_krs])
    pb = P.sb([128, 1], F32, "pbias")
    b_pb = P.buf("pbias")
    K.dma(pb[:], pbias_d, [], [b_pb])
    msk = P.sb([128, 4, NT], BF16, "mask")
    b_msk = P.buf("mask")
    K.dma(msk[:], mask_d.rearrange("i p q -> p i q"), [], [b_msk], eng="pool")
    kn = [P.sb([128, 2 * T], BF16, "kn") for _ in range(2)]
    vv = [P.sb([128, 2 * NKT, 128], BF16, "vv") for _ in range(2)]
    qn = [P.sb([128, T], BF16, "qn") for _ in range(2)]
    qr = [P.sb([64, T], BF16, "qr") for _ in range(2)]
    b_hd = P.bufs_n(2, "headin")
    pt = [P.sb([128, NT], BF16, "pt") for _ in range(3)]
    b_pt = P.bufs_n(3, "pt")
    pti = Ring([0, 1, 2])
    rec = [P.sb([128, NT], F32, "rec") for _ in range(2)]
    b_rec = P.bufs_n(2, "rec")
    ob = [P.sb([128, NT], BF16, "ob") for _ in range(2)]
    b_ob = P.bufs_n(2, "ob")
    b_td = P.buf("tokd")
    ps_s, ps_o, ps_r = Ring([0, 1, 2]), Ring([3, 4]), Ring([5, 6])
    fin = 0
    def load_head(h):
        s_ = h % 2
        K.dma(kn[s_][:, 0:T], knp_d[h], [], [b_hd[s_]])
        K.dma(kn[s_][:, T:2 * T], kn_d[h], [], [b_hd[s_]])
        K.dma(vv[s_][:, 0:NKT, :], vp_d[:, h * 128:(h + 1) * 128].rearrange("(t p) d -> p t d", p=128), [], [b_hd[s_]])
        K.dma(vv[s_][:, NKT:2 * NKT, :], v_d[:, h * 128:(h + 1) * 128].rearrange("(t p) d -> p t d", p=128), [], [b_hd[s_]])
        K.dma(qn[s_][:], qn_d[h], [], [b_hd[s_]])
        K.dma(qr[s_][:], qr_d[h], [], [b_hd[s_]])
    load_head(0)
    for h in range(H):
        s_ = h % 2
        if h + 1 < H:
            load_head(h + 1)
        for qb in range(QB):
            qsl = slice(qb * NT, (qb + 1) * NT)
            tiles = list(range(NKT)) + [NKT + j for j in range(4 * qb + 4)]
            po, pr = ps_o.next(), ps_r.next()
            n = len(tiles)

            def score(kt):
                psi = ps_s.next()
                ksl = slice(kt * 128, (kt + 1) * 128)
                K.mm(psi, [(kn[s_][:, ksl], qn[s_][:, qsl]), (kr[0:64, ksl], qr[s_][0:64, qsl])], [b_hd[s_], b_krs])
                return psi
            pend = [score(tiles[0])]
            if n > 1:
                pend.append(score(tiles[1]))
            for i, kt in enumerate(tiles):
                psi = pend.pop(0)
                if i + 2 < n:
                    pend.append(score(tiles[i + 2]))
                pi_ = pti.next()
                prev = kt < NKT
                if prev:
                    P.op("act", lambda h_, psi=psi, pi_=pi_: h_.activation(out=pt[pi_][:], in_=K.ps[psi][:], func=AF.Exp, scale=scale, bias=pb[:, 0:1]),
                         reads=[K.b_ps[psi], b_pb], writes=[b_pt[pi_]])
                else:
                    P.op("act", lambda h_, psi=psi, pi_=pi_: h_.activation(out=pt[pi_][:], in_=K.ps[psi][:], func=AF.Exp, scale=scale),
                         reads=[K.b_ps[psi]], writes=[b_pt[pi_]])
                    di = kt - NKT - 4 * qb
                    if di >= 0:
                        P.op("pool", lambda h_, pi_=pi_, di=di: h_.tensor_tensor(out=pt[pi_][:], in0=pt[pi_][:], in1=msk[:, di, :], op=ALU.mult),
                             reads=[b_msk], writes=[b_pt[pi_]])
                ptap = pt[pi_][:]

                def mo(h_, po=po, pr=pr, kt=kt, ptap=ptap, i=i, n=n, s_=s_):
                    h_.matmul(K.ps[po][:], lhsT=vv[s_][:, kt, :], rhs=ptap, start=(i == 0), stop=(i == n - 1))
                    return h_.matmul(K.ps[pr][:], lhsT=K.ones[:], rhs=ptap, start=(i == 0), stop=(i == n - 1))
                P.op("pe", mo, reads=[b_pt[pi_], b_hd[s_], K.b_ones], writes=[K.b_ps[po], K.b_ps[pr]])
            fi = fin % 2
            fin += 1
            P.op("dve", lambda h_, pr=pr, fi=fi: h_.reciprocal(out=rec[fi][:], in_=K.ps[pr][:]), reads=[K.b_ps[pr]], writes=[b_rec[fi]])
            P.op("dve", lambda h_, po=po, fi=fi: h_.tensor_tensor(out=ob[fi][:], in0=K.ps[po][:], in1=rec[fi][:], op=ALU.mult),
                 reads=[K.b_ps[po], b_rec[fi]], writes=[b_ob[fi]])
            K.dma(tok_d[h * 128:(h + 1) * 128, qsl], ob[fi][:], [b_ob[fi]], [b_td])
    P.barrier()
    P.release(mark)


class Cfg:
    def __init__(self, D=2048, DFF=5632, TOKW=1536, MEMW=512, NM=256, G=96, R=512, H=12, SEQ=4096, B=4, L=4, Tp=1024):
        self.D, self.DFF, self.TOKW, self.MEMW, self.NM, self.G, self.R, self.H = D, DFF, TOKW, MEMW, NM, G, R, H
        self.SEQ, self.B, self.L, self.Tp = SEQ, B, L, Tp
        self.T = SEQ // 2
        self.KT = D // 128
        self.NA = L // 2
        self.NP = G // 2
        self.NQ = G // 8
        self.RK = R // 128
        self.NTK = TOKW // 128


def gain_layout(cfg, norms, mem_norm, kv_in_norm, kv_norm, mla_q_norm, s5_b_glu):
    cols, off = [], {}

    def add(name, v):
        v = np.asarray(v, np.float32)
        n = v.shape[-1] // 128
        a = v.reshape(-1, n, 128)
        a = np.transpose(a, (2, 0, 1)).reshape(128, -1)
        off[name] = (sum(c.shape[1] for c in cols), n)
        cols.append(a)
    add("norms", norms)
    add("mem_norm", mem_norm)
    add("kv_in", kv_in_norm)
    add("kv", kv_norm)
    add("q", mla_q_norm)
    add("bglu", s5_b_glu)
    return np.ascontiguousarray(np.concatenate(cols, 1)), off


def build_segment(cfg, seg, W):
    c = cfg
    nc = bass.Bass("TRN2", target_bir_lowering=False)
    D, T, Tp = c.D, c.T, c.Tp

    def din(name, shape, dt=F32):
        return nc.dram_tensor(name, list(shape), dt, kind="ExternalInput").ap()

    def dout(name, shape, dt=F32):
        return nc.dram_tensor(name, list(shape), dt, kind="ExternalOutput").ap()

    def dtmp(name, shape, dt=F32):
        return nc.dram_tensor(name, list(shape), dt, kind="Internal").ap()
    K = KB(nc)
    P = K.P
    gshape = W["gains"].shape
    gains_d = din("gains", gshape)
    goff = W["goff"]
    gains = P.sb(list(gshape), F32, "gains")
    b_g = P.buf("gains")
    K.dma(gains[:], gains_d, [], [b_g])

    def gn(l, i):
        o = goff["norms"][0] + (l * 6 + i) * c.KT
        return gains[:, o:o + c.KT]

    def gsl(name, idx, n):
        o = goff[name][0] + idx * n
        return gains[:, o:o + n]
    x_in = din("x_in", [D, T])
    x = dout("x_out", [D, T])
    b_xc = P.buf("xcopy")
    K.dma(x, x_in, [], [b_xc])
    P.barrier()
    wts = {}

    def wt(name, l=None, j=None):
        key = (name, l, j)
        if key not in wts:
            a = W[name]
            shp = a.shape
            if l is not None:
                shp = shp[1:]
            if j is not None:
                shp = shp[1:]
            nm_ = f"{name}_{l}_{j}".replace("None", "x")
            wts[key] = (nm_, din(nm_, shp))
        return wts[key][1]

    def ffn(l, i):
        ffn_stage(K, x, gn(l, 2 * i if i == 0 else 4), gn(l, 1 if i == 0 else 5), b_g,
                  wt("ffn_w_gate", l, i), wt("ffn_w_up", l, i), wt("ffn_w_down", l, i), D, c.DFF, T, Tp)

    def s5prm(l):
        return dict(lam_s=din(f"s5lam_s{l}", [128, 3, c.NP]), lam_r=din(f"s5lam_r{l}", [128, 3, c.NP * 128]),
                    bT=din(f"s5bT{l}", [2, 128, c.NP, 128]), cP=din(f"s5cP{l}", [2, 128, c.NP, 128]))
    mem_d = din("memT", [D, c.NM])

    def pre_A(l, u_d, memo_d):
        mk = P.mark()
        MKV = mem_kv_setup(K, mem_d, gsl("mem_norm", l, c.KT), b_g, wt("mem_w_kv", l), D, c.NM, c.MEMW)
        mixer_pre_A(K, x, gn(l, 2), b_g, wt("a_w_in", l), u_d, memo_d, MKV, D, c.TOKW, c.MEMW, c.NM, T, Tp)
        P.release(mk)

    def post_A(l, yg_d, memo_d):
        mixer_post(K, x, gn(l, 3), b_g, wt("w_out", l), yg_d, memo_d, D, c.TOKW, c.MEMW, T, Tp,
                   W_glu=wt("s5_w_glu", l), bglu=gsl("bglu", l, c.NTK))

    if seg in (1, 2, 3):
        l_scan1 = {1: 0, 2: 1, 3: None}[seg]
        l_scan2 = {1: None, 2: 0, 3: 1}[seg]
        if l_scan2 is not None:
            l = l_scan2
            u_d = din("u_in", [c.TOKW, T], BF16)
            memo_d = din("memo_in", [c.MEMW, T], BF16)
            carry_in = din("carry_in", [128, 2, c.NP])
            carry_dummy = dtmp("carry_dummy", [128, 2, c.NP])
            yg_d = dtmp("yg", [c.TOKW, T], BF16)
            mk = P.mark()
            S = s5_setup(K, s5prm(l), c.NP)
            s5_scan(K, S, c.NP, T, u_d, carry_in, carry_dummy, yg_d, din(f"s5d{l}", [128, c.NQ]), True)
            P.release(mk)
            post_A(l, yg_d, memo_d)
            ffn(l, 1)
        if l_scan1 is not None:
            l = l_scan1
            ffn(l, 0)
            u_o = dout("u_out", [c.TOKW, T], BF16)
            memo_o = dout("memo_out", [c.MEMW, T], BF16)
            carry_o = dout("carry_out", [128, 2, c.NP])
            zero_c = din("zero_carry", [128, 2, c.NP])
            pre_A(l, u_o, memo_o)
            mk = P.mark()
            S = s5_setup(K, s5prm(l), c.NP)
            s5_scan(K, S, c.NP, T, u_o, zero_c, carry_o, None, None, False)
            P.release(mk)
        if seg == 3:
            RT = rope_tables(K, din("pos", [T], I32), din("invf", [64, 1]), din("sgn", [64, 1]), T)
            kv_stage(K, x, gsl("kv_in", 0, c.KT), gsl("kv", 0, c.RK), b_g, wt("w_dkv"), wt("w_kr"), wt("w_uk"), wt("w_uv"), RT,
                     dout("kn_out", [c.H, 128, T], BF16), dout("kr_out", [64, T], BF16), dout("v_out", [T, c.H * 128], BF16),
                     D, c.R, c.H, T, Tp)
    else:
        RT = rope_tables(K, din("pos", [T], I32), din("invf", [64, 1]), din("sgn", [64, 1]), T)
        kn_d, kr_d, v_d = din("kn", [c.H, 128, T], BF16), din("kr", [64, T], BF16), din("v", [T, c.H * 128], BF16)
        knp_d, krp_d, vp_d = din("knp", [c.H, 128, T], BF16), din("krp", [64, T], BF16), din("vp", [T, c.H * 128], BF16)
        pbias_d = din("pbias", [128, 1])
        mask_d = din("cmask", [4, 128, NT])
        qn_d = dtmp("qn", [c.H, 128, T], BF16)
        qr_d = dtmp("qr", [c.H, 64, T], BF16)
        memo_d = dtmp("memo", [c.MEMW, T], BF16)
        tok_d = dtmp("tok", [c.TOKW, T], BF16)
        for l in range(c.NA, c.L):
            j = l - c.NA
            ffn(l, 0)
            mk = P.mark()
            MKV = mem_kv_setup(K, mem_d, gsl("mem_norm", l, c.KT), b_g, wt("mem_w_kv", l), D, c.NM, c.MEMW)
            mixer_pre_B(K, x, gn(l, 2), gsl("q", j, c.RK), b_g, wt("b_w_in", j), wt("mla_w_uq", j), RT, qn_d, qr_d, memo_d, MKV,
                        D, c.R, c.H, c.MEMW, c.NM, T, Tp)
            P.release(mk)
            mla_attn(K, qn_d, qr_d, kn_d, kr_d, v_d, knp_d, krp_d, vp_d, pbias_d, mask_d, tok_d, c.H, T)
            mixer_post(K, x, gn(l, 3), b_g, wt("w_out", l), tok_d, memo_d, D, c.TOKW, c.MEMW, T, Tp)
            ffn(l, 1)
    P.barrier()
    P.emit()
    return nc, wts


def run_model(cfg, inp, dbg=None):
    c = cfg
    T = c.T
    NCORE = 2 * c.B
    f32 = np.float32
    gains, goff = gain_layout(c, inp["norms"], inp["mem_norm"], inp["kv_in_norm"], inp["kv_norm"], inp["mla_q_norm"], inp["s5_b_glu"])
    W = dict(inp)
    W["gains"], W["goff"] = gains, goff
    s5l = [s5_host_layout(*(np.asarray(inp[k][l], f32) for k in ("s5_lambda_re", "s5_lambda_im", "s5_log_dt", "s5_b_re", "s5_b_im",
                                                                 "s5_c_re", "s5_c_im", "s5_d"))) for l in range(c.NA)]
    xT = [np.ascontiguousarray(np.asarray(inp["x"][cid // 2, (cid % 2) * T:(cid % 2 + 1) * T, :], f32).T) for cid in range(NCORE)]
    memT = [np.ascontiguousarray(np.asarray(inp["mem"][b], f32).T) for b in range(c.B)]
    inv_freq = (10000.0 ** (-np.arange(0, 64, 2, dtype=np.float32) / 64)).astype(f32)
    invf = np.concatenate([inv_freq, inv_freq])[:, None].astype(f32) / f32(2 * np.pi)
    sgn = np.concatenate([-np.ones(32, f32), np.ones(32, f32)])[:, None]
    kk, qq = np.arange(128)[:, None], np.arange(NT)[None, :]
    cmask = np.stack([(qq >= 128 * i + kk).astype(f32) for i in range(4)], 0)
    zero_carry = np.zeros((128, 2, c.NP), f32)

    def launch(seg, per_core):
        nc, wts = build_segment(c, seg, W)
        shared = {"gains": gains}
        for (name, l, j), (nm_, ap) in wts.items():
            a = inp[name]
            if l is not None:
                a = a[l]
            if j is not None:
                a = a[j]
            shared[nm_] = np.ascontiguousarray(np.asarray(a, f32))
        maps = []
        for cid in range(NCORE):
            m = dict(shared)
            m["memT"] = memT[cid // 2]
            m.update(per_core[cid])
            maps.append(m)
        res = run_bass_kernel_spmd(nc, maps, core_ids=list(range(NCORE)))
        if dbg is not None:
            dbg[seg] = res.results
        return res.results

    def s5in(l):
        return {f"s5lam_s{l}": s5l[l]["lam_s"], f"s5lam_r{l}": s5l[l]["lam_r"], f"s5bT{l}": s5l[l]["bT"], f"s5cP{l}": s5l[l]["cP"]}
    pos = [np.ascontiguousarray(np.asarray(inp["positions"][cid // 2, (cid % 2) * T:(cid % 2 + 1) * T], np.int32)) for cid in range(NCORE)]
    r = launch(1, [dict(x_in=xT[cid], zero_carry=zero_carry, **s5in(0)) for cid in range(NCORE)])
    for seg in (2, 3):
        l2 = seg - 2
        pc = []
        for cid in range(NCORE):
            cin = r[cid - 1]["carry_out"] if cid % 2 == 1 else zero_carry
            d = dict(x_in=r[cid]["x_out"], u_in=r[cid]["u_out"], memo_in=r[cid]["memo_out"], carry_in=cin,
                     zero_carry=zero_carry, **s5in(l2))
            d[f"s5d{l2}"] = s5l[l2]["d_s"]
            if seg == 2:
                d.update(s5in(1))
            else:
                d.update(pos=pos[cid], invf=invf, sgn=sgn)
            pc.append(d)
        r = launch(seg, pc)
    pc = []
    for cid in range(NCORE):
        prev = r[cid - 1] if cid % 2 == 1 else r[cid]
        pc.append(dict(x_in=r[cid]["x_out"], kn=r[cid]["kn_out"], kr=r[cid]["kr_out"], v=r[cid]["v_out"],
                       knp=prev["kn_out"], krp=prev["kr_out"], vp=prev["v_out"],
                       pbias=np.full((128, 1), 0.0 if cid % 2 == 1 else -30000.0, f32), cmask=cmask,
                       pos=pos[cid], invf=invf, sgn=sgn))
    r = launch(4, pc)
    out = np.empty((c.B, c.SEQ, c.D), f32)
    for cid in range(NCORE):
        out[cid // 2, (cid % 2) * T:(cid % 2 + 1) * T, :] = r[cid]["x_out"].T
    return out


def build_fused(cfg, W):
    c = cfg
    nc = bass.Bass("TRN2", target_bir_lowering=False)
    D, T, Tp = c.D, c.T, c.Tp

    def din(name, shape, dt=F32):
        return nc.dram_tensor(name, list(shape), dt, kind="ExternalInput").ap()

    def dout(name, shape, dt=F32):
        return nc.dram_tensor(name, list(shape), dt, kind="ExternalOutput").ap()

    def dtmp(name, shape, dt=F32):
        return nc.dram_tensor(name, list(shape), dt, kind="Internal").ap()
    K = KB(nc)
    P = K.P
    gshape = W["gains"].shape
    goff = W["goff"]
    gains = P.sb(list(gshape), F32, "gains")
    b_g = P.buf("gains")
    K.dma(gains[:], din("gains", gshape), [], [b_g])
    cflag = P.sb([128, 1], F32, "cflag")
    K.dma(cflag[:], din("cflag", [128, 1]), [], [b_g])

    def gn(l, i):
        o = goff["norms"][0] + (l * 6 + i) * c.KT
        return gains[:, o:o + c.KT]

    def gsl(name, idx, n):
        o = goff[name][0] + idx * n
        return gains[:, o:o + n]
    x = dout("x_out", [D, T])
    xp = dtmp("xp", [D, T])
    b_xc = P.buf("xcopy")
    K.dma(x, din("x_in", [D, T]), [], [b_xc])
    K.dma(xp, din("xp_in", [D, T]), [], [b_xc])
    P.barrier()
    wts = {}

    def wt(name, l=None, j=None):
        key = (name, l, j)
        if key not in wts:
            shp = W[name].shape
            if l is not None:
                shp = shp[1:]
            if j is not None:
                shp = shp[1:]
            nm_ = f"{name}_{l}_{j}".replace("None", "x")
            wts[key] = (nm_, din(nm_, shp))
        return wts[key][1]

    def fspec(l, i):
        return (gn(l, 0 if i == 0 else 4), gn(l, 1 if i == 0 else 5), wt("ffn_w_gate", l, i), wt("ffn_w_up", l, i), wt("ffn_w_down", l, i))

    def ffn(l, i, xx):
        ffn_multi(K, xx, [fspec(l, i)], b_g, D, c.DFF, T, Tp)

    def ffn2(l, xx):
        ffn_multi(K, xx, [fspec(l, 1), fspec(l + 1, 0)], b_g, D, c.DFF, T, Tp)
    s5p = [dict(lam_s=din(f"s5lam_s{l}", [128, 3, c.NP]), lam_r=din(f"s5lam_r{l}", [128, 3, c.NP * 128]),
                bT=din(f"s5bT{l}", [2, 128, c.NP, 128]), cP=din(f"s5cP{l}", [2, 128, c.NP, 128]),
                d=din(f"s5d{l}", [128, c.NQ])) for l in range(c.NA)]
    mem_d = din("memT", [D, c.NM])
    u_d = dtmp("u", [c.TOKW, T], BF16)
    memo_d = dtmp("memo", [c.MEMW, T], BF16)
    yg_d = dtmp("yg", [c.TOKW, T], BF16)
    zero_c = din("zero_carry", [128, 2, c.NP])
    carryA = [dtmp(f"carryA{l}", [128, 2, c.NP]) for l in range(c.NA)]
    carry_dummy = dtmp("carry_dummy", [128, 2, c.NP])
    kn_d, kr_d, v_d = dtmp("kn", [c.H, 128, T], BF16), dtmp("kr", [64, T], BF16), dtmp("v", [T, c.H * 128], BF16)
    knp_d, krp_d, vp_d = dtmp("knp", [c.H, 128, T], BF16), dtmp("krp", [64, T], BF16), dtmp("vp", [T, c.H * 128], BF16)
    invf_d, sgn_d = din("invf", [64, 1]), din("sgn", [64, 1])

    streams = [dict(x=xp, u=u_d, memo=memo_d, yg=yg_d),
               dict(x=x, u=dtmp("u2", [c.TOKW, T], BF16), memo=dtmp("memo2", [c.MEMW, T], BF16), yg=dtmp("yg2", [c.TOKW, T], BF16))]

    def pre_scan(l, st, MKV):
        mixer_pre_A(K, st["x"], gn(l, 2), b_g, wt("a_w_in", l), st["u"], st["memo"], MKV, D, c.TOKW, c.MEMW, c.NM, T, Tp)

    def post_scan(l, st):
        mixer_post(K, st["x"], gn(l, 3), b_g, wt("w_out", l), st["yg"], st["memo"], D, c.TOKW, c.MEMW, T, Tp,
                   W_glu=wt("s5_w_glu", l), bglu=gsl("bglu", l, c.NTK))

    def kv(xx, pos_name, kn_, kr_, v_):
        mk = P.mark()
        RT = rope_tables(K, din(pos_name, [T], I32), invf_d, sgn_d, T)
        kv_stage(K, xx, gsl("kv_in", 0, c.KT), gsl("kv", 0, c.RK), b_g, wt("w_dkv"), wt("w_kr"), wt("w_uk"), wt("w_uv"), RT,
                 kn_, kr_, v_, D, c.R, c.H, T, Tp)
        return mk, RT
    for st in streams:
        ffn(0, 0, st["x"])
    for l in range(c.NA):
        mk = P.mark()
        MKV = mem_kv_setup(K, mem_d, gsl("mem_norm", l, c.KT), b_g, wt("mem_w_kv", l), D, c.NM, c.MEMW)
        for st in streams:
            pre_scan(l, st, MKV)
        P.release(mk)
        mk = P.mark()
        S = s5_setup(K, s5p[l], c.NP)
        s5_scan(K, S, c.NP, T, streams[0]["u"], zero_c, carryA[l], streams[0]["yg"], s5p[l]["d"], True)
        s5_scan(K, S, c.NP, T, streams[1]["u"], carryA[l], carry_dummy, streams[1]["yg"], s5p[l]["d"], True, flag=cflag, b_flag=b_g)
        P.release(mk)
        for si, st in enumerate(streams):
            post_scan(l, st)
            if l < c.NA - 1:
                ffn2(l, st["x"])
            else:
                ffn(l, 1, st["x"])
                if si == 0:
                    mk, _ = kv(xp, "posp", knp_d, krp_d, vp_d)
                    P.release(mk)
    mk, RT = kv(x, "pos", kn_d, kr_d, v_d)
    pbias_d = din("pbias", [128, 1])
    mask_d = din("cmask", [4, 128, NT])
    qn_d = dtmp("qn", [c.H, 128, T], BF16)
    qr_d = dtmp("qr", [c.H, 64, T], BF16)
    tok_d = dtmp("tok", [c.TOKW, T], BF16)
    for l in range(c.NA, c.L):
        j = l - c.NA
        if l == c.NA:
            ffn(l, 0, x)
        mk2 = P.mark()
        MKV = mem_kv_setup(K, mem_d, gsl("mem_norm", l, c.KT), b_g, wt("mem_w_kv", l), D, c.NM, c.MEMW)
        mixer_pre_B(K, x, gn(l, 2), gsl("q", j, c.RK), b_g, wt("b_w_in", j), wt("mla_w_uq", j), RT, qn_d, qr_d, memo_d, MKV,
                    D, c.R, c.H, c.MEMW, c.NM, T, Tp)
        P.release(mk2)
        mla_attn(K, qn_d, qr_d, kn_d, kr_d, v_d, knp_d, krp_d, vp_d, pbias_d, mask_d, tok_d, c.H, T)
        mixer_post(K, x, gn(l, 3), b_g, wt("w_out", l), tok_d, memo_d, D, c.TOKW, c.MEMW, T, Tp)
        if l == c.L - 1:
            ffn(l, 1, x)
        else:
            ffn2(l, x)
    P.barrier()
    P.emit()
    return nc, wts


def run_fused(cfg, inp):
    c = cfg
    T = c.T
    NCORE = 2 * c.B
    f32 = np.float32
    gains, goff = gain_layout(c, inp["norms"], inp["mem_norm"], inp["kv_in_norm"], inp["kv_norm"], inp["mla_q_norm"], inp["s5_b_glu"])
    W = dict(inp)
    W["gains"], W["goff"] = gains, goff
    nc, wts = build_fused(c, W)
    shared = {"gains": gains}
    for (name, l, j), (nm_, ap) in wts.items():
        a = inp[name]
        if l is not None:
            a = a[l]
        if j is not None:
            a = a[j]
        shared[nm_] = np.ascontiguousarray(np.asarray(a, f32))
    for l in range(c.NA):
        lay = s5_host_layout(*(np.asarray(inp[k][l], f32) for k in ("s5_lambda_re", "s5_lambda_im", "s5_log_dt", "s5_b_re", "s5_b_im",
                                                                    "s5_c_re", "s5_c_im", "s5_d")))
        shared.update({f"s5lam_s{l}": lay["lam_s"], f"s5lam_r{l}": lay["lam_r"], f"s5bT{l}": lay["bT"], f"s5cP{l}": lay["cP"],
                       f"s5d{l}": lay["d_s"]})
    inv_freq = (10000.0 ** (-np.arange(0, 64, 2, dtype=np.float32) / 64)).astype(f32)
    shared["invf"] = np.concatenate([inv_freq, inv_freq])[:, None].astype(f32) / f32(2 * np.pi)
    shared["sgn"] = np.concatenate([-np.ones(32, f32), np.ones(32, f32)])[:, None]
    kk, qq = np.arange(128)[:, None], np.arange(NT)[None, :]
    shared["cmask"] = np.stack([(qq >= 128 * i + kk).astype(f32) for i in range(4)], 0)
    shared["zero_carry"] = np.zeros((128, 2, c.NP), f32)
    xa = np.asarray(inp["x"], f32)
    pa = np.asarray(inp["positions"], np.int32)
    maps = []
    for cid in range(NCORE):
        b, hh = divmod(cid, 2)
        m = dict(shared)
        m["memT"] = np.ascontiguousarray(np.asarray(inp["mem"][b], f32).T)
        m["x_in"] = np.ascontiguousarray(xa[b, hh * T:(hh + 1) * T, :].T)
        m["xp_in"] = np.ascontiguousarray(xa[b, 0:T, :].T)
        m["pos"] = np.ascontiguousarray(pa[b, hh * T:(hh + 1) * T])
        m["posp"] = np.ascontiguousarray(pa[b, 0:T])
        m["cflag"] = np.full((128, 1), float(hh), f32)
        m["pbias"] = np.full((128, 1), 0.0 if hh == 1 else -30000.0, f32)
        maps.append(m)
    res = run_bass_kernel_spmd(nc, maps, core_ids=list(range(NCORE)))
    out = np.empty((c.B, c.SEQ, c.D), f32)
    for cid in range(NCORE):
        b, hh = divmod(cid, 2)
        out[b, hh * T:(hh + 1) * T, :] = res.results[cid]["x_out"].T
    return out


def kernel(**inputs):
    return run_fused(Cfg(), inputs)
```
